# Optimizing a Trainium2 kernel written in Bass

```python
import math
import jax, jax.numpy as jnp
from jax import lax
import numpy as np

D_MODEL = 1024
BATCH = 4
SEQ = 8192
DEPTH = 1

MEM_LEN = 256
MIX_WIDTH = D_MODEL
GDN_HEADS = 4
GDN_HEAD_DIM = 128
GDN_WIDTH = GDN_HEADS * GDN_HEAD_DIM
CFM_WIDTH = MIX_WIDTH - GDN_WIDTH
GDN_SHORT_CONV = 4
CHUNK = 64
CFM_KERNEL = 31
XATTN_HEADS = 4
XATTN_HEAD_DIM = D_MODEL // XATTN_HEADS
D_FF = 2816
FFN_CONV = 3
EPS = 1e-6
IN_COLS = 4 * GDN_WIDTH + 2 * GDN_HEADS + 2 * CFM_WIDTH

kernel_name = "hybrid_gdn_conformer_xattn_convffn"


def rmsnorm(x, g):
    xf = x.astype(jnp.float32)
    y = xf * lax.rsqrt(jnp.mean(xf * xf, axis=-1, keepdims=True) + EPS)
    return (y * g.astype(jnp.float32)).astype(x.dtype)


def layernorm(x, g, b):
    xf = x.astype(jnp.float32)
    mu = jnp.mean(xf, axis=-1, keepdims=True)
    xc = xf - mu
    var = jnp.mean(xc * xc, axis=-1, keepdims=True)
    return (xc * lax.rsqrt(var + EPS) * g.astype(jnp.float32) + b.astype(jnp.float32)).astype(x.dtype)


def l2norm(x):
    return x * lax.rsqrt(jnp.sum(x * x, axis=-1, keepdims=True) + EPS)


def causal_dwconv(x, w):
    K, C = w.shape
    return lax.conv_general_dilated(
        x, w[:, None, :].astype(x.dtype), window_strides=(1,), padding=[(K - 1, 0)],
        dimension_numbers=("NWC", "WIO", "NWC"), feature_group_count=C)


def gated_delta_rule_chunked(q, k, v, g, beta):
    B, T, H, Dk = q.shape
    Dv = v.shape[-1]
    N = T // CHUNK

    def to_chunks(a):
        a = a.reshape((B, N, CHUNK, H) + a.shape[3:])
        return jnp.moveaxis(a, (1, 3), (0, 2))

    qc = to_chunks(q * (Dk ** -0.5))
    kc = to_chunks(k)
    vc = to_chunks(v)
    bc = to_chunks(beta)
    gc = jnp.cumsum(to_chunks(g), axis=-1)

    idx = jnp.arange(CHUNK)
    causal = idx[:, None] >= idx[None, :]
    strict = idx[:, None] > idx[None, :]
    diff = gc[..., :, None] - gc[..., None, :]
    decay = jnp.where(causal, jnp.exp(jnp.where(causal, diff, 0.0)), 0.0)

    kb = kc * bc[..., None]
    L = jnp.where(strict, jnp.einsum('nbhid,nbhjd->nbhij', kb, kc) * decay, 0.0)
    eye = jnp.eye(CHUNK, dtype=jnp.float32)
    rhs = jnp.concatenate([vc * bc[..., None], kb * jnp.exp(gc)[..., None]], axis=-1)
    sol = lax.linalg.triangular_solve(eye + L, rhs, left_side=True, lower=True, unit_diagonal=True)
    u = sol[..., :Dv]
    w = sol[..., Dv:]
    attn_intra = jnp.einsum('nbhid,nbhjd->nbhij', qc, kc) * decay

    def step(S, inp):
        q_i, k_i, u_i, w_i, g_i, a_i = inp
        v_new = u_i - jnp.einsum('bhcd,bhde->bhce', w_i, S)
        o = (jnp.einsum('bhcd,bhde->bhce', q_i * jnp.exp(g_i)[..., None], S)
             + jnp.einsum('bhij,bhje->bhie', a_i, v_new))
        g_last = g_i[..., -1]
        k_dec = k_i * jnp.exp(g_last[..., None] - g_i)[..., None]
        S = S * jnp.exp(g_last)[..., None, None] + jnp.einsum('bhcd,bhce->bhde', k_dec, v_new)
        return S, o

    S0 = jnp.zeros((B, H, Dk, Dv), jnp.float32)
    _, o = lax.scan(step, S0, (qc, kc, u, w, gc, attn_intra))
    return jnp.moveaxis(o, (0, 2), (1, 3)).reshape(B, T, H, Dv)


def hybrid_mixer(h, w_in, gdn_conv_w, a_log, dt_bias, gdn_norm_g,
                 cfm_dw_w, cfm_dw_b, cfm_ln_g, cfm_ln_b, w_out):
    B, T, _ = h.shape
    z = h @ w_in
    s1 = 3 * GDN_WIDTH
    s2 = 4 * GDN_WIDTH
    s3 = s2 + GDN_HEADS
    s4 = s3 + GDN_HEADS
    qkv, gate, a_logit, b_logit, cfm_in = jnp.split(z, [s1, s2, s3, s4], axis=-1)

    qkv = jax.nn.silu(causal_dwconv(qkv, gdn_conv_w)).astype(jnp.float32)
    q, k, v = jnp.split(qkv, 3, axis=-1)
    shp = (B, T, GDN_HEADS, GDN_HEAD_DIM)
    q = l2norm(q.reshape(shp))
    k = l2norm(k.reshape(shp))
    v = v.reshape(shp)
    g = -jnp.exp(a_log.astype(jnp.float32)) * jax.nn.softplus(
        a_logit.astype(jnp.float32) + dt_bias.astype(jnp.float32))
    beta = jax.nn.sigmoid(b_logit.astype(jnp.float32))
    o = gated_delta_rule_chunked(q, k, v, g, beta)
    o = rmsnorm(o, gdn_norm_g) * jax.nn.silu(gate.astype(jnp.float32).reshape(shp))
    o_gdn = o.reshape(B, T, GDN_WIDTH).astype(h.dtype)

    c = jax.nn.glu(cfm_in, axis=-1)
    c = causal_dwconv(c, cfm_dw_w) + cfm_dw_b
    c = jax.nn.silu(layernorm(c, cfm_ln_g, cfm_ln_b))

    return jnp.concatenate([o_gdn, c], axis=-1) @ w_out


def memory_cross_attention(h, mem_n, w_q, w_kv, w_o):
    B, T, _ = h.shape
    M = mem_n.shape[1]
    q = (h @ w_q).reshape(B, T, XATTN_HEADS, XATTN_HEAD_DIM)
    k, v = jnp.split(mem_n @ w_kv, 2, axis=-1)
    k = k.reshape(B, M, XATTN_HEADS, XATTN_HEAD_DIM)
    v = v.reshape(B, M, XATTN_HEADS, XATTN_HEAD_DIM)
    s = jnp.einsum('bthd,bmhd->bhtm', q, k).astype(jnp.float32) * (XATTN_HEAD_DIM ** -0.5)
    p = jax.nn.softmax(s, axis=-1).astype(h.dtype)
    o = jnp.einsum('bhtm,bmhd->bthd', p, v).reshape(B, T, D_MODEL)
    return o @ w_o


def conv_glu_ffn(h, w_up, conv_w, conv_b, w_down):
    u = causal_dwconv(h @ w_up, conv_w) + conv_b
    a, b = jnp.split(u, 2, axis=-1)
    return (jax.nn.silu(a) * b) @ w_down


def setup_inputs(seed: int = 0) -> dict:
    key = jax.random.key(seed)
    ks = jax.random.split(key, 24)
    f32 = jnp.float32

    def nrm(k, shape, scale):
        return jax.random.normal(k, shape, f32) * scale

    x = nrm(ks[0], (BATCH, SEQ, D_MODEL), 1.0)
    mem = nrm(ks[1], (BATCH, MEM_LEN, D_MODEL), 1.0)
    norm_g = 1.0 + nrm(ks[2], (DEPTH, 6, D_MODEL), 0.05)
    w_in = nrm(ks[3], (DEPTH, D_MODEL, IN_COLS), D_MODEL ** -0.5)
    gdn_conv_w = nrm(ks[4], (DEPTH, GDN_SHORT_CONV, 3 * GDN_WIDTH), GDN_SHORT_CONV ** -0.5)
    gdn_a_log = jnp.log(jax.random.uniform(ks[5], (DEPTH, GDN_HEADS), f32, 1.0, 16.0))
    dt = jnp.exp(jax.random.uniform(ks[6], (DEPTH, GDN_HEADS), f32,
                                    math.log(1e-3), math.log(1e-1)))
    gdn_dt_bias = dt + jnp.log(-jnp.expm1(-dt))
    gdn_norm_g = 1.0 + nrm(ks[7], (DEPTH, GDN_HEAD_DIM), 0.05)
    cfm_dw_w = nrm(ks[8], (DEPTH, CFM_KERNEL, CFM_WIDTH), CFM_KERNEL ** -0.5)
    cfm_dw_b = nrm(ks[9], (DEPTH, CFM_WIDTH), 0.02)
    cfm_ln_g = 1.0 + nrm(ks[10], (DEPTH, CFM_WIDTH), 0.05)
    cfm_ln_b = nrm(ks[11], (DEPTH, CFM_WIDTH), 0.02)
    w_out = nrm(ks[12], (DEPTH, MIX_WIDTH, D_MODEL), MIX_WIDTH ** -0.5)
    mem_norm_g = 1.0 + nrm(ks[13], (DEPTH, D_MODEL), 0.05)
    xa_w_q = nrm(ks[14], (DEPTH, D_MODEL, D_MODEL), D_MODEL ** -0.5)
    xa_w_kv = nrm(ks[15], (DEPTH, D_MODEL, 2 * D_MODEL), D_MODEL ** -0.5)
    xa_w_o = nrm(ks[16], (DEPTH, D_MODEL, D_MODEL), D_MODEL ** -0.5)
    ffn_w_up = nrm(ks[17], (DEPTH, D_MODEL, 2 * D_FF), D_MODEL ** -0.5)
    ffn_conv_w = nrm(ks[18], (DEPTH, FFN_CONV, 2 * D_FF), FFN_CONV ** -0.5)
    ffn_conv_b = nrm(ks[19], (DEPTH, 2 * D_FF), 0.02)
    ffn_w_down = nrm(ks[20], (DEPTH, D_FF, D_MODEL), D_FF ** -0.5)
    return {"x": x, "mem": mem, "norm_g": norm_g, "w_in": w_in, "gdn_conv_w": gdn_conv_w,
            "gdn_a_log": gdn_a_log, "gdn_dt_bias": gdn_dt_bias, "gdn_norm_g": gdn_norm_g,
            "cfm_dw_w": cfm_dw_w, "cfm_dw_b": cfm_dw_b, "cfm_ln_g": cfm_ln_g, "cfm_ln_b": cfm_ln_b,
            "w_out": w_out, "mem_norm_g": mem_norm_g, "xa_w_q": xa_w_q, "xa_w_kv": xa_w_kv,
            "xa_w_o": xa_w_o, "ffn_w_up": ffn_w_up, "ffn_conv_w": ffn_conv_w,
            "ffn_conv_b": ffn_conv_b, "ffn_w_down": ffn_w_down}


def reference(x, mem, norm_g, w_in, gdn_conv_w, gdn_a_log, gdn_dt_bias, gdn_norm_g,
              cfm_dw_w, cfm_dw_b, cfm_ln_g, cfm_ln_b, w_out, mem_norm_g,
              xa_w_q, xa_w_kv, xa_w_o, ffn_w_up, ffn_conv_w, ffn_conv_b, ffn_w_down):
    h = x
    for l in range(DEPTH):
        g = norm_g[l]
        y = hybrid_mixer(rmsnorm(h, g[0]), w_in[l], gdn_conv_w[l], gdn_a_log[l], gdn_dt_bias[l],
                         gdn_norm_g[l], cfm_dw_w[l], cfm_dw_b[l], cfm_ln_g[l], cfm_ln_b[l], w_out[l])
        h = h + rmsnorm(y, g[1])
        y = memory_cross_attention(rmsnorm(h, g[2]), rmsnorm(mem, mem_norm_g[l]),
                                   xa_w_q[l], xa_w_kv[l], xa_w_o[l])
        h = h + rmsnorm(y, g[3])
        y = conv_glu_ffn(rmsnorm(h, g[4]), ffn_w_up[l], ffn_conv_w[l], ffn_conv_b[l], ffn_w_down[l])
        h = h + rmsnorm(y, g[5])
    return h
```

```python
import numpy as np
from contextlib import ExitStack
import concourse.bass as bass
import concourse.mybir as mybir
from concourse.bass_utils import run_bass_kernel_spmd

F32 = mybir.dt.float32
BF16 = mybir.dt.bfloat16
AF = mybir.ActivationFunctionType
ALU = mybir.AluOpType
AX = mybir.AxisListType

D = 1024
NT = 512
NB = NT // 128
H = 4
DFF = 2816
MEM = 256
EPS = 1e-6
NEGV = -30000.0
KC = 8

PO = {}
_o = 0
for _n, _w in [("ng", 48), ("gcw", 48), ("cdw", 124), ("cdb", 4), ("clg", 4), ("clb", 4),
               ("fcw", 132), ("fcb", 44), ("gng", 1), ("alog", 16), ("dtb", 16), ("mng", 8),
               ("flag", 1)]:
    PO[_n] = _o
    _o += _w
NPAR = _o

SL_IN, SL_OUT, SL_Q, SL_KV, SL_O, SL_UP, SL_DN = 0, 6, 8, 10, 14, 16, 27
NSLAB = 35


import os
STOP = float(os.environ.get("KSTOP", "1000"))


class StopBuild(Exception):
    pass


def ckpt(n):
    if n >= STOP:
        raise StopBuild()


class Buf:
    __slots__ = ("w", "rs", "const")

    def __init__(self):
        self.w = None
        self.rs = []
        self.const = False


class Tile:
    def __init__(self, t, bufs=None):
        self.t = t
        self.bufs = bufs if bufs is not None else [Buf()]
        self.h = None

    def split(self):
        self.bufs = [Buf(), Buf()]
        self.h = [Tile(self.t, [self.bufs[0]]), Tile(self.t, [self.bufs[1]])]
        return self

    def __getitem__(self, idx):
        return self.t[idx]


class Ins:
    __slots__ = ("eng", "fn", "idx", "dma_key", "dma_val", "deps", "needs_inc", "ordinal", "waits")

    def __init__(self, eng, fn, idx, dma_key):
        self.eng = eng
        self.fn = fn
        self.idx = idx
        self.dma_key = dma_key
        self.dma_val = None
        self.deps = []
        self.needs_inc = False
        self.ordinal = 0
        self.waits = []


class Prog:
    ENGS = ["pe", "act", "dve", "pool", "sp"]

    def __init__(self, nc, es):
        self.nc = nc
        self.es = es
        self.ins = {e: [] for e in self.ENGS}
        self.dma_cnt = {}
        self.dma_sem = {}
        self.barrier_keys = set()
        self.sem = {}
        self.nfresh = 0
        self.free_banks = []
        self.final = []

    def sb(self, name, shape, dt):
        return Tile(self.es.enter_context(self.nc.sbuf_tensor(name, list(shape), dt)))

    def mkbanks(self):
        for i in range(8):
            t = Tile(self.es.enter_context(self.nc.psum_tensor("bank%d" % i, [128, 512], F32)))
            self.free_banks.append(t)

    def bank(self):
        assert self.free_banks, "out of PSUM banks"
        return self.free_banks.pop(0)

    def free(self, *bs):
        for b in bs:
            self.free_banks.append(b)

    def _add(self, eng, fn, r, w, dma_key=None):
        ins = Ins(eng, fn, len(self.ins[eng]), dma_key)
        deps = {}
        rb = [b for t in r for b in t.bufs]
        wb = [b for t in w for b in t.bufs]
        for b in rb:
            if b.w is not None:
                deps[id(b.w)] = (b.w, True)
        for b in wb:
            if b.w is not None and id(b.w) not in deps:
                deps[id(b.w)] = (b.w, False)
            for x in b.rs:
                if id(x) not in deps:
                    deps[id(x)] = (x, False)
        for b in rb:
            if not b.const:
                b.rs.append(ins)
        for b in wb:
            b.w = ins
            b.rs = []
        ins.deps = [v for k, v in deps.items() if v[0] is not ins]
        self.ins[eng].append(ins)
        if dma_key is not None:
            if dma_key == "fresh":
                dma_key = "fresh%d" % self.nfresh
                self.nfresh += 1
                ins.dma_key = dma_key
            self.dma_cnt[dma_key] = self.dma_cnt.get(dma_key, 0) + 16
            ins.dma_val = self.dma_cnt[dma_key]
        return ins

    def pe(self, fn, r, w):
        return self._add("pe", fn, r, w)

    def act(self, fn, r, w):
        return self._add("act", fn, r, w)

    def dve(self, fn, r, w):
        return self._add("dve", fn, r, w)

    def pool(self, fn, r, w):
        return self._add("pool", fn, r, w)

    def dma(self, q, fn, r, w, key):
        return self._add(q, fn, r, w, dma_key=key)

    def resolve(self):
        for e in self.ENGS:
            for ins in self.ins[e]:
                for (y, raw) in ins.deps:
                    if y.dma_key is not None:
                        continue
                    if y.eng == e:
                        if e in ("act", "dve", "pool") and raw:
                            y.needs_inc = True
                        continue
                    y.needs_inc = True
        for e in self.ENGS:
            c = 0
            for ins in self.ins[e]:
                if ins.dma_key is None and ins.needs_inc:
                    c += 1
                    ins.ordinal = c
        for k in list(self.dma_cnt.keys()):
            self.dma_sem[k] = self.es.enter_context(self.nc.semaphore("d_" + k))
        for e in self.ENGS:
            self.sem[e] = self.es.enter_context(self.nc.semaphore("e_" + e))
        for e in self.ENGS:
            waited = {}
            for ins in self.ins[e]:
                ws = {}
                for (y, raw) in ins.deps:
                    if y.dma_key is not None:
                        k = ("d", y.dma_key)
                        v = self.dma_cnt[y.dma_key] if y.dma_key in self.barrier_keys else y.dma_val
                    else:
                        if y.eng == e and not (e in ("act", "dve", "pool") and raw):
                            continue
                        k = ("e", y.eng)
                        v = y.ordinal
                    if waited.get(k, 0) >= v:
                        continue
                    if ws.get(k, 0) < v:
                        ws[k] = v
                for k, v in ws.items():
                    waited[k] = v
                    sem = self.dma_sem[k[1]] if k[0] == "d" else self.sem[k[1]]
                    ins.waits.append((sem, v))

    def emit(self):
        self.resolve()
        nc = self.nc
        prog = self

        def run(e, eng):
            for ins in prog.ins[e]:
                for (sem, v) in ins.waits:
                    eng.wait_ge(sem, v)
                bi = ins.fn(eng)
                if ins.dma_key is not None:
                    bi.then_inc(prog.dma_sem[ins.dma_key], 16)
                elif ins.needs_inc:
                    bi.then_inc(prog.sem[e], 1)
            if e == "sp":
                for k in prog.final:
                    eng.wait_ge(prog.dma_sem[k], prog.dma_cnt[k])

        with nc.Block() as block:
            @block.tensor
            def _(eng):
                run("pe", eng)

            @block.scalar
            def _(eng):
                run("act", eng)

            @block.vector
            def _(eng):
                run("dve", eng)

            @block.gpsimd
            def _(eng):
                run("pool", eng)

            @block.sync
            def _(eng):
                run("sp", eng)


def build(NF, NS, dbg=False):
    NTOK = (NS + 1 + NF) * NT
    nc = bass.Bass("TRN2", target_bir_lowering=False)
    xin = nc.dram_tensor("xin", [NTOK, D], F32, kind="ExternalInput").ap()
    memin = nc.dram_tensor("memin", [MEM, D], F32, kind="ExternalInput").ap()
    params = nc.dram_tensor("params", [128, NPAR], F32, kind="ExternalInput").ap()
    consts = nc.dram_tensor("consts", [128, 4 * 128], F32, kind="ExternalInput").ap()
    w_in = nc.dram_tensor("w_in", [D, 3080], F32, kind="ExternalInput").ap()
    w_out = nc.dram_tensor("w_out", [D, D], F32, kind="ExternalInput").ap()
    w_q = nc.dram_tensor("w_q", [D, D], F32, kind="ExternalInput").ap()
    w_kv = nc.dram_tensor("w_kv", [D, 2 * D], F32, kind="ExternalInput").ap()
    w_o = nc.dram_tensor("w_o", [D, D], F32, kind="ExternalInput").ap()
    w_up = nc.dram_tensor("w_up", [D, 2 * DFF], F32, kind="ExternalInput").ap()
    w_dn = nc.dram_tensor("w_dn", [DFF, D], F32, kind="ExternalInput").ap()
    yout = nc.dram_tensor("yout", [NF * NT, D], F32, kind="ExternalOutput").ap()
    wscr = nc.dram_tensor("wscr", [NSLAB, 128, 4096], BF16, kind="Internal").ap()
    dbg_out = None
    if dbg:
        dbg_out = nc.dram_tensor("dbg", [128, 16, NT], F32, kind="ExternalOutput").ap()

    es = ExitStack()
    with es:
        P = Prog(nc, es)
        P.mkbanks()
        DR = Tile(None)

        cst = P.sb("cst", [128, 4, 128], F32)
        par = P.sb("par", [128, NPAR], F32)
        ident_b = P.sb("ident_b", [128, 128], BF16)
        identB4 = P.sb("identB4", [128, 4, 128], BF16)
        ones_b = P.sb("ones_b", [128, 128], BF16)
        ones_f = P.sb("ones_f", [128, 128], F32)
        epsc = P.sb("epsc", [128, 1], F32)
        lnc = P.sb("lnc", [128, 1], F32)
        expA = P.sb("expA", [128, 16], F32)
        wab = P.sb("wab", [128, KC, 8], BF16)
        NSLOT = 3
        slots = [P.sb("slot%d" % i, [128, 4096], BF16) for i in range(NSLOT)]
        KT = P.sb("KT", [128, 8, MEM], BF16)
        Vm = P.sb("Vm", [128, 2, D], BF16)
        hT = P.sb("hT", [128, KC, NT], F32)
        hnT = P.sb("hnT", [128, KC, NT], BF16)
        yT = P.sb("yT", [128, KC, NT], F32)
        xst = [P.sb("xst%d" % i, [128, D], F32) for i in range(2)]
        big = [P.sb("big%d" % i, [128, NT], BF16) for i in range(22)]
        zc = [P.sb("zc%d" % i, [128, 3 + NT], BF16) for i in range(4)]
        zhist = P.sb("zhist", [128, 12, 3], BF16)
        cbuf = [P.sb("cbuf%d" % i, [128, 30 + NT], BF16) for i in range(4)]
        ubuf = [P.sb("ubuf%d" % i, [128, 2 + NT], BF16) for i in range(4)]
        uhist = P.sb("uhist", [128, 44, 2], BF16)
        ocat = [P.sb("ocat%d" % i, [128, NT], BF16) for i in range(8)]
        sq = [P.sb("sq%d" % i, [128, NT], BF16) for i in range(2)]
        rstd = P.sb("rstd", [128, NT], F32)
        lnt = P.sb("lnt", [128, NT], F32)
        dg = [P.sb("dg%d" % i, [128, 128], BF16) for i in range(12)]
        rden = P.sb("rden", [128, NT], F32)
        msb = P.sb("msb", [128, NT], F32)
        var = rden
        sa = [P.sb("sa%d" % i, [128, NT], BF16) for i in range(2)]
        abtok = P.sb("abtok", [128, NB, 8], F32)
        ssq = P.sb("ssq", [128, NB, 8], F32)
        tk = {n: P.sb("tk_" + n, [128, NB, 4], F32) for n in
              ["a", "g", "lb", "lrq", "lrk", "gc", "gl", "vL", "vA", "bj", "b", "sw", "skd", "egl", "so", "t1", "t2"]}
        Sf = P.sb("Sf", [128, 4, 128], F32)
        Sb = P.sb("Sb", [128, 4, 128], BF16)
        NSET = 4
        BS = []
        for i in range(NSET):
            BS.append({n: P.sb("%s_%d" % (n, i), [128, 4, 128], dt) for n, dt in
                       [("kdec", BF16), ("bek", BF16), ("bv", BF16), ("M", BF16), ("M2", BF16), ("P", BF16), ("X", BF16),
                        ("attnT", BF16), ("usb", F32), ("wT", BF16)]})
        ELs = [P.sb("EL%d" % i, [128, 4, 128], BF16) for i in range(2)]
        EAs = [P.sb("EA%d" % i, [128, 4, 128], BF16) for i in range(2)]
        vnew = P.sb("vnew", [128, 4, 128], BF16)
        osb = P.sb("osb", [128, 4, 128], F32)
        scr4 = P.sb("scr4", [128, 4, 128], F32)
        o1s = scr4
        osq = scr4
        Stmp = P.sb("Stmp", [128, 4, 128], F32)
        ssqo = P.sb("ssqo", [128, 4], F32)
        rso = P.sb("rso", [128, 4], F32)
        onb = P.sb("onb", [128, 4, 128], BF16)
        ost = xst

        def pcol(name, i=0, n=1):
            o = PO[name] + i
            return par[:, o:o + n]

        ident_f = cst[:, 0, :]
        utri = cst[:, 1, :]
        negi = cst[:, 2, :]
        negs = cst[:, 3, :]

        P.dma("sp", lambda e: e.dma_start(out=cst[:], in_=consts.rearrange("p (a c) -> p a c", a=4)), [], [cst], "fresh")
        P.dma("sp", lambda e: e.dma_start(out=par[:], in_=params), [], [par], "fresh")
        P.dma("pool", lambda e: e.dma_start(out=wab[:], in_=w_in[:, 2048:2056].rearrange("(kc p) n -> p kc n", p=128)),
              [], [wab], "fresh")

        def conv_cols(W, s, nw, off, c0, n, r0=0, nk=KC):
            o = wscr[s, :, 0:nk * nw].rearrange("p (kc n) -> p kc n", n=nw)[:, :, off:off + n]
            i = W[r0:r0 + nk * 128, c0:c0 + n].rearrange("(kc p) n -> p kc n", p=128)
            early = s in (SL_IN + 1, SL_IN + 2, SL_KV, SL_KV + 1, SL_KV + 2, SL_KV + 3)
            conv_list.append((early, o, i))

        P.barrier_keys.add("wc1")
        P.barrier_keys.add("wc2")
        conv_list = []
        DR2 = Tile(None)
        for s in range(4):
            conv_cols(w_in, SL_IN + s, 512, 0, s * 512, 512)
        for s in range(2):
            for q in range(2):
                conv_cols(w_in, SL_IN + 4 + s, 512, q * 128, 2056 + (2 * s + q) * 128, 128)
                conv_cols(w_in, SL_IN + 4 + s, 512, 256 + q * 128, 2056 + 512 + (2 * s + q) * 128, 128)
        for s in range(4):
            conv_cols(w_kv, SL_KV + s, 512, 0, s * 512, 512)
        for s in range(2):
            conv_cols(w_out, SL_OUT + s, 512, 0, s * 512, 512)
            conv_cols(w_q, SL_Q + s, 512, 0, s * 512, 512)
            conv_cols(w_o, SL_O + s, 512, 0, s * 512, 512)
        for s in range(11):
            for q in range(2):
                conv_cols(w_up, SL_UP + s, 512, q * 128, (2 * s + q) * 128, 128)
                conv_cols(w_up, SL_UP + s, 512, 256 + q * 128, DFF + (2 * s + q) * 128, 128)
        for ng in range(4):
            for kh in range(2):
                conv_cols(w_dn, SL_DN + ng * 2 + kh, 256, 0, ng * 256, 256, r0=kh * 1408, nk=11)

        late_list = [c for c in conv_list if not c[0]]

        def issue_late(n):
            for _ in range(min(n, len(late_list))):
                (early, o, i) = late_list.pop(0)
                DR2.bufs[0].w = P.dma("pool", lambda e, o=o, i=i: e.dma_start(out=o, in_=i), [], [], "wc2")

        P.pool(lambda e: e.memset(ones_b[:], 1.0), [], [ones_b])
        P.pool(lambda e: e.memset(ones_f[:], 1.0), [], [ones_f])
        P.pool(lambda e: e.memset(epsc[:], EPS), [], [epsc])
        P.pool(lambda e: e.memset(lnc[:], -0.5 * float(np.log(128.0))), [], [lnc])
        P.pool(lambda e: e.memset(Sf[:], 0.0), [], [Sf])
        P.pool(lambda e: e.memset(Sb[:], 0.0), [], [Sb])
        P.pool(lambda e: e.memset(uhist[:], 0.0), [], [uhist])
        P.pool(lambda e: e.memset(zhist[:], 0.0), [], [zhist])
        for t in zc + cbuf + ubuf:
            P.pool(lambda e, t=t: e.memset(t[:], 0.0), [], [t])
        P.dve(lambda e: e.tensor_copy(out=ident_b[:], in_=ident_f), [cst], [ident_b])
        for h in range(4):
            P.dve(lambda e, h=h: e.tensor_copy(out=identB4[:, h, :], in_=ident_f), [cst], [identB4])
        P.act(lambda e: e.activation(out=expA[:], in_=pcol("alog", 0, 16), func=AF.Exp), [par], [expA])
        for t in (cst, par, ident_b, identB4, ones_b, ones_f, epsc, lnc, expA, wab):
            pass

        for (early, o, i) in [c for c in conv_list if c[0]]:
            DR.bufs[0].w = P.dma("pool", lambda e, o=o, i=i: e.dma_start(out=o, in_=i), [], [], "wc1")

        slot_ctr = [0]

        def load_slab(s):
            sl = slots[slot_ctr[0] % NSLOT]
            key = "slot%d" % (slot_ctr[0] % NSLOT)
            slot_ctr[0] += 1
            drt = DR if s in (SL_IN + 1, SL_IN + 2, SL_KV, SL_KV + 1, SL_KV + 2, SL_KV + 3) else DR2
            P.dma("sp", lambda e: e.dma_start(out=sl[:], in_=wscr[s]), [drt], [sl], key)
            return sl

        def slab_ap(sl, nw, nk=KC):
            return sl[:, 0:nk * nw].rearrange("p (kc n) -> p kc n", n=nw)

        ev_ctr = [0]

        def copy_ev(out_ap, in_ap, r, w, scale=None):
            ev_ctr[0] += 1
            if ev_ctr[0] % 2 == 0 and scale is None:
                P.dve(lambda e: e.tensor_copy(out=out_ap, in_=in_ap), r, w)
            else:
                if scale is None:
                    P.act(lambda e: e.activation(out=out_ap, in_=in_ap, func=AF.Copy), r, w)
                else:
                    P.act(lambda e: e.activation(out=out_ap, in_=in_ap, func=AF.Copy, scale=scale), r, w)

        def rsqrt_from(ps_ap, r, scale, out_t):
            P.act(lambda e: e.activation(out=lnt[:], in_=ps_ap, func=AF.Ln, bias=epsc[:], scale=scale), r + [epsc], [lnt])
            P.act(lambda e: e.activation(out=out_t[:], in_=lnt[:], func=AF.Exp, scale=-0.5), [lnt], [out_t])

        def prenorm(gi):
            st = P.bank()
            for c in range(KC):
                s = sq[c % 2]
                P.act(lambda e, c=c, s=s: e.activation(out=s[:], in_=hT[:, c, :], func=AF.Square), [hT], [s])
                P.pe(lambda e, c=c, s=s: e.matmul(st[:], lhsT=ones_b[:], rhs=s[:], start=(c == 0), stop=(c == KC - 1)),
                     [ones_b, s], [st])
            rsqrt_from(st[:], [st], 1.0 / D, rstd)
            P.free(st)
            for c in range(KC):
                P.dve(lambda e, c=c: e.scalar_tensor_tensor(out=hnT[:, c, :], in0=hT[:, c, :], scalar=pcol("ng", gi * 8 + c),
                                                            in1=rstd[:], op0=ALU.mult, op1=ALU.mult),
                      [hT, par, rstd], [hnT])

        def postnorm_residual(gi, st):
            rsqrt_from(st[:], [st], 1.0 / D, rstd)
            P.free(st)
            for c in range(KC):
                P.dve(lambda e, c=c: e.scalar_tensor_tensor(out=yT[:, c, :], in0=yT[:, c, :], scalar=pcol("ng", gi * 8 + c),
                                                            in1=rstd[:], op0=ALU.mult, op1=ALU.mult),
                      [yT, par, rstd], [yT])
                (P.pool if c % 2 == 0 else P.dve)(lambda e, c=c: e.tensor_tensor(out=hT[:, c, :], in0=hT[:, c, :], in1=yT[:, c, :], op=ALU.add),
                                                        [hT, yT], [hT])

        def proj_to_yT(slab_ids, rhs_list, nk_per=KC):
            st = P.bank()
            m = 0
            for s in slab_ids:
                sl = load_slab(s)
                sa_ = slab_ap(sl, 512)
                for q in range(4):
                    pb = P.bank()
                    for k in range(KC):
                        P.pe(lambda e, k=k, q=q, pb=pb, sa_=sa_: e.matmul(pb[:], lhsT=sa_[:, k, q * 128:(q + 1) * 128],
                                                                          rhs=rhs_list[k][0], start=(k == 0), stop=(k == KC - 1)),
                             [sl, rhs_list[k][1]], [pb])
                    s2 = sq[m % 2]
                    KV = int(os.environ.get("KVAR", "0"))
                    P.dve(lambda e, pb=pb, m=m: e.tensor_copy(out=yT[:, m, :], in_=pb[:]), [pb], [yT])
                    P.free(pb)
                    P.act(lambda e, m=m, s2=s2: e.activation(out=s2[:], in_=yT[:, m, :], func=AF.Square), [yT], [s2])
                    if KV not in (1, 2):
                        P.pe(lambda e, m=m, s2=s2: e.matmul(st[:], lhsT=ones_b[:], rhs=s2[:], start=(m == 0), stop=(m == 7)),
                             [ones_b, s2], [st])
                    m += 1
            return st

        dg_ctr = [0]

        def conv_mm(pb, src, src_t, ntap, wname, wbase):
            for k in range(ntap):
                d = dg[dg_ctr[0] % len(dg)]
                dg_ctr[0] += 1
                P.dve(lambda e, d=d, k=k: e.tensor_scalar(out=d[:], in0=ident_b[:], scalar1=pcol(wname, wbase + k), scalar2=None,
                                                           op0=ALU.mult),
                       [ident_b, par], [d])
                P.pe(lambda e, d=d, k=k: e.matmul(pb[:], lhsT=d[:], rhs=src[:, k:k + NT], start=(k == 0), stop=(k == ntap - 1)),
                     [d, src_t], [pb])

        if STOP > 0:
            for blk in range(2):
                xs = xst[blk % 2]
                P.dma("sp", lambda e, blk=blk, xs=xs: e.dma_start(out=xs[:], in_=memin[blk * 128:(blk + 1) * 128, :]), [], [xs],
                      "xst%d" % (blk % 2))
                for half in range(2):
                    pb = P.bank()
                    for q in range(4):
                        c = half * 4 + q
                        P.pe(lambda e, c=c, q=q, xs=xs, pb=pb: e.transpose(out=pb[:, q * 128:(q + 1) * 128], in_=xs[:, c * 128:(c + 1) * 128],
                                                                           identity=ident_f), [xs, cst], [pb])
                    P.dve(lambda e, half=half, blk=blk, pb=pb: e.tensor_copy(
                        out=yT[:, half * 4:half * 4 + 4, blk * 128:(blk + 1) * 128],
                        in_=pb[:].rearrange("p (q c) -> p q c", q=4)), [pb], [yT])
                    P.free(pb)
            st = P.bank()
            for c in range(KC):
                s = sq[c % 2]
                P.act(lambda e, c=c, s=s: e.activation(out=s[:, 0:MEM], in_=yT[:, c, 0:MEM], func=AF.Square), [yT], [s])
                P.pe(lambda e, c=c, s=s: e.matmul(st[:, 0:MEM], lhsT=ones_b[:], rhs=s[:, 0:MEM], start=(c == 0), stop=(c == KC - 1)),
                     [ones_b, s], [st])
            P.act(lambda e: e.activation(out=lnt[:, 0:MEM], in_=st[:, 0:MEM], func=AF.Ln, bias=epsc[:], scale=1.0 / D), [st, epsc], [lnt])
            P.act(lambda e: e.activation(out=rstd[:, 0:MEM], in_=lnt[:, 0:MEM], func=AF.Exp, scale=-0.5), [lnt], [rstd])
            P.free(st)
            for c in range(KC):
                P.dve(lambda e, c=c: e.scalar_tensor_tensor(out=hnT[:, c, 0:MEM], in0=yT[:, c, 0:MEM], scalar=pcol("mng", c),
                                                            in1=rstd[:, 0:MEM], op0=ALU.mult, op1=ALU.mult),
                      [yT, par, rstd], [hnT])
            for s in range(2):
                sl = load_slab(SL_KV + s)
                sa_ = slab_ap(sl, 512)
                for q in range(4):
                    pb = P.bank()
                    for k in range(KC):
                        P.pe(lambda e, k=k, q=q, pb=pb, sa_=sa_: e.matmul(pb[:, 0:MEM], lhsT=sa_[:, k, q * 128:(q + 1) * 128], rhs=hnT[:, k, 0:MEM],
                                                                          start=(k == 0), stop=(k == KC - 1)), [sl, hnT], [pb])
                    P.dve(lambda e, pb=pb, s=s, q=q: e.tensor_copy(out=KT[:, s * 4 + q, :], in_=pb[:, 0:MEM]), [pb], [KT])
                    P.free(pb)
            for s in range(2):
                sl = load_slab(SL_KV + 2 + s)
                sa_ = slab_ap(sl, 512)
                for mc in range(2):
                    pb = P.bank()
                    for k in range(KC):
                        P.pe(lambda e, k=k, mc=mc, pb=pb, sa_=sa_: e.matmul(pb[:], lhsT=hnT[:, k, mc * 128:(mc + 1) * 128], rhs=sa_[:, k, :],
                                                                            start=(k == 0), stop=(k == KC - 1)), [sl, hnT], [pb])
                    P.dve(lambda e, pb=pb, s=s, mc=mc: e.tensor_copy(out=Vm[:, mc, s * 512:(s + 1) * 512], in_=pb[:]), [pb], [Vm])
                    P.free(pb)

        def bc4(t, blk):
            return t[:, blk, :].unsqueeze(2).to_broadcast([128, 4, 128])

        def v3(b):
            return b[:].rearrange("p (h c) -> p h c", h=4)

        def vb3(b):
            return b[:].bitcast(BF16)[:, 0:512].rearrange("p (h c) -> p h c", h=4)

        def gdn_scalars(full):
            T = tk
            P.dve(lambda e: e.tensor_tensor(out=T["a"][:], in0=abtok[:, :, 0:4], in1=pcol("dtb", 0, 16).rearrange("p (b h) -> p b h", h=4),
                                            op=ALU.add), [abtok, par], [T["a"]])
            P.act(lambda e: e.activation(out=T["t1"][:], in_=T["a"][:], func=AF.Exp), [T["a"]], [T["t1"]])
            P.act(lambda e: e.activation(out=T["t2"][:], in_=T["t1"][:], func=AF.Ln, bias=1.0), [T["t1"]], [T["t2"]])
            P.dve(lambda e: e.scalar_tensor_tensor(out=T["g"][:], in0=T["t2"][:], scalar=-1.0, in1=expA[:].rearrange("p (b h) -> p b h", h=4),
                                                   op0=ALU.mult, op1=ALU.mult), [T["t2"], expA], [T["g"]])
            P.act(lambda e: e.activation(out=T["t1"][:], in_=abtok[:, :, 4:8], func=AF.Exp, scale=-1.0), [abtok], [T["t1"]])
            P.act(lambda e: e.activation(out=T["t2"][:], in_=T["t1"][:], func=AF.Ln, bias=1.0), [T["t1"]], [T["t2"]])
            P.dve(lambda e: e.tensor_scalar(out=T["lb"][:], in0=T["t2"][:], scalar1=-1.0, scalar2=None, op0=ALU.mult), [T["t2"]], [T["lb"]])
            P.act(lambda e: e.activation(out=T["t1"][:], in_=ssq[:, :, 4:8], func=AF.Ln, bias=epsc[:]), [ssq, epsc], [T["t1"]])
            P.dve(lambda e: e.tensor_scalar(out=T["lrk"][:], in0=T["t1"][:], scalar1=-0.5, scalar2=None, op0=ALU.mult), [T["t1"]], [T["lrk"]])
            if full:
                P.act(lambda e: e.activation(out=T["t1"][:], in_=ssq[:, :, 0:4], func=AF.Ln, bias=epsc[:]), [ssq, epsc], [T["t1"]])
                P.dve(lambda e: e.tensor_scalar(out=T["lrq"][:], in0=T["t1"][:], scalar1=-0.5, scalar2=None, op0=ALU.mult),
                      [T["t1"]], [T["lrq"]])
            pb = P.bank()
            g2 = T["g"][:].rearrange("p b h -> p (b h)")
            P.pe(lambda e: e.matmul(pb[:, 0:16], lhsT=utri, rhs=g2, start=True, stop=True), [cst, T["g"]], [pb])
            P.pe(lambda e: e.matmul(pb[:, 16:32], lhsT=ones_f[:], rhs=g2, start=True, stop=True), [ones_f, T["g"]], [pb])
            P.dve(lambda e: e.tensor_copy(out=T["gc"][:].rearrange("p b h -> p (b h)"), in_=pb[:, 0:16]), [pb], [T["gc"]])
            P.dve(lambda e: e.tensor_copy(out=T["gl"][:].rearrange("p b h -> p (b h)"), in_=pb[:, 16:32]), [pb], [T["gl"]])
            P.free(pb)
            P.dve(lambda e: e.tensor_tensor(out=T["bj"][:], in0=T["lrk"][:], in1=T["gc"][:], op=ALU.subtract), [T["lrk"], T["gc"]], [T["bj"]])
            P.dve(lambda e: e.tensor_tensor(out=T["t1"][:], in0=T["gc"][:], in1=T["lb"][:], op=ALU.add), [T["gc"], T["lb"]], [T["t1"]])
            P.dve(lambda e: e.tensor_tensor(out=T["vL"][:], in0=T["t1"][:], in1=T["lrk"][:], op=ALU.add), [T["t1"], T["lrk"]], [T["vL"]])
            P.act(lambda e: e.activation(out=T["b"][:], in_=T["lb"][:], func=AF.Exp), [T["lb"]], [T["b"]])
            P.act(lambda e: e.activation(out=T["sw"][:], in_=T["vL"][:], func=AF.Exp), [T["vL"]], [T["sw"]])
            P.dve(lambda e: e.tensor_tensor(out=T["t2"][:], in0=T["gl"][:], in1=T["bj"][:], op=ALU.add), [T["gl"], T["bj"]], [T["t2"]])
            P.act(lambda e: e.activation(out=T["skd"][:], in_=T["t2"][:], func=AF.Exp), [T["t2"]], [T["skd"]])
            P.act(lambda e: e.activation(out=T["egl"][:], in_=T["gl"][:], func=AF.Exp), [T["gl"]], [T["egl"]])
            if full:
                P.dve(lambda e: e.scalar_tensor_tensor(out=T["vA"][:], in0=T["gc"][:], scalar=lnc[:], in1=T["lrq"][:],
                                                       op0=ALU.add, op1=ALU.add), [T["gc"], lnc, T["lrq"]], [T["vA"]])
                P.act(lambda e: e.activation(out=T["so"][:], in_=T["vA"][:], func=AF.Exp), [T["vA"]], [T["so"]])

        def gdn_pre(blk, full, qT, kT_, vT_):
            T = tk
            B = BS[blk % NSET]
            kdec, bek, bv, Mx, Px, Xx, attnT, usb, wT = (B[n] for n in ("kdec", "bek", "bv", "M", "P", "X", "attnT", "usb", "wT"))
            EL = ELs[blk % 2]
            EA = EAs[blk % 2]
            cs = slice(blk * 128, (blk + 1) * 128)
            pk = P.bank()
            for h in range(4):
                P.pe(lambda e, h=h: e.transpose(out=pk[:].bitcast(BF16)[:, h * 128:(h + 1) * 128], in_=kT_[h][:, cs], identity=ident_b[:]),
                     [kT_[h], ident_b], [pk])
            P.dve(lambda e: e.tensor_tensor(out=kdec[:], in0=vb3(pk), in1=bc4(T["skd"], blk), op=ALU.mult), [pk, T["skd"]], [kdec])
            P.dve(lambda e: e.tensor_tensor(out=bek[:], in0=vb3(pk), in1=bc4(T["sw"], blk), op=ALU.mult), [pk, T["sw"]], [bek])
            P.free(pk)
            yield
            pv = P.bank()
            for h in range(4):
                P.pe(lambda e, h=h: e.transpose(out=pv[:].bitcast(BF16)[:, h * 128:(h + 1) * 128], in_=vT_[h][:, cs], identity=ident_b[:]),
                     [vT_[h], ident_b], [pv])
            P.dve(lambda e: e.tensor_tensor(out=bv[:], in0=vb3(pv), in1=bc4(T["b"], blk), op=ALU.mult), [pv, T["b"]], [bv])
            P.free(pv)
            yield

            def emat(vec, neg, out_t):
                pe_ = P.bank()
                for h in range(4):
                    o = pe_[:, h * 128:(h + 1) * 128]
                    P.pe(lambda e, h=h, o=o: e.matmul(o, lhsT=vec[:, blk, h:h + 1].to_broadcast([128, 128]), rhs=ident_f, start=True, stop=False),
                         [vec, cst], [pe_])
                    P.pe(lambda e, h=h, o=o: e.matmul(o, lhsT=ident_f, rhs=T["bj"][:, blk, h:h + 1].to_broadcast([128, 128]), start=False, stop=False),
                         [T["bj"], cst], [pe_])
                    P.pe(lambda e, h=h, o=o: e.matmul(o, lhsT=ident_f, rhs=neg, start=False, stop=True), [cst], [pe_])
                P.act(lambda e: e.activation(out=out_t[:].rearrange("p h c -> p (h c)"), in_=pe_[:], func=AF.Exp), [pe_], [out_t])
                P.free(pe_)

            emat(T["vL"], negs, EL)
            pl = P.bank()
            for h in range(4):
                P.pe(lambda e, h=h: e.matmul(pl[:, h * 128:(h + 1) * 128], lhsT=kT_[h][:, cs], rhs=kT_[h][:, cs], start=True, stop=True),
                     [kT_[h]], [pl])
            P.dve(lambda e: e.tensor_tensor(out=Mx[:], in0=v3(pl), in1=EL[:], op=ALU.mult), [pl, EL], [Mx])
            P.free(pl)
            yield
            if full:
                emat(T["vA"], negi, EA)
                pa = P.bank()
                for h in range(4):
                    P.pe(lambda e, h=h: e.matmul(pa[:, h * 128:(h + 1) * 128], lhsT=kT_[h][:, cs], rhs=qT[h][:, cs], start=True, stop=True),
                         [kT_[h], qT[h]], [pa])
                P.dve(lambda e: e.tensor_tensor(out=attnT[:], in0=v3(pa), in1=EA[:], op=ALU.mult), [pa, EA], [attnT])
                P.free(pa)
                yield
            pt = P.bank()
            for h in range(4):
                P.pe(lambda e, h=h: e.transpose(out=pt[:].bitcast(BF16)[:, h * 128:(h + 1) * 128], in_=Mx[:, h, :], identity=ident_b[:]),
                     [Mx, ident_b], [pt])
            P.act(lambda e: e.activation(out=Px[:], in_=vb3(pt), func=AF.Copy), [pt], [Px])
            P.free(pt)
            P.dve(lambda e: e.scalar_tensor_tensor(out=Xx[:], in0=Mx[:], scalar=-1.0, in1=identB4[:], op0=ALU.mult, op1=ALU.add),
                  [Mx, identB4], [Xx])
            yield
            Mc, Mo = Mx, B["M2"]
            for k in range(6):
                if k < 5:
                    pm = P.bank()
                    for h in range(4):
                        P.pe(lambda e, h=h, pm=pm, Mc=Mc: e.matmul(pm[:, h * 128:(h + 1) * 128], lhsT=Px[:, h, :], rhs=Mc[:, h, :], start=True, stop=True),
                             [Mc, Px], [pm])
                    yield
                    P.dve(lambda e, pm=pm, Mo=Mo: e.tensor_copy(out=Mo[:].rearrange("p h c -> p (h c)"), in_=pm[:]), [pm], [Mo])
                    P.free(pm)
                pp = P.bank()
                for h in range(4):
                    P.pe(lambda e, h=h, pp=pp, Mc=Mc: e.matmul(pp[:, h * 128:(h + 1) * 128], lhsT=Mc[:, h, :], rhs=Px[:, h, :], start=True, stop=True),
                         [Mc, Px], [pp])
                yield
                P.act(lambda e, pp=pp: e.activation(out=Px[:].rearrange("p h c -> p (h c)"), in_=pp[:], func=AF.Copy), [pp], [Px])
                P.free(pp)
                px = P.bank()
                for h in range(4):
                    P.pe(lambda e, h=h, px=px: e.matmul(px[:, h * 128:(h + 1) * 128], lhsT=Px[:, h, :], rhs=Xx[:, h, :], start=True, stop=True),
                         [Px, Xx], [px])
                yield
                P.dve(lambda e, px=px: e.tensor_tensor(out=Xx[:], in0=v3(px), in1=Xx[:], op=ALU.add), [px, Xx], [Xx])
                P.free(px)
                Mc, Mo = Mo, Mc
            AT = Xx
            pu = P.bank()
            for h in range(4):
                P.pe(lambda e, h=h: e.matmul(pu[:, h * 128:(h + 1) * 128], lhsT=AT[:, h, :], rhs=bv[:, h, :], start=True, stop=True), [AT, bv], [pu])
            P.act(lambda e: e.activation(out=usb[:].rearrange("p h c -> p (h c)"), in_=pu[:], func=AF.Copy), [pu], [usb])
            P.free(pu)
            yield
            pw = P.bank()
            for h in range(4):
                P.pe(lambda e, h=h: e.matmul(pw[:, h * 128:(h + 1) * 128], lhsT=bek[:, h, :], rhs=AT[:, h, :], start=True, stop=True), [AT, bek], [pw])
            P.dve(lambda e: e.tensor_copy(out=wT[:].rearrange("p h c -> p (h c)"), in_=pw[:]), [pw], [wT])
            P.free(pw)
            yield

        def gdn_scan_all(full, qT, sg):
            T = tk
            po = {}

            def main1(blk):
                B = BS[blk % NSET]
                pws = P.bank()
                for h in range(4):
                    P.pe(lambda e, h=h: e.matmul(pws[:, h * 128:(h + 1) * 128], lhsT=B["wT"][:, h, :], rhs=Sb[:, h, :], start=True, stop=True),
                         [B["wT"], Sb], [pws])
                P.dve(lambda e: e.tensor_tensor(out=vnew[:], in0=B["usb"][:], in1=v3(pws), op=ALU.subtract), [B["usb"], pws], [vnew])
                P.free(pws)

            def main2(blk):
                B = BS[blk % NSET]
                cs = slice(blk * 128, (blk + 1) * 128)
                pds = P.bank()
                for h in range(4):
                    P.pe(lambda e, h=h: e.matmul(pds[:, h * 128:(h + 1) * 128], lhsT=B["kdec"][:, h, :], rhs=vnew[:, h, :], start=True, stop=True),
                         [B["kdec"], vnew], [pds])
                if full:
                    po1 = P.bank()
                    po2 = P.bank()
                    for h in range(4):
                        P.pe(lambda e, h=h: e.matmul(po1[:, h * 128:(h + 1) * 128], lhsT=qT[h][:, cs], rhs=Sb[:, h, :], start=True, stop=True),
                             [qT[h], Sb], [po1])
                    for h in range(4):
                        P.pe(lambda e, h=h: e.matmul(po2[:, h * 128:(h + 1) * 128], lhsT=B["attnT"][:, h, :], rhs=vnew[:, h, :], start=True, stop=True),
                             [B["attnT"], vnew], [po2])
                    po[blk] = (po1, po2)
                P.pool(lambda e: e.tensor_tensor(out=Stmp[:], in0=Sf[:], in1=bc4(T["egl"], blk), op=ALU.mult), [Sf, T["egl"]], [Stmp])
                P.dve(lambda e: e.tensor_tensor(out=Sf[:], in0=v3(pds), in1=Stmp[:], op=ALU.add), [pds, Stmp], [Sf])
                P.free(pds)
                P.act(lambda e: e.activation(out=Sb[:], in_=Sf[:], func=AF.Copy), [Sf], [Sb])

            def tail1(blk):
                po1, po2 = po[blk]
                P.dve(lambda e: e.tensor_tensor(out=o1s[:], in0=v3(po1), in1=bc4(T["so"], blk), op=ALU.mult), [po1, T["so"]], [o1s])
                P.dve(lambda e: e.tensor_tensor(out=osb[:], in0=v3(po2), in1=o1s[:], op=ALU.add), [po2, o1s], [osb])
                P.free(po1, po2)
                P.pool(lambda e: e.tensor_tensor(out=osq[:], in0=osb[:], in1=osb[:], op=ALU.mult), [osb], [osq])
                P.dve(lambda e: e.tensor_reduce(out=ssqo[:], in_=osq[:], axis=AX.X, op=ALU.add), [osq], [ssqo])
                P.act(lambda e: e.activation(out=rso[:], in_=ssqo[:], func=AF.Ln, bias=epsc[:], scale=1.0 / 128), [ssqo, epsc], [rso])
                P.act(lambda e: e.activation(out=rso[:], in_=rso[:], func=AF.Exp, scale=-0.5), [rso], [rso])
                P.dve(lambda e: e.tensor_tensor(out=onb[:], in0=osb[:], in1=rso[:].unsqueeze(2).to_broadcast([128, 4, 128]), op=ALU.mult),
                      [osb, rso], [onb])

            def tail2(blk):
                cs = slice(blk * 128, (blk + 1) * 128)
                pot = P.bank()
                for h in range(4):
                    P.pe(lambda e, h=h: e.transpose(out=pot[:].bitcast(BF16)[:, h * 128:(h + 1) * 128], in_=onb[:, h, :], identity=ident_b[:]),
                         [onb, ident_b], [pot])
                for h in range(4):
                    P.dve(lambda e, h=h: e.scalar_tensor_tensor(out=ocat[h][:, cs], in0=pot[:].bitcast(BF16)[:, h * 128:(h + 1) * 128],
                                                                scalar=pcol("gng"), in1=sg[h][:, cs], op0=ALU.mult, op1=ALU.mult),
                          [pot, par, sg[h]], [ocat[h]])
                P.free(pot)

            for blk in range(NB):
                main1(blk)
                yield
                if full and blk >= 1:
                    tail1(blk - 1)
                    yield
                main2(blk)
                yield
                if full and blk >= 1:
                    tail2(blk - 1)
                    yield
            if full:
                tail1(NB - 1)
                yield
                tail2(NB - 1)
                yield

        def interleave(gens):
            gens = list(gens)
            while gens:
                for g in list(gens):
                    try:
                        next(g)
                    except StopIteration:
                        gens.remove(g)

        for t_ in [hT, hnT, yT, rstd, lnt, rden] + sq + big + ocat + ubuf + sa:
            t_.split()

        def prenorm_h(gi, c0, c1, hb):
            st = P.bank()
            for c in range(KC):
                s_ = sq[c % 2]
                P.act(lambda e, c=c, s_=s_: e.activation(out=s_[:, c0:c1], in_=hT[:, c, c0:c1], func=AF.Square), [hT.h[hb]], [s_.h[hb]])
                P.pe(lambda e, c=c, s_=s_: e.matmul(st[:, c0:c1], lhsT=ones_b[:], rhs=s_[:, c0:c1], start=(c == 0), stop=(c == KC - 1)),
                     [ones_b, s_.h[hb]], [st])
            P.act(lambda e: e.activation(out=lnt[:, c0:c1], in_=st[:, c0:c1], func=AF.Ln, bias=epsc[:], scale=1.0 / D), [st, epsc], [lnt.h[hb]])
            P.act(lambda e: e.activation(out=rstd[:, c0:c1], in_=lnt[:, c0:c1], func=AF.Exp, scale=-0.5), [lnt.h[hb]], [rstd.h[hb]])
            P.free(st)
            for c in range(KC):
                P.dve(lambda e, c=c: e.scalar_tensor_tensor(out=hnT[:, c, c0:c1], in0=hT[:, c, c0:c1], scalar=pcol("ng", gi * 8 + c),
                                                            in1=rstd[:, c0:c1], op0=ALU.mult, op1=ALU.mult),
                      [hT.h[hb], par, rstd.h[hb]], [hnT.h[hb]])

        def postnorm_h(gi, st, c0, c1, hb):
            P.act(lambda e: e.activation(out=lnt[:, c0:c1], in_=st[:, c0:c1], func=AF.Ln, bias=epsc[:], scale=1.0 / D), [st, epsc], [lnt.h[hb]])
            P.act(lambda e: e.activation(out=rstd[:, c0:c1], in_=lnt[:, c0:c1], func=AF.Exp, scale=-0.5), [lnt.h[hb]], [rstd.h[hb]])
            P.free(st)
            for c in range(KC):
                P.dve(lambda e, c=c: e.scalar_tensor_tensor(out=yT[:, c, c0:c1], in0=yT[:, c, c0:c1], scalar=pcol("ng", gi * 8 + c),
                                                            in1=rstd[:, c0:c1], op0=ALU.mult, op1=ALU.mult),
                      [yT.h[hb], par, rstd.h[hb]], [yT.h[hb]])
                (P.pool if c % 2 == 0 else P.dve)(lambda e, c=c: e.tensor_tensor(out=hT[:, c, c0:c1], in0=hT[:, c, c0:c1], in1=yT[:, c, c0:c1], op=ALU.add),
                                                        [hT.h[hb], yT.h[hb]], [hT.h[hb]])

        def proj_h(slab_ids, rhs_tiles, c0, c1, hb):
            st = P.bank()
            m = 0
            for s_ in slab_ids:
                sl = load_slab(s_)
                sa_ = slab_ap(sl, 512)
                for q in range(4):
                    pb = P.bank()
                    for k in range(KC):
                        P.pe(lambda e, k=k, q=q, pb=pb, sa_=sa_: e.matmul(pb[:, c0:c1], lhsT=sa_[:, k, q * 128:(q + 1) * 128],
                                                                          rhs=rhs_tiles[k][:, c0:c1], start=(k == 0), stop=(k == KC - 1)),
                             [sl, rhs_tiles[k].h[hb]], [pb])
                    s2 = sq[m % 2]
                    P.dve(lambda e, pb=pb, m=m: e.tensor_copy(out=yT[:, m, c0:c1], in_=pb[:, c0:c1]), [pb], [yT.h[hb]])
                    P.free(pb)
                    P.act(lambda e, m=m, s2=s2: e.activation(out=s2[:, c0:c1], in_=yT[:, m, c0:c1], func=AF.Square), [yT.h[hb]], [s2.h[hb]])
                    P.pe(lambda e, m=m, s2=s2: e.matmul(st[:, c0:c1], lhsT=ones_b[:], rhs=s2[:, c0:c1], start=(m == 0), stop=(m == 7)),
                         [ones_b, s2.h[hb]], [st])
                    m += 1
            return st

        def xattn_h(c0, c1, hb):
            qx = big[0:8]
            ox = big[8:16]
            m = 0
            for s_ in range(2):
                sl = load_slab(SL_Q + s_)
                sa_ = slab_ap(sl, 512)
                for q in range(4):
                    pb = P.bank()
                    for k in range(KC):
                        P.pe(lambda e, k=k, q=q, pb=pb, sa_=sa_: e.matmul(pb[:, c0:c1], lhsT=sa_[:, k, q * 128:(q + 1) * 128], rhs=hnT[:, k, c0:c1],
                                                                          start=(k == 0), stop=(k == KC - 1)), [sl, hnT.h[hb]], [pb])
                    P.act(lambda e, pb=pb, m=m: e.activation(out=qx[m][:, c0:c1], in_=pb[:, c0:c1], func=AF.Copy, scale=1.0 / 16.0), [pb], [qx[m].h[hb]])
                    P.free(pb)
                    m += 1
            for h in range(4):
                pts = [ubuf[(2 * h) % 4], ubuf[(2 * h + 1) % 4]]
                for mc in range(2):
                    pb = P.bank()
                    for dc in range(2):
                        P.pe(lambda e, dc=dc, mc=mc, h=h, pb=pb: e.matmul(pb[:, c0:c1], lhsT=KT[:, 2 * h + dc, mc * 128:(mc + 1) * 128],
                                                                          rhs=qx[2 * h + dc][:, c0:c1], start=(dc == 0), stop=(dc == 1)),
                             [KT, qx[2 * h + dc].h[hb]], [pb])
                    P.act(lambda e, pb=pb, mc=mc, pts=pts: e.activation(out=pts[mc][:, c0:c1], in_=pb[:, c0:c1], func=AF.Exp), [pb], [pts[mc].h[hb]])
                    P.free(pb)
                pd = P.bank()
                for mc in range(2):
                    P.pe(lambda e, mc=mc, pts=pts, pd=pd: e.matmul(pd[:, c0:c1], lhsT=ones_b[:], rhs=pts[mc][:, c0:c1], start=(mc == 0), stop=(mc == 1)),
                         [ones_b, pts[mc].h[hb]], [pd])
                P.dve(lambda e, pd=pd: e.reciprocal(out=rden[:, c0:c1], in_=pd[:, c0:c1]), [pd], [rden.h[hb]])
                P.free(pd)
                for dvc in range(2):
                    pb = P.bank()
                    for mc in range(2):
                        P.pe(lambda e, mc=mc, dvc=dvc, h=h, pb=pb, pts=pts: e.matmul(pb[:, c0:c1], lhsT=Vm[:, mc, (2 * h + dvc) * 128:(2 * h + dvc + 1) * 128],
                                                                                    rhs=pts[mc][:, c0:c1], start=(mc == 0), stop=(mc == 1)),
                             [Vm, pts[mc].h[hb]], [pb])
                    P.dve(lambda e, pb=pb, h=h, dvc=dvc: e.tensor_tensor(out=ox[2 * h + dvc][:, c0:c1], in0=pb[:, c0:c1], in1=rden[:, c0:c1], op=ALU.mult),
                          [pb, rden.h[hb]], [ox[2 * h + dvc].h[hb]])
                    P.free(pb)
            return proj_h([SL_O, SL_O + 1], ox, c0, c1, hb)

        def ffn_h(c0, c1, hb, only_up):
            curu = {}
            n = c1 - c0

            def ffn_up(i):
                s_, q = i // 2, i % 2
                if q == 0:
                    curu["sl"] = load_slab(SL_UP + s_)
                sl = curu["sl"]
                sa_ = slab_ap(sl, 512)
                ua, ub = ubuf[2 * (i % 2)], ubuf[2 * (i % 2) + 1]
                pa_ = P.bank()
                pb_ = P.bank()
                for k in range(KC):
                    P.pe(lambda e, k=k: e.matmul(pa_[:, c0:c1], lhsT=sa_[:, k, q * 128:(q + 1) * 128], rhs=hnT[:, k, c0:c1],
                                                 start=(k == 0), stop=(k == KC - 1)), [sl, hnT.h[hb]], [pa_])
                for k in range(KC):
                    P.pe(lambda e, k=k: e.matmul(pb_[:, c0:c1], lhsT=sa_[:, k, 256 + q * 128:256 + (q + 1) * 128], rhs=hnT[:, k, c0:c1],
                                                 start=(k == 0), stop=(k == KC - 1)), [sl, hnT.h[hb]], [pb_])
                P.pool(lambda e: e.tensor_copy(out=ua[:, c0:c0 + 2], in_=uhist[:, i, :]), [uhist], [ua.h[hb]])
                P.pool(lambda e: e.tensor_copy(out=ub[:, c0:c0 + 2], in_=uhist[:, 22 + i, :]), [uhist], [ub.h[hb]])
                P.act(lambda e: e.activation(out=ua[:, 2 + c0:2 + c1], in_=pa_[:, c0:c1], func=AF.Copy), [pa_], [ua.h[hb]])
                P.dve(lambda e: e.tensor_copy(out=ub[:, 2 + c0:2 + c1], in_=pb_[:, c0:c1]), [pb_], [ub.h[hb]])
                P.free(pa_, pb_)
                P.pool(lambda e: e.tensor_copy(out=uhist[:, i, :], in_=ua[:, c1:c1 + 2]), [ua.h[hb]], [uhist])
                P.pool(lambda e: e.tensor_copy(out=uhist[:, 22 + i, :], in_=ub[:, c1:c1 + 2]), [ub.h[hb]], [uhist])

            def conv_h(pb, src, wbase):
                for k in range(3):
                    d = dg[dg_ctr[0] % len(dg)]
                    dg_ctr[0] += 1
                    P.dve(lambda e, d=d, k=k: e.tensor_scalar(out=d[:], in0=ident_b[:], scalar1=pcol("fcw", wbase + k), scalar2=None, op0=ALU.mult),
                          [ident_b, par], [d])
                    P.pe(lambda e, d=d, k=k: e.matmul(pb[:, c0:c1], lhsT=d[:], rhs=src[:, c0 + k:c1 + k], start=(k == 0), stop=(k == 2)),
                         [d, src.h[hb]], [pb])

            def ffn_act(i):
                ua, ub = ubuf[2 * (i % 2)], ubuf[2 * (i % 2) + 1]
                pca = P.bank()
                pcb = P.bank()
                conv_h(pca, ua, i * 3)
                conv_h(pcb, ub, (22 + i) * 3)
                sat = sa[i % 2]
                P.act(lambda e: e.activation(out=sat[:, c0:c1], in_=pca[:, c0:c1], func=AF.Silu, bias=pcol("fcb", i)), [pca, par], [sat.h[hb]])
                P.dve(lambda e: e.scalar_tensor_tensor(out=big[i][:, c0:c1], in0=pcb[:, c0:c1], scalar=pcol("fcb", 22 + i), in1=sat[:, c0:c1],
                                                       op0=ALU.add, op1=ALU.mult), [pcb, par, sat.h[hb]], [big[i].h[hb]])
                P.free(pca, pcb)

            if only_up:
                for i in range(22):
                    ffn_up(i)
                return None
            for i in range(23):
                if i < 22:
                    ffn_up(i)
                if i >= 1:
                    ffn_act(i - 1)
            st = P.bank()
            for ng in range(4):
                pbs = [P.bank(), P.bank()]
                for kh in range(2):
                    sl = load_slab(SL_DN + ng * 2 + kh)
                    sa_ = slab_ap(sl, 256, 11)
                    for m2 in range(2):
                        for k in range(11):
                            kk = kh * 11 + k
                            P.pe(lambda e, k=k, kk=kk, m2=m2, pbs=pbs, sa_=sa_: e.matmul(pbs[m2][:, c0:c1], lhsT=sa_[:, k, m2 * 128:(m2 + 1) * 128],
                                                                                        rhs=big[kk][:, c0:c1], start=(kk == 0), stop=(kk == 21)),
                                 [sl, big[kk].h[hb]], [pbs[m2]])
                for m2 in range(2):
                    m = ng * 2 + m2
                    s2 = sq[m % 2]
                    P.dve(lambda e, m=m, pb=pbs[m2]: e.tensor_copy(out=yT[:, m, c0:c1], in_=pb[:, c0:c1]), [pbs[m2]], [yT.h[hb]])
                    P.act(lambda e, s2=s2, m=m: e.activation(out=s2[:, c0:c1], in_=yT[:, m, c0:c1], func=AF.Square), [yT.h[hb]], [s2.h[hb]])
                    P.pe(lambda e, m=m, s2=s2: e.matmul(st[:, c0:c1], lhsT=ones_b[:], rhs=s2[:, c0:c1], start=(m == 0), stop=(m == 7)),
                         [ones_b, s2.h[hb]], [st])
                P.free(*pbs)
            return st

        def store_h(c0, c1, hb, out_idx):
            for blk in range(c0 // 128, c1 // 128):
                os_ = ost[blk % 2]
                for half in range(2):
                    pb = P.bank()
                    for q in range(4):
                        c = half * 4 + q
                        P.pe(lambda e, c=c, q=q, blk=blk, pb=pb: e.transpose(out=pb[:, q * 128:(q + 1) * 128], in_=hT[:, c, blk * 128:(blk + 1) * 128],
                                                                             identity=ident_f), [hT.h[hb], cst], [pb])
                    copy_ev(os_[:, half * 512:(half + 1) * 512], pb[:], [pb], [os_])
                    P.free(pb)
                r0 = out_idx * NT + blk * 128
                P.dma("sp", lambda e, os_=os_, r0=r0: e.dma_start(out=yout[r0:r0 + 128, :], in_=os_[:]), [os_], [], "xst%d" % (blk % 2))

        def back_schedule(halves, out_idx):
            halo = out_idx is None
            if len(halves) == 1:
                (c0, c1, hb) = halves[0]
                st = proj_h([SL_OUT, SL_OUT + 1], ocat, c0, c1, hb)
                postnorm_h(1, st, c0, c1, hb)
                prenorm_h(2, c0, c1, hb)
                st = xattn_h(c0, c1, hb)
                postnorm_h(3, st, c0, c1, hb)
                prenorm_h(4, c0, c1, hb)
                st = ffn_h(c0, c1, hb, halo)
                if not halo:
                    postnorm_h(5, st, c0, c1, hb)
                    store_h(c0, c1, hb, out_idx)
            else:
                A, B = halves
                stA = proj_h([SL_OUT, SL_OUT + 1], ocat, *A)
                postnorm_h(1, stA, *A)
                stB = proj_h([SL_OUT, SL_OUT + 1], ocat, *B)
                prenorm_h(2, *A)
                postnorm_h(1, stB, *B)
                ckpt(7)
                stA = xattn_h(*A)
                prenorm_h(2, *B)
                postnorm_h(3, stA, *A)
                stB = xattn_h(*B)
                prenorm_h(4, *A)
                postnorm_h(3, stB, *B)
                ckpt(8)
                stA = ffn_h(A[0], A[1], A[2], False)
                prenorm_h(4, *B)
                postnorm_h(5, stA, *A)
                stB = ffn_h(B[0], B[1], B[2], False)
                store_h(A[0], A[1], A[2], out_idx)
                postnorm_h(5, stB, *B)
                ckpt(9)
                store_h(B[0], B[1], B[2], out_idx)
            if halo:
                P.pool(lambda e: e.tensor_scalar(out=uhist[:], in0=uhist[:], scalar1=pcol("flag"), scalar2=None, op0=ALU.mult), [uhist, par], [uhist])

        def tile(ti, full, out_idx):
            tok0 = ti * NT
            for blk in range(NB):
                xs = xst[blk % 2]
                P.dma("sp", lambda e, blk=blk, xs=xs: e.dma_start(out=xs[:], in_=xin[tok0 + blk * 128: tok0 + (blk + 1) * 128, :]),
                      [], [xs], "xst%d" % (blk % 2))
                for half in range(2):
                    pb = P.bank()
                    for q in range(4):
                        c = half * 4 + q
                        P.pe(lambda e, c=c, q=q, xs=xs, pb=pb: e.transpose(out=pb[:, q * 128:(q + 1) * 128], in_=xs[:, c * 128:(c + 1) * 128],
                                                                           identity=ident_f), [xs, cst], [pb])
                    copy_ev(hT[:, half * 4:half * 4 + 4, blk * 128:(blk + 1) * 128], pb[:].rearrange("p (q c) -> p q c", q=4), [pb], [hT])
                    P.free(pb)
            ckpt(1)
            prenorm(0)
            ckpt(2)
            qs, ks, vs, sgs = big[0:4], big[4:8], big[8:12], big[12:16]
            chunks = []
            for s_, dst in ([(0, qs)] if full else []) + [(1, ks), (2, vs)]:
                for q in range(4):
                    chunks.append((s_, q, dst))
            cur = {}

            def qkv_proj(s_, q):
                if q == 0:
                    cur["sl"] = load_slab(SL_IN + s_)
                sl = cur["sl"]
                sa_ = slab_ap(sl, 512)
                ch = s_ * 4 + q
                pb = P.bank()
                for k in range(KC):
                    P.pe(lambda e, k=k: e.matmul(pb[:], lhsT=sa_[:, k, q * 128:(q + 1) * 128], rhs=hnT[:, k, :],
                                                 start=(k == 0), stop=(k == KC - 1)), [sl, hnT], [pb])
                z = zc[ch % 4]
                P.pool(lambda e: e.tensor_copy(out=z[:, 0:3], in_=zhist[:, ch, :]), [zhist], [z])
                copy_ev(z[:, 3:3 + NT], pb[:], [pb], [z])
                P.free(pb)
                P.pool(lambda e: e.tensor_copy(out=zhist[:, ch, :], in_=z[:, NT:NT + 3]), [z], [zhist])

            def qkv_conv(s_, q, dst):
                ch = s_ * 4 + q
                z = zc[ch % 4]
                pc = P.bank()
                conv_mm(pc, z, z, 4, "gcw", ch * 4)
                P.act(lambda e: e.activation(out=dst[q][:], in_=pc[:], func=AF.Silu), [pc], [dst[q]])
                P.free(pc)

            for i in range(len(chunks) + 1):
                if i < len(chunks):
                    qkv_proj(chunks[i][0], chunks[i][1])
                if i >= 1:
                    qkv_conv(*chunks[i - 1])
            pab = P.bank()
            for blk in range(NB):
                for k in range(KC):
                    P.pe(lambda e, k=k, blk=blk: e.matmul(pab[:, blk * 8:(blk + 1) * 8], lhsT=hnT[:, k, blk * 128:(blk + 1) * 128], rhs=wab[:, k, :],
                                                          start=(k == 0), stop=(k == KC - 1)), [hnT, wab], [pab])
            P.dve(lambda e: e.tensor_copy(out=abtok[:].rearrange("p b c -> p (b c)"), in_=pab[:, 0:NB * 8]), [pab], [abtok])
            P.free(pab)
            pss = P.bank()
            lst = ([(qs[h], h) for h in range(4)] if full else []) + [(ks[h], 4 + h) for h in range(4)]
            for i, (src, col) in enumerate(lst):
                s2 = sq[i % 2]
                P.pool(lambda e, src=src, s2=s2: e.tensor_tensor(out=s2[:], in0=src[:], in1=src[:], op=ALU.mult), [src], [s2])
                for blk in range(NB):
                    P.pe(lambda e, blk=blk, s2=s2, col=col: e.matmul(pss[:, blk * 8 + col:blk * 8 + col + 1], lhsT=s2[:, blk * 128:(blk + 1) * 128],
                                                                      rhs=ones_b[:, 0:1], start=True, stop=True), [s2, ones_b], [pss])
            if full:
                P.dve(lambda e: e.tensor_copy(out=ssq[:].rearrange("p b c -> p (b c)"), in_=pss[:, 0:NB * 8]), [pss], [ssq])
            else:
                P.dve(lambda e: e.tensor_copy(out=ssq[:, :, 4:8], in_=pss[:, 0:NB * 8].rearrange("p (b c) -> p b c", c=8)[:, :, 4:8]), [pss], [ssq])
            P.free(pss)
            ckpt(3)
            gdn_scalars(full)
            ckpt(4)

            def filler():
                sl = load_slab(SL_IN + 3)
                sa_ = slab_ap(sl, 512)
                for q in range(4):
                    pb = P.bank()
                    for k in range(KC):
                        P.pe(lambda e, k=k, q=q, pb=pb, sa_=sa_: e.matmul(pb[:], lhsT=sa_[:, k, q * 128:(q + 1) * 128], rhs=hnT[:, k, :],
                                                                          start=(k == 0), stop=(k == KC - 1)), [sl, hnT], [pb])
                        if k % 4 == 3:
                            yield
                    P.act(lambda e, pb=pb, q=q: e.activation(out=sgs[q][:], in_=pb[:], func=AF.Silu), [pb], [sgs[q]])
                    P.free(pb)
                for s_ in range(2):
                    sl = load_slab(SL_IN + 4 + s_)
                    sa_ = slab_ap(sl, 512)
                    for q in range(2):
                        ch = 2 * s_ + q
                        pa_ = P.bank()
                        pb_ = P.bank()
                        for k in range(KC):
                            P.pe(lambda e, k=k, q=q, pa_=pa_, sa_=sa_: e.matmul(pa_[:], lhsT=sa_[:, k, q * 128:(q + 1) * 128], rhs=hnT[:, k, :],
                                                                                start=(k == 0), stop=(k == KC - 1)), [sl, hnT], [pa_])
                            if k % 4 == 3:
                                yield
                        for k in range(KC):
                            P.pe(lambda e, k=k, q=q, pb_=pb_, sa_=sa_: e.matmul(pb_[:], lhsT=sa_[:, k, 256 + q * 128:256 + (q + 1) * 128], rhs=hnT[:, k, :],
                                                                                start=(k == 0), stop=(k == KC - 1)), [sl, hnT], [pb_])
                            if k % 4 == 3:
                                yield
                        cb = cbuf[ch]
                        P.pool(lambda e, cb=cb: e.tensor_copy(out=cb[:, 0:30], in_=cb[:, NT:NT + 30]), [cb], [cb])
                        P.act(lambda e, pb_=pb_: e.activation(out=lnt[:], in_=pb_[:], func=AF.Sigmoid), [pb_], [lnt])
                        P.dve(lambda e, pa_=pa_, cb=cb: e.tensor_tensor(out=cb[:, 30:30 + NT], in0=pa_[:], in1=lnt[:], op=ALU.mult), [pa_, lnt], [cb])
                        P.free(pa_, pb_)
                pmean = P.bank()
                pe2 = P.bank()
                for ch in range(4):
                    pc = P.bank()
                    for k in range(31):
                        d = dg[dg_ctr[0] % len(dg)]
                        dg_ctr[0] += 1
                        P.dve(lambda e, d=d, k=k, ch=ch: e.tensor_scalar(out=d[:], in0=ident_b[:], scalar1=pcol("cdw", ch * 31 + k), scalar2=None,
                                                                         op0=ALU.mult), [ident_b, par], [d])
                        P.pe(lambda e, d=d, k=k, ch=ch, pc=pc: e.matmul(pc[:], lhsT=d[:], rhs=cbuf[ch][:, k:k + NT], start=(k == 0), stop=(k == 30)),
                             [d, cbuf[ch]], [pc])
                        if k % 4 == 3:
                            yield
                    P.act(lambda e, pc=pc, ch=ch: e.activation(out=yT[:, ch, :], in_=pc[:], func=AF.Identity, bias=pcol("cdb", ch)), [pc, par], [yT])
                    s2 = sq[ch % 2]
                    P.act(lambda e, pc=pc, ch=ch, s2=s2: e.activation(out=s2[:], in_=pc[:], func=AF.Square, bias=pcol("cdb", ch)), [pc, par], [s2])
                    P.free(pc)
                    P.pe(lambda e, ch=ch: e.matmul(pmean[:], lhsT=ones_f[:], rhs=yT[:, ch, :], start=(ch == 0), stop=(ch == 3)), [ones_f, yT], [pmean])
                    P.pe(lambda e, ch=ch, s2=s2: e.matmul(pe2[:], lhsT=ones_b[:], rhs=s2[:], start=(ch == 0), stop=(ch == 3)), [ones_b, s2], [pe2])
                    yield
                P.act(lambda e: e.activation(out=msb[:], in_=pmean[:], func=AF.Copy, scale=1.0 / 512), [pmean], [msb])
                P.pool(lambda e: e.tensor_tensor(out=var[:], in0=msb[:], in1=msb[:], op=ALU.mult), [msb], [var])
                P.dve(lambda e: e.scalar_tensor_tensor(out=var[:], in0=pe2[:], scalar=1.0 / 512, in1=var[:], op0=ALU.mult, op1=ALU.subtract),
                      [pe2, var], [var])
                P.free(pmean, pe2)
                P.act(lambda e: e.activation(out=lnt[:], in_=var[:], func=AF.Ln, bias=epsc[:]), [var, epsc], [lnt])
                P.act(lambda e: e.activation(out=rstd[:], in_=lnt[:], func=AF.Exp, scale=-0.5), [lnt], [rstd])
                yield
                for ch in range(4):
                    P.pool(lambda e, ch=ch: e.tensor_tensor(out=yT[:, ch, :], in0=yT[:, ch, :], in1=msb[:], op=ALU.subtract), [yT, msb], [yT])
                    P.dve(lambda e, ch=ch: e.tensor_tensor(out=yT[:, ch, :], in0=yT[:, ch, :], in1=rstd[:], op=ALU.mult), [yT, rstd], [yT])
                    P.act(lambda e, ch=ch: e.activation(out=ocat[4 + ch][:], in_=yT[:, ch, :], func=AF.Silu, bias=pcol("clb", ch), scale=pcol("clg", ch)),
                          [yT, par], [ocat[4 + ch]])
                    yield

            fil = filler() if full else None

            def run_with_filler(gens):
                nonlocal fil
                gens = list(gens)
                while gens:
                    for g in list(gens):
                        try:
                            next(g)
                        except StopIteration:
                            gens.remove(g)
                    if fil is not None:
                        try:
                            next(fil)
                        except StopIteration:
                            fil = None

            run_with_filler([gdn_pre(blk, full, qs, ks, vs) for blk in range(NB)])
            ckpt(5)
            run_with_filler([gdn_scan_all(full, qs, sgs)])
            if fil is not None:
                for _ in fil:
                    pass
            if not full:
                return
            ckpt(6)
            if out_idx is None:
                halves = [(NT - 128, NT, 1)]
            else:
                halves = [(0, NT // 2, 0), (NT // 2, NT, 1)]
            back_schedule(halves, out_idx)

        try:
            ti = 0
            per = (len(late_list) + max(NS, 1) - 1) // max(NS, 1)
            for _ in range(NS):
                tile(ti, False, None)
                issue_late(per)
                ti += 1
            issue_late(len(late_list))
            tile(ti, True, None)
            ti += 1
            for i in range(NF):
                tile(ti, True, i)
                ti += 1
            P.final = ["xst0", "xst1"]
        except StopBuild:
            P.final = [k for k in P.dma_cnt.keys()]
        P.emit()
    return nc


def make_consts():
    c = np.zeros((128, 4, 128), np.float32)
    c[:, 0, :] = np.eye(128, dtype=np.float32)
    j = np.arange(128)[:, None]
    cc = np.arange(128)[None, :]
    c[:, 1, :] = (j <= cc).astype(np.float32)
    c[:, 2, :] = np.where(cc >= j, 0.0, NEGV)
    c[:, 3, :] = np.where(cc > j, 0.0, NEGV)
    return c.reshape(128, 512)


def make_params(inp, flag):
    p = np.zeros((128, NPAR), np.float32)

    def put(name, arr):
        p[:, PO[name]:PO[name] + arr.shape[1]] = arr

    def chunked(v):
        return np.ascontiguousarray(v.reshape(-1, 128).T)

    ng = inp["norm_g"][0]
    put("ng", np.concatenate([chunked(ng[i]) for i in range(6)], axis=1))
    gcw = inp["gdn_conv_w"][0]
    put("gcw", np.ascontiguousarray(gcw.reshape(4, 12, 128).transpose(2, 1, 0)).reshape(128, 48))
    cdw = inp["cfm_dw_w"][0]
    put("cdw", np.ascontiguousarray(cdw.reshape(31, 4, 128).transpose(2, 1, 0)).reshape(128, 124))
    put("cdb", chunked(inp["cfm_dw_b"][0]))
    put("clg", chunked(inp["cfm_ln_g"][0]))
    put("clb", chunked(inp["cfm_ln_b"][0]))
    fcw = inp["ffn_conv_w"][0]
    put("fcw", np.ascontiguousarray(fcw.reshape(3, 44, 128).transpose(2, 1, 0)).reshape(128, 132))
    put("fcb", chunked(inp["ffn_conv_b"][0]))
    put("gng", inp["gdn_norm_g"][0].reshape(128, 1))
    put("alog", np.tile(inp["gdn_a_log"][0][None, :], (128, 4)))
    put("dtb", np.tile(inp["gdn_dt_bias"][0][None, :], (128, 4)))
    put("mng", chunked(inp["mem_norm_g"][0]))
    p[:, PO["flag"]] = flag
    return p


_NC_CACHE = {}


def run(inputs, NF, NS, n_batch, dbg=False):
    inp = {k: np.asarray(v, dtype=np.float32) for k, v in inputs.items()}
    key = (NF, NS, dbg)
    if key not in _NC_CACHE:
        _NC_CACHE[key] = build(NF, NS, dbg)
    nc = _NC_CACHE[key]
    half = NF * NT
    nprev = (NS + 1) * NT
    consts = make_consts()
    in_maps = []
    for b in range(n_batch):
        for j in range(2):
            xall = np.zeros((nprev + half, D), np.float32)
            if j == 1:
                xall[:nprev] = inp["x"][b, 0:half]
            xall[nprev:] = inp["x"][b, j * half:(j + 1) * half]
            in_maps.append({
                "xin": xall, "memin": np.ascontiguousarray(inp["mem"][b]),
                "params": make_params(inp, float(j)), "consts": consts,
                "w_in": np.ascontiguousarray(inp["w_in"][0]), "w_out": np.ascontiguousarray(inp["w_out"][0]),
                "w_q": np.ascontiguousarray(inp["xa_w_q"][0]), "w_kv": np.ascontiguousarray(inp["xa_w_kv"][0]),
                "w_o": np.ascontiguousarray(inp["xa_w_o"][0]), "w_up": np.ascontiguousarray(inp["ffn_w_up"][0]),
                "w_dn": np.ascontiguousarray(inp["ffn_w_down"][0]),
            })
    res = run_bass_kernel_spmd(nc, in_maps, core_ids=list(range(len(in_maps))))
    out = np.zeros((n_batch, 2 * half, D), np.float32)
    for b in range(n_batch):
        for j in range(2):
            out[b, j * half:(j + 1) * half] = res.results[2 * b + j]["yout"]
    return out, res


def kernel(**inputs):
    out, _ = run(inputs, NF=8, NS=7, n_batch=4)
    return out
```

```python
import numpy as np
from contextlib import ExitStack
import concourse.bass as bass
import concourse.mybir as mybir
from concourse.bass_utils import run_bass_kernel_spmd

F32 = mybir.dt.float32
BF16 = mybir.dt.bfloat16
AF = mybir.ActivationFunctionType
ALU = mybir.AluOpType
AX = mybir.AxisListType

D = 1024
NT = 512
NB = NT // 128
H = 4
DFF = 2816
MEM = 256
EPS = 1e-6
NEGV = -30000.0
KC = 8

PO = {}
_o = 0
for _n, _w in [("ng", 48), ("gcw", 48), ("cdw", 124), ("cdb", 4), ("clg", 4), ("clb", 4),
               ("fcw", 132), ("fcb", 44), ("gng", 1), ("alog", 16), ("dtb", 16), ("mng", 8),
               ("flag", 1)]:
    PO[_n] = _o
    _o += _w
NPAR = _o

SL_IN, SL_OUT, SL_Q, SL_KV, SL_O, SL_UP, SL_DN = 0, 6, 8, 10, 14, 16, 27
NSLAB = 35


import os
STOP = float(os.environ.get("KSTOP", "1000"))


class StopBuild(Exception):
    pass


def ckpt(n):
    if n >= STOP:
        raise StopBuild()


class Buf:
    __slots__ = ("w", "rs", "const")

    def __init__(self):
        self.w = None
        self.rs = []
        self.const = False


class Tile:
    def __init__(self, t, bufs=None):
        self.t = t
        self.bufs = bufs if bufs is not None else [Buf()]
        self.h = None

    def split(self):
        self.bufs = [Buf(), Buf()]
        self.h = [Tile(self.t, [self.bufs[0]]), Tile(self.t, [self.bufs[1]])]
        return self

    def hv(self, hb):
        return self if hb is None else self.hv(hb)

    def __getitem__(self, idx):
        return self.t[idx]


class Ins:
    __slots__ = ("eng", "fn", "idx", "dma_key", "dma_val", "deps", "needs_inc", "ordinal", "waits")

    def __init__(self, eng, fn, idx, dma_key):
        self.eng = eng
        self.fn = fn
        self.idx = idx
        self.dma_key = dma_key
        self.dma_val = None
        self.deps = []
        self.needs_inc = False
        self.ordinal = 0
        self.waits = []


class Prog:
    ENGS = ["pe", "act", "dve", "pool", "sp"]

    def __init__(self, nc, es):
        self.nc = nc
        self.es = es
        self.ins = {e: [] for e in self.ENGS}
        self.dma_cnt = {}
        self.dma_sem = {}
        self.barrier_keys = set()
        self.sem = {}
        self.nfresh = 0
        self.free_banks = []
        self.final = []

    def sb(self, name, shape, dt):
        return Tile(self.es.enter_context(self.nc.sbuf_tensor(name, list(shape), dt)))

    def mkbanks(self):
        for i in range(8):
            t = Tile(self.es.enter_context(self.nc.psum_tensor("bank%d" % i, [128, 512], F32)))
            self.free_banks.append(t)

    def bank(self):
        assert self.free_banks, "out of PSUM banks"
        return self.free_banks.pop(0)

    def free(self, *bs):
        for b in bs:
            self.free_banks.append(b)

    def _add(self, eng, fn, r, w, dma_key=None):
        ins = Ins(eng, fn, len(self.ins[eng]), dma_key)
        deps = {}
        rb = [b for t in r for b in t.bufs]
        wb = [b for t in w for b in t.bufs]
        for b in rb:
            if b.w is not None:
                deps[id(b.w)] = (b.w, True)
        for b in wb:
            if b.w is not None and id(b.w) not in deps:
                deps[id(b.w)] = (b.w, False)
            for x in b.rs:
                if id(x) not in deps:
                    deps[id(x)] = (x, False)
        for b in rb:
            if not b.const:
                b.rs.append(ins)
        for b in wb:
            b.w = ins
            b.rs = []
        ins.deps = [v for k, v in deps.items() if v[0] is not ins]
        self.ins[eng].append(ins)
        if dma_key is not None:
            if dma_key == "fresh":
                dma_key = "fresh%d" % self.nfresh
                self.nfresh += 1
                ins.dma_key = dma_key
            self.dma_cnt[dma_key] = self.dma_cnt.get(dma_key, 0) + 16
            ins.dma_val = self.dma_cnt[dma_key]
        return ins

    def pe(self, fn, r, w):
        return self._add("pe", fn, r, w)

    def act(self, fn, r, w):
        return self._add("act", fn, r, w)

    def dve(self, fn, r, w):
        return self._add("dve", fn, r, w)

    def pool(self, fn, r, w):
        return self._add("pool", fn, r, w)

    def dma(self, q, fn, r, w, key):
        return self._add(q, fn, r, w, dma_key=key)

    def resolve(self):
        for e in self.ENGS:
            for ins in self.ins[e]:
                for (y, raw) in ins.deps:
                    if y.dma_key is not None:
                        continue
                    if y.eng == e:
                        if e in ("act", "dve", "pool") and raw:
                            y.needs_inc = True
                        continue
                    y.needs_inc = True
        for e in self.ENGS:
            c = 0
            for ins in self.ins[e]:
                if ins.dma_key is None and ins.needs_inc:
                    c += 1
                    ins.ordinal = c
        for k in list(self.dma_cnt.keys()):
            self.dma_sem[k] = self.es.enter_context(self.nc.semaphore("d_" + k))
        for e in self.ENGS:
            self.sem[e] = self.es.enter_context(self.nc.semaphore("e_" + e))
        for e in self.ENGS:
            waited = {}
            for ins in self.ins[e]:
                ws = {}
                for (y, raw) in ins.deps:
                    if y.dma_key is not None:
                        k = ("d", y.dma_key)
                        v = self.dma_cnt[y.dma_key] if y.dma_key in self.barrier_keys else y.dma_val
                    else:
                        if y.eng == e and not (e in ("act", "dve", "pool") and raw):
                            continue
                        k = ("e", y.eng)
                        v = y.ordinal
                    if waited.get(k, 0) >= v:
                        continue
                    if ws.get(k, 0) < v:
                        ws[k] = v
                for k, v in ws.items():
                    waited[k] = v
                    sem = self.dma_sem[k[1]] if k[0] == "d" else self.sem[k[1]]
                    ins.waits.append((sem, v))

    def emit(self):
        self.resolve()
        nc = self.nc
        prog = self

        def run(e, eng):
            for ins in prog.ins[e]:
                for (sem, v) in ins.waits:
                    eng.wait_ge(sem, v)
                bi = ins.fn(eng)
                if ins.dma_key is not None:
                    bi.then_inc(prog.dma_sem[ins.dma_key], 16)
                elif ins.needs_inc:
                    bi.then_inc(prog.sem[e], 1)
            if e == "sp":
                for k in prog.final:
                    eng.wait_ge(prog.dma_sem[k], prog.dma_cnt[k])

        with nc.Block() as block:
            @block.tensor
            def _(eng):
                run("pe", eng)

            @block.scalar
            def _(eng):
                run("act", eng)

            @block.vector
            def _(eng):
                run("dve", eng)

            @block.gpsimd
            def _(eng):
                run("pool", eng)

            @block.sync
            def _(eng):
                run("sp", eng)


def build(NF, NS, dbg=False):
    NTOK = (NS + 1 + NF) * NT
    nc = bass.Bass("TRN2", target_bir_lowering=False)
    xin = nc.dram_tensor("xin", [NTOK, D], F32, kind="ExternalInput").ap()
    memin = nc.dram_tensor("memin", [MEM, D], F32, kind="ExternalInput").ap()
    params = nc.dram_tensor("params", [128, NPAR], F32, kind="ExternalInput").ap()
    consts = nc.dram_tensor("consts", [128, 4 * 128], F32, kind="ExternalInput").ap()
    w_in = nc.dram_tensor("w_in", [D, 3080], F32, kind="ExternalInput").ap()
    w_out = nc.dram_tensor("w_out", [D, D], F32, kind="ExternalInput").ap()
    w_q = nc.dram_tensor("w_q", [D, D], F32, kind="ExternalInput").ap()
    w_kv = nc.dram_tensor("w_kv", [D, 2 * D], F32, kind="ExternalInput").ap()
    w_o = nc.dram_tensor("w_o", [D, D], F32, kind="ExternalInput").ap()
    w_up = nc.dram_tensor("w_up", [D, 2 * DFF], F32, kind="ExternalInput").ap()
    w_dn = nc.dram_tensor("w_dn", [DFF, D], F32, kind="ExternalInput").ap()
    yout = nc.dram_tensor("yout", [NF * NT, D], F32, kind="ExternalOutput").ap()
    wscr = nc.dram_tensor("wscr", [NSLAB, 128, 4096], BF16, kind="Internal").ap()
    dbg_out = None
    if dbg:
        dbg_out = nc.dram_tensor("dbg", [128, 16, NT], F32, kind="ExternalOutput").ap()

    es = ExitStack()
    with es:
        P = Prog(nc, es)
        P.mkbanks()
        DR = Tile(None)

        cst = P.sb("cst", [128, 4, 128], F32)
        par = P.sb("par", [128, NPAR], F32)
        ident_b = P.sb("ident_b", [128, 128], BF16)
        identB4 = P.sb("identB4", [128, 4, 128], BF16)
        ones_b = P.sb("ones_b", [128, 128], BF16)
        ones_f = P.sb("ones_f", [128, 128], F32)
        epsc = P.sb("epsc", [128, 1], F32)
        lnc = P.sb("lnc", [128, 1], F32)
        expA = P.sb("expA", [128, 16], F32)
        wab = P.sb("wab", [128, KC, 8], BF16)
        NSLOT = 3
        slots = [P.sb("slot%d" % i, [128, 4096], BF16) for i in range(NSLOT)]
        KT = P.sb("KT", [128, 8, MEM], BF16)
        Vm = P.sb("Vm", [128, 2, D], BF16)
        hT = P.sb("hT", [128, KC, NT], F32)
        hnT = P.sb("hnT", [128, KC, NT], BF16)
        yT = P.sb("yT", [128, KC, NT], F32)
        xst = [P.sb("xst%d" % i, [128, D], F32) for i in range(2)]
        big = [P.sb("big%d" % i, [128, NT], BF16) for i in range(22)]
        zc = [P.sb("zc%d" % i, [128, 3 + NT], BF16) for i in range(4)]
        zhist = P.sb("zhist", [128, 12, 3], BF16)
        cbuf = [P.sb("cbuf%d" % i, [128, 30 + NT], BF16) for i in range(4)]
        ubuf = [P.sb("ubuf%d" % i, [128, 2 + NT], BF16) for i in range(4)]
        uhist = P.sb("uhist", [128, 44, 2], BF16)
        ocat = [P.sb("ocat%d" % i, [128, NT], BF16) for i in range(8)]
        sq = [P.sb("sq%d" % i, [128, NT], BF16) for i in range(2)]
        rstd = P.sb("rstd", [128, NT], F32)
        lnt = P.sb("lnt", [128, NT], F32)
        dg = [P.sb("dg%d" % i, [128, 128], BF16) for i in range(12)]
        rden = P.sb("rden", [128, NT], F32)
        msb = P.sb("msb", [128, NT], F32)
        var = rden
        sa = [P.sb("sa%d" % i, [128, NT], BF16) for i in range(2)]
        abtok = P.sb("abtok", [128, NB, 8], F32)
        ssq = P.sb("ssq", [128, NB, 8], F32)
        tk = {n: P.sb("tk_" + n, [128, NB, 4], F32) for n in
              ["a", "g", "lb", "lrq", "lrk", "gc", "gl", "vL", "vA", "bj", "b", "sw", "skd", "egl", "so", "t1", "t2"]}
        Sf = P.sb("Sf", [128, 4, 128], F32)
        Sb = P.sb("Sb", [128, 4, 128], BF16)
        NSET = 4
        BS = []
        for i in range(NSET):
            BS.append({n: P.sb("%s_%d" % (n, i), [128, 4, 128], dt) for n, dt in
                       [("kdec", BF16), ("bek", BF16), ("bv", BF16), ("M", BF16), ("M2", BF16), ("P", BF16), ("X", BF16),
                        ("attnT", BF16), ("usb", F32), ("wT", BF16)]})
        ELs = [P.sb("EL%d" % i, [128, 4, 128], BF16) for i in range(2)]
        EAs = [P.sb("EA%d" % i, [128, 4, 128], BF16) for i in range(2)]
        vnew = P.sb("vnew", [128, 4, 128], BF16)
        osb = P.sb("osb", [128, 4, 128], F32)
        scr4 = P.sb("scr4", [128, 4, 128], F32)
        o1s = scr4
        osq = scr4
        Stmp = P.sb("Stmp", [128, 4, 128], F32)
        ssqo = P.sb("ssqo", [128, 4], F32)
        rso = P.sb("rso", [128, 4], F32)
        onb = P.sb("onb", [128, 4, 128], BF16)
        ost = xst

        def pcol(name, i=0, n=1):
            o = PO[name] + i
            return par[:, o:o + n]

        ident_f = cst[:, 0, :]
        utri = cst[:, 1, :]
        negi = cst[:, 2, :]
        negs = cst[:, 3, :]

        P.dma("sp", lambda e: e.dma_start(out=cst[:], in_=consts.rearrange("p (a c) -> p a c", a=4)), [], [cst], "fresh")
        P.dma("sp", lambda e: e.dma_start(out=par[:], in_=params), [], [par], "fresh")
        P.dma("pool", lambda e: e.dma_start(out=wab[:], in_=w_in[:, 2048:2056].rearrange("(kc p) n -> p kc n", p=128)),
              [], [wab], "fresh")

        def conv_cols(W, s, nw, off, c0, n, r0=0, nk=KC):
            o = wscr[s, :, 0:nk * nw].rearrange("p (kc n) -> p kc n", n=nw)[:, :, off:off + n]
            i = W[r0:r0 + nk * 128, c0:c0 + n].rearrange("(kc p) n -> p kc n", p=128)
            early = s in (SL_IN + 1, SL_IN + 2, SL_KV, SL_KV + 1, SL_KV + 2, SL_KV + 3)
            conv_list.append((early, o, i))

        P.barrier_keys.add("wc1")
        P.barrier_keys.add("wc2")
        conv_list = []
        DR2 = Tile(None)
        for s in range(4):
            conv_cols(w_in, SL_IN + s, 512, 0, s * 512, 512)
        for s in range(2):
            for q in range(2):
                conv_cols(w_in, SL_IN + 4 + s, 512, q * 128, 2056 + (2 * s + q) * 128, 128)
                conv_cols(w_in, SL_IN + 4 + s, 512, 256 + q * 128, 2056 + 512 + (2 * s + q) * 128, 128)
        for s in range(4):
            conv_cols(w_kv, SL_KV + s, 512, 0, s * 512, 512)
        for s in range(2):
            conv_cols(w_out, SL_OUT + s, 512, 0, s * 512, 512)
            conv_cols(w_q, SL_Q + s, 512, 0, s * 512, 512)
            conv_cols(w_o, SL_O + s, 512, 0, s * 512, 512)
        for s in range(11):
            for q in range(2):
                conv_cols(w_up, SL_UP + s, 512, q * 128, (2 * s + q) * 128, 128)
                conv_cols(w_up, SL_UP + s, 512, 256 + q * 128, DFF + (2 * s + q) * 128, 128)
        for ng in range(4):
            for kh in range(2):
                conv_cols(w_dn, SL_DN + ng * 2 + kh, 256, 0, ng * 256, 256, r0=kh * 1408, nk=11)

        late_list = [c for c in conv_list if not c[0]]

        def issue_late(n):
            for _ in range(min(n, len(late_list))):
                (early, o, i) = late_list.pop(0)
                DR2.bufs[0].w = P.dma("pool", lambda e, o=o, i=i: e.dma_start(out=o, in_=i), [], [], "wc2")

        P.pool(lambda e: e.memset(ones_b[:], 1.0), [], [ones_b])
        P.pool(lambda e: e.memset(ones_f[:], 1.0), [], [ones_f])
        P.pool(lambda e: e.memset(epsc[:], EPS), [], [epsc])
        P.pool(lambda e: e.memset(lnc[:], -0.5 * float(np.log(128.0))), [], [lnc])
        P.pool(lambda e: e.memset(Sf[:], 0.0), [], [Sf])
        P.pool(lambda e: e.memset(Sb[:], 0.0), [], [Sb])
        P.pool(lambda e: e.memset(uhist[:], 0.0), [], [uhist])
        P.pool(lambda e: e.memset(zhist[:], 0.0), [], [zhist])
        for t in zc + cbuf + ubuf:
            P.pool(lambda e, t=t: e.memset(t[:], 0.0), [], [t])
        P.dve(lambda e: e.tensor_copy(out=ident_b[:], in_=ident_f), [cst], [ident_b])
        for h in range(4):
            P.dve(lambda e, h=h: e.tensor_copy(out=identB4[:, h, :], in_=ident_f), [cst], [identB4])
        P.act(lambda e: e.activation(out=expA[:], in_=pcol("alog", 0, 16), func=AF.Exp), [par], [expA])
        for t in (cst, par, ident_b, identB4, ones_b, ones_f, epsc, lnc, expA, wab):
            pass

        for (early, o, i) in [c for c in conv_list if c[0]]:
            DR.bufs[0].w = P.dma("pool", lambda e, o=o, i=i: e.dma_start(out=o, in_=i), [], [], "wc1")

        slot_ctr = [0]

        def load_slab(s):
            sl = slots[slot_ctr[0] % NSLOT]
            key = "slot%d" % (slot_ctr[0] % NSLOT)
            slot_ctr[0] += 1
            drt = DR if s in (SL_IN + 1, SL_IN + 2, SL_KV, SL_KV + 1, SL_KV + 2, SL_KV + 3) else DR2
            P.dma("sp", lambda e: e.dma_start(out=sl[:], in_=wscr[s]), [drt], [sl], key)
            return sl

        def slab_ap(sl, nw, nk=KC):
            return sl[:, 0:nk * nw].rearrange("p (kc n) -> p kc n", n=nw)

        ev_ctr = [0]

        def copy_ev(out_ap, in_ap, r, w, scale=None):
            ev_ctr[0] += 1
            if ev_ctr[0] % 2 == 0 and scale is None:
                P.dve(lambda e: e.tensor_copy(out=out_ap, in_=in_ap), r, w)
            else:
                if scale is None:
                    P.act(lambda e: e.activation(out=out_ap, in_=in_ap, func=AF.Copy), r, w)
                else:
                    P.act(lambda e: e.activation(out=out_ap, in_=in_ap, func=AF.Copy, scale=scale), r, w)

        def rsqrt_from(ps_ap, r, scale, out_t):
            P.act(lambda e: e.activation(out=lnt[:], in_=ps_ap, func=AF.Ln, bias=epsc[:], scale=scale), r + [epsc], [lnt])
            P.act(lambda e: e.activation(out=out_t[:], in_=lnt[:], func=AF.Exp, scale=-0.5), [lnt], [out_t])

        def prenorm(gi):
            st = P.bank()
            for c in range(KC):
                s = sq[c % 2]
                P.act(lambda e, c=c, s=s: e.activation(out=s[:], in_=hT[:, c, :], func=AF.Square), [hT], [s])
                P.pe(lambda e, c=c, s=s: e.matmul(st[:], lhsT=ones_b[:], rhs=s[:], start=(c == 0), stop=(c == KC - 1)),
                     [ones_b, s], [st])
            rsqrt_from(st[:], [st], 1.0 / D, rstd)
            P.free(st)
            for c in range(KC):
                P.dve(lambda e, c=c: e.scalar_tensor_tensor(out=hnT[:, c, :], in0=hT[:, c, :], scalar=pcol("ng", gi * 8 + c),
                                                            in1=rstd[:], op0=ALU.mult, op1=ALU.mult),
                      [hT, par, rstd], [hnT])

        def postnorm_residual(gi, st):
            rsqrt_from(st[:], [st], 1.0 / D, rstd)
            P.free(st)
            for c in range(KC):
                P.dve(lambda e, c=c: e.scalar_tensor_tensor(out=yT[:, c, :], in0=yT[:, c, :], scalar=pcol("ng", gi * 8 + c),
                                                            in1=rstd[:], op0=ALU.mult, op1=ALU.mult),
                      [yT, par, rstd], [yT])
                (P.pool if c % 2 == 0 else P.dve)(lambda e, c=c: e.tensor_tensor(out=hT[:, c, :], in0=hT[:, c, :], in1=yT[:, c, :], op=ALU.add),
                                                        [hT, yT], [hT])

        def proj_to_yT(slab_ids, rhs_list, nk_per=KC):
            st = P.bank()
            m = 0
            for s in slab_ids:
                sl = load_slab(s)
                sa_ = slab_ap(sl, 512)
                for q in range(4):
                    pb = P.bank()
                    for k in range(KC):
                        P.pe(lambda e, k=k, q=q, pb=pb, sa_=sa_: e.matmul(pb[:], lhsT=sa_[:, k, q * 128:(q + 1) * 128],
                                                                          rhs=rhs_list[k][0], start=(k == 0), stop=(k == KC - 1)),
                             [sl, rhs_list[k][1]], [pb])
                    s2 = sq[m % 2]
                    KV = int(os.environ.get("KVAR", "0"))
                    P.dve(lambda e, pb=pb, m=m: e.tensor_copy(out=yT[:, m, :], in_=pb[:]), [pb], [yT])
                    P.free(pb)
                    P.act(lambda e, m=m, s2=s2: e.activation(out=s2[:], in_=yT[:, m, :], func=AF.Square), [yT], [s2])
                    if KV not in (1, 2):
                        P.pe(lambda e, m=m, s2=s2: e.matmul(st[:], lhsT=ones_b[:], rhs=s2[:], start=(m == 0), stop=(m == 7)),
                             [ones_b, s2], [st])
                    m += 1
            return st

        dg_ctr = [0]

        def conv_mm(pb, src, src_t, ntap, wname, wbase):
            for k in range(ntap):
                d = dg[dg_ctr[0] % len(dg)]
                dg_ctr[0] += 1
                P.dve(lambda e, d=d, k=k: e.tensor_scalar(out=d[:], in0=ident_b[:], scalar1=pcol(wname, wbase + k), scalar2=None,
                                                           op0=ALU.mult),
                       [ident_b, par], [d])
                P.pe(lambda e, d=d, k=k: e.matmul(pb[:], lhsT=d[:], rhs=src[:, k:k + NT], start=(k == 0), stop=(k == ntap - 1)),
                     [d, src_t], [pb])

        if STOP > 0:
            for blk in range(2):
                xs = xst[blk % 2]
                P.dma("sp", lambda e, blk=blk, xs=xs: e.dma_start(out=xs[:], in_=memin[blk * 128:(blk + 1) * 128, :]), [], [xs],
                      "xst%d" % (blk % 2))
                for half in range(2):
                    pb = P.bank()
                    for q in range(4):
                        c = half * 4 + q
                        P.pe(lambda e, c=c, q=q, xs=xs, pb=pb: e.transpose(out=pb[:, q * 128:(q + 1) * 128], in_=xs[:, c * 128:(c + 1) * 128],
                                                                           identity=ident_f), [xs, cst], [pb])
                    P.dve(lambda e, half=half, blk=blk, pb=pb: e.tensor_copy(
                        out=yT[:, half * 4:half * 4 + 4, blk * 128:(blk + 1) * 128],
                        in_=pb[:].rearrange("p (q c) -> p q c", q=4)), [pb], [yT])
                    P.free(pb)
            st = P.bank()
            for c in range(KC):
                s = sq[c % 2]
                P.act(lambda e, c=c, s=s: e.activation(out=s[:, 0:MEM], in_=yT[:, c, 0:MEM], func=AF.Square), [yT], [s])
                P.pe(lambda e, c=c, s=s: e.matmul(st[:, 0:MEM], lhsT=ones_b[:], rhs=s[:, 0:MEM], start=(c == 0), stop=(c == KC - 1)),
                     [ones_b, s], [st])
            P.act(lambda e: e.activation(out=lnt[:, 0:MEM], in_=st[:, 0:MEM], func=AF.Ln, bias=epsc[:], scale=1.0 / D), [st, epsc], [lnt])
            P.act(lambda e: e.activation(out=rstd[:, 0:MEM], in_=lnt[:, 0:MEM], func=AF.Exp, scale=-0.5), [lnt], [rstd])
            P.free(st)
            for c in range(KC):
                P.dve(lambda e, c=c: e.scalar_tensor_tensor(out=hnT[:, c, 0:MEM], in0=yT[:, c, 0:MEM], scalar=pcol("mng", c),
                                                            in1=rstd[:, 0:MEM], op0=ALU.mult, op1=ALU.mult),
                      [yT, par, rstd], [hnT])
            for s in range(2):
                sl = load_slab(SL_KV + s)
                sa_ = slab_ap(sl, 512)
                for q in range(4):
                    pb = P.bank()
                    for k in range(KC):
                        P.pe(lambda e, k=k, q=q, pb=pb, sa_=sa_: e.matmul(pb[:, 0:MEM], lhsT=sa_[:, k, q * 128:(q + 1) * 128], rhs=hnT[:, k, 0:MEM],
                                                                          start=(k == 0), stop=(k == KC - 1)), [sl, hnT], [pb])
                    P.dve(lambda e, pb=pb, s=s, q=q: e.tensor_copy(out=KT[:, s * 4 + q, :], in_=pb[:, 0:MEM]), [pb], [KT])
                    P.free(pb)
            for s in range(2):
                sl = load_slab(SL_KV + 2 + s)
                sa_ = slab_ap(sl, 512)
                for mc in range(2):
                    pb = P.bank()
                    for k in range(KC):
                        P.pe(lambda e, k=k, mc=mc, pb=pb, sa_=sa_: e.matmul(pb[:], lhsT=hnT[:, k, mc * 128:(mc + 1) * 128], rhs=sa_[:, k, :],
                                                                            start=(k == 0), stop=(k == KC - 1)), [sl, hnT], [pb])
                    P.dve(lambda e, pb=pb, s=s, mc=mc: e.tensor_copy(out=Vm[:, mc, s * 512:(s + 1) * 512], in_=pb[:]), [pb], [Vm])
                    P.free(pb)

        def bc4(t, blk):
            return t[:, blk, :].unsqueeze(2).to_broadcast([128, 4, 128])

        def v3(b):
            return b[:].rearrange("p (h c) -> p h c", h=4)

        def vb3(b):
            return b[:].bitcast(BF16)[:, 0:512].rearrange("p (h c) -> p h c", h=4)

        def gdn_scalars(full):
            T = tk
            P.dve(lambda e: e.tensor_tensor(out=T["a"][:], in0=abtok[:, :, 0:4], in1=pcol("dtb", 0, 16).rearrange("p (b h) -> p b h", h=4),
                                            op=ALU.add), [abtok, par], [T["a"]])
            P.act(lambda e: e.activation(out=T["t1"][:], in_=T["a"][:], func=AF.Exp), [T["a"]], [T["t1"]])
            P.act(lambda e: e.activation(out=T["t2"][:], in_=T["t1"][:], func=AF.Ln, bias=1.0), [T["t1"]], [T["t2"]])
            P.dve(lambda e: e.scalar_tensor_tensor(out=T["g"][:], in0=T["t2"][:], scalar=-1.0, in1=expA[:].rearrange("p (b h) -> p b h", h=4),
                                                   op0=ALU.mult, op1=ALU.mult), [T["t2"], expA], [T["g"]])
            P.act(lambda e: e.activation(out=T["t1"][:], in_=abtok[:, :, 4:8], func=AF.Exp, scale=-1.0), [abtok], [T["t1"]])
            P.act(lambda e: e.activation(out=T["t2"][:], in_=T["t1"][:], func=AF.Ln, bias=1.0), [T["t1"]], [T["t2"]])
            P.dve(lambda e: e.tensor_scalar(out=T["lb"][:], in0=T["t2"][:], scalar1=-1.0, scalar2=None, op0=ALU.mult), [T["t2"]], [T["lb"]])
            P.act(lambda e: e.activation(out=T["t1"][:], in_=ssq[:, :, 4:8], func=AF.Ln, bias=epsc[:]), [ssq, epsc], [T["t1"]])
            P.dve(lambda e: e.tensor_scalar(out=T["lrk"][:], in0=T["t1"][:], scalar1=-0.5, scalar2=None, op0=ALU.mult), [T["t1"]], [T["lrk"]])
            if full:
                P.act(lambda e: e.activation(out=T["t1"][:], in_=ssq[:, :, 0:4], func=AF.Ln, bias=epsc[:]), [ssq, epsc], [T["t1"]])
                P.dve(lambda e: e.tensor_scalar(out=T["lrq"][:], in0=T["t1"][:], scalar1=-0.5, scalar2=None, op0=ALU.mult),
                      [T["t1"]], [T["lrq"]])
            pb = P.bank()
            g2 = T["g"][:].rearrange("p b h -> p (b h)")
            P.pe(lambda e: e.matmul(pb[:, 0:16], lhsT=utri, rhs=g2, start=True, stop=True), [cst, T["g"]], [pb])
            P.pe(lambda e: e.matmul(pb[:, 16:32], lhsT=ones_f[:], rhs=g2, start=True, stop=True), [ones_f, T["g"]], [pb])
            P.dve(lambda e: e.tensor_copy(out=T["gc"][:].rearrange("p b h -> p (b h)"), in_=pb[:, 0:16]), [pb], [T["gc"]])
            P.dve(lambda e: e.tensor_copy(out=T["gl"][:].rearrange("p b h -> p (b h)"), in_=pb[:, 16:32]), [pb], [T["gl"]])
            P.free(pb)
            P.dve(lambda e: e.tensor_tensor(out=T["bj"][:], in0=T["lrk"][:], in1=T["gc"][:], op=ALU.subtract), [T["lrk"], T["gc"]], [T["bj"]])
            P.dve(lambda e: e.tensor_tensor(out=T["t1"][:], in0=T["gc"][:], in1=T["lb"][:], op=ALU.add), [T["gc"], T["lb"]], [T["t1"]])
            P.dve(lambda e: e.tensor_tensor(out=T["vL"][:], in0=T["t1"][:], in1=T["lrk"][:], op=ALU.add), [T["t1"], T["lrk"]], [T["vL"]])
            P.act(lambda e: e.activation(out=T["b"][:], in_=T["lb"][:], func=AF.Exp), [T["lb"]], [T["b"]])
            P.act(lambda e: e.activation(out=T["sw"][:], in_=T["vL"][:], func=AF.Exp), [T["vL"]], [T["sw"]])
            P.dve(lambda e: e.tensor_tensor(out=T["t2"][:], in0=T["gl"][:], in1=T["bj"][:], op=ALU.add), [T["gl"], T["bj"]], [T["t2"]])
            P.act(lambda e: e.activation(out=T["skd"][:], in_=T["t2"][:], func=AF.Exp), [T["t2"]], [T["skd"]])
            P.act(lambda e: e.activation(out=T["egl"][:], in_=T["gl"][:], func=AF.Exp), [T["gl"]], [T["egl"]])
            if full:
                P.dve(lambda e: e.scalar_tensor_tensor(out=T["vA"][:], in0=T["gc"][:], scalar=lnc[:], in1=T["lrq"][:],
                                                       op0=ALU.add, op1=ALU.add), [T["gc"], lnc, T["lrq"]], [T["vA"]])
                P.act(lambda e: e.activation(out=T["so"][:], in_=T["vA"][:], func=AF.Exp), [T["vA"]], [T["so"]])

        def gdn_pre(blk, full, qT, kT_, vT_):
            T = tk
            B = BS[blk % NSET]
            kdec, bek, bv, Mx, Px, Xx, attnT, usb, wT = (B[n] for n in ("kdec", "bek", "bv", "M", "P", "X", "attnT", "usb", "wT"))
            EL = ELs[blk % 2]
            EA = EAs[blk % 2]
            cs = slice(blk * 128, (blk + 1) * 128)
            pk = P.bank()
            for h in range(4):
                P.pe(lambda e, h=h: e.transpose(out=pk[:].bitcast(BF16)[:, h * 128:(h + 1) * 128], in_=kT_[h][:, cs], identity=ident_b[:]),
                     [kT_[h], ident_b], [pk])
            P.dve(lambda e: e.tensor_tensor(out=kdec[:], in0=vb3(pk), in1=bc4(T["skd"], blk), op=ALU.mult), [pk, T["skd"]], [kdec])
            P.dve(lambda e: e.tensor_tensor(out=bek[:], in0=vb3(pk), in1=bc4(T["sw"], blk), op=ALU.mult), [pk, T["sw"]], [bek])
            P.free(pk)
            yield
            pv = P.bank()
            for h in range(4):
                P.pe(lambda e, h=h: e.transpose(out=pv[:].bitcast(BF16)[:, h * 128:(h + 1) * 128], in_=vT_[h][:, cs], identity=ident_b[:]),
                     [vT_[h], ident_b], [pv])
            P.dve(lambda e: e.tensor_tensor(out=bv[:], in0=vb3(pv), in1=bc4(T["b"], blk), op=ALU.mult), [pv, T["b"]], [bv])
            P.free(pv)
            yield

            def emat(vec, neg, out_t):
                pe_ = P.bank()
                for h in range(4):
                    o = pe_[:, h * 128:(h + 1) * 128]
                    P.pe(lambda e, h=h, o=o: e.matmul(o, lhsT=vec[:, blk, h:h + 1].to_broadcast([128, 128]), rhs=ident_f, start=True, stop=False),
                         [vec, cst], [pe_])
                    P.pe(lambda e, h=h, o=o: e.matmul(o, lhsT=ident_f, rhs=T["bj"][:, blk, h:h + 1].to_broadcast([128, 128]), start=False, stop=False),
                         [T["bj"], cst], [pe_])
                    P.pe(lambda e, h=h, o=o: e.matmul(o, lhsT=ident_f, rhs=neg, start=False, stop=True), [cst], [pe_])
                P.act(lambda e: e.activation(out=out_t[:].rearrange("p h c -> p (h c)"), in_=pe_[:], func=AF.Exp), [pe_], [out_t])
                P.free(pe_)

            emat(T["vL"], negs, EL)
            pl = P.bank()
            for h in range(4):
                P.pe(lambda e, h=h: e.matmul(pl[:, h * 128:(h + 1) * 128], lhsT=kT_[h][:, cs], rhs=kT_[h][:, cs], start=True, stop=True),
                     [kT_[h]], [pl])
            P.dve(lambda e: e.tensor_tensor(out=Mx[:], in0=v3(pl), in1=EL[:], op=ALU.mult), [pl, EL], [Mx])
            P.free(pl)
            yield
            if full:
                emat(T["vA"], negi, EA)
                pa = P.bank()
                for h in range(4):
                    P.pe(lambda e, h=h: e.matmul(pa[:, h * 128:(h + 1) * 128], lhsT=kT_[h][:, cs], rhs=qT[h][:, cs], start=True, stop=True),
                         [kT_[h], qT[h]], [pa])
                P.dve(lambda e: e.tensor_tensor(out=attnT[:], in0=v3(pa), in1=EA[:], op=ALU.mult), [pa, EA], [attnT])
                P.free(pa)
                yield
            pt = P.bank()
            for h in range(4):
                P.pe(lambda e, h=h: e.transpose(out=pt[:].bitcast(BF16)[:, h * 128:(h + 1) * 128], in_=Mx[:, h, :], identity=ident_b[:]),
                     [Mx, ident_b], [pt])
            P.act(lambda e: e.activation(out=Px[:], in_=vb3(pt), func=AF.Copy), [pt], [Px])
            P.free(pt)
            P.dve(lambda e: e.scalar_tensor_tensor(out=Xx[:], in0=Mx[:], scalar=-1.0, in1=identB4[:], op0=ALU.mult, op1=ALU.add),
                  [Mx, identB4], [Xx])
            yield
            Mc, Mo = Mx, B["M2"]
            for k in range(6):
                if k < 5:
                    pm = P.bank()
                    for h in range(4):
                        P.pe(lambda e, h=h, pm=pm, Mc=Mc: e.matmul(pm[:, h * 128:(h + 1) * 128], lhsT=Px[:, h, :], rhs=Mc[:, h, :], start=True, stop=True),
                             [Mc, Px], [pm])
                    yield
                    P.dve(lambda e, pm=pm, Mo=Mo: e.tensor_copy(out=Mo[:].rearrange("p h c -> p (h c)"), in_=pm[:]), [pm], [Mo])
                    P.free(pm)
                pp = P.bank()
                for h in range(4):
                    P.pe(lambda e, h=h, pp=pp, Mc=Mc: e.matmul(pp[:, h * 128:(h + 1) * 128], lhsT=Mc[:, h, :], rhs=Px[:, h, :], start=True, stop=True),
                         [Mc, Px], [pp])
                yield
                P.act(lambda e, pp=pp: e.activation(out=Px[:].rearrange("p h c -> p (h c)"), in_=pp[:], func=AF.Copy), [pp], [Px])
                P.free(pp)
                px = P.bank()
                for h in range(4):
                    P.pe(lambda e, h=h, px=px: e.matmul(px[:, h * 128:(h + 1) * 128], lhsT=Px[:, h, :], rhs=Xx[:, h, :], start=True, stop=True),
                         [Px, Xx], [px])
                yield
                P.dve(lambda e, px=px: e.tensor_tensor(out=Xx[:], in0=v3(px), in1=Xx[:], op=ALU.add), [px, Xx], [Xx])
                P.free(px)
                Mc, Mo = Mo, Mc
            AT = Xx
            pu = P.bank()
            for h in range(4):
                P.pe(lambda e, h=h: e.matmul(pu[:, h * 128:(h + 1) * 128], lhsT=AT[:, h, :], rhs=bv[:, h, :], start=True, stop=True), [AT, bv], [pu])
            P.act(lambda e: e.activation(out=usb[:].rearrange("p h c -> p (h c)"), in_=pu[:], func=AF.Copy), [pu], [usb])
            P.free(pu)
            yield
            pw = P.bank()
            for h in range(4):
                P.pe(lambda e, h=h: e.matmul(pw[:, h * 128:(h + 1) * 128], lhsT=bek[:, h, :], rhs=AT[:, h, :], start=True, stop=True), [AT, bek], [pw])
            P.dve(lambda e: e.tensor_copy(out=wT[:].rearrange("p h c -> p (h c)"), in_=pw[:]), [pw], [wT])
            P.free(pw)
            yield

        def gdn_scan_all(full, qT, sg):
            T = tk
            po = {}

            def main1(blk):
                B = BS[blk % NSET]
                pws = P.bank()
                for h in range(4):
                    P.pe(lambda e, h=h: e.matmul(pws[:, h * 128:(h + 1) * 128], lhsT=B["wT"][:, h, :], rhs=Sb[:, h, :], start=True, stop=True),
                         [B["wT"], Sb], [pws])
                P.dve(lambda e: e.tensor_tensor(out=vnew[:], in0=B["usb"][:], in1=v3(pws), op=ALU.subtract), [B["usb"], pws], [vnew])
                P.free(pws)

            def main2(blk):
                B = BS[blk % NSET]
                cs = slice(blk * 128, (blk + 1) * 128)
                pds = P.bank()
                for h in range(4):
                    P.pe(lambda e, h=h: e.matmul(pds[:, h * 128:(h + 1) * 128], lhsT=B["kdec"][:, h, :], rhs=vnew[:, h, :], start=True, stop=True),
                         [B["kdec"], vnew], [pds])
                if full:
                    po1 = P.bank()
                    po2 = P.bank()
                    for h in range(4):
                        P.pe(lambda e, h=h: e.matmul(po1[:, h * 128:(h + 1) * 128], lhsT=qT[h][:, cs], rhs=Sb[:, h, :], start=True, stop=True),
                             [qT[h], Sb], [po1])
                    for h in range(4):
                        P.pe(lambda e, h=h: e.matmul(po2[:, h * 128:(h + 1) * 128], lhsT=B["attnT"][:, h, :], rhs=vnew[:, h, :], start=True, stop=True),
                             [B["attnT"], vnew], [po2])
                    po[blk] = (po1, po2)
                P.pool(lambda e: e.tensor_tensor(out=Stmp[:], in0=Sf[:], in1=bc4(T["egl"], blk), op=ALU.mult), [Sf, T["egl"]], [Stmp])
                P.dve(lambda e: e.tensor_tensor(out=Sf[:], in0=v3(pds), in1=Stmp[:], op=ALU.add), [pds, Stmp], [Sf])
                P.free(pds)
                P.act(lambda e: e.activation(out=Sb[:], in_=Sf[:], func=AF.Copy), [Sf], [Sb])

            def tail1(blk):
                po1, po2 = po[blk]
                P.dve(lambda e: e.tensor_tensor(out=o1s[:], in0=v3(po1), in1=bc4(T["so"], blk), op=ALU.mult), [po1, T["so"]], [o1s])
                P.dve(lambda e: e.tensor_tensor(out=osb[:], in0=v3(po2), in1=o1s[:], op=ALU.add), [po2, o1s], [osb])
                P.free(po1, po2)
                P.pool(lambda e: e.tensor_tensor(out=osq[:], in0=osb[:], in1=osb[:], op=ALU.mult), [osb], [osq])
                P.dve(lambda e: e.tensor_reduce(out=ssqo[:], in_=osq[:], axis=AX.X, op=ALU.add), [osq], [ssqo])
                P.act(lambda e: e.activation(out=rso[:], in_=ssqo[:], func=AF.Ln, bias=epsc[:], scale=1.0 / 128), [ssqo, epsc], [rso])
                P.act(lambda e: e.activation(out=rso[:], in_=rso[:], func=AF.Exp, scale=-0.5), [rso], [rso])
                P.dve(lambda e: e.tensor_tensor(out=onb[:], in0=osb[:], in1=rso[:].unsqueeze(2).to_broadcast([128, 4, 128]), op=ALU.mult),
                      [osb, rso], [onb])

            def tail2(blk):
                cs = slice(blk * 128, (blk + 1) * 128)
                pot = P.bank()
                for h in range(4):
                    P.pe(lambda e, h=h: e.transpose(out=pot[:].bitcast(BF16)[:, h * 128:(h + 1) * 128], in_=onb[:, h, :], identity=ident_b[:]),
                         [onb, ident_b], [pot])
                for h in range(4):
                    P.dve(lambda e, h=h: e.scalar_tensor_tensor(out=ocat[h][:, cs], in0=pot[:].bitcast(BF16)[:, h * 128:(h + 1) * 128],
                                                                scalar=pcol("gng"), in1=sg[h][:, cs], op0=ALU.mult, op1=ALU.mult),
                          [pot, par, sg[h]], [ocat[h]])
                P.free(pot)

            for blk in range(NB):
                main1(blk)
                yield
                if full and blk >= 1:
                    tail1(blk - 1)
                    yield
                main2(blk)
                yield
                if full and blk >= 1:
                    tail2(blk - 1)
                    yield
            if full:
                tail1(NB - 1)
                yield
                tail2(NB - 1)
                yield

        def interleave(gens):
            gens = list(gens)
            while gens:
                for g in list(gens):
                    try:
                        next(g)
                    except StopIteration:
                        gens.remove(g)

        for t_ in [hT, hnT, yT, rstd, lnt, rden] + sq + big + ocat + ubuf + sa:
            t_.split()

        def prenorm_h(gi, c0, c1, hb):
            st = P.bank()
            for c in range(KC):
                s_ = sq[c % 2]
                P.act(lambda e, c=c, s_=s_: e.activation(out=s_[:, c0:c1], in_=hT[:, c, c0:c1], func=AF.Square), [hT.hv(hb)], [s_.hv(hb)])
                P.pe(lambda e, c=c, s_=s_: e.matmul(st[:, c0:c1], lhsT=ones_b[:], rhs=s_[:, c0:c1], start=(c == 0), stop=(c == KC - 1)),
                     [ones_b, s_.hv(hb)], [st])
            P.act(lambda e: e.activation(out=lnt[:, c0:c1], in_=st[:, c0:c1], func=AF.Ln, bias=epsc[:], scale=1.0 / D), [st, epsc], [lnt.hv(hb)])
            P.act(lambda e: e.activation(out=rstd[:, c0:c1], in_=lnt[:, c0:c1], func=AF.Exp, scale=-0.5), [lnt.hv(hb)], [rstd.hv(hb)])
            P.free(st)
            for c in range(KC):
                P.dve(lambda e, c=c: e.scalar_tensor_tensor(out=hnT[:, c, c0:c1], in0=hT[:, c, c0:c1], scalar=pcol("ng", gi * 8 + c),
                                                            in1=rstd[:, c0:c1], op0=ALU.mult, op1=ALU.mult),
                      [hT.hv(hb), par, rstd.hv(hb)], [hnT.hv(hb)])

        def postnorm_h(gi, st, c0, c1, hb):
            P.act(lambda e: e.activation(out=lnt[:, c0:c1], in_=st[:, c0:c1], func=AF.Ln, bias=epsc[:], scale=1.0 / D), [st, epsc], [lnt.hv(hb)])
            P.act(lambda e: e.activation(out=rstd[:, c0:c1], in_=lnt[:, c0:c1], func=AF.Exp, scale=-0.5), [lnt.hv(hb)], [rstd.hv(hb)])
            P.free(st)
            for c in range(KC):
                P.dve(lambda e, c=c: e.scalar_tensor_tensor(out=yT[:, c, c0:c1], in0=yT[:, c, c0:c1], scalar=pcol("ng", gi * 8 + c),
                                                            in1=rstd[:, c0:c1], op0=ALU.mult, op1=ALU.mult),
                      [yT.hv(hb), par, rstd.hv(hb)], [yT.hv(hb)])
                (P.pool if c % 2 == 0 else P.dve)(lambda e, c=c: e.tensor_tensor(out=hT[:, c, c0:c1], in0=hT[:, c, c0:c1], in1=yT[:, c, c0:c1], op=ALU.add),
                                                        [hT.hv(hb), yT.hv(hb)], [hT.hv(hb)])

        def proj_h(slab_ids, rhs_tiles, c0, c1, hb):
            st = P.bank()
            m = 0
            for s_ in slab_ids:
                sl = load_slab(s_)
                sa_ = slab_ap(sl, 512)
                for q in range(4):
                    pb = P.bank()
                    for k in range(KC):
                        P.pe(lambda e, k=k, q=q, pb=pb, sa_=sa_: e.matmul(pb[:, c0:c1], lhsT=sa_[:, k, q * 128:(q + 1) * 128],
                                                                          rhs=rhs_tiles[k][:, c0:c1], start=(k == 0), stop=(k == KC - 1)),
                             [sl, rhs_tiles[k].hv(hb)], [pb])
                    s2 = sq[m % 2]
                    P.dve(lambda e, pb=pb, m=m: e.tensor_copy(out=yT[:, m, c0:c1], in_=pb[:, c0:c1]), [pb], [yT.hv(hb)])
                    P.free(pb)
                    P.act(lambda e, m=m, s2=s2: e.activation(out=s2[:, c0:c1], in_=yT[:, m, c0:c1], func=AF.Square), [yT.hv(hb)], [s2.hv(hb)])
                    P.pe(lambda e, m=m, s2=s2: e.matmul(st[:, c0:c1], lhsT=ones_b[:], rhs=s2[:, c0:c1], start=(m == 0), stop=(m == 7)),
                         [ones_b, s2.hv(hb)], [st])
                    m += 1
            return st

        def xattn_h(c0, c1, hb):
            qx = big[0:8]
            ox = big[8:16]
            m = 0
            for s_ in range(2):
                sl = load_slab(SL_Q + s_)
                sa_ = slab_ap(sl, 512)
                for q in range(4):
                    pb = P.bank()
                    for k in range(KC):
                        P.pe(lambda e, k=k, q=q, pb=pb, sa_=sa_: e.matmul(pb[:, c0:c1], lhsT=sa_[:, k, q * 128:(q + 1) * 128], rhs=hnT[:, k, c0:c1],
                                                                          start=(k == 0), stop=(k == KC - 1)), [sl, hnT.hv(hb)], [pb])
                    P.act(lambda e, pb=pb, m=m: e.activation(out=qx[m][:, c0:c1], in_=pb[:, c0:c1], func=AF.Copy, scale=1.0 / 16.0), [pb], [qx[m].hv(hb)])
                    P.free(pb)
                    m += 1
            for h in range(4):
                pts = [ubuf[(2 * h) % 4], ubuf[(2 * h + 1) % 4]]
                for mc in range(2):
                    pb = P.bank()
                    for dc in range(2):
                        P.pe(lambda e, dc=dc, mc=mc, h=h, pb=pb: e.matmul(pb[:, c0:c1], lhsT=KT[:, 2 * h + dc, mc * 128:(mc + 1) * 128],
                                                                          rhs=qx[2 * h + dc][:, c0:c1], start=(dc == 0), stop=(dc == 1)),
                             [KT, qx[2 * h + dc].hv(hb)], [pb])
                    P.act(lambda e, pb=pb, mc=mc, pts=pts: e.activation(out=pts[mc][:, c0:c1], in_=pb[:, c0:c1], func=AF.Exp), [pb], [pts[mc].hv(hb)])
                    P.free(pb)
                pd = P.bank()
                for mc in range(2):
                    P.pe(lambda e, mc=mc, pts=pts, pd=pd: e.matmul(pd[:, c0:c1], lhsT=ones_b[:], rhs=pts[mc][:, c0:c1], start=(mc == 0), stop=(mc == 1)),
                         [ones_b, pts[mc].hv(hb)], [pd])
                P.dve(lambda e, pd=pd: e.reciprocal(out=rden[:, c0:c1], in_=pd[:, c0:c1]), [pd], [rden.hv(hb)])
                P.free(pd)
                for dvc in range(2):
                    pb = P.bank()
                    for mc in range(2):
                        P.pe(lambda e, mc=mc, dvc=dvc, h=h, pb=pb, pts=pts: e.matmul(pb[:, c0:c1], lhsT=Vm[:, mc, (2 * h + dvc) * 128:(2 * h + dvc + 1) * 128],
                                                                                    rhs=pts[mc][:, c0:c1], start=(mc == 0), stop=(mc == 1)),
                             [Vm, pts[mc].hv(hb)], [pb])
                    P.dve(lambda e, pb=pb, h=h, dvc=dvc: e.tensor_tensor(out=ox[2 * h + dvc][:, c0:c1], in0=pb[:, c0:c1], in1=rden[:, c0:c1], op=ALU.mult),
                          [pb, rden.hv(hb)], [ox[2 * h + dvc].hv(hb)])
                    P.free(pb)
            return proj_h([SL_O, SL_O + 1], ox, c0, c1, hb)

        def ffn_h(c0, c1, hb, only_up):
            curu = {}
            n = c1 - c0

            def ffn_up(i):
                s_, q = i // 2, i % 2
                if q == 0:
                    curu["sl"] = load_slab(SL_UP + s_)
                sl = curu["sl"]
                sa_ = slab_ap(sl, 512)
                ua, ub = ubuf[2 * (i % 2)], ubuf[2 * (i % 2) + 1]
                pa_ = P.bank()
                pb_ = P.bank()
                for k in range(KC):
                    P.pe(lambda e, k=k: e.matmul(pa_[:, c0:c1], lhsT=sa_[:, k, q * 128:(q + 1) * 128], rhs=hnT[:, k, c0:c1],
                                                 start=(k == 0), stop=(k == KC - 1)), [sl, hnT.hv(hb)], [pa_])
                for k in range(KC):
                    P.pe(lambda e, k=k: e.matmul(pb_[:, c0:c1], lhsT=sa_[:, k, 256 + q * 128:256 + (q + 1) * 128], rhs=hnT[:, k, c0:c1],
                                                 start=(k == 0), stop=(k == KC - 1)), [sl, hnT.hv(hb)], [pb_])
                P.pool(lambda e: e.tensor_copy(out=ua[:, c0:c0 + 2], in_=uhist[:, i, :]), [uhist], [ua.hv(hb)])
                P.pool(lambda e: e.tensor_copy(out=ub[:, c0:c0 + 2], in_=uhist[:, 22 + i, :]), [uhist], [ub.hv(hb)])
                P.act(lambda e: e.activation(out=ua[:, 2 + c0:2 + c1], in_=pa_[:, c0:c1], func=AF.Copy), [pa_], [ua.hv(hb)])
                P.dve(lambda e: e.tensor_copy(out=ub[:, 2 + c0:2 + c1], in_=pb_[:, c0:c1]), [pb_], [ub.hv(hb)])
                P.free(pa_, pb_)
                P.pool(lambda e: e.tensor_copy(out=uhist[:, i, :], in_=ua[:, c1:c1 + 2]), [ua.hv(hb)], [uhist])
                P.pool(lambda e: e.tensor_copy(out=uhist[:, 22 + i, :], in_=ub[:, c1:c1 + 2]), [ub.hv(hb)], [uhist])

            def conv_h(pb, src, wbase):
                for k in range(3):
                    d = dg[dg_ctr[0] % len(dg)]
                    dg_ctr[0] += 1
                    P.dve(lambda e, d=d, k=k: e.tensor_scalar(out=d[:], in0=ident_b[:], scalar1=pcol("fcw", wbase + k), scalar2=None, op0=ALU.mult),
                          [ident_b, par], [d])
                    P.pe(lambda e, d=d, k=k: e.matmul(pb[:, c0:c1], lhsT=d[:], rhs=src[:, c0 + k:c1 + k], start=(k == 0), stop=(k == 2)),
                         [d, src.hv(hb)], [pb])

            def ffn_act(i):
                ua, ub = ubuf[2 * (i % 2)], ubuf[2 * (i % 2) + 1]
                pca = P.bank()
                pcb = P.bank()
                conv_h(pca, ua, i * 3)
                conv_h(pcb, ub, (22 + i) * 3)
                sat = sa[i % 2]
                P.act(lambda e: e.activation(out=sat[:, c0:c1], in_=pca[:, c0:c1], func=AF.Silu, bias=pcol("fcb", i)), [pca, par], [sat.hv(hb)])
                P.dve(lambda e: e.scalar_tensor_tensor(out=big[i][:, c0:c1], in0=pcb[:, c0:c1], scalar=pcol("fcb", 22 + i), in1=sat[:, c0:c1],
                                                       op0=ALU.add, op1=ALU.mult), [pcb, par, sat.hv(hb)], [big[i].hv(hb)])
                P.free(pca, pcb)

            if only_up:
                for i in range(22):
                    ffn_up(i)
                return None
            for i in range(23):
                if i < 22:
                    ffn_up(i)
                if i >= 1:
                    ffn_act(i - 1)
            st = P.bank()
            for ng in range(4):
                pbs = [P.bank(), P.bank()]
                for kh in range(2):
                    sl = load_slab(SL_DN + ng * 2 + kh)
                    sa_ = slab_ap(sl, 256, 11)
                    for m2 in range(2):
                        for k in range(11):
                            kk = kh * 11 + k
                            P.pe(lambda e, k=k, kk=kk, m2=m2, pbs=pbs, sa_=sa_: e.matmul(pbs[m2][:, c0:c1], lhsT=sa_[:, k, m2 * 128:(m2 + 1) * 128],
                                                                                        rhs=big[kk][:, c0:c1], start=(kk == 0), stop=(kk == 21)),
                                 [sl, big[kk].hv(hb)], [pbs[m2]])
                for m2 in range(2):
                    m = ng * 2 + m2
                    s2 = sq[m % 2]
                    P.dve(lambda e, m=m, pb=pbs[m2]: e.tensor_copy(out=yT[:, m, c0:c1], in_=pb[:, c0:c1]), [pbs[m2]], [yT.hv(hb)])
                    P.act(lambda e, s2=s2, m=m: e.activation(out=s2[:, c0:c1], in_=yT[:, m, c0:c1], func=AF.Square), [yT.hv(hb)], [s2.hv(hb)])
                    P.pe(lambda e, m=m, s2=s2: e.matmul(st[:, c0:c1], lhsT=ones_b[:], rhs=s2[:, c0:c1], start=(m == 0), stop=(m == 7)),
                         [ones_b, s2.hv(hb)], [st])
                P.free(*pbs)
            return st

        def store_h(c0, c1, hb, out_idx):
            for blk in range(c0 // 128, c1 // 128):
                os_ = ost[blk % 2]
                for half in range(2):
                    pb = P.bank()
                    for q in range(4):
                        c = half * 4 + q
                        P.pe(lambda e, c=c, q=q, blk=blk, pb=pb: e.transpose(out=pb[:, q * 128:(q + 1) * 128], in_=hT[:, c, blk * 128:(blk + 1) * 128],
                                                                             identity=ident_f), [hT.hv(hb), cst], [pb])
                    copy_ev(os_[:, half * 512:(half + 1) * 512], pb[:], [pb], [os_])
                    P.free(pb)
                r0 = out_idx * NT + blk * 128
                P.dma("sp", lambda e, os_=os_, r0=r0: e.dma_start(out=yout[r0:r0 + 128, :], in_=os_[:]), [os_], [], "xst%d" % (blk % 2))

        def back_schedule(halves, out_idx):
            halo = out_idx is None
            if len(halves) == 1:
                (c0, c1, hb) = halves[0]
                st = proj_h([SL_OUT, SL_OUT + 1], ocat, c0, c1, hb)
                postnorm_h(1, st, c0, c1, hb)
                prenorm_h(2, c0, c1, hb)
                st = xattn_h(c0, c1, hb)
                postnorm_h(3, st, c0, c1, hb)
                prenorm_h(4, c0, c1, hb)
                st = ffn_h(c0, c1, hb, halo)
                if not halo:
                    postnorm_h(5, st, c0, c1, hb)
                    store_h(c0, c1, hb, out_idx)
            else:
                A, B = halves
                stA = proj_h([SL_OUT, SL_OUT + 1], ocat, *A)
                postnorm_h(1, stA, *A)
                stB = proj_h([SL_OUT, SL_OUT + 1], ocat, *B)
                prenorm_h(2, *A)
                postnorm_h(1, stB, *B)
                ckpt(7)
                stA = xattn_h(*A)
                prenorm_h(2, *B)
                postnorm_h(3, stA, *A)
                stB = xattn_h(*B)
                prenorm_h(4, *A)
                postnorm_h(3, stB, *B)
                ckpt(8)
                stA = ffn_h(A[0], A[1], A[2], False)
                prenorm_h(4, *B)
                postnorm_h(5, stA, *A)
                stB = ffn_h(B[0], B[1], B[2], False)
                store_h(A[0], A[1], A[2], out_idx)
                postnorm_h(5, stB, *B)
                ckpt(9)
                store_h(B[0], B[1], B[2], out_idx)
            if halo:
                P.pool(lambda e: e.tensor_scalar(out=uhist[:], in0=uhist[:], scalar1=pcol("flag"), scalar2=None, op0=ALU.mult), [uhist, par], [uhist])

        def tile(ti, full, out_idx):
            tok0 = ti * NT
            for blk in range(NB):
                xs = xst[blk % 2]
                P.dma("sp", lambda e, blk=blk, xs=xs: e.dma_start(out=xs[:], in_=xin[tok0 + blk * 128: tok0 + (blk + 1) * 128, :]),
                      [], [xs], "xst%d" % (blk % 2))
                for half in range(2):
                    pb = P.bank()
                    for q in range(4):
                        c = half * 4 + q
                        P.pe(lambda e, c=c, q=q, xs=xs, pb=pb: e.transpose(out=pb[:, q * 128:(q + 1) * 128], in_=xs[:, c * 128:(c + 1) * 128],
                                                                           identity=ident_f), [xs, cst], [pb])
                    copy_ev(hT[:, half * 4:half * 4 + 4, blk * 128:(blk + 1) * 128], pb[:].rearrange("p (q c) -> p q c", q=4), [pb], [hT])
                    P.free(pb)
            ckpt(1)
            prenorm(0)
            ckpt(2)
            qs, ks, vs, sgs = big[0:4], big[4:8], big[8:12], big[12:16]
            chunks = []
            for s_, dst in ([(0, qs)] if full else []) + [(1, ks), (2, vs)]:
                for q in range(4):
                    chunks.append((s_, q, dst))
            cur = {}

            def qkv_proj(s_, q):
                if q == 0:
                    cur["sl"] = load_slab(SL_IN + s_)
                sl = cur["sl"]
                sa_ = slab_ap(sl, 512)
                ch = s_ * 4 + q
                pb = P.bank()
                for k in range(KC):
                    P.pe(lambda e, k=k: e.matmul(pb[:], lhsT=sa_[:, k, q * 128:(q + 1) * 128], rhs=hnT[:, k, :],
                                                 start=(k == 0), stop=(k == KC - 1)), [sl, hnT], [pb])
                z = zc[ch % 4]
                P.pool(lambda e: e.tensor_copy(out=z[:, 0:3], in_=zhist[:, ch, :]), [zhist], [z])
                copy_ev(z[:, 3:3 + NT], pb[:], [pb], [z])
                P.free(pb)
                P.pool(lambda e: e.tensor_copy(out=zhist[:, ch, :], in_=z[:, NT:NT + 3]), [z], [zhist])

            def qkv_conv(s_, q, dst):
                ch = s_ * 4 + q
                z = zc[ch % 4]
                pc = P.bank()
                conv_mm(pc, z, z, 4, "gcw", ch * 4)
                P.act(lambda e: e.activation(out=dst[q][:], in_=pc[:], func=AF.Silu), [pc], [dst[q]])
                P.free(pc)

            for i in range(len(chunks) + 1):
                if i < len(chunks):
                    qkv_proj(chunks[i][0], chunks[i][1])
                if i >= 1:
                    qkv_conv(*chunks[i - 1])
            pab = P.bank()
            for blk in range(NB):
                for k in range(KC):
                    P.pe(lambda e, k=k, blk=blk: e.matmul(pab[:, blk * 8:(blk + 1) * 8], lhsT=hnT[:, k, blk * 128:(blk + 1) * 128], rhs=wab[:, k, :],
                                                          start=(k == 0), stop=(k == KC - 1)), [hnT, wab], [pab])
            P.dve(lambda e: e.tensor_copy(out=abtok[:].rearrange("p b c -> p (b c)"), in_=pab[:, 0:NB * 8]), [pab], [abtok])
            P.free(pab)
            pss = P.bank()
            lst = ([(qs[h], h) for h in range(4)] if full else []) + [(ks[h], 4 + h) for h in range(4)]
            for i, (src, col) in enumerate(lst):
                s2 = sq[i % 2]
                P.pool(lambda e, src=src, s2=s2: e.tensor_tensor(out=s2[:], in0=src[:], in1=src[:], op=ALU.mult), [src], [s2])
                for blk in range(NB):
                    P.pe(lambda e, blk=blk, s2=s2, col=col: e.matmul(pss[:, blk * 8 + col:blk * 8 + col + 1], lhsT=s2[:, blk * 128:(blk + 1) * 128],
                                                                      rhs=ones_b[:, 0:1], start=True, stop=True), [s2, ones_b], [pss])
            if full:
                P.dve(lambda e: e.tensor_copy(out=ssq[:].rearrange("p b c -> p (b c)"), in_=pss[:, 0:NB * 8]), [pss], [ssq])
            else:
                P.dve(lambda e: e.tensor_copy(out=ssq[:, :, 4:8], in_=pss[:, 0:NB * 8].rearrange("p (b c) -> p b c", c=8)[:, :, 4:8]), [pss], [ssq])
            P.free(pss)
            ckpt(3)
            gdn_scalars(full)
            ckpt(4)

            def filler():
                sl = load_slab(SL_IN + 3)
                sa_ = slab_ap(sl, 512)
                for q in range(4):
                    pb = P.bank()
                    for k in range(KC):
                        P.pe(lambda e, k=k, q=q, pb=pb, sa_=sa_: e.matmul(pb[:], lhsT=sa_[:, k, q * 128:(q + 1) * 128], rhs=hnT[:, k, :],
                                                                          start=(k == 0), stop=(k == KC - 1)), [sl, hnT], [pb])
                        if k % 4 == 3:
                            yield
                    P.act(lambda e, pb=pb, q=q: e.activation(out=sgs[q][:], in_=pb[:], func=AF.Silu), [pb], [sgs[q]])
                    P.free(pb)
                for s_ in range(2):
                    sl = load_slab(SL_IN + 4 + s_)
                    sa_ = slab_ap(sl, 512)
                    for q in range(2):
                        ch = 2 * s_ + q
                        pa_ = P.bank()
                        pb_ = P.bank()
                        for k in range(KC):
                            P.pe(lambda e, k=k, q=q, pa_=pa_, sa_=sa_: e.matmul(pa_[:], lhsT=sa_[:, k, q * 128:(q + 1) * 128], rhs=hnT[:, k, :],
                                                                                start=(k == 0), stop=(k == KC - 1)), [sl, hnT], [pa_])
                            if k % 4 == 3:
                                yield
                        for k in range(KC):
                            P.pe(lambda e, k=k, q=q, pb_=pb_, sa_=sa_: e.matmul(pb_[:], lhsT=sa_[:, k, 256 + q * 128:256 + (q + 1) * 128], rhs=hnT[:, k, :],
                                                                                start=(k == 0), stop=(k == KC - 1)), [sl, hnT], [pb_])
                            if k % 4 == 3:
                                yield
                        cb = cbuf[ch]
                        P.pool(lambda e, cb=cb: e.tensor_copy(out=cb[:, 0:30], in_=cb[:, NT:NT + 30]), [cb], [cb])
                        P.act(lambda e, pb_=pb_: e.activation(out=lnt[:], in_=pb_[:], func=AF.Sigmoid), [pb_], [lnt])
                        P.dve(lambda e, pa_=pa_, cb=cb: e.tensor_tensor(out=cb[:, 30:30 + NT], in0=pa_[:], in1=lnt[:], op=ALU.mult), [pa_, lnt], [cb])
                        P.free(pa_, pb_)
                pmean = P.bank()
                pe2 = P.bank()
                for ch in range(4):
                    pc = P.bank()
                    for k in range(31):
                        d = dg[dg_ctr[0] % len(dg)]
                        dg_ctr[0] += 1
                        P.dve(lambda e, d=d, k=k, ch=ch: e.tensor_scalar(out=d[:], in0=ident_b[:], scalar1=pcol("cdw", ch * 31 + k), scalar2=None,
                                                                         op0=ALU.mult), [ident_b, par], [d])
                        P.pe(lambda e, d=d, k=k, ch=ch, pc=pc: e.matmul(pc[:], lhsT=d[:], rhs=cbuf[ch][:, k:k + NT], start=(k == 0), stop=(k == 30)),
                             [d, cbuf[ch]], [pc])
                        if k % 4 == 3:
                            yield
                    P.act(lambda e, pc=pc, ch=ch: e.activation(out=yT[:, ch, :], in_=pc[:], func=AF.Identity, bias=pcol("cdb", ch)), [pc, par], [yT])
                    s2 = sq[ch % 2]
                    P.act(lambda e, pc=pc, ch=ch, s2=s2: e.activation(out=s2[:], in_=pc[:], func=AF.Square, bias=pcol("cdb", ch)), [pc, par], [s2])
                    P.free(pc)
                    P.pe(lambda e, ch=ch: e.matmul(pmean[:], lhsT=ones_f[:], rhs=yT[:, ch, :], start=(ch == 0), stop=(ch == 3)), [ones_f, yT], [pmean])
                    P.pe(lambda e, ch=ch, s2=s2: e.matmul(pe2[:], lhsT=ones_b[:], rhs=s2[:], start=(ch == 0), stop=(ch == 3)), [ones_b, s2], [pe2])
                    yield
                P.act(lambda e: e.activation(out=msb[:], in_=pmean[:], func=AF.Copy, scale=1.0 / 512), [pmean], [msb])
                P.pool(lambda e: e.tensor_tensor(out=var[:], in0=msb[:], in1=msb[:], op=ALU.mult), [msb], [var])
                P.dve(lambda e: e.scalar_tensor_tensor(out=var[:], in0=pe2[:], scalar=1.0 / 512, in1=var[:], op0=ALU.mult, op1=ALU.subtract),
                      [pe2, var], [var])
                P.free(pmean, pe2)
                P.act(lambda e: e.activation(out=lnt[:], in_=var[:], func=AF.Ln, bias=epsc[:]), [var, epsc], [lnt])
                P.act(lambda e: e.activation(out=rstd[:], in_=lnt[:], func=AF.Exp, scale=-0.5), [lnt], [rstd])
                yield
                for ch in range(4):
                    P.pool(lambda e, ch=ch: e.tensor_tensor(out=yT[:, ch, :], in0=yT[:, ch, :], in1=msb[:], op=ALU.subtract), [yT, msb], [yT])
                    P.dve(lambda e, ch=ch: e.tensor_tensor(out=yT[:, ch, :], in0=yT[:, ch, :], in1=rstd[:], op=ALU.mult), [yT, rstd], [yT])
                    P.act(lambda e, ch=ch: e.activation(out=ocat[4 + ch][:], in_=yT[:, ch, :], func=AF.Silu, bias=pcol("clb", ch), scale=pcol("clg", ch)),
                          [yT, par], [ocat[4 + ch]])
                    yield

            fil = filler() if full else None

            def run_with_filler(gens):
                nonlocal fil
                gens = list(gens)
                while gens:
                    for g in list(gens):
                        try:
                            next(g)
                        except StopIteration:
                            gens.remove(g)
                    if fil is not None:
                        try:
                            next(fil)
                        except StopIteration:
                            fil = None

            run_with_filler([gdn_pre(blk, full, qs, ks, vs) for blk in range(NB)])
            ckpt(5)
            run_with_filler([gdn_scan_all(full, qs, sgs)])
            if fil is not None:
                for _ in fil:
                    pass
            if not full:
                return
            ckpt(6)
            if out_idx is None:
                halves = [(NT - 128, NT, None)]
            else:
                halves = [(0, NT, None)]
            back_schedule(halves, out_idx)

        try:
            ti = 0
            per = (len(late_list) + max(NS, 1) - 1) // max(NS, 1)
            for _ in range(NS):
                tile(ti, False, None)
                issue_late(per)
                ti += 1
            issue_late(len(late_list))
            tile(ti, True, None)
            ti += 1
            for i in range(NF):
                tile(ti, True, i)
                ti += 1
            P.final = ["xst0", "xst1"]
        except StopBuild:
            P.final = [k for k in P.dma_cnt.keys()]
        P.emit()
    return nc


def make_consts():
    c = np.zeros((128, 4, 128), np.float32)
    c[:, 0, :] = np.eye(128, dtype=np.float32)
    j = np.arange(128)[:, None]
    cc = np.arange(128)[None, :]
    c[:, 1, :] = (j <= cc).astype(np.float32)
    c[:, 2, :] = np.where(cc >= j, 0.0, NEGV)
    c[:, 3, :] = np.where(cc > j, 0.0, NEGV)
    return c.reshape(128, 512)


def make_params(inp, flag):
    p = np.zeros((128, NPAR), np.float32)

    def put(name, arr):
        p[:, PO[name]:PO[name] + arr.shape[1]] = arr

    def chunked(v):
        return np.ascontiguousarray(v.reshape(-1, 128).T)

    ng = inp["norm_g"][0]
    put("ng", np.concatenate([chunked(ng[i]) for i in range(6)], axis=1))
    gcw = inp["gdn_conv_w"][0]
    put("gcw", np.ascontiguousarray(gcw.reshape(4, 12, 128).transpose(2, 1, 0)).reshape(128, 48))
    cdw = inp["cfm_dw_w"][0]
    put("cdw", np.ascontiguousarray(cdw.reshape(31, 4, 128).transpose(2, 1, 0)).reshape(128, 124))
    put("cdb", chunked(inp["cfm_dw_b"][0]))
    put("clg", chunked(inp["cfm_ln_g"][0]))
    put("clb", chunked(inp["cfm_ln_b"][0]))
    fcw = inp["ffn_conv_w"][0]
    put("fcw", np.ascontiguousarray(fcw.reshape(3, 44, 128).transpose(2, 1, 0)).reshape(128, 132))
    put("fcb", chunked(inp["ffn_conv_b"][0]))
    put("gng", inp["gdn_norm_g"][0].reshape(128, 1))
    put("alog", np.tile(inp["gdn_a_log"][0][None, :], (128, 4)))
    put("dtb", np.tile(inp["gdn_dt_bias"][0][None, :], (128, 4)))
    put("mng", chunked(inp["mem_norm_g"][0]))
    p[:, PO["flag"]] = flag
    return p


_NC_CACHE = {}


def run(inputs, NF, NS, n_batch, dbg=False):
    inp = {k: np.asarray(v, dtype=np.float32) for k, v in inputs.items()}
    key = (NF, NS, dbg)
    if key not in _NC_CACHE:
        _NC_CACHE[key] = build(NF, NS, dbg)
    nc = _NC_CACHE[key]
    half = NF * NT
    nprev = (NS + 1) * NT
    consts = make_consts()
    in_maps = []
    for b in range(n_batch):
        for j in range(2):
            xall = np.zeros((nprev + half, D), np.float32)
            if j == 1:
                xall[:nprev] = inp["x"][b, 0:half]
            xall[nprev:] = inp["x"][b, j * half:(j + 1) * half]
            in_maps.append({
                "xin": xall, "memin": np.ascontiguousarray(inp["mem"][b]),
                "params": make_params(inp, float(j)), "consts": consts,
                "w_in": np.ascontiguousarray(inp["w_in"][0]), "w_out": np.ascontiguousarray(inp["w_out"][0]),
                "w_q": np.ascontiguousarray(inp["xa_w_q"][0]), "w_kv": np.ascontiguousarray(inp["xa_w_kv"][0]),
                "w_o": np.ascontiguousarray(inp["xa_w_o"][0]), "w_up": np.ascontiguousarray(inp["ffn_w_up"][0]),
                "w_dn": np.ascontiguousarray(inp["ffn_w_down"][0]),
            })
    res = run_bass_kernel_spmd(nc, in_maps, core_ids=list(range(len(in_maps))))
    out = np.zeros((n_batch, 2 * half, D), np.float32)
    for b in range(n_batch):
        for j in range(2):
            out[b, j * half:(j + 1) * half] = res.results[2 * b + j]["yout"]
    return out, res


def kernel(**inputs):
    out, _ = run(inputs, NF=8, NS=7, n_batch=4)
    return out
```

```python
import numpy as np
from contextlib import ExitStack
import concourse.bass as bass
import concourse.mybir as mybir
from concourse.bass_utils import run_bass_kernel_spmd

F32 = mybir.dt.float32
BF16 = mybir.dt.bfloat16
AF = mybir.ActivationFunctionType
ALU = mybir.AluOpType
AX = mybir.AxisListType

D = 1024
NT = 512
NB = NT // 128
H = 4
DFF = 2816
MEM = 256
EPS = 1e-6
NEGV = -30000.0
KC = 8

PO = {}
_o = 0
for _n, _w in [("ng", 48), ("gcw", 48), ("cdw", 124), ("cdb", 4), ("clg", 4), ("clb", 4),
               ("fcw", 132), ("fcb", 44), ("gng", 1), ("alog", 16), ("dtb", 16), ("mng", 8),
               ("flag", 1)]:
    PO[_n] = _o
    _o += _w
NPAR = _o

SL_IN, SL_OUT, SL_Q, SL_KV, SL_O, SL_UP, SL_DN = 0, 6, 8, 10, 14, 16, 27
NSLAB = 35


import os
STOP = float(os.environ.get("KSTOP", "1000"))


class StopBuild(Exception):
    pass


def ckpt(n):
    if n >= STOP:
        raise StopBuild()


class Buf:
    __slots__ = ("w", "rs", "const")

    def __init__(self):
        self.w = None
        self.rs = []
        self.const = False


class Tile:
    def __init__(self, t, bufs=None):
        self.t = t
        self.bufs = bufs if bufs is not None else [Buf()]
        self.h = None

    def split(self):
        self.bufs = [Buf(), Buf()]
        self.h = [Tile(self.t, [self.bufs[0]]), Tile(self.t, [self.bufs[1]])]
        return self

    def hv(self, hb):
        return self if hb is None else self.hv(hb)

    def __getitem__(self, idx):
        return self.t[idx]


class Ins:
    __slots__ = ("eng", "fn", "idx", "dma_key", "dma_val", "deps", "needs_inc", "ordinal", "waits")

    def __init__(self, eng, fn, idx, dma_key):
        self.eng = eng
        self.fn = fn
        self.idx = idx
        self.dma_key = dma_key
        self.dma_val = None
        self.deps = []
        self.needs_inc = False
        self.ordinal = 0
        self.waits = []


class Prog:
    ENGS = ["pe", "act", "dve", "pool", "sp"]

    def __init__(self, nc, es):
        self.nc = nc
        self.es = es
        self.ins = {e: [] for e in self.ENGS}
        self.dma_cnt = {}
        self.dma_sem = {}
        self.barrier_keys = set()
        self.sem = {}
        self.nfresh = 0
        self.free_banks = []
        self.final = []

    def sb(self, name, shape, dt):
        return Tile(self.es.enter_context(self.nc.sbuf_tensor(name, list(shape), dt)))

    def mkbanks(self):
        for i in range(8):
            t = Tile(self.es.enter_context(self.nc.psum_tensor("bank%d" % i, [128, 512], F32)))
            self.free_banks.append(t)

    def bank(self):
        assert self.free_banks, "out of PSUM banks"
        return self.free_banks.pop(0)

    def free(self, *bs):
        for b in bs:
            self.free_banks.append(b)

    def _add(self, eng, fn, r, w, dma_key=None):
        ins = Ins(eng, fn, len(self.ins[eng]), dma_key)
        deps = {}
        rb = [b for t in r for b in t.bufs]
        wb = [b for t in w for b in t.bufs]
        for b in rb:
            if b.w is not None:
                deps[id(b.w)] = (b.w, True)
        for b in wb:
            if b.w is not None and id(b.w) not in deps:
                deps[id(b.w)] = (b.w, False)
            for x in b.rs:
                if id(x) not in deps:
                    deps[id(x)] = (x, False)
        for b in rb:
            if not b.const:
                b.rs.append(ins)
        for b in wb:
            b.w = ins
            b.rs = []
        ins.deps = [v for k, v in deps.items() if v[0] is not ins]
        self.ins[eng].append(ins)
        if dma_key is not None:
            if dma_key == "fresh":
                dma_key = "fresh%d" % self.nfresh
                self.nfresh += 1
                ins.dma_key = dma_key
            self.dma_cnt[dma_key] = self.dma_cnt.get(dma_key, 0) + 16
            ins.dma_val = self.dma_cnt[dma_key]
        return ins

    def pe(self, fn, r, w):
        return self._add("pe", fn, r, w)

    def act(self, fn, r, w):
        return self._add("act", fn, r, w)

    def dve(self, fn, r, w):
        return self._add("dve", fn, r, w)

    def pool(self, fn, r, w):
        return self._add("pool", fn, r, w)

    def dma(self, q, fn, r, w, key):
        return self._add(q, fn, r, w, dma_key=key)

    def resolve(self):
        for e in self.ENGS:
            for ins in self.ins[e]:
                for (y, raw) in ins.deps:
                    if y.dma_key is not None:
                        continue
                    if y.eng == e:
                        if e in ("act", "dve", "pool") and raw:
                            y.needs_inc = True
                        continue
                    y.needs_inc = True
        for e in self.ENGS:
            c = 0
            for ins in self.ins[e]:
                if ins.dma_key is None and ins.needs_inc:
                    c += 1
                    ins.ordinal = c
        for k in list(self.dma_cnt.keys()):
            self.dma_sem[k] = self.es.enter_context(self.nc.semaphore("d_" + k))
        for e in self.ENGS:
            self.sem[e] = self.es.enter_context(self.nc.semaphore("e_" + e))
        for e in self.ENGS:
            waited = {}
            for ins in self.ins[e]:
                ws = {}
                for (y, raw) in ins.deps:
                    if y.dma_key is not None:
                        k = ("d", y.dma_key)
                        v = self.dma_cnt[y.dma_key] if y.dma_key in self.barrier_keys else y.dma_val
                    else:
                        if y.eng == e and not (e in ("act", "dve", "pool") and raw):
                            continue
                        k = ("e", y.eng)
                        v = y.ordinal
                    if waited.get(k, 0) >= v:
                        continue
                    if ws.get(k, 0) < v:
                        ws[k] = v
                for k, v in ws.items():
                    waited[k] = v
                    sem = self.dma_sem[k[1]] if k[0] == "d" else self.sem[k[1]]
                    ins.waits.append((sem, v))

    def emit(self):
        self.resolve()
        nc = self.nc
        prog = self

        def run(e, eng):
            for ins in prog.ins[e]:
                for (sem, v) in ins.waits:
                    eng.wait_ge(sem, v)
                bi = ins.fn(eng)
                if ins.dma_key is not None:
                    bi.then_inc(prog.dma_sem[ins.dma_key], 16)
                elif ins.needs_inc:
                    bi.then_inc(prog.sem[e], 1)
            if e == "sp":
                for k in prog.final:
                    eng.wait_ge(prog.dma_sem[k], prog.dma_cnt[k])

        with nc.Block() as block:
            @block.tensor
            def _(eng):
                run("pe", eng)

            @block.scalar
            def _(eng):
                run("act", eng)

            @block.vector
            def _(eng):
                run("dve", eng)

            @block.gpsimd
            def _(eng):
                run("pool", eng)

            @block.sync
            def _(eng):
                run("sp", eng)


def build(NF, NS, dbg=False):
    NTOK = (NS + 1 + NF) * NT
    nc = bass.Bass("TRN2", target_bir_lowering=False)
    xin = nc.dram_tensor("xin", [NTOK, D], F32, kind="ExternalInput").ap()
    memin = nc.dram_tensor("memin", [MEM, D], F32, kind="ExternalInput").ap()
    params = nc.dram_tensor("params", [128, NPAR], F32, kind="ExternalInput").ap()
    consts = nc.dram_tensor("consts", [128, 4 * 128], F32, kind="ExternalInput").ap()
    w_in = nc.dram_tensor("w_in", [D, 3080], F32, kind="ExternalInput").ap()
    w_out = nc.dram_tensor("w_out", [D, D], F32, kind="ExternalInput").ap()
    w_q = nc.dram_tensor("w_q", [D, D], F32, kind="ExternalInput").ap()
    w_kv = nc.dram_tensor("w_kv", [D, 2 * D], F32, kind="ExternalInput").ap()
    w_o = nc.dram_tensor("w_o", [D, D], F32, kind="ExternalInput").ap()
    w_up = nc.dram_tensor("w_up", [D, 2 * DFF], F32, kind="ExternalInput").ap()
    w_dn = nc.dram_tensor("w_dn", [DFF, D], F32, kind="ExternalInput").ap()
    yout = nc.dram_tensor("yout", [NF * NT, D], F32, kind="ExternalOutput").ap()
    wscr = nc.dram_tensor("wscr", [NSLAB, 128, 4096], BF16, kind="Internal").ap()
    dbg_out = None
    if dbg:
        dbg_out = nc.dram_tensor("dbg", [128, 16, NT], F32, kind="ExternalOutput").ap()

    es = ExitStack()
    with es:
        P = Prog(nc, es)
        P.mkbanks()
        DR = Tile(None)

        cst = P.sb("cst", [128, 4, 128], F32)
        par = P.sb("par", [128, NPAR], F32)
        ident_b = P.sb("ident_b", [128, 128], BF16)
        identB4 = P.sb("identB4", [128, 4, 128], BF16)
        ones_b = P.sb("ones_b", [128, 128], BF16)
        ones_f = P.sb("ones_f", [128, 128], F32)
        epsc = P.sb("epsc", [128, 1], F32)
        lnc = P.sb("lnc", [128, 1], F32)
        expA = P.sb("expA", [128, 16], F32)
        wab = P.sb("wab", [128, KC, 8], BF16)
        NSLOT = 3
        slots = [P.sb("slot%d" % i, [128, 4096], BF16) for i in range(NSLOT)]
        KT = P.sb("KT", [128, 8, MEM], BF16)
        Vm = P.sb("Vm", [128, 2, D], BF16)
        hT = P.sb("hT", [128, KC, NT], F32)
        hnT = P.sb("hnT", [128, KC, NT], BF16)
        yT = P.sb("yT", [128, KC, NT], F32)
        xst = [P.sb("xst%d" % i, [128, D], F32) for i in range(2)]
        big = [P.sb("big%d" % i, [128, NT], BF16) for i in range(22)]
        zc = [P.sb("zc%d" % i, [128, 3 + NT], BF16) for i in range(4)]
        zhist = P.sb("zhist", [128, 12, 3], BF16)
        cbuf = [P.sb("cbuf%d" % i, [128, 30 + NT], BF16) for i in range(4)]
        ubuf = [P.sb("ubuf%d" % i, [128, 2 + NT], BF16) for i in range(4)]
        uhist = P.sb("uhist", [128, 44, 2], BF16)
        ocat = [P.sb("ocat%d" % i, [128, NT], BF16) for i in range(8)]
        sq = [P.sb("sq%d" % i, [128, NT], BF16) for i in range(2)]
        rstd = P.sb("rstd", [128, NT], F32)
        lnt = P.sb("lnt", [128, NT], F32)
        dg = [P.sb("dg%d" % i, [128, 128], BF16) for i in range(12)]
        rden = P.sb("rden", [128, NT], F32)
        msb = P.sb("msb", [128, NT], F32)
        var = rden
        sa = [P.sb("sa%d" % i, [128, NT], BF16) for i in range(2)]
        abtok = P.sb("abtok", [128, NB, 8], F32)
        ssq = P.sb("ssq", [128, NB, 8], F32)
        tk = {n: P.sb("tk_" + n, [128, NB, 4], F32) for n in
              ["a", "g", "lb", "lrq", "lrk", "gc", "gl", "vL", "vA", "bj", "b", "sw", "skd", "egl", "so", "t1", "t2"]}
        Sf = P.sb("Sf", [128, 4, 128], F32)
        Sb = P.sb("Sb", [128, 4, 128], BF16)
        NSET = 4
        BS = []
        for i in range(NSET):
            BS.append({n: P.sb("%s_%d" % (n, i), [128, 4, 128], dt) for n, dt in
                       [("kdec", BF16), ("bek", BF16), ("bv", BF16), ("M", BF16), ("M2", BF16), ("P", BF16), ("X", BF16),
                        ("attnT", BF16), ("usb", F32), ("wT", BF16)]})
        ELs = [P.sb("EL%d" % i, [128, 4, 128], BF16) for i in range(2)]
        EAs = [P.sb("EA%d" % i, [128, 4, 128], BF16) for i in range(2)]
        vnew = P.sb("vnew", [128, 4, 128], BF16)
        osb = P.sb("osb", [128, 4, 128], F32)
        scr4 = P.sb("scr4", [128, 4, 128], F32)
        o1s = scr4
        osq = scr4
        Stmp = P.sb("Stmp", [128, 4, 128], F32)
        ssqo = P.sb("ssqo", [128, 4], F32)
        rso = P.sb("rso", [128, 4], F32)
        onb = P.sb("onb", [128, 4, 128], BF16)
        ost = xst

        def pcol(name, i=0, n=1):
            o = PO[name] + i
            return par[:, o:o + n]

        ident_f = cst[:, 0, :]
        utri = cst[:, 1, :]
        negi = cst[:, 2, :]
        negs = cst[:, 3, :]

        P.dma("sp", lambda e: e.dma_start(out=cst[:], in_=consts.rearrange("p (a c) -> p a c", a=4)), [], [cst], "fresh")
        P.dma("sp", lambda e: e.dma_start(out=par[:], in_=params), [], [par], "fresh")
        P.dma("pool", lambda e: e.dma_start(out=wab[:], in_=w_in[:, 2048:2056].rearrange("(kc p) n -> p kc n", p=128)),
              [], [wab], "fresh")

        def conv_cols(W, s, nw, off, c0, n, r0=0, nk=KC):
            o = wscr[s, :, 0:nk * nw].rearrange("p (kc n) -> p kc n", n=nw)[:, :, off:off + n]
            i = W[r0:r0 + nk * 128, c0:c0 + n].rearrange("(kc p) n -> p kc n", p=128)
            early = s in (SL_IN + 1, SL_IN + 2, SL_KV, SL_KV + 1, SL_KV + 2, SL_KV + 3)
            conv_list.append((early, o, i))

        P.barrier_keys.add("wc1")
        P.barrier_keys.add("wc2")
        conv_list = []
        DR2 = Tile(None)
        for s in range(4):
            conv_cols(w_in, SL_IN + s, 512, 0, s * 512, 512)
        for s in range(2):
            for q in range(2):
                conv_cols(w_in, SL_IN + 4 + s, 512, q * 128, 2056 + (2 * s + q) * 128, 128)
                conv_cols(w_in, SL_IN + 4 + s, 512, 256 + q * 128, 2056 + 512 + (2 * s + q) * 128, 128)
        for s in range(4):
            conv_cols(w_kv, SL_KV + s, 512, 0, s * 512, 512)
        for s in range(2):
            conv_cols(w_out, SL_OUT + s, 512, 0, s * 512, 512)
            conv_cols(w_q, SL_Q + s, 512, 0, s * 512, 512)
            conv_cols(w_o, SL_O + s, 512, 0, s * 512, 512)
        for s in range(11):
            for q in range(2):
                conv_cols(w_up, SL_UP + s, 512, q * 128, (2 * s + q) * 128, 128)
                conv_cols(w_up, SL_UP + s, 512, 256 + q * 128, DFF + (2 * s + q) * 128, 128)
        for ng in range(4):
            for kh in range(2):
                conv_cols(w_dn, SL_DN + ng * 2 + kh, 256, 0, ng * 256, 256, r0=kh * 1408, nk=11)

        late_list = [c for c in conv_list if not c[0]]

        def issue_late(n):
            for _ in range(min(n, len(late_list))):
                (early, o, i) = late_list.pop(0)
                DR2.bufs[0].w = P.dma("pool", lambda e, o=o, i=i: e.dma_start(out=o, in_=i), [], [], "wc2")

        P.pool(lambda e: e.memset(ones_b[:], 1.0), [], [ones_b])
        P.pool(lambda e: e.memset(ones_f[:], 1.0), [], [ones_f])
        P.pool(lambda e: e.memset(epsc[:], EPS), [], [epsc])
        P.pool(lambda e: e.memset(lnc[:], -0.5 * float(np.log(128.0))), [], [lnc])
        P.pool(lambda e: e.memset(Sf[:], 0.0), [], [Sf])
        P.pool(lambda e: e.memset(Sb[:], 0.0), [], [Sb])
        P.pool(lambda e: e.memset(uhist[:], 0.0), [], [uhist])
        P.pool(lambda e: e.memset(zhist[:], 0.0), [], [zhist])
        for t in zc + cbuf + ubuf:
            P.pool(lambda e, t=t: e.memset(t[:], 0.0), [], [t])
        P.dve(lambda e: e.tensor_copy(out=ident_b[:], in_=ident_f), [cst], [ident_b])
        for h in range(4):
            P.dve(lambda e, h=h: e.tensor_copy(out=identB4[:, h, :], in_=ident_f), [cst], [identB4])
        P.act(lambda e: e.activation(out=expA[:], in_=pcol("alog", 0, 16), func=AF.Exp), [par], [expA])
        for t in (cst, par, ident_b, identB4, ones_b, ones_f, epsc, lnc, expA, wab):
            pass

        for (early, o, i) in [c for c in conv_list if c[0]]:
            DR.bufs[0].w = P.dma("pool", lambda e, o=o, i=i: e.dma_start(out=o, in_=i), [], [], "wc1")

        slot_ctr = [0]

        def load_slab(s):
            sl = slots[slot_ctr[0] % NSLOT]
            key = "slot%d" % (slot_ctr[0] % NSLOT)
            slot_ctr[0] += 1
            drt = DR if s in (SL_IN + 1, SL_IN + 2, SL_KV, SL_KV + 1, SL_KV + 2, SL_KV + 3) else DR2
            P.dma("sp", lambda e: e.dma_start(out=sl[:], in_=wscr[s]), [drt], [sl], key)
            return sl

        def slab_ap(sl, nw, nk=KC):
            return sl[:, 0:nk * nw].rearrange("p (kc n) -> p kc n", n=nw)

        ev_ctr = [0]

        def copy_ev(out_ap, in_ap, r, w, scale=None):
            ev_ctr[0] += 1
            if ev_ctr[0] % 2 == 0 and scale is None:
                P.dve(lambda e: e.tensor_copy(out=out_ap, in_=in_ap), r, w)
            else:
                if scale is None:
                    P.act(lambda e: e.activation(out=out_ap, in_=in_ap, func=AF.Copy), r, w)
                else:
                    P.act(lambda e: e.activation(out=out_ap, in_=in_ap, func=AF.Copy, scale=scale), r, w)

        def rsqrt_from(ps_ap, r, scale, out_t):
            P.act(lambda e: e.activation(out=lnt[:], in_=ps_ap, func=AF.Ln, bias=epsc[:], scale=scale), r + [epsc], [lnt])
            P.act(lambda e: e.activation(out=out_t[:], in_=lnt[:], func=AF.Exp, scale=-0.5), [lnt], [out_t])

        def prenorm(gi):
            st = P.bank()
            for c in range(KC):
                s = sq[c % 2]
                P.act(lambda e, c=c, s=s: e.activation(out=s[:], in_=hT[:, c, :], func=AF.Square), [hT], [s])
                P.pe(lambda e, c=c, s=s: e.matmul(st[:], lhsT=ones_b[:], rhs=s[:], start=(c == 0), stop=(c == KC - 1)),
                     [ones_b, s], [st])
            rsqrt_from(st[:], [st], 1.0 / D, rstd)
            P.free(st)
            for c in range(KC):
                P.dve(lambda e, c=c: e.scalar_tensor_tensor(out=hnT[:, c, :], in0=hT[:, c, :], scalar=pcol("ng", gi * 8 + c),
                                                            in1=rstd[:], op0=ALU.mult, op1=ALU.mult),
                      [hT, par, rstd], [hnT])

        def postnorm_residual(gi, st):
            rsqrt_from(st[:], [st], 1.0 / D, rstd)
            P.free(st)
            for c in range(KC):
                P.dve(lambda e, c=c: e.scalar_tensor_tensor(out=yT[:, c, :], in0=yT[:, c, :], scalar=pcol("ng", gi * 8 + c),
                                                            in1=rstd[:], op0=ALU.mult, op1=ALU.mult),
                      [yT, par, rstd], [yT])
                (P.pool if c % 2 == 0 else P.dve)(lambda e, c=c: e.tensor_tensor(out=hT[:, c, :], in0=hT[:, c, :], in1=yT[:, c, :], op=ALU.add),
                                                        [hT, yT], [hT])

        def proj_to_yT(slab_ids, rhs_list, nk_per=KC):
            st = P.bank()
            m = 0
            for s in slab_ids:
                sl = load_slab(s)
                sa_ = slab_ap(sl, 512)
                for q in range(4):
                    pb = P.bank()
                    for k in range(KC):
                        P.pe(lambda e, k=k, q=q, pb=pb, sa_=sa_: e.matmul(pb[:], lhsT=sa_[:, k, q * 128:(q + 1) * 128],
                                                                          rhs=rhs_list[k][0], start=(k == 0), stop=(k == KC - 1)),
                             [sl, rhs_list[k][1]], [pb])
                    s2 = sq[m % 2]
                    KV = int(os.environ.get("KVAR", "0"))
                    P.dve(lambda e, pb=pb, m=m: e.tensor_copy(out=yT[:, m, :], in_=pb[:]), [pb], [yT])
                    P.free(pb)
                    P.act(lambda e, m=m, s2=s2: e.activation(out=s2[:], in_=yT[:, m, :], func=AF.Square), [yT], [s2])
                    if KV not in (1, 2):
                        P.pe(lambda e, m=m, s2=s2: e.matmul(st[:], lhsT=ones_b[:], rhs=s2[:], start=(m == 0), stop=(m == 7)),
                             [ones_b, s2], [st])
                    m += 1
            return st

        dg_ctr = [0]

        def conv_mm(pb, src, src_t, ntap, wname, wbase):
            for k in range(ntap):
                d = dg[dg_ctr[0] % len(dg)]
                dg_ctr[0] += 1
                P.dve(lambda e, d=d, k=k: e.tensor_scalar(out=d[:], in0=ident_b[:], scalar1=pcol(wname, wbase + k), scalar2=None,
                                                           op0=ALU.mult),
                       [ident_b, par], [d])
                P.pe(lambda e, d=d, k=k: e.matmul(pb[:], lhsT=d[:], rhs=src[:, k:k + NT], start=(k == 0), stop=(k == ntap - 1)),
                     [d, src_t], [pb])

        if STOP > 0:
            for blk in range(2):
                xs = xst[blk % 2]
                P.dma("sp", lambda e, blk=blk, xs=xs: e.dma_start(out=xs[:], in_=memin[blk * 128:(blk + 1) * 128, :]), [], [xs],
                      "xst%d" % (blk % 2))
                for half in range(2):
                    pb = P.bank()
                    for q in range(4):
                        c = half * 4 + q
                        P.pe(lambda e, c=c, q=q, xs=xs, pb=pb: e.transpose(out=pb[:, q * 128:(q + 1) * 128], in_=xs[:, c * 128:(c + 1) * 128],
                                                                           identity=ident_f), [xs, cst], [pb])
                    P.dve(lambda e, half=half, blk=blk, pb=pb: e.tensor_copy(
                        out=yT[:, half * 4:half * 4 + 4, blk * 128:(blk + 1) * 128],
                        in_=pb[:].rearrange("p (q c) -> p q c", q=4)), [pb], [yT])
                    P.free(pb)
            st = P.bank()
            for c in range(KC):
                s = sq[c % 2]
                P.act(lambda e, c=c, s=s: e.activation(out=s[:, 0:MEM], in_=yT[:, c, 0:MEM], func=AF.Square), [yT], [s])
                P.pe(lambda e, c=c, s=s: e.matmul(st[:, 0:MEM], lhsT=ones_b[:], rhs=s[:, 0:MEM], start=(c == 0), stop=(c == KC - 1)),
                     [ones_b, s], [st])
            P.act(lambda e: e.activation(out=lnt[:, 0:MEM], in_=st[:, 0:MEM], func=AF.Ln, bias=epsc[:], scale=1.0 / D), [st, epsc], [lnt])
            P.act(lambda e: e.activation(out=rstd[:, 0:MEM], in_=lnt[:, 0:MEM], func=AF.Exp, scale=-0.5), [lnt], [rstd])
            P.free(st)
            for c in range(KC):
                P.dve(lambda e, c=c: e.scalar_tensor_tensor(out=hnT[:, c, 0:MEM], in0=yT[:, c, 0:MEM], scalar=pcol("mng", c),
                                                            in1=rstd[:, 0:MEM], op0=ALU.mult, op1=ALU.mult),
                      [yT, par, rstd], [hnT])
            for s in range(2):
                sl = load_slab(SL_KV + s)
                sa_ = slab_ap(sl, 512)
                for q in range(4):
                    pb = P.bank()
                    for k in range(KC):
                        P.pe(lambda e, k=k, q=q, pb=pb, sa_=sa_: e.matmul(pb[:, 0:MEM], lhsT=sa_[:, k, q * 128:(q + 1) * 128], rhs=hnT[:, k, 0:MEM],
                                                                          start=(k == 0), stop=(k == KC - 1)), [sl, hnT], [pb])
                    P.dve(lambda e, pb=pb, s=s, q=q: e.tensor_copy(out=KT[:, s * 4 + q, :], in_=pb[:, 0:MEM]), [pb], [KT])
                    P.free(pb)
            for s in range(2):
                sl = load_slab(SL_KV + 2 + s)
                sa_ = slab_ap(sl, 512)
                for mc in range(2):
                    pb = P.bank()
                    for k in range(KC):
                        P.pe(lambda e, k=k, mc=mc, pb=pb, sa_=sa_: e.matmul(pb[:], lhsT=hnT[:, k, mc * 128:(mc + 1) * 128], rhs=sa_[:, k, :],
                                                                            start=(k == 0), stop=(k == KC - 1)), [sl, hnT], [pb])
                    P.dve(lambda e, pb=pb, s=s, mc=mc: e.tensor_copy(out=Vm[:, mc, s * 512:(s + 1) * 512], in_=pb[:]), [pb], [Vm])
                    P.free(pb)

        def bc4(t, blk):
            return t[:, blk, :].unsqueeze(2).to_broadcast([128, 4, 128])

        def v3(b):
            return b[:].rearrange("p (h c) -> p h c", h=4)

        def vb3(b):
            return b[:].bitcast(BF16)[:, 0:512].rearrange("p (h c) -> p h c", h=4)

        def gdn_scalars(full):
            T = tk
            P.dve(lambda e: e.tensor_tensor(out=T["a"][:], in0=abtok[:, :, 0:4], in1=pcol("dtb", 0, 16).rearrange("p (b h) -> p b h", h=4),
                                            op=ALU.add), [abtok, par], [T["a"]])
            P.act(lambda e: e.activation(out=T["t1"][:], in_=T["a"][:], func=AF.Exp), [T["a"]], [T["t1"]])
            P.act(lambda e: e.activation(out=T["t2"][:], in_=T["t1"][:], func=AF.Ln, bias=1.0), [T["t1"]], [T["t2"]])
            P.dve(lambda e: e.scalar_tensor_tensor(out=T["g"][:], in0=T["t2"][:], scalar=-1.0, in1=expA[:].rearrange("p (b h) -> p b h", h=4),
                                                   op0=ALU.mult, op1=ALU.mult), [T["t2"], expA], [T["g"]])
            P.act(lambda e: e.activation(out=T["t1"][:], in_=abtok[:, :, 4:8], func=AF.Exp, scale=-1.0), [abtok], [T["t1"]])
            P.act(lambda e: e.activation(out=T["t2"][:], in_=T["t1"][:], func=AF.Ln, bias=1.0), [T["t1"]], [T["t2"]])
            P.dve(lambda e: e.tensor_scalar(out=T["lb"][:], in0=T["t2"][:], scalar1=-1.0, scalar2=None, op0=ALU.mult), [T["t2"]], [T["lb"]])
            P.act(lambda e: e.activation(out=T["t1"][:], in_=ssq[:, :, 4:8], func=AF.Ln, bias=epsc[:]), [ssq, epsc], [T["t1"]])
            P.dve(lambda e: e.tensor_scalar(out=T["lrk"][:], in0=T["t1"][:], scalar1=-0.5, scalar2=None, op0=ALU.mult), [T["t1"]], [T["lrk"]])
            if full:
                P.act(lambda e: e.activation(out=T["t1"][:], in_=ssq[:, :, 0:4], func=AF.Ln, bias=epsc[:]), [ssq, epsc], [T["t1"]])
                P.dve(lambda e: e.tensor_scalar(out=T["lrq"][:], in0=T["t1"][:], scalar1=-0.5, scalar2=None, op0=ALU.mult),
                      [T["t1"]], [T["lrq"]])
            pb = P.bank()
            g2 = T["g"][:].rearrange("p b h -> p (b h)")
            P.pe(lambda e: e.matmul(pb[:, 0:16], lhsT=utri, rhs=g2, start=True, stop=True), [cst, T["g"]], [pb])
            P.pe(lambda e: e.matmul(pb[:, 16:32], lhsT=ones_f[:], rhs=g2, start=True, stop=True), [ones_f, T["g"]], [pb])
            P.dve(lambda e: e.tensor_copy(out=T["gc"][:].rearrange("p b h -> p (b h)"), in_=pb[:, 0:16]), [pb], [T["gc"]])
            P.dve(lambda e: e.tensor_copy(out=T["gl"][:].rearrange("p b h -> p (b h)"), in_=pb[:, 16:32]), [pb], [T["gl"]])
            P.free(pb)
            P.dve(lambda e: e.tensor_tensor(out=T["bj"][:], in0=T["lrk"][:], in1=T["gc"][:], op=ALU.subtract), [T["lrk"], T["gc"]], [T["bj"]])
            P.dve(lambda e: e.tensor_tensor(out=T["t1"][:], in0=T["gc"][:], in1=T["lb"][:], op=ALU.add), [T["gc"], T["lb"]], [T["t1"]])
            P.dve(lambda e: e.tensor_tensor(out=T["vL"][:], in0=T["t1"][:], in1=T["lrk"][:], op=ALU.add), [T["t1"], T["lrk"]], [T["vL"]])
            P.act(lambda e: e.activation(out=T["b"][:], in_=T["lb"][:], func=AF.Exp), [T["lb"]], [T["b"]])
            P.act(lambda e: e.activation(out=T["sw"][:], in_=T["vL"][:], func=AF.Exp), [T["vL"]], [T["sw"]])
            P.dve(lambda e: e.tensor_tensor(out=T["t2"][:], in0=T["gl"][:], in1=T["bj"][:], op=ALU.add), [T["gl"], T["bj"]], [T["t2"]])
            P.act(lambda e: e.activation(out=T["skd"][:], in_=T["t2"][:], func=AF.Exp), [T["t2"]], [T["skd"]])
            P.act(lambda e: e.activation(out=T["egl"][:], in_=T["gl"][:], func=AF.Exp), [T["gl"]], [T["egl"]])
            if full:
                P.dve(lambda e: e.scalar_tensor_tensor(out=T["vA"][:], in0=T["gc"][:], scalar=lnc[:], in1=T["lrq"][:],
                                                       op0=ALU.add, op1=ALU.add), [T["gc"], lnc, T["lrq"]], [T["vA"]])
                P.act(lambda e: e.activation(out=T["so"][:], in_=T["vA"][:], func=AF.Exp), [T["vA"]], [T["so"]])

        def gdn_pre(blk, full, qT, kT_, vT_):
            T = tk
            B = BS[blk % NSET]
            kdec, bek, bv, Mx, Px, Xx, attnT, usb, wT = (B[n] for n in ("kdec", "bek", "bv", "M", "P", "X", "attnT", "usb", "wT"))
            EL = ELs[blk % 2]
            EA = EAs[blk % 2]
            cs = slice(blk * 128, (blk + 1) * 128)
            pk = P.bank()
            for h in range(4):
                P.pe(lambda e, h=h: e.transpose(out=pk[:].bitcast(BF16)[:, h * 128:(h + 1) * 128], in_=kT_[h][:, cs], identity=ident_b[:]),
                     [kT_[h], ident_b], [pk])
            P.dve(lambda e: e.tensor_tensor(out=kdec[:], in0=vb3(pk), in1=bc4(T["skd"], blk), op=ALU.mult), [pk, T["skd"]], [kdec])
            P.dve(lambda e: e.tensor_tensor(out=bek[:], in0=vb3(pk), in1=bc4(T["sw"], blk), op=ALU.mult), [pk, T["sw"]], [bek])
            P.free(pk)
            yield
            pv = P.bank()
            for h in range(4):
                P.pe(lambda e, h=h: e.transpose(out=pv[:].bitcast(BF16)[:, h * 128:(h + 1) * 128], in_=vT_[h][:, cs], identity=ident_b[:]),
                     [vT_[h], ident_b], [pv])
            P.dve(lambda e: e.tensor_tensor(out=bv[:], in0=vb3(pv), in1=bc4(T["b"], blk), op=ALU.mult), [pv, T["b"]], [bv])
            P.free(pv)
            yield

            def emat(vec, neg, out_t):
                pe_ = P.bank()
                for h in range(4):
                    o = pe_[:, h * 128:(h + 1) * 128]
                    P.pe(lambda e, h=h, o=o: e.matmul(o, lhsT=vec[:, blk, h:h + 1].to_broadcast([128, 128]), rhs=ident_f, start=True, stop=False),
                         [vec, cst], [pe_])
                    P.pe(lambda e, h=h, o=o: e.matmul(o, lhsT=ident_f, rhs=T["bj"][:, blk, h:h + 1].to_broadcast([128, 128]), start=False, stop=False),
                         [T["bj"], cst], [pe_])
                    P.pe(lambda e, h=h, o=o: e.matmul(o, lhsT=ident_f, rhs=neg, start=False, stop=True), [cst], [pe_])
                P.act(lambda e: e.activation(out=out_t[:].rearrange("p h c -> p (h c)"), in_=pe_[:], func=AF.Exp), [pe_], [out_t])
                P.free(pe_)

            emat(T["vL"], negs, EL)
            pl = P.bank()
            for h in range(4):
                P.pe(lambda e, h=h: e.matmul(pl[:, h * 128:(h + 1) * 128], lhsT=kT_[h][:, cs], rhs=kT_[h][:, cs], start=True, stop=True),
                     [kT_[h]], [pl])
            P.dve(lambda e: e.tensor_tensor(out=Mx[:], in0=v3(pl), in1=EL[:], op=ALU.mult), [pl, EL], [Mx])
            P.free(pl)
            yield
            if full:
                emat(T["vA"], negi, EA)
                pa = P.bank()
                for h in range(4):
                    P.pe(lambda e, h=h: e.matmul(pa[:, h * 128:(h + 1) * 128], lhsT=kT_[h][:, cs], rhs=qT[h][:, cs], start=True, stop=True),
                         [kT_[h], qT[h]], [pa])
                P.dve(lambda e: e.tensor_tensor(out=attnT[:], in0=v3(pa), in1=EA[:], op=ALU.mult), [pa, EA], [attnT])
                P.free(pa)
                yield
            pt = P.bank()
            for h in range(4):
                P.pe(lambda e, h=h: e.transpose(out=pt[:].bitcast(BF16)[:, h * 128:(h + 1) * 128], in_=Mx[:, h, :], identity=ident_b[:]),
                     [Mx, ident_b], [pt])
            P.act(lambda e: e.activation(out=Px[:], in_=vb3(pt), func=AF.Copy), [pt], [Px])
            P.free(pt)
            P.dve(lambda e: e.scalar_tensor_tensor(out=Xx[:], in0=Mx[:], scalar=-1.0, in1=identB4[:], op0=ALU.mult, op1=ALU.add),
                  [Mx, identB4], [Xx])
            yield
            Mc, Mo = Mx, B["M2"]
            for k in range(6):
                if k < 5:
                    pm = P.bank()
                    for h in range(4):
                        P.pe(lambda e, h=h, pm=pm, Mc=Mc: e.matmul(pm[:, h * 128:(h + 1) * 128], lhsT=Px[:, h, :], rhs=Mc[:, h, :], start=True, stop=True),
                             [Mc, Px], [pm])
                    yield
                    P.dve(lambda e, pm=pm, Mo=Mo: e.tensor_copy(out=Mo[:].rearrange("p h c -> p (h c)"), in_=pm[:]), [pm], [Mo])
                    P.free(pm)
                pp = P.bank()
                for h in range(4):
                    P.pe(lambda e, h=h, pp=pp, Mc=Mc: e.matmul(pp[:, h * 128:(h + 1) * 128], lhsT=Mc[:, h, :], rhs=Px[:, h, :], start=True, stop=True),
                         [Mc, Px], [pp])
                yield
                P.act(lambda e, pp=pp: e.activation(out=Px[:].rearrange("p h c -> p (h c)"), in_=pp[:], func=AF.Copy), [pp], [Px])
                P.free(pp)
                px = P.bank()
                for h in range(4):
                    P.pe(lambda e, h=h, px=px: e.matmul(px[:, h * 128:(h + 1) * 128], lhsT=Px[:, h, :], rhs=Xx[:, h, :], start=True, stop=True),
                         [Px, Xx], [px])
                yield
                P.dve(lambda e, px=px: e.tensor_tensor(out=Xx[:], in0=v3(px), in1=Xx[:], op=ALU.add), [px, Xx], [Xx])
                P.free(px)
                Mc, Mo = Mo, Mc
            AT = Xx
            pu = P.bank()
            for h in range(4):
                P.pe(lambda e, h=h: e.matmul(pu[:, h * 128:(h + 1) * 128], lhsT=AT[:, h, :], rhs=bv[:, h, :], start=True, stop=True), [AT, bv], [pu])
            P.act(lambda e: e.activation(out=usb[:].rearrange("p h c -> p (h c)"), in_=pu[:], func=AF.Copy), [pu], [usb])
            P.free(pu)
            yield
            pw = P.bank()
            for h in range(4):
                P.pe(lambda e, h=h: e.matmul(pw[:, h * 128:(h + 1) * 128], lhsT=bek[:, h, :], rhs=AT[:, h, :], start=True, stop=True), [AT, bek], [pw])
            P.dve(lambda e: e.tensor_copy(out=wT[:].rearrange("p h c -> p (h c)"), in_=pw[:]), [pw], [wT])
            P.free(pw)
            yield

        def gdn_scan_all(full, qT, sg):
            T = tk
            po = {}

            def main1(blk):
                B = BS[blk % NSET]
                pws = P.bank()
                for h in range(4):
                    P.pe(lambda e, h=h: e.matmul(pws[:, h * 128:(h + 1) * 128], lhsT=B["wT"][:, h, :], rhs=Sb[:, h, :], start=True, stop=True),
                         [B["wT"], Sb], [pws])
                P.dve(lambda e: e.tensor_tensor(out=vnew[:], in0=B["usb"][:], in1=v3(pws), op=ALU.subtract), [B["usb"], pws], [vnew])
                P.free(pws)

            def main2(blk):
                B = BS[blk % NSET]
                cs = slice(blk * 128, (blk + 1) * 128)
                pds = P.bank()
                for h in range(4):
                    P.pe(lambda e, h=h: e.matmul(pds[:, h * 128:(h + 1) * 128], lhsT=B["kdec"][:, h, :], rhs=vnew[:, h, :], start=True, stop=True),
                         [B["kdec"], vnew], [pds])
                if full:
                    po1 = P.bank()
                    po2 = P.bank()
                    for h in range(4):
                        P.pe(lambda e, h=h: e.matmul(po1[:, h * 128:(h + 1) * 128], lhsT=qT[h][:, cs], rhs=Sb[:, h, :], start=True, stop=True),
                             [qT[h], Sb], [po1])
                    for h in range(4):
                        P.pe(lambda e, h=h: e.matmul(po2[:, h * 128:(h + 1) * 128], lhsT=B["attnT"][:, h, :], rhs=vnew[:, h, :], start=True, stop=True),
                             [B["attnT"], vnew], [po2])
                    po[blk] = (po1, po2)
                for h in range(4):
                    P.dve(lambda e, h=h: e.scalar_tensor_tensor(out=Sf[:, h, :], in0=Sf[:, h, :], scalar=T["egl"][:, blk, h:h + 1],
                                                                in1=pds[:, h * 128:(h + 1) * 128], op0=ALU.mult, op1=ALU.add),
                          [Sf, T["egl"], pds], [Sf])
                P.free(pds)
                P.act(lambda e: e.activation(out=Sb[:], in_=Sf[:], func=AF.Copy), [Sf], [Sb])

            def tail1(blk):
                po1, po2 = po[blk]
                P.dve(lambda e: e.tensor_tensor(out=o1s[:], in0=v3(po1), in1=bc4(T["so"], blk), op=ALU.mult), [po1, T["so"]], [o1s])
                P.dve(lambda e: e.tensor_tensor(out=osb[:], in0=v3(po2), in1=o1s[:], op=ALU.add), [po2, o1s], [osb])
                P.free(po1, po2)
                P.pool(lambda e: e.tensor_tensor(out=osq[:], in0=osb[:], in1=osb[:], op=ALU.mult), [osb], [osq])
                P.dve(lambda e: e.tensor_reduce(out=ssqo[:], in_=osq[:], axis=AX.X, op=ALU.add), [osq], [ssqo])
                P.act(lambda e: e.activation(out=rso[:], in_=ssqo[:], func=AF.Ln, bias=epsc[:], scale=1.0 / 128), [ssqo, epsc], [rso])
                P.act(lambda e: e.activation(out=rso[:], in_=rso[:], func=AF.Exp, scale=-0.5), [rso], [rso])
                P.dve(lambda e: e.tensor_tensor(out=onb[:], in0=osb[:], in1=rso[:].unsqueeze(2).to_broadcast([128, 4, 128]), op=ALU.mult),
                      [osb, rso], [onb])

            def tail2(blk):
                cs = slice(blk * 128, (blk + 1) * 128)
                pot = P.bank()
                for h in range(4):
                    P.pe(lambda e, h=h: e.transpose(out=pot[:].bitcast(BF16)[:, h * 128:(h + 1) * 128], in_=onb[:, h, :], identity=ident_b[:]),
                         [onb, ident_b], [pot])
                for h in range(4):
                    P.dve(lambda e, h=h: e.scalar_tensor_tensor(out=ocat[h][:, cs], in0=pot[:].bitcast(BF16)[:, h * 128:(h + 1) * 128],
                                                                scalar=pcol("gng"), in1=sg[h][:, cs], op0=ALU.mult, op1=ALU.mult),
                          [pot, par, sg[h]], [ocat[h]])
                P.free(pot)

            for blk in range(NB):
                main1(blk)
                yield
                if full and blk >= 1:
                    tail1(blk - 1)
                    yield
                main2(blk)
                yield
                if full and blk >= 1:
                    tail2(blk - 1)
                    yield
            if full:
                tail1(NB - 1)
                yield
                tail2(NB - 1)
                yield

        def interleave(gens):
            gens = list(gens)
            while gens:
                for g in list(gens):
                    try:
                        next(g)
                    except StopIteration:
                        gens.remove(g)

        for t_ in [hT, hnT, yT, rstd, lnt, rden] + sq + big + ocat + ubuf + sa:
            t_.split()

        def prenorm_h(gi, c0, c1, hb):
            st = P.bank()
            for c in range(KC):
                s_ = sq[c % 2]
                P.act(lambda e, c=c, s_=s_: e.activation(out=s_[:, c0:c1], in_=hT[:, c, c0:c1], func=AF.Square), [hT.hv(hb)], [s_.hv(hb)])
                P.pe(lambda e, c=c, s_=s_: e.matmul(st[:, c0:c1], lhsT=ones_b[:], rhs=s_[:, c0:c1], start=(c == 0), stop=(c == KC - 1)),
                     [ones_b, s_.hv(hb)], [st])
            P.act(lambda e: e.activation(out=lnt[:, c0:c1], in_=st[:, c0:c1], func=AF.Ln, bias=epsc[:], scale=1.0 / D), [st, epsc], [lnt.hv(hb)])
            P.act(lambda e: e.activation(out=rstd[:, c0:c1], in_=lnt[:, c0:c1], func=AF.Exp, scale=-0.5), [lnt.hv(hb)], [rstd.hv(hb)])
            P.free(st)
            for c in range(KC):
                P.dve(lambda e, c=c: e.scalar_tensor_tensor(out=hnT[:, c, c0:c1], in0=hT[:, c, c0:c1], scalar=pcol("ng", gi * 8 + c),
                                                            in1=rstd[:, c0:c1], op0=ALU.mult, op1=ALU.mult),
                      [hT.hv(hb), par, rstd.hv(hb)], [hnT.hv(hb)])

        def postnorm_h(gi, st, c0, c1, hb):
            P.act(lambda e: e.activation(out=lnt[:, c0:c1], in_=st[:, c0:c1], func=AF.Ln, bias=epsc[:], scale=1.0 / D), [st, epsc], [lnt.hv(hb)])
            P.act(lambda e: e.activation(out=rstd[:, c0:c1], in_=lnt[:, c0:c1], func=AF.Exp, scale=-0.5), [lnt.hv(hb)], [rstd.hv(hb)])
            P.free(st)
            for c in range(KC):
                P.dve(lambda e, c=c: e.scalar_tensor_tensor(out=yT[:, c, c0:c1], in0=yT[:, c, c0:c1], scalar=pcol("ng", gi * 8 + c),
                                                            in1=rstd[:, c0:c1], op0=ALU.mult, op1=ALU.mult),
                      [yT.hv(hb), par, rstd.hv(hb)], [yT.hv(hb)])
                (P.pool if c % 2 == 0 else P.dve)(lambda e, c=c: e.tensor_tensor(out=hT[:, c, c0:c1], in0=hT[:, c, c0:c1], in1=yT[:, c, c0:c1], op=ALU.add),
                                                        [hT.hv(hb), yT.hv(hb)], [hT.hv(hb)])

        def proj_h(slab_ids, rhs_tiles, c0, c1, hb):
            st = P.bank()
            m = 0
            for s_ in slab_ids:
                sl = load_slab(s_)
                sa_ = slab_ap(sl, 512)
                for q in range(4):
                    pb = P.bank()
                    for k in range(KC):
                        P.pe(lambda e, k=k, q=q, pb=pb, sa_=sa_: e.matmul(pb[:, c0:c1], lhsT=sa_[:, k, q * 128:(q + 1) * 128],
                                                                          rhs=rhs_tiles[k][:, c0:c1], start=(k == 0), stop=(k == KC - 1)),
                             [sl, rhs_tiles[k].hv(hb)], [pb])
                    s2 = sq[m % 2]
                    P.dve(lambda e, pb=pb, m=m: e.tensor_copy(out=yT[:, m, c0:c1], in_=pb[:, c0:c1]), [pb], [yT.hv(hb)])
                    P.free(pb)
                    P.act(lambda e, m=m, s2=s2: e.activation(out=s2[:, c0:c1], in_=yT[:, m, c0:c1], func=AF.Square), [yT.hv(hb)], [s2.hv(hb)])
                    P.pe(lambda e, m=m, s2=s2: e.matmul(st[:, c0:c1], lhsT=ones_b[:], rhs=s2[:, c0:c1], start=(m == 0), stop=(m == 7)),
                         [ones_b, s2.hv(hb)], [st])
                    m += 1
            return st

        def xattn_h(c0, c1, hb):
            qx = big[0:8]
            ox = big[8:16]
            m = 0
            for s_ in range(2):
                sl = load_slab(SL_Q + s_)
                sa_ = slab_ap(sl, 512)
                for q in range(4):
                    pb = P.bank()
                    for k in range(KC):
                        P.pe(lambda e, k=k, q=q, pb=pb, sa_=sa_: e.matmul(pb[:, c0:c1], lhsT=sa_[:, k, q * 128:(q + 1) * 128], rhs=hnT[:, k, c0:c1],
                                                                          start=(k == 0), stop=(k == KC - 1)), [sl, hnT.hv(hb)], [pb])
                    P.act(lambda e, pb=pb, m=m: e.activation(out=qx[m][:, c0:c1], in_=pb[:, c0:c1], func=AF.Copy, scale=1.0 / 16.0), [pb], [qx[m].hv(hb)])
                    P.free(pb)
                    m += 1
            for h in range(4):
                pts = [ubuf[(2 * h) % 4], ubuf[(2 * h + 1) % 4]]
                for mc in range(2):
                    pb = P.bank()
                    for dc in range(2):
                        P.pe(lambda e, dc=dc, mc=mc, h=h, pb=pb: e.matmul(pb[:, c0:c1], lhsT=KT[:, 2 * h + dc, mc * 128:(mc + 1) * 128],
                                                                          rhs=qx[2 * h + dc][:, c0:c1], start=(dc == 0), stop=(dc == 1)),
                             [KT, qx[2 * h + dc].hv(hb)], [pb])
                    P.act(lambda e, pb=pb, mc=mc, pts=pts: e.activation(out=pts[mc][:, c0:c1], in_=pb[:, c0:c1], func=AF.Exp), [pb], [pts[mc].hv(hb)])
                    P.free(pb)
                pd = P.bank()
                for mc in range(2):
                    P.pe(lambda e, mc=mc, pts=pts, pd=pd: e.matmul(pd[:, c0:c1], lhsT=ones_b[:], rhs=pts[mc][:, c0:c1], start=(mc == 0), stop=(mc == 1)),
                         [ones_b, pts[mc].hv(hb)], [pd])
                P.dve(lambda e, pd=pd: e.reciprocal(out=rden[:, c0:c1], in_=pd[:, c0:c1]), [pd], [rden.hv(hb)])
                P.free(pd)
                for dvc in range(2):
                    pb = P.bank()
                    for mc in range(2):
                        P.pe(lambda e, mc=mc, dvc=dvc, h=h, pb=pb, pts=pts: e.matmul(pb[:, c0:c1], lhsT=Vm[:, mc, (2 * h + dvc) * 128:(2 * h + dvc + 1) * 128],
                                                                                    rhs=pts[mc][:, c0:c1], start=(mc == 0), stop=(mc == 1)),
                             [Vm, pts[mc].hv(hb)], [pb])
                    P.dve(lambda e, pb=pb, h=h, dvc=dvc: e.tensor_tensor(out=ox[2 * h + dvc][:, c0:c1], in0=pb[:, c0:c1], in1=rden[:, c0:c1], op=ALU.mult),
                          [pb, rden.hv(hb)], [ox[2 * h + dvc].hv(hb)])
                    P.free(pb)
            return proj_h([SL_O, SL_O + 1], ox, c0, c1, hb)

        def ffn_h(c0, c1, hb, only_up):
            curu = {}
            n = c1 - c0

            def ffn_up(i):
                s_, q = i // 2, i % 2
                if q == 0:
                    curu["sl"] = load_slab(SL_UP + s_)
                sl = curu["sl"]
                sa_ = slab_ap(sl, 512)
                ua, ub = ubuf[2 * (i % 2)], ubuf[2 * (i % 2) + 1]
                pa_ = P.bank()
                pb_ = P.bank()
                for k in range(KC):
                    P.pe(lambda e, k=k: e.matmul(pa_[:, c0:c1], lhsT=sa_[:, k, q * 128:(q + 1) * 128], rhs=hnT[:, k, c0:c1],
                                                 start=(k == 0), stop=(k == KC - 1)), [sl, hnT.hv(hb)], [pa_])
                for k in range(KC):
                    P.pe(lambda e, k=k: e.matmul(pb_[:, c0:c1], lhsT=sa_[:, k, 256 + q * 128:256 + (q + 1) * 128], rhs=hnT[:, k, c0:c1],
                                                 start=(k == 0), stop=(k == KC - 1)), [sl, hnT.hv(hb)], [pb_])
                P.pool(lambda e: e.tensor_copy(out=ua[:, c0:c0 + 2], in_=uhist[:, i, :]), [uhist], [ua.hv(hb)])
                P.pool(lambda e: e.tensor_copy(out=ub[:, c0:c0 + 2], in_=uhist[:, 22 + i, :]), [uhist], [ub.hv(hb)])
                P.act(lambda e: e.activation(out=ua[:, 2 + c0:2 + c1], in_=pa_[:, c0:c1], func=AF.Copy), [pa_], [ua.hv(hb)])
                P.dve(lambda e: e.tensor_copy(out=ub[:, 2 + c0:2 + c1], in_=pb_[:, c0:c1]), [pb_], [ub.hv(hb)])
                P.free(pa_, pb_)
                P.pool(lambda e: e.tensor_copy(out=uhist[:, i, :], in_=ua[:, c1:c1 + 2]), [ua.hv(hb)], [uhist])
                P.pool(lambda e: e.tensor_copy(out=uhist[:, 22 + i, :], in_=ub[:, c1:c1 + 2]), [ub.hv(hb)], [uhist])

            def conv_h(pb, src, wbase):
                for k in range(3):
                    d = dg[dg_ctr[0] % len(dg)]
                    dg_ctr[0] += 1
                    P.dve(lambda e, d=d, k=k: e.tensor_scalar(out=d[:], in0=ident_b[:], scalar1=pcol("fcw", wbase + k), scalar2=None, op0=ALU.mult),
                          [ident_b, par], [d])
                    P.pe(lambda e, d=d, k=k: e.matmul(pb[:, c0:c1], lhsT=d[:], rhs=src[:, c0 + k:c1 + k], start=(k == 0), stop=(k == 2)),
                         [d, src.hv(hb)], [pb])

            def ffn_act(i):
                ua, ub = ubuf[2 * (i % 2)], ubuf[2 * (i % 2) + 1]
                pca = P.bank()
                pcb = P.bank()
                conv_h(pca, ua, i * 3)
                conv_h(pcb, ub, (22 + i) * 3)
                sat = sa[i % 2]
                P.act(lambda e: e.activation(out=sat[:, c0:c1], in_=pca[:, c0:c1], func=AF.Silu, bias=pcol("fcb", i)), [pca, par], [sat.hv(hb)])
                P.dve(lambda e: e.scalar_tensor_tensor(out=big[i][:, c0:c1], in0=pcb[:, c0:c1], scalar=pcol("fcb", 22 + i), in1=sat[:, c0:c1],
                                                       op0=ALU.add, op1=ALU.mult), [pcb, par, sat.hv(hb)], [big[i].hv(hb)])
                P.free(pca, pcb)

            if only_up:
                for i in range(22):
                    ffn_up(i)
                return None
            for i in range(23):
                if i < 22:
                    ffn_up(i)
                if i >= 1:
                    ffn_act(i - 1)
            st = P.bank()
            for ng in range(4):
                pbs = [P.bank(), P.bank()]
                for kh in range(2):
                    sl = load_slab(SL_DN + ng * 2 + kh)
                    sa_ = slab_ap(sl, 256, 11)
                    for m2 in range(2):
                        for k in range(11):
                            kk = kh * 11 + k
                            P.pe(lambda e, k=k, kk=kk, m2=m2, pbs=pbs, sa_=sa_: e.matmul(pbs[m2][:, c0:c1], lhsT=sa_[:, k, m2 * 128:(m2 + 1) * 128],
                                                                                        rhs=big[kk][:, c0:c1], start=(kk == 0), stop=(kk == 21)),
                                 [sl, big[kk].hv(hb)], [pbs[m2]])
                for m2 in range(2):
                    m = ng * 2 + m2
                    s2 = sq[m % 2]
                    P.dve(lambda e, m=m, pb=pbs[m2]: e.tensor_copy(out=yT[:, m, c0:c1], in_=pb[:, c0:c1]), [pbs[m2]], [yT.hv(hb)])
                    P.act(lambda e, s2=s2, m=m: e.activation(out=s2[:, c0:c1], in_=yT[:, m, c0:c1], func=AF.Square), [yT.hv(hb)], [s2.hv(hb)])
                    P.pe(lambda e, m=m, s2=s2: e.matmul(st[:, c0:c1], lhsT=ones_b[:], rhs=s2[:, c0:c1], start=(m == 0), stop=(m == 7)),
                         [ones_b, s2.hv(hb)], [st])
                P.free(*pbs)
            return st

        def store_h(c0, c1, hb, out_idx):
            for blk in range(c0 // 128, c1 // 128):
                os_ = ost[blk % 2]
                for half in range(2):
                    pb = P.bank()
                    for q in range(4):
                        c = half * 4 + q
                        P.pe(lambda e, c=c, q=q, blk=blk, pb=pb: e.transpose(out=pb[:, q * 128:(q + 1) * 128], in_=hT[:, c, blk * 128:(blk + 1) * 128],
                                                                             identity=ident_f), [hT.hv(hb), cst], [pb])
                    copy_ev(os_[:, half * 512:(half + 1) * 512], pb[:], [pb], [os_])
                    P.free(pb)
                r0 = out_idx * NT + blk * 128
                P.dma("sp", lambda e, os_=os_, r0=r0: e.dma_start(out=yout[r0:r0 + 128, :], in_=os_[:]), [os_], [], "xst%d" % (blk % 2))

        def back_schedule(halves, out_idx):
            halo = out_idx is None
            if len(halves) == 1:
                (c0, c1, hb) = halves[0]
                st = proj_h([SL_OUT, SL_OUT + 1], ocat, c0, c1, hb)
                postnorm_h(1, st, c0, c1, hb)
                prenorm_h(2, c0, c1, hb)
                st = xattn_h(c0, c1, hb)
                postnorm_h(3, st, c0, c1, hb)
                prenorm_h(4, c0, c1, hb)
                st = ffn_h(c0, c1, hb, halo)
                if not halo:
                    postnorm_h(5, st, c0, c1, hb)
                    store_h(c0, c1, hb, out_idx)
            else:
                A, B = halves
                stA = proj_h([SL_OUT, SL_OUT + 1], ocat, *A)
                postnorm_h(1, stA, *A)
                stB = proj_h([SL_OUT, SL_OUT + 1], ocat, *B)
                prenorm_h(2, *A)
                postnorm_h(1, stB, *B)
                ckpt(7)
                stA = xattn_h(*A)
                prenorm_h(2, *B)
                postnorm_h(3, stA, *A)
                stB = xattn_h(*B)
                prenorm_h(4, *A)
                postnorm_h(3, stB, *B)
                ckpt(8)
                stA = ffn_h(A[0], A[1], A[2], False)
                prenorm_h(4, *B)
                postnorm_h(5, stA, *A)
                stB = ffn_h(B[0], B[1], B[2], False)
                store_h(A[0], A[1], A[2], out_idx)
                postnorm_h(5, stB, *B)
                ckpt(9)
                store_h(B[0], B[1], B[2], out_idx)
            if halo:
                P.pool(lambda e: e.tensor_scalar(out=uhist[:], in0=uhist[:], scalar1=pcol("flag"), scalar2=None, op0=ALU.mult), [uhist, par], [uhist])

        def tile(ti, full, out_idx):
            tok0 = ti * NT
            for blk in range(NB):
                xs = xst[blk % 2]
                P.dma("sp", lambda e, blk=blk, xs=xs: e.dma_start(out=xs[:], in_=xin[tok0 + blk * 128: tok0 + (blk + 1) * 128, :]),
                      [], [xs], "xst%d" % (blk % 2))
                for half in range(2):
                    pb = P.bank()
                    for q in range(4):
                        c = half * 4 + q
                        P.pe(lambda e, c=c, q=q, xs=xs, pb=pb: e.transpose(out=pb[:, q * 128:(q + 1) * 128], in_=xs[:, c * 128:(c + 1) * 128],
                                                                           identity=ident_f), [xs, cst], [pb])
                    copy_ev(hT[:, half * 4:half * 4 + 4, blk * 128:(blk + 1) * 128], pb[:].rearrange("p (q c) -> p q c", q=4), [pb], [hT])
                    P.free(pb)
            ckpt(1)
            prenorm(0)
            ckpt(2)
            qs, ks, vs, sgs = big[0:4], big[4:8], big[8:12], big[12:16]
            chunks = []
            for s_, dst in ([(0, qs)] if full else []) + [(1, ks), (2, vs)]:
                for q in range(4):
                    chunks.append((s_, q, dst))
            cur = {}

            def qkv_proj(s_, q):
                if q == 0:
                    cur["sl"] = load_slab(SL_IN + s_)
                sl = cur["sl"]
                sa_ = slab_ap(sl, 512)
                ch = s_ * 4 + q
                pb = P.bank()
                for k in range(KC):
                    P.pe(lambda e, k=k: e.matmul(pb[:], lhsT=sa_[:, k, q * 128:(q + 1) * 128], rhs=hnT[:, k, :],
                                                 start=(k == 0), stop=(k == KC - 1)), [sl, hnT], [pb])
                z = zc[ch % 4]
                P.pool(lambda e: e.tensor_copy(out=z[:, 0:3], in_=zhist[:, ch, :]), [zhist], [z])
                copy_ev(z[:, 3:3 + NT], pb[:], [pb], [z])
                P.free(pb)
                P.pool(lambda e: e.tensor_copy(out=zhist[:, ch, :], in_=z[:, NT:NT + 3]), [z], [zhist])

            def qkv_conv(s_, q, dst):
                ch = s_ * 4 + q
                z = zc[ch % 4]
                pc = P.bank()
                conv_mm(pc, z, z, 4, "gcw", ch * 4)
                P.act(lambda e: e.activation(out=dst[q][:], in_=pc[:], func=AF.Silu), [pc], [dst[q]])
                P.free(pc)

            for i in range(len(chunks) + 1):
                if i < len(chunks):
                    qkv_proj(chunks[i][0], chunks[i][1])
                if i >= 1:
                    qkv_conv(*chunks[i - 1])
            pab = P.bank()
            for blk in range(NB):
                for k in range(KC):
                    P.pe(lambda e, k=k, blk=blk: e.matmul(pab[:, blk * 8:(blk + 1) * 8], lhsT=hnT[:, k, blk * 128:(blk + 1) * 128], rhs=wab[:, k, :],
                                                          start=(k == 0), stop=(k == KC - 1)), [hnT, wab], [pab])
            P.dve(lambda e: e.tensor_copy(out=abtok[:].rearrange("p b c -> p (b c)"), in_=pab[:, 0:NB * 8]), [pab], [abtok])
            P.free(pab)
            pss = P.bank()
            lst = ([(qs[h], h) for h in range(4)] if full else []) + [(ks[h], 4 + h) for h in range(4)]
            for i, (src, col) in enumerate(lst):
                s2 = sq[i % 2]
                P.pool(lambda e, src=src, s2=s2: e.tensor_tensor(out=s2[:], in0=src[:], in1=src[:], op=ALU.mult), [src], [s2])
                for blk in range(NB):
                    P.pe(lambda e, blk=blk, s2=s2, col=col: e.matmul(pss[:, blk * 8 + col:blk * 8 + col + 1], lhsT=s2[:, blk * 128:(blk + 1) * 128],
                                                                      rhs=ones_b[:, 0:1], start=True, stop=True), [s2, ones_b], [pss])
            if full:
                P.dve(lambda e: e.tensor_copy(out=ssq[:].rearrange("p b c -> p (b c)"), in_=pss[:, 0:NB * 8]), [pss], [ssq])
            else:
                P.dve(lambda e: e.tensor_copy(out=ssq[:, :, 4:8], in_=pss[:, 0:NB * 8].rearrange("p (b c) -> p b c", c=8)[:, :, 4:8]), [pss], [ssq])
            P.free(pss)
            ckpt(3)
            gdn_scalars(full)
            ckpt(4)

            def filler():
                sl = load_slab(SL_IN + 3)
                sa_ = slab_ap(sl, 512)
                for q in range(4):
                    pb = P.bank()
                    for k in range(KC):
                        P.pe(lambda e, k=k, q=q, pb=pb, sa_=sa_: e.matmul(pb[:], lhsT=sa_[:, k, q * 128:(q + 1) * 128], rhs=hnT[:, k, :],
                                                                          start=(k == 0), stop=(k == KC - 1)), [sl, hnT], [pb])
                        if k % 4 == 3:
                            yield
                    P.act(lambda e, pb=pb, q=q: e.activation(out=sgs[q][:], in_=pb[:], func=AF.Silu), [pb], [sgs[q]])
                    P.free(pb)
                for s_ in range(2):
                    sl = load_slab(SL_IN + 4 + s_)
                    sa_ = slab_ap(sl, 512)
                    for q in range(2):
                        ch = 2 * s_ + q
                        pa_ = P.bank()
                        pb_ = P.bank()
                        for k in range(KC):
                            P.pe(lambda e, k=k, q=q, pa_=pa_, sa_=sa_: e.matmul(pa_[:], lhsT=sa_[:, k, q * 128:(q + 1) * 128], rhs=hnT[:, k, :],
                                                                                start=(k == 0), stop=(k == KC - 1)), [sl, hnT], [pa_])
                            if k % 4 == 3:
                                yield
                        for k in range(KC):
                            P.pe(lambda e, k=k, q=q, pb_=pb_, sa_=sa_: e.matmul(pb_[:], lhsT=sa_[:, k, 256 + q * 128:256 + (q + 1) * 128], rhs=hnT[:, k, :],
                                                                                start=(k == 0), stop=(k == KC - 1)), [sl, hnT], [pb_])
                            if k % 4 == 3:
                                yield
                        cb = cbuf[ch]
                        P.pool(lambda e, cb=cb: e.tensor_copy(out=cb[:, 0:30], in_=cb[:, NT:NT + 30]), [cb], [cb])
                        P.act(lambda e, pb_=pb_: e.activation(out=lnt[:], in_=pb_[:], func=AF.Sigmoid), [pb_], [lnt])
                        P.dve(lambda e, pa_=pa_, cb=cb: e.tensor_tensor(out=cb[:, 30:30 + NT], in0=pa_[:], in1=lnt[:], op=ALU.mult), [pa_, lnt], [cb])
                        P.free(pa_, pb_)
                pmean = P.bank()
                pe2 = P.bank()
                for ch in range(4):
                    pc = P.bank()
                    for k in range(31):
                        d = dg[dg_ctr[0] % len(dg)]
                        dg_ctr[0] += 1
                        P.dve(lambda e, d=d, k=k, ch=ch: e.tensor_scalar(out=d[:], in0=ident_b[:], scalar1=pcol("cdw", ch * 31 + k), scalar2=None,
                                                                         op0=ALU.mult), [ident_b, par], [d])
                        P.pe(lambda e, d=d, k=k, ch=ch, pc=pc: e.matmul(pc[:], lhsT=d[:], rhs=cbuf[ch][:, k:k + NT], start=(k == 0), stop=(k == 30)),
                             [d, cbuf[ch]], [pc])
                        if k % 4 == 3:
                            yield
                    P.act(lambda e, pc=pc, ch=ch: e.activation(out=yT[:, ch, :], in_=pc[:], func=AF.Identity, bias=pcol("cdb", ch)), [pc, par], [yT])
                    s2 = sq[ch % 2]
                    P.act(lambda e, pc=pc, ch=ch, s2=s2: e.activation(out=s2[:], in_=pc[:], func=AF.Square, bias=pcol("cdb", ch)), [pc, par], [s2])
                    P.free(pc)
                    P.pe(lambda e, ch=ch: e.matmul(pmean[:], lhsT=ones_f[:], rhs=yT[:, ch, :], start=(ch == 0), stop=(ch == 3)), [ones_f, yT], [pmean])
                    P.pe(lambda e, ch=ch, s2=s2: e.matmul(pe2[:], lhsT=ones_b[:], rhs=s2[:], start=(ch == 0), stop=(ch == 3)), [ones_b, s2], [pe2])
                    yield
                P.act(lambda e: e.activation(out=msb[:], in_=pmean[:], func=AF.Copy, scale=1.0 / 512), [pmean], [msb])
                P.pool(lambda e: e.tensor_tensor(out=var[:], in0=msb[:], in1=msb[:], op=ALU.mult), [msb], [var])
                P.dve(lambda e: e.scalar_tensor_tensor(out=var[:], in0=pe2[:], scalar=1.0 / 512, in1=var[:], op0=ALU.mult, op1=ALU.subtract),
                      [pe2, var], [var])
                P.free(pmean, pe2)
                P.act(lambda e: e.activation(out=lnt[:], in_=var[:], func=AF.Ln, bias=epsc[:]), [var, epsc], [lnt])
                P.act(lambda e: e.activation(out=rstd[:], in_=lnt[:], func=AF.Exp, scale=-0.5), [lnt], [rstd])
                yield
                for ch in range(4):
                    P.pool(lambda e, ch=ch: e.tensor_tensor(out=yT[:, ch, :], in0=yT[:, ch, :], in1=msb[:], op=ALU.subtract), [yT, msb], [yT])
                    P.dve(lambda e, ch=ch: e.tensor_tensor(out=yT[:, ch, :], in0=yT[:, ch, :], in1=rstd[:], op=ALU.mult), [yT, rstd], [yT])
                    P.act(lambda e, ch=ch: e.activation(out=ocat[4 + ch][:], in_=yT[:, ch, :], func=AF.Silu, bias=pcol("clb", ch), scale=pcol("clg", ch)),
                          [yT, par], [ocat[4 + ch]])
                    yield

            fil = filler() if full else None

            def run_with_filler(gens, nfill, every):
                nonlocal fil
                gens = list(gens)
                rnd = 0
                while gens:
                    for g in list(gens):
                        try:
                            next(g)
                        except StopIteration:
                            gens.remove(g)
                    rnd += 1
                    if rnd % every == 0:
                        for _ in range(nfill):
                            if fil is not None:
                                try:
                                    next(fil)
                                except StopIteration:
                                    fil = None

            run_with_filler([gdn_pre(blk, full, qs, ks, vs) for blk in range(NB)], 1, 2)
            ckpt(5)
            run_with_filler([gdn_scan_all(full, qs, sgs)], 3, 1)
            if fil is not None:
                for _ in fil:
                    pass
            if not full:
                return
            ckpt(6)
            if out_idx is None:
                halves = [(NT - 128, NT, None)]
            else:
                halves = [(0, NT, None)]
            back_schedule(halves, out_idx)

        try:
            ti = 0
            per = (len(late_list) + max(NS, 1) - 1) // max(NS, 1)
            for _ in range(NS):
                tile(ti, False, None)
                issue_late(per)
                ti += 1
            issue_late(len(late_list))
            tile(ti, True, None)
            ti += 1
            for i in range(NF):
                tile(ti, True, i)
                ti += 1
            P.final = ["xst0", "xst1"]
        except StopBuild:
            P.final = [k for k in P.dma_cnt.keys()]
        P.emit()
    return nc


def make_consts():
    c = np.zeros((128, 4, 128), np.float32)
    c[:, 0, :] = np.eye(128, dtype=np.float32)
    j = np.arange(128)[:, None]
    cc = np.arange(128)[None, :]
    c[:, 1, :] = (j <= cc).astype(np.float32)
    c[:, 2, :] = np.where(cc >= j, 0.0, NEGV)
    c[:, 3, :] = np.where(cc > j, 0.0, NEGV)
    return c.reshape(128, 512)


def make_params(inp, flag):
    p = np.zeros((128, NPAR), np.float32)

    def put(name, arr):
        p[:, PO[name]:PO[name] + arr.shape[1]] = arr

    def chunked(v):
        return np.ascontiguousarray(v.reshape(-1, 128).T)

    ng = inp["norm_g"][0]
    put("ng", np.concatenate([chunked(ng[i]) for i in range(6)], axis=1))
    gcw = inp["gdn_conv_w"][0]
    put("gcw", np.ascontiguousarray(gcw.reshape(4, 12, 128).transpose(2, 1, 0)).reshape(128, 48))
    cdw = inp["cfm_dw_w"][0]
    put("cdw", np.ascontiguousarray(cdw.reshape(31, 4, 128).transpose(2, 1, 0)).reshape(128, 124))
    put("cdb", chunked(inp["cfm_dw_b"][0]))
    put("clg", chunked(inp["cfm_ln_g"][0]))
    put("clb", chunked(inp["cfm_ln_b"][0]))
    fcw = inp["ffn_conv_w"][0]
    put("fcw", np.ascontiguousarray(fcw.reshape(3, 44, 128).transpose(2, 1, 0)).reshape(128, 132))
    put("fcb", chunked(inp["ffn_conv_b"][0]))
    put("gng", inp["gdn_norm_g"][0].reshape(128, 1))
    put("alog", np.tile(inp["gdn_a_log"][0][None, :], (128, 4)))
    put("dtb", np.tile(inp["gdn_dt_bias"][0][None, :], (128, 4)))
    put("mng", chunked(inp["mem_norm_g"][0]))
    p[:, PO["flag"]] = flag
    return p


_NC_CACHE = {}


def run(inputs, NF, NS, n_batch, dbg=False):
    inp = {k: np.asarray(v, dtype=np.float32) for k, v in inputs.items()}
    key = (NF, NS, dbg)
    if key not in _NC_CACHE:
        _NC_CACHE[key] = build(NF, NS, dbg)
    nc = _NC_CACHE[key]
    half = NF * NT
    nprev = (NS + 1) * NT
    consts = make_consts()
    in_maps = []
    for b in range(n_batch):
        for j in range(2):
            xall = np.zeros((nprev + half, D), np.float32)
            if j == 1:
                xall[:nprev] = inp["x"][b, 0:half]
            xall[nprev:] = inp["x"][b, j * half:(j + 1) * half]
            in_maps.append({
                "xin": xall, "memin": np.ascontiguousarray(inp["mem"][b]),
                "params": make_params(inp, float(j)), "consts": consts,
                "w_in": np.ascontiguousarray(inp["w_in"][0]), "w_out": np.ascontiguousarray(inp["w_out"][0]),
                "w_q": np.ascontiguousarray(inp["xa_w_q"][0]), "w_kv": np.ascontiguousarray(inp["xa_w_kv"][0]),
                "w_o": np.ascontiguousarray(inp["xa_w_o"][0]), "w_up": np.ascontiguousarray(inp["ffn_w_up"][0]),
                "w_dn": np.ascontiguousarray(inp["ffn_w_down"][0]),
            })
    res = run_bass_kernel_spmd(nc, in_maps, core_ids=list(range(len(in_maps))))
    out = np.zeros((n_batch, 2 * half, D), np.float32)
    for b in range(n_batch):
        for j in range(2):
            out[b, j * half:(j + 1) * half] = res.results[2 * b + j]["yout"]
    return out, res


def kernel(**inputs):
    out, _ = run(inputs, NF=8, NS=7, n_batch=4)
    return out
```

```python
import numpy as np
from contextlib import ExitStack
import concourse.bass as bass
import concourse.mybir as mybir
from concourse.bass_utils import run_bass_kernel_spmd

F32 = mybir.dt.float32
BF16 = mybir.dt.bfloat16
AF = mybir.ActivationFunctionType
ALU = mybir.AluOpType
AX = mybir.AxisListType

D = 1024
NT = 512
NB = NT // 128
H = 4
DFF = 2816
MEM = 256
EPS = 1e-6
NEGV = -30000.0
KC = 8

PO = {}
_o = 0
for _n, _w in [("ng", 48), ("gcw", 48), ("cdw", 124), ("cdb", 4), ("clg", 4), ("clb", 4),
               ("fcw", 132), ("fcb", 44), ("gng", 1), ("alog", 16), ("dtb", 16), ("mng", 8),
               ("flag", 1)]:
    PO[_n] = _o
    _o += _w
NPAR = _o

SL_IN, SL_OUT, SL_Q, SL_KV, SL_O, SL_UP, SL_DN = 0, 6, 8, 10, 14, 16, 27
NSLAB = 35


import os
STOP = float(os.environ.get("KSTOP", "1000"))


class StopBuild(Exception):
    pass


def ckpt(n):
    if n >= STOP:
        raise StopBuild()


class Buf:
    __slots__ = ("w", "rs", "const")

    def __init__(self):
        self.w = None
        self.rs = []
        self.const = False


class Tile:
    def __init__(self, t, bufs=None):
        self.t = t
        self.bufs = bufs if bufs is not None else [Buf()]
        self.h = None

    def split(self):
        self.bufs = [Buf(), Buf()]
        self.h = [Tile(self.t, [self.bufs[0]]), Tile(self.t, [self.bufs[1]])]
        return self

    def hv(self, hb):
        return self if hb is None else self.hv(hb)

    def __getitem__(self, idx):
        return self.t[idx]


class Ins:
    __slots__ = ("eng", "fn", "idx", "dma_key", "dma_val", "deps", "needs_inc", "ordinal", "waits")

    def __init__(self, eng, fn, idx, dma_key):
        self.eng = eng
        self.fn = fn
        self.idx = idx
        self.dma_key = dma_key
        self.dma_val = None
        self.deps = []
        self.needs_inc = False
        self.ordinal = 0
        self.waits = []


class Prog:
    ENGS = ["pe", "act", "dve", "pool", "sp"]

    def __init__(self, nc, es):
        self.nc = nc
        self.es = es
        self.ins = {e: [] for e in self.ENGS}
        self.dma_cnt = {}
        self.dma_sem = {}
        self.barrier_keys = set()
        self.sem = {}
        self.nfresh = 0
        self.free_banks = []
        self.final = []

    def sb(self, name, shape, dt):
        return Tile(self.es.enter_context(self.nc.sbuf_tensor(name, list(shape), dt)))

    def mkbanks(self):
        for i in range(8):
            t = Tile(self.es.enter_context(self.nc.psum_tensor("bank%d" % i, [128, 512], F32)))
            self.free_banks.append(t)

    def bank(self):
        assert self.free_banks, "out of PSUM banks"
        return self.free_banks.pop(0)

    def free(self, *bs):
        for b in bs:
            self.free_banks.append(b)

    def _add(self, eng, fn, r, w, dma_key=None):
        ins = Ins(eng, fn, len(self.ins[eng]), dma_key)
        deps = {}
        rb = [b for t in r for b in t.bufs]
        wb = [b for t in w for b in t.bufs]
        for b in rb:
            if b.w is not None:
                deps[id(b.w)] = (b.w, True)
        for b in wb:
            if b.w is not None and id(b.w) not in deps:
                deps[id(b.w)] = (b.w, False)
            for x in b.rs:
                if id(x) not in deps:
                    deps[id(x)] = (x, False)
        for b in rb:
            if not b.const:
                b.rs.append(ins)
        for b in wb:
            b.w = ins
            b.rs = []
        ins.deps = [v for k, v in deps.items() if v[0] is not ins]
        self.ins[eng].append(ins)
        if dma_key is not None:
            if dma_key == "fresh":
                dma_key = "fresh%d" % self.nfresh
                self.nfresh += 1
                ins.dma_key = dma_key
            self.dma_cnt[dma_key] = self.dma_cnt.get(dma_key, 0) + 16
            ins.dma_val = self.dma_cnt[dma_key]
        return ins

    def pe(self, fn, r, w):
        return self._add("pe", fn, r, w)

    def act(self, fn, r, w):
        return self._add("act", fn, r, w)

    def dve(self, fn, r, w):
        return self._add("dve", fn, r, w)

    def pool(self, fn, r, w):
        return self._add("pool", fn, r, w)

    def dma(self, q, fn, r, w, key):
        return self._add(q, fn, r, w, dma_key=key)

    def resolve(self):
        for e in self.ENGS:
            for ins in self.ins[e]:
                for (y, raw) in ins.deps:
                    if y.dma_key is not None:
                        continue
                    if y.eng == e:
                        if e in ("act", "dve", "pool") and raw:
                            y.needs_inc = True
                        continue
                    y.needs_inc = True
        for e in self.ENGS:
            c = 0
            for ins in self.ins[e]:
                if ins.dma_key is None and ins.needs_inc:
                    c += 1
                    ins.ordinal = c
        for k in list(self.dma_cnt.keys()):
            self.dma_sem[k] = self.es.enter_context(self.nc.semaphore("d_" + k))
        for e in self.ENGS:
            self.sem[e] = self.es.enter_context(self.nc.semaphore("e_" + e))
        for e in self.ENGS:
            waited = {}
            for ins in self.ins[e]:
                ws = {}
                for (y, raw) in ins.deps:
                    if y.dma_key is not None:
                        k = ("d", y.dma_key)
                        v = self.dma_cnt[y.dma_key] if y.dma_key in self.barrier_keys else y.dma_val
                    else:
                        if y.eng == e and not (e in ("act", "dve", "pool") and raw):
                            continue
                        k = ("e", y.eng)
                        v = y.ordinal
                    if waited.get(k, 0) >= v:
                        continue
                    if ws.get(k, 0) < v:
                        ws[k] = v
                for k, v in ws.items():
                    waited[k] = v
                    sem = self.dma_sem[k[1]] if k[0] == "d" else self.sem[k[1]]
                    ins.waits.append((sem, v))

    def emit(self):
        self.resolve()
        nc = self.nc
        prog = self

        def run(e, eng):
            for ins in prog.ins[e]:
                for (sem, v) in ins.waits:
                    eng.wait_ge(sem, v)
                bi = ins.fn(eng)
                if ins.dma_key is not None:
                    bi.then_inc(prog.dma_sem[ins.dma_key], 16)
                elif ins.needs_inc:
                    bi.then_inc(prog.sem[e], 1)
            if e == "sp":
                for k in prog.final:
                    eng.wait_ge(prog.dma_sem[k], prog.dma_cnt[k])

        with nc.Block() as block:
            @block.tensor
            def _(eng):
                run("pe", eng)

            @block.scalar
            def _(eng):
                run("act", eng)

            @block.vector
            def _(eng):
                run("dve", eng)

            @block.gpsimd
            def _(eng):
                run("pool", eng)

            @block.sync
            def _(eng):
                run("sp", eng)


def build(NF, NS, dbg=False):
    NTOK = (NS + 1 + NF) * NT
    nc = bass.Bass("TRN2", target_bir_lowering=False)
    xin = nc.dram_tensor("xin", [NTOK, D], F32, kind="ExternalInput").ap()
    memin = nc.dram_tensor("memin", [MEM, D], F32, kind="ExternalInput").ap()
    params = nc.dram_tensor("params", [128, NPAR], F32, kind="ExternalInput").ap()
    consts = nc.dram_tensor("consts", [128, 4 * 128], F32, kind="ExternalInput").ap()
    w_in = nc.dram_tensor("w_in", [D, 3080], F32, kind="ExternalInput").ap()
    w_out = nc.dram_tensor("w_out", [D, D], F32, kind="ExternalInput").ap()
    w_q = nc.dram_tensor("w_q", [D, D], F32, kind="ExternalInput").ap()
    w_kv = nc.dram_tensor("w_kv", [D, 2 * D], F32, kind="ExternalInput").ap()
    w_o = nc.dram_tensor("w_o", [D, D], F32, kind="ExternalInput").ap()
    w_up = nc.dram_tensor("w_up", [D, 2 * DFF], F32, kind="ExternalInput").ap()
    w_dn = nc.dram_tensor("w_dn", [DFF, D], F32, kind="ExternalInput").ap()
    yout = nc.dram_tensor("yout", [NF * NT, D], F32, kind="ExternalOutput").ap()
    wscr = nc.dram_tensor("wscr", [NSLAB, 128, 4096], BF16, kind="Internal").ap()
    dbg_out = None
    if dbg:
        dbg_out = nc.dram_tensor("dbg", [128, 16, NT], F32, kind="ExternalOutput").ap()

    es = ExitStack()
    with es:
        P = Prog(nc, es)
        P.mkbanks()
        DR = Tile(None)

        cst = P.sb("cst", [128, 4, 128], F32)
        par = P.sb("par", [128, NPAR], F32)
        ident_b = P.sb("ident_b", [128, 128], BF16)
        identB4 = P.sb("identB4", [128, 4, 128], BF16)
        ones_b = P.sb("ones_b", [128, 128], BF16)
        ones_f = P.sb("ones_f", [128, 128], F32)
        epsc = P.sb("epsc", [128, 1], F32)
        lnc = P.sb("lnc", [128, 1], F32)
        expA = P.sb("expA", [128, 16], F32)
        wab = P.sb("wab", [128, KC, 8], BF16)
        NSLOT = 3
        slots = [P.sb("slot%d" % i, [128, 4096], BF16) for i in range(NSLOT)]
        KT = P.sb("KT", [128, 8, MEM], BF16)
        Vm = P.sb("Vm", [128, 2, D], BF16)
        hT = P.sb("hT", [128, KC, NT], F32)
        hnT = P.sb("hnT", [128, KC, NT], BF16)
        yT = P.sb("yT", [128, KC, NT], F32)
        xst = [P.sb("xst%d" % i, [128, D], F32) for i in range(2)]
        big = [P.sb("big%d" % i, [128, NT], BF16) for i in range(22)]
        zc = [P.sb("zc%d" % i, [128, 3 + NT], BF16) for i in range(4)]
        zhist = P.sb("zhist", [128, 12, 3], BF16)
        cbuf = [P.sb("cbuf%d" % i, [128, 30 + NT], BF16) for i in range(4)]
        ubuf = [P.sb("ubuf%d" % i, [128, 2 + NT], BF16) for i in range(4)]
        uhist = P.sb("uhist", [128, 44, 2], BF16)
        ocat = [P.sb("ocat%d" % i, [128, NT], BF16) for i in range(8)]
        sq = [P.sb("sq%d" % i, [128, NT], BF16) for i in range(2)]
        rstd = P.sb("rstd", [128, NT], F32)
        lnt = P.sb("lnt", [128, NT], F32)
        dg = [P.sb("dg%d" % i, [128, 128], BF16) for i in range(12)]
        rden = P.sb("rden", [128, NT], F32)
        msb = P.sb("msb", [128, NT], F32)
        var = rden
        sa = [P.sb("sa%d" % i, [128, NT], BF16) for i in range(2)]
        abtok = P.sb("abtok", [128, NB, 8], F32)
        ssq = P.sb("ssq", [128, NB, 8], F32)
        tk = {n: P.sb("tk_" + n, [128, NB, 4], F32) for n in
              ["a", "g", "lb", "lrq", "lrk", "gc", "gl", "vL", "vA", "bj", "b", "sw", "skd", "egl", "so", "t1", "t2"]}
        Sf = P.sb("Sf", [128, 4, 128], F32)
        Sb = P.sb("Sb", [128, 4, 128], BF16)
        NSET = 4
        BS = []
        for i in range(NSET):
            BS.append({n: P.sb("%s_%d" % (n, i), [128, 4, 128], dt) for n, dt in
                       [("kdec", BF16), ("bek", BF16), ("bv", BF16), ("M", BF16), ("M2", BF16), ("P", BF16), ("X", BF16),
                        ("attnT", BF16), ("usb", F32), ("wT", BF16)]})
        ELs = [P.sb("EL%d" % i, [128, 4, 128], BF16) for i in range(2)]
        EAs = [P.sb("EA%d" % i, [128, 4, 128], BF16) for i in range(2)]
        vnew = P.sb("vnew", [128, 4, 128], BF16)
        osb = P.sb("osb", [128, 4, 128], F32)
        scr4 = P.sb("scr4", [128, 4, 128], F32)
        o1s = scr4
        osq = scr4
        Stmp = P.sb("Stmp", [128, 4, 128], F32)
        ssqo = P.sb("ssqo", [128, 4], F32)
        rso = P.sb("rso", [128, 4], F32)
        onb = P.sb("onb", [128, 4, 128], BF16)
        ost = xst

        def pcol(name, i=0, n=1):
            o = PO[name] + i
            return par[:, o:o + n]

        ident_f = cst[:, 0, :]
        utri = cst[:, 1, :]
        negi = cst[:, 2, :]
        negs = cst[:, 3, :]

        P.dma("sp", lambda e: e.dma_start(out=cst[:], in_=consts.rearrange("p (a c) -> p a c", a=4)), [], [cst], "fresh")
        P.dma("sp", lambda e: e.dma_start(out=par[:], in_=params), [], [par], "fresh")
        P.dma("pool", lambda e: e.dma_start(out=wab[:], in_=w_in[:, 2048:2056].rearrange("(kc p) n -> p kc n", p=128)),
              [], [wab], "fresh")

        def conv_cols(W, s, nw, off, c0, n, r0=0, nk=KC):
            o = wscr[s, :, 0:nk * nw].rearrange("p (kc n) -> p kc n", n=nw)[:, :, off:off + n]
            i = W[r0:r0 + nk * 128, c0:c0 + n].rearrange("(kc p) n -> p kc n", p=128)
            early = s in (SL_IN + 1, SL_IN + 2, SL_KV, SL_KV + 1, SL_KV + 2, SL_KV + 3)
            conv_list.append((early, o, i))

        P.barrier_keys.add("wc1")
        P.barrier_keys.add("wc2")
        conv_list = []
        DR2 = Tile(None)
        for s in range(4):
            conv_cols(w_in, SL_IN + s, 512, 0, s * 512, 512)
        for s in range(2):
            for q in range(2):
                conv_cols(w_in, SL_IN + 4 + s, 512, q * 128, 2056 + (2 * s + q) * 128, 128)
                conv_cols(w_in, SL_IN + 4 + s, 512, 256 + q * 128, 2056 + 512 + (2 * s + q) * 128, 128)
        for s in range(4):
            conv_cols(w_kv, SL_KV + s, 512, 0, s * 512, 512)
        for s in range(2):
            conv_cols(w_out, SL_OUT + s, 512, 0, s * 512, 512)
            conv_cols(w_q, SL_Q + s, 512, 0, s * 512, 512)
            conv_cols(w_o, SL_O + s, 512, 0, s * 512, 512)
        for s in range(11):
            for q in range(2):
                conv_cols(w_up, SL_UP + s, 512, q * 128, (2 * s + q) * 128, 128)
                conv_cols(w_up, SL_UP + s, 512, 256 + q * 128, DFF + (2 * s + q) * 128, 128)
        for ng in range(4):
            for kh in range(2):
                conv_cols(w_dn, SL_DN + ng * 2 + kh, 256, 0, ng * 256, 256, r0=kh * 1408, nk=11)

        late_list = [c for c in conv_list if not c[0]]

        def issue_late(n):
            for _ in range(min(n, len(late_list))):
                (early, o, i) = late_list.pop(0)
                DR2.bufs[0].w = P.dma("pool", lambda e, o=o, i=i: e.dma_start(out=o, in_=i), [], [], "wc2")

        P.pool(lambda e: e.memset(ones_b[:], 1.0), [], [ones_b])
        P.pool(lambda e: e.memset(ones_f[:], 1.0), [], [ones_f])
        P.pool(lambda e: e.memset(epsc[:], EPS), [], [epsc])
        P.pool(lambda e: e.memset(lnc[:], -0.5 * float(np.log(128.0))), [], [lnc])
        P.pool(lambda e: e.memset(Sf[:], 0.0), [], [Sf])
        P.pool(lambda e: e.memset(Sb[:], 0.0), [], [Sb])
        P.pool(lambda e: e.memset(uhist[:], 0.0), [], [uhist])
        P.pool(lambda e: e.memset(zhist[:], 0.0), [], [zhist])
        for t in zc + cbuf + ubuf:
            P.pool(lambda e, t=t: e.memset(t[:], 0.0), [], [t])
        P.dve(lambda e: e.tensor_copy(out=ident_b[:], in_=ident_f), [cst], [ident_b])
        for h in range(4):
            P.dve(lambda e, h=h: e.tensor_copy(out=identB4[:, h, :], in_=ident_f), [cst], [identB4])
        P.act(lambda e: e.activation(out=expA[:], in_=pcol("alog", 0, 16), func=AF.Exp), [par], [expA])
        for t in (cst, par, ident_b, identB4, ones_b, ones_f, epsc, lnc, expA, wab):
            pass

        for (early, o, i) in [c for c in conv_list if c[0]]:
            DR.bufs[0].w = P.dma("pool", lambda e, o=o, i=i: e.dma_start(out=o, in_=i), [], [], "wc1")

        slot_ctr = [0]

        def load_slab(s):
            sl = slots[slot_ctr[0] % NSLOT]
            key = "slot%d" % (slot_ctr[0] % NSLOT)
            slot_ctr[0] += 1
            drt = DR if s in (SL_IN + 1, SL_IN + 2, SL_KV, SL_KV + 1, SL_KV + 2, SL_KV + 3) else DR2
            P.dma("sp", lambda e: e.dma_start(out=sl[:], in_=wscr[s]), [drt], [sl], key)
            return sl

        def slab_ap(sl, nw, nk=KC):
            return sl[:, 0:nk * nw].rearrange("p (kc n) -> p kc n", n=nw)

        ev_ctr = [0]

        def copy_ev(out_ap, in_ap, r, w, scale=None):
            ev_ctr[0] += 1
            if ev_ctr[0] % 2 == 0 and scale is None:
                P.dve(lambda e: e.tensor_copy(out=out_ap, in_=in_ap), r, w)
            else:
                if scale is None:
                    P.act(lambda e: e.activation(out=out_ap, in_=in_ap, func=AF.Copy), r, w)
                else:
                    P.act(lambda e: e.activation(out=out_ap, in_=in_ap, func=AF.Copy, scale=scale), r, w)

        def rsqrt_from(ps_ap, r, scale, out_t):
            P.act(lambda e: e.activation(out=lnt[:], in_=ps_ap, func=AF.Ln, bias=epsc[:], scale=scale), r + [epsc], [lnt])
            P.act(lambda e: e.activation(out=out_t[:], in_=lnt[:], func=AF.Exp, scale=-0.5), [lnt], [out_t])

        def prenorm(gi):
            st = P.bank()
            for c in range(KC):
                s = sq[c % 2]
                P.act(lambda e, c=c, s=s: e.activation(out=s[:], in_=hT[:, c, :], func=AF.Square), [hT], [s])
                P.pe(lambda e, c=c, s=s: e.matmul(st[:], lhsT=ones_b[:], rhs=s[:], start=(c == 0), stop=(c == KC - 1)),
                     [ones_b, s], [st])
            rsqrt_from(st[:], [st], 1.0 / D, rstd)
            P.free(st)
            for c in range(KC):
                P.dve(lambda e, c=c: e.scalar_tensor_tensor(out=hnT[:, c, :], in0=hT[:, c, :], scalar=pcol("ng", gi * 8 + c),
                                                            in1=rstd[:], op0=ALU.mult, op1=ALU.mult),
                      [hT, par, rstd], [hnT])

        def postnorm_residual(gi, st):
            rsqrt_from(st[:], [st], 1.0 / D, rstd)
            P.free(st)
            for c in range(KC):
                P.dve(lambda e, c=c: e.scalar_tensor_tensor(out=yT[:, c, :], in0=yT[:, c, :], scalar=pcol("ng", gi * 8 + c),
                                                            in1=rstd[:], op0=ALU.mult, op1=ALU.mult),
                      [yT, par, rstd], [yT])
                (P.pool if c % 2 == 0 else P.dve)(lambda e, c=c: e.tensor_tensor(out=hT[:, c, :], in0=hT[:, c, :], in1=yT[:, c, :], op=ALU.add),
                                                        [hT, yT], [hT])

        def proj_to_yT(slab_ids, rhs_list, nk_per=KC):
            st = P.bank()
            m = 0
            for s in slab_ids:
                sl = load_slab(s)
                sa_ = slab_ap(sl, 512)
                for q in range(4):
                    pb = P.bank()
                    for k in range(KC):
                        P.pe(lambda e, k=k, q=q, pb=pb, sa_=sa_: e.matmul(pb[:], lhsT=sa_[:, k, q * 128:(q + 1) * 128],
                                                                          rhs=rhs_list[k][0], start=(k == 0), stop=(k == KC - 1)),
                             [sl, rhs_list[k][1]], [pb])
                    s2 = sq[m % 2]
                    KV = int(os.environ.get("KVAR", "0"))
                    P.dve(lambda e, pb=pb, m=m: e.tensor_copy(out=yT[:, m, :], in_=pb[:]), [pb], [yT])
                    P.free(pb)
                    P.act(lambda e, m=m, s2=s2: e.activation(out=s2[:], in_=yT[:, m, :], func=AF.Square), [yT], [s2])
                    if KV not in (1, 2):
                        P.pe(lambda e, m=m, s2=s2: e.matmul(st[:], lhsT=ones_b[:], rhs=s2[:], start=(m == 0), stop=(m == 7)),
                             [ones_b, s2], [st])
                    m += 1
            return st

        dg_ctr = [0]

        def conv_mm(pb, src, src_t, ntap, wname, wbase):
            for k in range(ntap):
                d = dg[dg_ctr[0] % len(dg)]
                dg_ctr[0] += 1
                P.dve(lambda e, d=d, k=k: e.tensor_scalar(out=d[:], in0=ident_b[:], scalar1=pcol(wname, wbase + k), scalar2=None,
                                                           op0=ALU.mult),
                       [ident_b, par], [d])
                P.pe(lambda e, d=d, k=k: e.matmul(pb[:], lhsT=d[:], rhs=src[:, k:k + NT], start=(k == 0), stop=(k == ntap - 1)),
                     [d, src_t], [pb])

        if STOP > 0:
            for blk in range(2):
                xs = xst[blk % 2]
                P.dma("sp", lambda e, blk=blk, xs=xs: e.dma_start(out=xs[:], in_=memin[blk * 128:(blk + 1) * 128, :]), [], [xs],
                      "xst%d" % (blk % 2))
                for half in range(2):
                    pb = P.bank()
                    for q in range(4):
                        c = half * 4 + q
                        P.pe(lambda e, c=c, q=q, xs=xs, pb=pb: e.transpose(out=pb[:, q * 128:(q + 1) * 128], in_=xs[:, c * 128:(c + 1) * 128],
                                                                           identity=ident_f), [xs, cst], [pb])
                    P.dve(lambda e, half=half, blk=blk, pb=pb: e.tensor_copy(
                        out=yT[:, half * 4:half * 4 + 4, blk * 128:(blk + 1) * 128],
                        in_=pb[:].rearrange("p (q c) -> p q c", q=4)), [pb], [yT])
                    P.free(pb)
            st = P.bank()
            for c in range(KC):
                s = sq[c % 2]
                P.act(lambda e, c=c, s=s: e.activation(out=s[:, 0:MEM], in_=yT[:, c, 0:MEM], func=AF.Square), [yT], [s])
                P.pe(lambda e, c=c, s=s: e.matmul(st[:, 0:MEM], lhsT=ones_b[:], rhs=s[:, 0:MEM], start=(c == 0), stop=(c == KC - 1)),
                     [ones_b, s], [st])
            P.act(lambda e: e.activation(out=lnt[:, 0:MEM], in_=st[:, 0:MEM], func=AF.Ln, bias=epsc[:], scale=1.0 / D), [st, epsc], [lnt])
            P.act(lambda e: e.activation(out=rstd[:, 0:MEM], in_=lnt[:, 0:MEM], func=AF.Exp, scale=-0.5), [lnt], [rstd])
            P.free(st)
            for c in range(KC):
                P.dve(lambda e, c=c: e.scalar_tensor_tensor(out=hnT[:, c, 0:MEM], in0=yT[:, c, 0:MEM], scalar=pcol("mng", c),
                                                            in1=rstd[:, 0:MEM], op0=ALU.mult, op1=ALU.mult),
                      [yT, par, rstd], [hnT])
            for s in range(2):
                sl = load_slab(SL_KV + s)
                sa_ = slab_ap(sl, 512)
                for q in range(4):
                    pb = P.bank()
                    for k in range(KC):
                        P.pe(lambda e, k=k, q=q, pb=pb, sa_=sa_: e.matmul(pb[:, 0:MEM], lhsT=sa_[:, k, q * 128:(q + 1) * 128], rhs=hnT[:, k, 0:MEM],
                                                                          start=(k == 0), stop=(k == KC - 1)), [sl, hnT], [pb])
                    P.dve(lambda e, pb=pb, s=s, q=q: e.tensor_copy(out=KT[:, s * 4 + q, :], in_=pb[:, 0:MEM]), [pb], [KT])
                    P.free(pb)
            for s in range(2):
                sl = load_slab(SL_KV + 2 + s)
                sa_ = slab_ap(sl, 512)
                for mc in range(2):
                    pb = P.bank()
                    for k in range(KC):
                        P.pe(lambda e, k=k, mc=mc, pb=pb, sa_=sa_: e.matmul(pb[:], lhsT=hnT[:, k, mc * 128:(mc + 1) * 128], rhs=sa_[:, k, :],
                                                                            start=(k == 0), stop=(k == KC - 1)), [sl, hnT], [pb])
                    P.dve(lambda e, pb=pb, s=s, mc=mc: e.tensor_copy(out=Vm[:, mc, s * 512:(s + 1) * 512], in_=pb[:]), [pb], [Vm])
                    P.free(pb)

        def bc4(t, blk):
            return t[:, blk, :].unsqueeze(2).to_broadcast([128, 4, 128])

        def v3(b):
            return b[:].rearrange("p (h c) -> p h c", h=4)

        def vb3(b):
            return b[:].bitcast(BF16)[:, 0:512].rearrange("p (h c) -> p h c", h=4)

        def gdn_scalars(full):
            T = tk
            P.dve(lambda e: e.tensor_tensor(out=T["a"][:], in0=abtok[:, :, 0:4], in1=pcol("dtb", 0, 16).rearrange("p (b h) -> p b h", h=4),
                                            op=ALU.add), [abtok, par], [T["a"]])
            P.act(lambda e: e.activation(out=T["t1"][:], in_=T["a"][:], func=AF.Exp), [T["a"]], [T["t1"]])
            P.act(lambda e: e.activation(out=T["t2"][:], in_=T["t1"][:], func=AF.Ln, bias=1.0), [T["t1"]], [T["t2"]])
            P.dve(lambda e: e.scalar_tensor_tensor(out=T["g"][:], in0=T["t2"][:], scalar=-1.0, in1=expA[:].rearrange("p (b h) -> p b h", h=4),
                                                   op0=ALU.mult, op1=ALU.mult), [T["t2"], expA], [T["g"]])
            P.act(lambda e: e.activation(out=T["t1"][:], in_=abtok[:, :, 4:8], func=AF.Exp, scale=-1.0), [abtok], [T["t1"]])
            P.act(lambda e: e.activation(out=T["t2"][:], in_=T["t1"][:], func=AF.Ln, bias=1.0), [T["t1"]], [T["t2"]])
            P.dve(lambda e: e.tensor_scalar(out=T["lb"][:], in0=T["t2"][:], scalar1=-1.0, scalar2=None, op0=ALU.mult), [T["t2"]], [T["lb"]])
            P.act(lambda e: e.activation(out=T["t1"][:], in_=ssq[:, :, 4:8], func=AF.Ln, bias=epsc[:]), [ssq, epsc], [T["t1"]])
            P.dve(lambda e: e.tensor_scalar(out=T["lrk"][:], in0=T["t1"][:], scalar1=-0.5, scalar2=None, op0=ALU.mult), [T["t1"]], [T["lrk"]])
            if full:
                P.act(lambda e: e.activation(out=T["t1"][:], in_=ssq[:, :, 0:4], func=AF.Ln, bias=epsc[:]), [ssq, epsc], [T["t1"]])
                P.dve(lambda e: e.tensor_scalar(out=T["lrq"][:], in0=T["t1"][:], scalar1=-0.5, scalar2=None, op0=ALU.mult),
                      [T["t1"]], [T["lrq"]])
            pb = P.bank()
            g2 = T["g"][:].rearrange("p b h -> p (b h)")
            P.pe(lambda e: e.matmul(pb[:, 0:16], lhsT=utri, rhs=g2, start=True, stop=True), [cst, T["g"]], [pb])
            P.pe(lambda e: e.matmul(pb[:, 16:32], lhsT=ones_f[:], rhs=g2, start=True, stop=True), [ones_f, T["g"]], [pb])
            P.dve(lambda e: e.tensor_copy(out=T["gc"][:].rearrange("p b h -> p (b h)"), in_=pb[:, 0:16]), [pb], [T["gc"]])
            P.dve(lambda e: e.tensor_copy(out=T["gl"][:].rearrange("p b h -> p (b h)"), in_=pb[:, 16:32]), [pb], [T["gl"]])
            P.free(pb)
            P.dve(lambda e: e.tensor_tensor(out=T["bj"][:], in0=T["lrk"][:], in1=T["gc"][:], op=ALU.subtract), [T["lrk"], T["gc"]], [T["bj"]])
            P.dve(lambda e: e.tensor_tensor(out=T["t1"][:], in0=T["gc"][:], in1=T["lb"][:], op=ALU.add), [T["gc"], T["lb"]], [T["t1"]])
            P.dve(lambda e: e.tensor_tensor(out=T["vL"][:], in0=T["t1"][:], in1=T["lrk"][:], op=ALU.add), [T["t1"], T["lrk"]], [T["vL"]])
            P.act(lambda e: e.activation(out=T["b"][:], in_=T["lb"][:], func=AF.Exp), [T["lb"]], [T["b"]])
            P.act(lambda e: e.activation(out=T["sw"][:], in_=T["vL"][:], func=AF.Exp), [T["vL"]], [T["sw"]])
            P.dve(lambda e: e.tensor_tensor(out=T["t2"][:], in0=T["gl"][:], in1=T["bj"][:], op=ALU.add), [T["gl"], T["bj"]], [T["t2"]])
            P.act(lambda e: e.activation(out=T["skd"][:], in_=T["t2"][:], func=AF.Exp), [T["t2"]], [T["skd"]])
            P.act(lambda e: e.activation(out=T["egl"][:], in_=T["gl"][:], func=AF.Exp), [T["gl"]], [T["egl"]])
            if full:
                P.dve(lambda e: e.scalar_tensor_tensor(out=T["vA"][:], in0=T["gc"][:], scalar=lnc[:], in1=T["lrq"][:],
                                                       op0=ALU.add, op1=ALU.add), [T["gc"], lnc, T["lrq"]], [T["vA"]])
                P.act(lambda e: e.activation(out=T["so"][:], in_=T["vA"][:], func=AF.Exp), [T["vA"]], [T["so"]])

        def gdn_pre(blk, full, qT, kT_, vT_):
            T = tk
            B = BS[blk % NSET]
            kdec, bek, bv, Mx, Px, Xx, attnT, usb, wT = (B[n] for n in ("kdec", "bek", "bv", "M", "P", "X", "attnT", "usb", "wT"))
            EL = ELs[blk % 2]
            EA = EAs[blk % 2]
            cs = slice(blk * 128, (blk + 1) * 128)
            pk = P.bank()
            for h in range(4):
                P.pe(lambda e, h=h: e.transpose(out=pk[:].bitcast(BF16)[:, h * 128:(h + 1) * 128], in_=kT_[h][:, cs], identity=ident_b[:]),
                     [kT_[h], ident_b], [pk])
            P.dve(lambda e: e.tensor_tensor(out=kdec[:], in0=vb3(pk), in1=bc4(T["skd"], blk), op=ALU.mult), [pk, T["skd"]], [kdec])
            P.dve(lambda e: e.tensor_tensor(out=bek[:], in0=vb3(pk), in1=bc4(T["sw"], blk), op=ALU.mult), [pk, T["sw"]], [bek])
            P.free(pk)
            yield
            pv = P.bank()
            for h in range(4):
                P.pe(lambda e, h=h: e.transpose(out=pv[:].bitcast(BF16)[:, h * 128:(h + 1) * 128], in_=vT_[h][:, cs], identity=ident_b[:]),
                     [vT_[h], ident_b], [pv])
            P.dve(lambda e: e.tensor_tensor(out=bv[:], in0=vb3(pv), in1=bc4(T["b"], blk), op=ALU.mult), [pv, T["b"]], [bv])
            P.free(pv)
            yield

            def emat(vec, neg, out_t):
                pe_ = P.bank()
                for h in range(4):
                    o = pe_[:, h * 128:(h + 1) * 128]
                    P.pe(lambda e, h=h, o=o: e.matmul(o, lhsT=vec[:, blk, h:h + 1].to_broadcast([128, 128]), rhs=ident_f, start=True, stop=False),
                         [vec, cst], [pe_])
                    P.pe(lambda e, h=h, o=o: e.matmul(o, lhsT=ident_f, rhs=T["bj"][:, blk, h:h + 1].to_broadcast([128, 128]), start=False, stop=False),
                         [T["bj"], cst], [pe_])
                    P.pe(lambda e, h=h, o=o: e.matmul(o, lhsT=ident_f, rhs=neg, start=False, stop=True), [cst], [pe_])
                P.act(lambda e: e.activation(out=out_t[:].rearrange("p h c -> p (h c)"), in_=pe_[:], func=AF.Exp), [pe_], [out_t])
                P.free(pe_)

            emat(T["vL"], negs, EL)
            pl = P.bank()
            for h in range(4):
                P.pe(lambda e, h=h: e.matmul(pl[:, h * 128:(h + 1) * 128], lhsT=kT_[h][:, cs], rhs=kT_[h][:, cs], start=True, stop=True),
                     [kT_[h]], [pl])
            P.dve(lambda e: e.tensor_tensor(out=Mx[:], in0=v3(pl), in1=EL[:], op=ALU.mult), [pl, EL], [Mx])
            P.free(pl)
            yield
            if full:
                emat(T["vA"], negi, EA)
                pa = P.bank()
                for h in range(4):
                    P.pe(lambda e, h=h: e.matmul(pa[:, h * 128:(h + 1) * 128], lhsT=kT_[h][:, cs], rhs=qT[h][:, cs], start=True, stop=True),
                         [kT_[h], qT[h]], [pa])
                P.dve(lambda e: e.tensor_tensor(out=attnT[:], in0=v3(pa), in1=EA[:], op=ALU.mult), [pa, EA], [attnT])
                P.free(pa)
                yield
            pt = P.bank()
            for h in range(4):
                P.pe(lambda e, h=h: e.transpose(out=pt[:].bitcast(BF16)[:, h * 128:(h + 1) * 128], in_=Mx[:, h, :], identity=ident_b[:]),
                     [Mx, ident_b], [pt])
            P.act(lambda e: e.activation(out=Px[:], in_=vb3(pt), func=AF.Copy), [pt], [Px])
            P.free(pt)
            P.dve(lambda e: e.scalar_tensor_tensor(out=Xx[:], in0=Mx[:], scalar=-1.0, in1=identB4[:], op0=ALU.mult, op1=ALU.add),
                  [Mx, identB4], [Xx])
            yield
            Mc, Mo = Mx, B["M2"]
            for k in range(6):
                if k < 5:
                    pm = P.bank()
                    for h in range(4):
                        P.pe(lambda e, h=h, pm=pm, Mc=Mc: e.matmul(pm[:, h * 128:(h + 1) * 128], lhsT=Px[:, h, :], rhs=Mc[:, h, :], start=True, stop=True),
                             [Mc, Px], [pm])
                    yield
                    P.dve(lambda e, pm=pm, Mo=Mo: e.tensor_copy(out=Mo[:].rearrange("p h c -> p (h c)"), in_=pm[:]), [pm], [Mo])
                    P.free(pm)
                pp = P.bank()
                for h in range(4):
                    P.pe(lambda e, h=h, pp=pp, Mc=Mc: e.matmul(pp[:, h * 128:(h + 1) * 128], lhsT=Mc[:, h, :], rhs=Px[:, h, :], start=True, stop=True),
                         [Mc, Px], [pp])
                yield
                P.act(lambda e, pp=pp: e.activation(out=Px[:].rearrange("p h c -> p (h c)"), in_=pp[:], func=AF.Copy), [pp], [Px])
                P.free(pp)
                px = P.bank()
                for h in range(4):
                    P.pe(lambda e, h=h, px=px: e.matmul(px[:, h * 128:(h + 1) * 128], lhsT=Px[:, h, :], rhs=Xx[:, h, :], start=True, stop=True),
                         [Px, Xx], [px])
                yield
                P.dve(lambda e, px=px: e.tensor_tensor(out=Xx[:], in0=v3(px), in1=Xx[:], op=ALU.add), [px, Xx], [Xx])
                P.free(px)
                Mc, Mo = Mo, Mc
            AT = Xx
            pu = P.bank()
            for h in range(4):
                P.pe(lambda e, h=h: e.matmul(pu[:, h * 128:(h + 1) * 128], lhsT=AT[:, h, :], rhs=bv[:, h, :], start=True, stop=True), [AT, bv], [pu])
            P.act(lambda e: e.activation(out=usb[:].rearrange("p h c -> p (h c)"), in_=pu[:], func=AF.Copy), [pu], [usb])
            P.free(pu)
            yield
            pw = P.bank()
            for h in range(4):
                P.pe(lambda e, h=h: e.matmul(pw[:, h * 128:(h + 1) * 128], lhsT=bek[:, h, :], rhs=AT[:, h, :], start=True, stop=True), [AT, bek], [pw])
            P.dve(lambda e: e.tensor_copy(out=wT[:].rearrange("p h c -> p (h c)"), in_=pw[:]), [pw], [wT])
            P.free(pw)
            yield

        def gdn_scan_all(full, qT, sg):
            T = tk
            po = {}

            def main1(blk):
                B = BS[blk % NSET]
                pws = P.bank()
                for h in range(4):
                    P.pe(lambda e, h=h: e.matmul(pws[:, h * 128:(h + 1) * 128], lhsT=B["wT"][:, h, :], rhs=Sb[:, h, :], start=True, stop=True),
                         [B["wT"], Sb], [pws])
                P.dve(lambda e: e.tensor_tensor(out=vnew[:], in0=B["usb"][:], in1=v3(pws), op=ALU.subtract), [B["usb"], pws], [vnew])
                P.free(pws)

            def main2(blk):
                B = BS[blk % NSET]
                cs = slice(blk * 128, (blk + 1) * 128)
                pds = P.bank()
                for h in range(4):
                    P.pe(lambda e, h=h: e.matmul(pds[:, h * 128:(h + 1) * 128], lhsT=B["kdec"][:, h, :], rhs=vnew[:, h, :], start=True, stop=True),
                         [B["kdec"], vnew], [pds])
                if full:
                    po1 = P.bank()
                    po2 = P.bank()
                    for h in range(4):
                        P.pe(lambda e, h=h: e.matmul(po1[:, h * 128:(h + 1) * 128], lhsT=qT[h][:, cs], rhs=Sb[:, h, :], start=True, stop=True),
                             [qT[h], Sb], [po1])
                    for h in range(4):
                        P.pe(lambda e, h=h: e.matmul(po2[:, h * 128:(h + 1) * 128], lhsT=B["attnT"][:, h, :], rhs=vnew[:, h, :], start=True, stop=True),
                             [B["attnT"], vnew], [po2])
                    po[blk] = (po1, po2)
                for h in range(4):
                    P.dve(lambda e, h=h: e.scalar_tensor_tensor(out=Sf[:, h, :], in0=Sf[:, h, :], scalar=T["egl"][:, blk, h:h + 1],
                                                                in1=pds[:, h * 128:(h + 1) * 128], op0=ALU.mult, op1=ALU.add),
                          [Sf, T["egl"], pds], [Sf])
                P.free(pds)
                P.act(lambda e: e.activation(out=Sb[:], in_=Sf[:], func=AF.Copy), [Sf], [Sb])

            def tail1(blk):
                po1, po2 = po[blk]
                P.dve(lambda e: e.tensor_tensor(out=o1s[:], in0=v3(po1), in1=bc4(T["so"], blk), op=ALU.mult), [po1, T["so"]], [o1s])
                P.dve(lambda e: e.tensor_tensor(out=osb[:], in0=v3(po2), in1=o1s[:], op=ALU.add), [po2, o1s], [osb])
                P.free(po1, po2)
                P.pool(lambda e: e.tensor_tensor(out=osq[:], in0=osb[:], in1=osb[:], op=ALU.mult), [osb], [osq])
                P.dve(lambda e: e.tensor_reduce(out=ssqo[:], in_=osq[:], axis=AX.X, op=ALU.add), [osq], [ssqo])
                P.act(lambda e: e.activation(out=rso[:], in_=ssqo[:], func=AF.Ln, bias=epsc[:], scale=1.0 / 128), [ssqo, epsc], [rso])
                P.act(lambda e: e.activation(out=rso[:], in_=rso[:], func=AF.Exp, scale=-0.5), [rso], [rso])
                P.dve(lambda e: e.tensor_tensor(out=onb[:], in0=osb[:], in1=rso[:].unsqueeze(2).to_broadcast([128, 4, 128]), op=ALU.mult),
                      [osb, rso], [onb])

            def tail2(blk):
                cs = slice(blk * 128, (blk + 1) * 128)
                pot = P.bank()
                for h in range(4):
                    P.pe(lambda e, h=h: e.transpose(out=pot[:].bitcast(BF16)[:, h * 128:(h + 1) * 128], in_=onb[:, h, :], identity=ident_b[:]),
                         [onb, ident_b], [pot])
                for h in range(4):
                    P.dve(lambda e, h=h: e.scalar_tensor_tensor(out=ocat[h][:, cs], in0=pot[:].bitcast(BF16)[:, h * 128:(h + 1) * 128],
                                                                scalar=pcol("gng"), in1=sg[h][:, cs], op0=ALU.mult, op1=ALU.mult),
                          [pot, par, sg[h]], [ocat[h]])
                P.free(pot)

            for blk in range(NB):
                main1(blk)
                yield
                if full and blk >= 1:
                    tail1(blk - 1)
                    yield
                main2(blk)
                yield
                if full and blk >= 1:
                    tail2(blk - 1)
                    yield
            if full:
                tail1(NB - 1)
                yield
                tail2(NB - 1)
                yield

        def interleave(gens):
            gens = list(gens)
            while gens:
                for g in list(gens):
                    try:
                        next(g)
                    except StopIteration:
                        gens.remove(g)

        for t_ in [hT, hnT, yT, rstd, lnt, rden] + sq + big + ocat + ubuf + sa:
            t_.split()

        def prenorm_h(gi, c0, c1, hb):
            st = P.bank()
            for c in range(KC):
                s_ = sq[c % 2]
                P.act(lambda e, c=c, s_=s_: e.activation(out=s_[:, c0:c1], in_=hT[:, c, c0:c1], func=AF.Square), [hT.hv(hb)], [s_.hv(hb)])
                P.pe(lambda e, c=c, s_=s_: e.matmul(st[:, c0:c1], lhsT=ones_b[:], rhs=s_[:, c0:c1], start=(c == 0), stop=(c == KC - 1)),
                     [ones_b, s_.hv(hb)], [st])
            P.act(lambda e: e.activation(out=lnt[:, c0:c1], in_=st[:, c0:c1], func=AF.Ln, bias=epsc[:], scale=1.0 / D), [st, epsc], [lnt.hv(hb)])
            P.act(lambda e: e.activation(out=rstd[:, c0:c1], in_=lnt[:, c0:c1], func=AF.Exp, scale=-0.5), [lnt.hv(hb)], [rstd.hv(hb)])
            P.free(st)
            for c in range(KC):
                P.dve(lambda e, c=c: e.scalar_tensor_tensor(out=hnT[:, c, c0:c1], in0=hT[:, c, c0:c1], scalar=pcol("ng", gi * 8 + c),
                                                            in1=rstd[:, c0:c1], op0=ALU.mult, op1=ALU.mult),
                      [hT.hv(hb), par, rstd.hv(hb)], [hnT.hv(hb)])

        def postnorm_h(gi, st, c0, c1, hb):
            P.act(lambda e: e.activation(out=lnt[:, c0:c1], in_=st[:, c0:c1], func=AF.Ln, bias=epsc[:], scale=1.0 / D), [st, epsc], [lnt.hv(hb)])
            P.act(lambda e: e.activation(out=rstd[:, c0:c1], in_=lnt[:, c0:c1], func=AF.Exp, scale=-0.5), [lnt.hv(hb)], [rstd.hv(hb)])
            P.free(st)
            for c in range(KC):
                P.dve(lambda e, c=c: e.scalar_tensor_tensor(out=yT[:, c, c0:c1], in0=yT[:, c, c0:c1], scalar=pcol("ng", gi * 8 + c),
                                                            in1=rstd[:, c0:c1], op0=ALU.mult, op1=ALU.mult),
                      [yT.hv(hb), par, rstd.hv(hb)], [yT.hv(hb)])
                (P.pool if c % 2 == 0 else P.dve)(lambda e, c=c: e.tensor_tensor(out=hT[:, c, c0:c1], in0=hT[:, c, c0:c1], in1=yT[:, c, c0:c1], op=ALU.add),
                                                        [hT.hv(hb), yT.hv(hb)], [hT.hv(hb)])

        def proj_h(slab_ids, rhs_tiles, c0, c1, hb):
            st = P.bank()
            m = 0
            for s_ in slab_ids:
                sl = load_slab(s_)
                sa_ = slab_ap(sl, 512)
                for q in range(4):
                    pb = P.bank()
                    for k in range(KC):
                        P.pe(lambda e, k=k, q=q, pb=pb, sa_=sa_: e.matmul(pb[:, c0:c1], lhsT=sa_[:, k, q * 128:(q + 1) * 128],
                                                                          rhs=rhs_tiles[k][:, c0:c1], start=(k == 0), stop=(k == KC - 1)),
                             [sl, rhs_tiles[k].hv(hb)], [pb])
                    s2 = sq[m % 2]
                    P.dve(lambda e, pb=pb, m=m: e.tensor_copy(out=yT[:, m, c0:c1], in_=pb[:, c0:c1]), [pb], [yT.hv(hb)])
                    P.free(pb)
                    P.act(lambda e, m=m, s2=s2: e.activation(out=s2[:, c0:c1], in_=yT[:, m, c0:c1], func=AF.Square), [yT.hv(hb)], [s2.hv(hb)])
                    P.pe(lambda e, m=m, s2=s2: e.matmul(st[:, c0:c1], lhsT=ones_b[:], rhs=s2[:, c0:c1], start=(m == 0), stop=(m == 7)),
                         [ones_b, s2.hv(hb)], [st])
                    m += 1
            return st

        def xattn_h(c0, c1, hb):
            qx = big[0:8]
            ox = big[8:16]
            m = 0
            for s_ in range(2):
                sl = load_slab(SL_Q + s_)
                sa_ = slab_ap(sl, 512)
                for q in range(4):
                    pb = P.bank()
                    for k in range(KC):
                        P.pe(lambda e, k=k, q=q, pb=pb, sa_=sa_: e.matmul(pb[:, c0:c1], lhsT=sa_[:, k, q * 128:(q + 1) * 128], rhs=hnT[:, k, c0:c1],
                                                                          start=(k == 0), stop=(k == KC - 1)), [sl, hnT.hv(hb)], [pb])
                    P.act(lambda e, pb=pb, m=m: e.activation(out=qx[m][:, c0:c1], in_=pb[:, c0:c1], func=AF.Copy, scale=1.0 / 16.0), [pb], [qx[m].hv(hb)])
                    P.free(pb)
                    m += 1
            for h in range(4):
                pts = [ubuf[(2 * h) % 4], ubuf[(2 * h + 1) % 4]]
                for mc in range(2):
                    pb = P.bank()
                    for dc in range(2):
                        P.pe(lambda e, dc=dc, mc=mc, h=h, pb=pb: e.matmul(pb[:, c0:c1], lhsT=KT[:, 2 * h + dc, mc * 128:(mc + 1) * 128],
                                                                          rhs=qx[2 * h + dc][:, c0:c1], start=(dc == 0), stop=(dc == 1)),
                             [KT, qx[2 * h + dc].hv(hb)], [pb])
                    P.act(lambda e, pb=pb, mc=mc, pts=pts: e.activation(out=pts[mc][:, c0:c1], in_=pb[:, c0:c1], func=AF.Exp), [pb], [pts[mc].hv(hb)])
                    P.free(pb)
                pd = P.bank()
                for mc in range(2):
                    P.pe(lambda e, mc=mc, pts=pts, pd=pd: e.matmul(pd[:, c0:c1], lhsT=ones_b[:], rhs=pts[mc][:, c0:c1], start=(mc == 0), stop=(mc == 1)),
                         [ones_b, pts[mc].hv(hb)], [pd])
                P.dve(lambda e, pd=pd: e.reciprocal(out=rden[:, c0:c1], in_=pd[:, c0:c1]), [pd], [rden.hv(hb)])
                P.free(pd)
                for dvc in range(2):
                    pb = P.bank()
                    for mc in range(2):
                        P.pe(lambda e, mc=mc, dvc=dvc, h=h, pb=pb, pts=pts: e.matmul(pb[:, c0:c1], lhsT=Vm[:, mc, (2 * h + dvc) * 128:(2 * h + dvc + 1) * 128],
                                                                                    rhs=pts[mc][:, c0:c1], start=(mc == 0), stop=(mc == 1)),
                             [Vm, pts[mc].hv(hb)], [pb])
                    P.dve(lambda e, pb=pb, h=h, dvc=dvc: e.tensor_tensor(out=ox[2 * h + dvc][:, c0:c1], in0=pb[:, c0:c1], in1=rden[:, c0:c1], op=ALU.mult),
                          [pb, rden.hv(hb)], [ox[2 * h + dvc].hv(hb)])
                    P.free(pb)
            return proj_h([SL_O, SL_O + 1], ox, c0, c1, hb)

        def ffn_h(c0, c1, hb, only_up):
            curu = {}
            n = c1 - c0

            def ffn_up_mm(i):
                s_, q = i // 2, i % 2
                if q == 0:
                    curu["sl"] = load_slab(SL_UP + s_)
                sl = curu["sl"]
                sa_ = slab_ap(sl, 512)
                pa_ = P.bank()
                pb_ = P.bank()
                for k in range(KC):
                    P.pe(lambda e, k=k: e.matmul(pa_[:, c0:c1], lhsT=sa_[:, k, q * 128:(q + 1) * 128], rhs=hnT[:, k, c0:c1],
                                                 start=(k == 0), stop=(k == KC - 1)), [sl, hnT.hv(hb)], [pa_])
                for k in range(KC):
                    P.pe(lambda e, k=k: e.matmul(pb_[:, c0:c1], lhsT=sa_[:, k, 256 + q * 128:256 + (q + 1) * 128], rhs=hnT[:, k, c0:c1],
                                                 start=(k == 0), stop=(k == KC - 1)), [sl, hnT.hv(hb)], [pb_])
                return pa_, pb_

            def ffn_up_ev(i, pa_, pb_):
                ua, ub = ubuf[2 * (i % 2)], ubuf[2 * (i % 2) + 1]
                P.pool(lambda e: e.tensor_copy(out=ua[:, c0:c0 + 2], in_=uhist[:, i, :]), [uhist], [ua.hv(hb)])
                P.pool(lambda e: e.tensor_copy(out=ub[:, c0:c0 + 2], in_=uhist[:, 22 + i, :]), [uhist], [ub.hv(hb)])
                P.act(lambda e: e.activation(out=ua[:, 2 + c0:2 + c1], in_=pa_[:, c0:c1], func=AF.Copy), [pa_], [ua.hv(hb)])
                P.dve(lambda e: e.tensor_copy(out=ub[:, 2 + c0:2 + c1], in_=pb_[:, c0:c1]), [pb_], [ub.hv(hb)])
                P.free(pa_, pb_)
                P.pool(lambda e: e.tensor_copy(out=uhist[:, i, :], in_=ua[:, c1:c1 + 2]), [ua.hv(hb)], [uhist])
                P.pool(lambda e: e.tensor_copy(out=uhist[:, 22 + i, :], in_=ub[:, c1:c1 + 2]), [ub.hv(hb)], [uhist])

            def ffn_conv(i):
                ua, ub = ubuf[2 * (i % 2)], ubuf[2 * (i % 2) + 1]
                ds = []
                for j in range(6):
                    d = dg[dg_ctr[0] % len(dg)]
                    dg_ctr[0] += 1
                    wb = (i * 3 + j) if j < 3 else ((22 + i) * 3 + j - 3)
                    P.dve(lambda e, d=d, wb=wb: e.tensor_scalar(out=d[:], in0=ident_b[:], scalar1=pcol("fcw", wb), scalar2=None, op0=ALU.mult),
                          [ident_b, par], [d])
                    ds.append(d)
                pca = P.bank()
                pcb = P.bank()
                for k in range(3):
                    P.pe(lambda e, k=k: e.matmul(pca[:, c0:c1], lhsT=ds[k][:], rhs=ua[:, c0 + k:c1 + k], start=(k == 0), stop=(k == 2)),
                         [ds[k], ua.hv(hb)], [pca])
                for k in range(3):
                    P.pe(lambda e, k=k: e.matmul(pcb[:, c0:c1], lhsT=ds[3 + k][:], rhs=ub[:, c0 + k:c1 + k], start=(k == 0), stop=(k == 2)),
                         [ds[3 + k], ub.hv(hb)], [pcb])
                return pca, pcb

            def ffn_act_ev(i, pca, pcb):
                sat = sa[i % 2]
                P.act(lambda e: e.activation(out=sat[:, c0:c1], in_=pca[:, c0:c1], func=AF.Silu, bias=pcol("fcb", i)), [pca, par], [sat.hv(hb)])
                P.dve(lambda e: e.scalar_tensor_tensor(out=big[i][:, c0:c1], in0=pcb[:, c0:c1], scalar=pcol("fcb", 22 + i), in1=sat[:, c0:c1],
                                                       op0=ALU.add, op1=ALU.mult), [pcb, par, sat.hv(hb)], [big[i].hv(hb)])
                P.free(pca, pcb)

            if only_up:
                for i in range(22):
                    ffn_up_ev(i, *ffn_up_mm(i))
                return None
            for i in range(23):
                if i < 22:
                    bk = ffn_up_mm(i)
                if i >= 1:
                    pc = ffn_conv(i - 1)
                if i < 22:
                    ffn_up_ev(i, *bk)
                if i >= 1:
                    ffn_act_ev(i - 1, *pc)
            st = P.bank()
            for ng in range(4):
                pbs = [P.bank(), P.bank()]
                for kh in range(2):
                    sl = load_slab(SL_DN + ng * 2 + kh)
                    sa_ = slab_ap(sl, 256, 11)
                    for m2 in range(2):
                        for k in range(11):
                            kk = kh * 11 + k
                            P.pe(lambda e, k=k, kk=kk, m2=m2, pbs=pbs, sa_=sa_: e.matmul(pbs[m2][:, c0:c1], lhsT=sa_[:, k, m2 * 128:(m2 + 1) * 128],
                                                                                        rhs=big[kk][:, c0:c1], start=(kk == 0), stop=(kk == 21)),
                                 [sl, big[kk].hv(hb)], [pbs[m2]])
                for m2 in range(2):
                    m = ng * 2 + m2
                    s2 = sq[m % 2]
                    P.dve(lambda e, m=m, pb=pbs[m2]: e.tensor_copy(out=yT[:, m, c0:c1], in_=pb[:, c0:c1]), [pbs[m2]], [yT.hv(hb)])
                    P.act(lambda e, s2=s2, m=m: e.activation(out=s2[:, c0:c1], in_=yT[:, m, c0:c1], func=AF.Square), [yT.hv(hb)], [s2.hv(hb)])
                    P.pe(lambda e, m=m, s2=s2: e.matmul(st[:, c0:c1], lhsT=ones_b[:], rhs=s2[:, c0:c1], start=(m == 0), stop=(m == 7)),
                         [ones_b, s2.hv(hb)], [st])
                P.free(*pbs)
            return st

        def store_h(c0, c1, hb, out_idx):
            for blk in range(c0 // 128, c1 // 128):
                os_ = ost[blk % 2]
                for half in range(2):
                    pb = P.bank()
                    for q in range(4):
                        c = half * 4 + q
                        P.pe(lambda e, c=c, q=q, blk=blk, pb=pb: e.transpose(out=pb[:, q * 128:(q + 1) * 128], in_=hT[:, c, blk * 128:(blk + 1) * 128],
                                                                             identity=ident_f), [hT.hv(hb), cst], [pb])
                    copy_ev(os_[:, half * 512:(half + 1) * 512], pb[:], [pb], [os_])
                    P.free(pb)
                r0 = out_idx * NT + blk * 128
                P.dma("sp", lambda e, os_=os_, r0=r0: e.dma_start(out=yout[r0:r0 + 128, :], in_=os_[:]), [os_], [], "xst%d" % (blk % 2))

        def back_schedule(halves, out_idx):
            halo = out_idx is None
            if len(halves) == 1:
                (c0, c1, hb) = halves[0]
                st = proj_h([SL_OUT, SL_OUT + 1], ocat, c0, c1, hb)
                postnorm_h(1, st, c0, c1, hb)
                prenorm_h(2, c0, c1, hb)
                st = xattn_h(c0, c1, hb)
                postnorm_h(3, st, c0, c1, hb)
                prenorm_h(4, c0, c1, hb)
                st = ffn_h(c0, c1, hb, halo)
                if not halo:
                    postnorm_h(5, st, c0, c1, hb)
                    store_h(c0, c1, hb, out_idx)
            else:
                A, B = halves
                stA = proj_h([SL_OUT, SL_OUT + 1], ocat, *A)
                postnorm_h(1, stA, *A)
                stB = proj_h([SL_OUT, SL_OUT + 1], ocat, *B)
                prenorm_h(2, *A)
                postnorm_h(1, stB, *B)
                ckpt(7)
                stA = xattn_h(*A)
                prenorm_h(2, *B)
                postnorm_h(3, stA, *A)
                stB = xattn_h(*B)
                prenorm_h(4, *A)
                postnorm_h(3, stB, *B)
                ckpt(8)
                stA = ffn_h(A[0], A[1], A[2], False)
                prenorm_h(4, *B)
                postnorm_h(5, stA, *A)
                stB = ffn_h(B[0], B[1], B[2], False)
                store_h(A[0], A[1], A[2], out_idx)
                postnorm_h(5, stB, *B)
                ckpt(9)
                store_h(B[0], B[1], B[2], out_idx)
            if halo:
                P.pool(lambda e: e.tensor_scalar(out=uhist[:], in0=uhist[:], scalar1=pcol("flag"), scalar2=None, op0=ALU.mult), [uhist, par], [uhist])

        def tile(ti, full, out_idx):
            tok0 = ti * NT
            for blk in range(NB):
                xs = xst[blk % 2]
                P.dma("sp", lambda e, blk=blk, xs=xs: e.dma_start(out=xs[:], in_=xin[tok0 + blk * 128: tok0 + (blk + 1) * 128, :]),
                      [], [xs], "xst%d" % (blk % 2))
                for half in range(2):
                    pb = P.bank()
                    for q in range(4):
                        c = half * 4 + q
                        P.pe(lambda e, c=c, q=q, xs=xs, pb=pb: e.transpose(out=pb[:, q * 128:(q + 1) * 128], in_=xs[:, c * 128:(c + 1) * 128],
                                                                           identity=ident_f), [xs, cst], [pb])
                    copy_ev(hT[:, half * 4:half * 4 + 4, blk * 128:(blk + 1) * 128], pb[:].rearrange("p (q c) -> p q c", q=4), [pb], [hT])
                    P.free(pb)
            ckpt(1)
            prenorm(0)
            ckpt(2)
            qs, ks, vs, sgs = big[0:4], big[4:8], big[8:12], big[12:16]
            chunks = []
            for s_, dst in ([(0, qs)] if full else []) + [(1, ks), (2, vs)]:
                for q in range(4):
                    chunks.append((s_, q, dst))
            cur = {}

            def qkv_proj_mm(s_, q):
                if q == 0:
                    cur["sl"] = load_slab(SL_IN + s_)
                sl = cur["sl"]
                sa_ = slab_ap(sl, 512)
                pb = P.bank()
                for k in range(KC):
                    P.pe(lambda e, k=k: e.matmul(pb[:], lhsT=sa_[:, k, q * 128:(q + 1) * 128], rhs=hnT[:, k, :],
                                                 start=(k == 0), stop=(k == KC - 1)), [sl, hnT], [pb])
                return pb

            def qkv_proj_ev(s_, q, pb):
                ch = s_ * 4 + q
                z = zc[ch % 4]
                P.pool(lambda e: e.tensor_copy(out=z[:, 0:3], in_=zhist[:, ch, :]), [zhist], [z])
                copy_ev(z[:, 3:3 + NT], pb[:], [pb], [z])
                P.free(pb)
                P.pool(lambda e: e.tensor_copy(out=zhist[:, ch, :], in_=z[:, NT:NT + 3]), [z], [zhist])

            def qkv_conv_mm(s_, q, dst):
                ch = s_ * 4 + q
                z = zc[ch % 4]
                pc = P.bank()
                conv_mm(pc, z, z, 4, "gcw", ch * 4)
                return pc

            def qkv_conv_ev(q, dst, pc):
                P.act(lambda e: e.activation(out=dst[q][:], in_=pc[:], func=AF.Silu), [pc], [dst[q]])
                P.free(pc)

            for i in range(len(chunks) + 1):
                if i < len(chunks):
                    pbq = qkv_proj_mm(chunks[i][0], chunks[i][1])
                if i >= 1:
                    pcq = qkv_conv_mm(*chunks[i - 1])
                if i < len(chunks):
                    qkv_proj_ev(chunks[i][0], chunks[i][1], pbq)
                if i >= 1:
                    qkv_conv_ev(chunks[i - 1][1], chunks[i - 1][2], pcq)
            pab = P.bank()
            for blk in range(NB):
                for k in range(KC):
                    P.pe(lambda e, k=k, blk=blk: e.matmul(pab[:, blk * 8:(blk + 1) * 8], lhsT=hnT[:, k, blk * 128:(blk + 1) * 128], rhs=wab[:, k, :],
                                                          start=(k == 0), stop=(k == KC - 1)), [hnT, wab], [pab])
            P.dve(lambda e: e.tensor_copy(out=abtok[:].rearrange("p b c -> p (b c)"), in_=pab[:, 0:NB * 8]), [pab], [abtok])
            P.free(pab)
            pss = P.bank()
            lst = ([(qs[h], h) for h in range(4)] if full else []) + [(ks[h], 4 + h) for h in range(4)]
            for i, (src, col) in enumerate(lst):
                s2 = sq[i % 2]
                P.pool(lambda e, src=src, s2=s2: e.tensor_tensor(out=s2[:], in0=src[:], in1=src[:], op=ALU.mult), [src], [s2])
                for blk in range(NB):
                    P.pe(lambda e, blk=blk, s2=s2, col=col: e.matmul(pss[:, blk * 8 + col:blk * 8 + col + 1], lhsT=s2[:, blk * 128:(blk + 1) * 128],
                                                                      rhs=ones_b[:, 0:1], start=True, stop=True), [s2, ones_b], [pss])
            if full:
                P.dve(lambda e: e.tensor_copy(out=ssq[:].rearrange("p b c -> p (b c)"), in_=pss[:, 0:NB * 8]), [pss], [ssq])
            else:
                P.dve(lambda e: e.tensor_copy(out=ssq[:, :, 4:8], in_=pss[:, 0:NB * 8].rearrange("p (b c) -> p b c", c=8)[:, :, 4:8]), [pss], [ssq])
            P.free(pss)
            ckpt(3)
            gdn_scalars(full)
            ckpt(4)

            def filler():
                sl = load_slab(SL_IN + 3)
                sa_ = slab_ap(sl, 512)
                for q in range(4):
                    pb = P.bank()
                    for k in range(KC):
                        P.pe(lambda e, k=k, q=q, pb=pb, sa_=sa_: e.matmul(pb[:], lhsT=sa_[:, k, q * 128:(q + 1) * 128], rhs=hnT[:, k, :],
                                                                          start=(k == 0), stop=(k == KC - 1)), [sl, hnT], [pb])
                        if k % 4 == 3:
                            yield
                    P.act(lambda e, pb=pb, q=q: e.activation(out=sgs[q][:], in_=pb[:], func=AF.Silu), [pb], [sgs[q]])
                    P.free(pb)
                for s_ in range(2):
                    sl = load_slab(SL_IN + 4 + s_)
                    sa_ = slab_ap(sl, 512)
                    for q in range(2):
                        ch = 2 * s_ + q
                        pa_ = P.bank()
                        pb_ = P.bank()
                        for k in range(KC):
                            P.pe(lambda e, k=k, q=q, pa_=pa_, sa_=sa_: e.matmul(pa_[:], lhsT=sa_[:, k, q * 128:(q + 1) * 128], rhs=hnT[:, k, :],
                                                                                start=(k == 0), stop=(k == KC - 1)), [sl, hnT], [pa_])
                            if k % 4 == 3:
                                yield
                        for k in range(KC):
                            P.pe(lambda e, k=k, q=q, pb_=pb_, sa_=sa_: e.matmul(pb_[:], lhsT=sa_[:, k, 256 + q * 128:256 + (q + 1) * 128], rhs=hnT[:, k, :],
                                                                                start=(k == 0), stop=(k == KC - 1)), [sl, hnT], [pb_])
                            if k % 4 == 3:
                                yield
                        cb = cbuf[ch]
                        P.pool(lambda e, cb=cb: e.tensor_copy(out=cb[:, 0:30], in_=cb[:, NT:NT + 30]), [cb], [cb])
                        P.act(lambda e, pb_=pb_: e.activation(out=lnt[:], in_=pb_[:], func=AF.Sigmoid), [pb_], [lnt])
                        P.dve(lambda e, pa_=pa_, cb=cb: e.tensor_tensor(out=cb[:, 30:30 + NT], in0=pa_[:], in1=lnt[:], op=ALU.mult), [pa_, lnt], [cb])
                        P.free(pa_, pb_)
                pmean = P.bank()
                pe2 = P.bank()
                for ch in range(4):
                    pc = P.bank()
                    for k in range(31):
                        d = dg[dg_ctr[0] % len(dg)]
                        dg_ctr[0] += 1
                        P.dve(lambda e, d=d, k=k, ch=ch: e.tensor_scalar(out=d[:], in0=ident_b[:], scalar1=pcol("cdw", ch * 31 + k), scalar2=None,
                                                                         op0=ALU.mult), [ident_b, par], [d])
                        P.pe(lambda e, d=d, k=k, ch=ch, pc=pc: e.matmul(pc[:], lhsT=d[:], rhs=cbuf[ch][:, k:k + NT], start=(k == 0), stop=(k == 30)),
                             [d, cbuf[ch]], [pc])
                        if k % 4 == 3:
                            yield
                    P.act(lambda e, pc=pc, ch=ch: e.activation(out=yT[:, ch, :], in_=pc[:], func=AF.Identity, bias=pcol("cdb", ch)), [pc, par], [yT])
                    s2 = sq[ch % 2]
                    P.act(lambda e, pc=pc, ch=ch, s2=s2: e.activation(out=s2[:], in_=pc[:], func=AF.Square, bias=pcol("cdb", ch)), [pc, par], [s2])
                    P.free(pc)
                    P.pe(lambda e, ch=ch: e.matmul(pmean[:], lhsT=ones_f[:], rhs=yT[:, ch, :], start=(ch == 0), stop=(ch == 3)), [ones_f, yT], [pmean])
                    P.pe(lambda e, ch=ch, s2=s2: e.matmul(pe2[:], lhsT=ones_b[:], rhs=s2[:], start=(ch == 0), stop=(ch == 3)), [ones_b, s2], [pe2])
                    yield
                P.act(lambda e: e.activation(out=msb[:], in_=pmean[:], func=AF.Copy, scale=1.0 / 512), [pmean], [msb])
                P.pool(lambda e: e.tensor_tensor(out=var[:], in0=msb[:], in1=msb[:], op=ALU.mult), [msb], [var])
                P.dve(lambda e: e.scalar_tensor_tensor(out=var[:], in0=pe2[:], scalar=1.0 / 512, in1=var[:], op0=ALU.mult, op1=ALU.subtract),
                      [pe2, var], [var])
                P.free(pmean, pe2)
                P.act(lambda e: e.activation(out=lnt[:], in_=var[:], func=AF.Ln, bias=epsc[:]), [var, epsc], [lnt])
                P.act(lambda e: e.activation(out=rstd[:], in_=lnt[:], func=AF.Exp, scale=-0.5), [lnt], [rstd])
                yield
                for ch in range(4):
                    P.pool(lambda e, ch=ch: e.tensor_tensor(out=yT[:, ch, :], in0=yT[:, ch, :], in1=msb[:], op=ALU.subtract), [yT, msb], [yT])
                    P.dve(lambda e, ch=ch: e.tensor_tensor(out=yT[:, ch, :], in0=yT[:, ch, :], in1=rstd[:], op=ALU.mult), [yT, rstd], [yT])
                    P.act(lambda e, ch=ch: e.activation(out=ocat[4 + ch][:], in_=yT[:, ch, :], func=AF.Silu, bias=pcol("clb", ch), scale=pcol("clg", ch)),
                          [yT, par], [ocat[4 + ch]])
                    yield

            fil = filler() if full else None

            def run_with_filler(gens, nfill, every):
                nonlocal fil
                gens = list(gens)
                rnd = 0
                while gens:
                    for g in list(gens):
                        try:
                            next(g)
                        except StopIteration:
                            gens.remove(g)
                    rnd += 1
                    if rnd % every == 0:
                        for _ in range(nfill):
                            if fil is not None:
                                try:
                                    next(fil)
                                except StopIteration:
                                    fil = None

            run_with_filler([gdn_pre(blk, full, qs, ks, vs) for blk in range(NB)], 1, 2)
            ckpt(5)
            run_with_filler([gdn_scan_all(full, qs, sgs)], 3, 1)
            if fil is not None:
                for _ in fil:
                    pass
            if not full:
                return
            ckpt(6)
            if out_idx is None:
                halves = [(NT - 128, NT, None)]
            else:
                halves = [(0, NT, None)]
            back_schedule(halves, out_idx)

        try:
            ti = 0
            per = (len(late_list) + max(NS, 1) - 1) // max(NS, 1)
            for _ in range(NS):
                tile(ti, False, None)
                issue_late(per)
                ti += 1
            issue_late(len(late_list))
            tile(ti, True, None)
            ti += 1
            for i in range(NF):
                tile(ti, True, i)
                ti += 1
            P.final = ["xst0", "xst1"]
        except StopBuild:
            P.final = [k for k in P.dma_cnt.keys()]
        P.emit()
    return nc


def make_consts():
    c = np.zeros((128, 4, 128), np.float32)
    c[:, 0, :] = np.eye(128, dtype=np.float32)
    j = np.arange(128)[:, None]
    cc = np.arange(128)[None, :]
    c[:, 1, :] = (j <= cc).astype(np.float32)
    c[:, 2, :] = np.where(cc >= j, 0.0, NEGV)
    c[:, 3, :] = np.where(cc > j, 0.0, NEGV)
    return c.reshape(128, 512)


def make_params(inp, flag):
    p = np.zeros((128, NPAR), np.float32)

    def put(name, arr):
        p[:, PO[name]:PO[name] + arr.shape[1]] = arr

    def chunked(v):
        return np.ascontiguousarray(v.reshape(-1, 128).T)

    ng = inp["norm_g"][0]
    put("ng", np.concatenate([chunked(ng[i]) for i in range(6)], axis=1))
    gcw = inp["gdn_conv_w"][0]
    put("gcw", np.ascontiguousarray(gcw.reshape(4, 12, 128).transpose(2, 1, 0)).reshape(128, 48))
    cdw = inp["cfm_dw_w"][0]
    put("cdw", np.ascontiguousarray(cdw.reshape(31, 4, 128).transpose(2, 1, 0)).reshape(128, 124))
    put("cdb", chunked(inp["cfm_dw_b"][0]))
    put("clg", chunked(inp["cfm_ln_g"][0]))
    put("clb", chunked(inp["cfm_ln_b"][0]))
    fcw = inp["ffn_conv_w"][0]
    put("fcw", np.ascontiguousarray(fcw.reshape(3, 44, 128).transpose(2, 1, 0)).reshape(128, 132))
    put("fcb", chunked(inp["ffn_conv_b"][0]))
    put("gng", inp["gdn_norm_g"][0].reshape(128, 1))
    put("alog", np.tile(inp["gdn_a_log"][0][None, :], (128, 4)))
    put("dtb", np.tile(inp["gdn_dt_bias"][0][None, :], (128, 4)))
    put("mng", chunked(inp["mem_norm_g"][0]))
    p[:, PO["flag"]] = flag
    return p


_NC_CACHE = {}


def run(inputs, NF, NS, n_batch, dbg=False):
    inp = {k: np.asarray(v, dtype=np.float32) for k, v in inputs.items()}
    key = (NF, NS, dbg)
    if key not in _NC_CACHE:
        _NC_CACHE[key] = build(NF, NS, dbg)
    nc = _NC_CACHE[key]
    half = NF * NT
    nprev = (NS + 1) * NT
    consts = make_consts()
    in_maps = []
    for b in range(n_batch):
        for j in range(2):
            xall = np.zeros((nprev + half, D), np.float32)
            if j == 1:
                xall[:nprev] = inp["x"][b, 0:half]
            xall[nprev:] = inp["x"][b, j * half:(j + 1) * half]
            in_maps.append({
                "xin": xall, "memin": np.ascontiguousarray(inp["mem"][b]),
                "params": make_params(inp, float(j)), "consts": consts,
                "w_in": np.ascontiguousarray(inp["w_in"][0]), "w_out": np.ascontiguousarray(inp["w_out"][0]),
                "w_q": np.ascontiguousarray(inp["xa_w_q"][0]), "w_kv": np.ascontiguousarray(inp["xa_w_kv"][0]),
                "w_o": np.ascontiguousarray(inp["xa_w_o"][0]), "w_up": np.ascontiguousarray(inp["ffn_w_up"][0]),
                "w_dn": np.ascontiguousarray(inp["ffn_w_down"][0]),
            })
    res = run_bass_kernel_spmd(nc, in_maps, core_ids=list(range(len(in_maps))))
    out = np.zeros((n_batch, 2 * half, D), np.float32)
    for b in range(n_batch):
        for j in range(2):
            out[b, j * half:(j + 1) * half] = res.results[2 * b + j]["yout"]
    return out, res


def kernel(**inputs):
    out, _ = run(inputs, NF=8, NS=7, n_batch=4)
    return out
```

```python
import numpy as np
from contextlib import ExitStack
import concourse.bass as bass
import concourse.mybir as mybir
from concourse.bass_utils import run_bass_kernel_spmd

F32 = mybir.dt.float32
BF16 = mybir.dt.bfloat16
AF = mybir.ActivationFunctionType
ALU = mybir.AluOpType
AX = mybir.AxisListType

D = 1024
NT = 512
NB = NT // 128
H = 4
DFF = 2816
MEM = 256
EPS = 1e-6
NEGV = -30000.0
KC = 8

PO = {}
_o = 0
for _n, _w in [("ng", 48), ("gcw", 48), ("cdw", 124), ("cdb", 4), ("clg", 4), ("clb", 4),
               ("fcw", 132), ("fcb", 44), ("gng", 1), ("alog", 16), ("dtb", 16), ("mng", 8),
               ("flag", 1)]:
    PO[_n] = _o
    _o += _w
NPAR = _o

SL_IN, SL_OUT, SL_Q, SL_KV, SL_O, SL_UP, SL_DN = 0, 6, 8, 10, 14, 16, 27
NSLAB = 35


import os
STOP = float(os.environ.get("KSTOP", "1000"))


class StopBuild(Exception):
    pass


def ckpt(n):
    if n >= STOP:
        raise StopBuild()


class Buf:
    __slots__ = ("w", "rs", "const")

    def __init__(self):
        self.w = None
        self.rs = []
        self.const = False


class Tile:
    def __init__(self, t, bufs=None):
        self.t = t
        self.bufs = bufs if bufs is not None else [Buf()]
        self.h = None

    def split(self):
        self.bufs = [Buf(), Buf()]
        self.h = [Tile(self.t, [self.bufs[0]]), Tile(self.t, [self.bufs[1]])]
        return self

    def hv(self, hb):
        return self if hb is None else self.hv(hb)

    def __getitem__(self, idx):
        return self.t[idx]


class Ins:
    __slots__ = ("eng", "fn", "idx", "dma_key", "dma_val", "deps", "needs_inc", "ordinal", "waits")

    def __init__(self, eng, fn, idx, dma_key):
        self.eng = eng
        self.fn = fn
        self.idx = idx
        self.dma_key = dma_key
        self.dma_val = None
        self.deps = []
        self.needs_inc = False
        self.ordinal = 0
        self.waits = []


class Prog:
    ENGS = ["pe", "act", "dve", "pool", "sp"]

    def __init__(self, nc, es):
        self.nc = nc
        self.es = es
        self.ins = {e: [] for e in self.ENGS}
        self.dma_cnt = {}
        self.dma_sem = {}
        self.barrier_keys = set()
        self.sem = {}
        self.nfresh = 0
        self.free_banks = []
        self.final = []

    def sb(self, name, shape, dt):
        return Tile(self.es.enter_context(self.nc.sbuf_tensor(name, list(shape), dt)))

    def mkbanks(self):
        for i in range(8):
            t = Tile(self.es.enter_context(self.nc.psum_tensor("bank%d" % i, [128, 512], F32)))
            self.free_banks.append(t)

    def bank(self):
        assert self.free_banks, "out of PSUM banks"
        return self.free_banks.pop(0)

    def free(self, *bs):
        for b in bs:
            self.free_banks.append(b)

    def _add(self, eng, fn, r, w, dma_key=None):
        ins = Ins(eng, fn, len(self.ins[eng]), dma_key)
        deps = {}
        rb = [b for t in r for b in t.bufs]
        wb = [b for t in w for b in t.bufs]
        for b in rb:
            if b.w is not None:
                deps[id(b.w)] = (b.w, True)
        for b in wb:
            if b.w is not None and id(b.w) not in deps:
                deps[id(b.w)] = (b.w, False)
            for x in b.rs:
                if id(x) not in deps:
                    deps[id(x)] = (x, False)
        for b in rb:
            if not b.const:
                b.rs.append(ins)
        for b in wb:
            b.w = ins
            b.rs = []
        ins.deps = [v for k, v in deps.items() if v[0] is not ins]
        self.ins[eng].append(ins)
        if dma_key is not None:
            if dma_key == "fresh":
                dma_key = "fresh%d" % self.nfresh
                self.nfresh += 1
                ins.dma_key = dma_key
            self.dma_cnt[dma_key] = self.dma_cnt.get(dma_key, 0) + 16
            ins.dma_val = self.dma_cnt[dma_key]
        return ins

    def pe(self, fn, r, w):
        return self._add("pe", fn, r, w)

    def act(self, fn, r, w):
        return self._add("act", fn, r, w)

    def dve(self, fn, r, w):
        return self._add("dve", fn, r, w)

    def pool(self, fn, r, w):
        return self._add("pool", fn, r, w)

    def dma(self, q, fn, r, w, key):
        return self._add(q, fn, r, w, dma_key=key)

    def resolve(self):
        for e in self.ENGS:
            for ins in self.ins[e]:
                for (y, raw) in ins.deps:
                    if y.dma_key is not None:
                        continue
                    if y.eng == e:
                        if e in ("act", "dve", "pool") and raw:
                            y.needs_inc = True
                        continue
                    y.needs_inc = True
        for e in self.ENGS:
            c = 0
            for ins in self.ins[e]:
                if ins.dma_key is None and ins.needs_inc:
                    c += 1
                    ins.ordinal = c
        for k in list(self.dma_cnt.keys()):
            self.dma_sem[k] = self.es.enter_context(self.nc.semaphore("d_" + k))
        for e in self.ENGS:
            self.sem[e] = self.es.enter_context(self.nc.semaphore("e_" + e))
        for e in self.ENGS:
            waited = {}
            for ins in self.ins[e]:
                ws = {}
                for (y, raw) in ins.deps:
                    if y.dma_key is not None:
                        k = ("d", y.dma_key)
                        v = self.dma_cnt[y.dma_key] if y.dma_key in self.barrier_keys else y.dma_val
                    else:
                        if y.eng == e and not (e in ("act", "dve", "pool") and raw):
                            continue
                        k = ("e", y.eng)
                        v = y.ordinal
                    if waited.get(k, 0) >= v:
                        continue
                    if ws.get(k, 0) < v:
                        ws[k] = v
                for k, v in ws.items():
                    waited[k] = v
                    sem = self.dma_sem[k[1]] if k[0] == "d" else self.sem[k[1]]
                    ins.waits.append((sem, v))

    def emit(self):
        self.resolve()
        nc = self.nc
        prog = self

        def run(e, eng):
            for ins in prog.ins[e]:
                for (sem, v) in ins.waits:
                    eng.wait_ge(sem, v)
                bi = ins.fn(eng)
                if ins.dma_key is not None:
                    bi.then_inc(prog.dma_sem[ins.dma_key], 16)
                elif ins.needs_inc:
                    bi.then_inc(prog.sem[e], 1)
            if e == "sp":
                for k in prog.final:
                    eng.wait_ge(prog.dma_sem[k], prog.dma_cnt[k])

        with nc.Block() as block:
            @block.tensor
            def _(eng):
                run("pe", eng)

            @block.scalar
            def _(eng):
                run("act", eng)

            @block.vector
            def _(eng):
                run("dve", eng)

            @block.gpsimd
            def _(eng):
                run("pool", eng)

            @block.sync
            def _(eng):
                run("sp", eng)


def build(NF, NS, dbg=False):
    NTOK = (NS + 1 + NF) * NT
    nc = bass.Bass("TRN2", target_bir_lowering=False)
    xin = nc.dram_tensor("xin", [NTOK, D], F32, kind="ExternalInput").ap()
    memin = nc.dram_tensor("memin", [MEM, D], F32, kind="ExternalInput").ap()
    params = nc.dram_tensor("params", [128, NPAR], F32, kind="ExternalInput").ap()
    consts = nc.dram_tensor("consts", [128, 4 * 128], F32, kind="ExternalInput").ap()
    w_in = nc.dram_tensor("w_in", [D, 3080], F32, kind="ExternalInput").ap()
    w_out = nc.dram_tensor("w_out", [D, D], F32, kind="ExternalInput").ap()
    w_q = nc.dram_tensor("w_q", [D, D], F32, kind="ExternalInput").ap()
    w_kv = nc.dram_tensor("w_kv", [D, 2 * D], F32, kind="ExternalInput").ap()
    w_o = nc.dram_tensor("w_o", [D, D], F32, kind="ExternalInput").ap()
    w_up = nc.dram_tensor("w_up", [D, 2 * DFF], F32, kind="ExternalInput").ap()
    w_dn = nc.dram_tensor("w_dn", [DFF, D], F32, kind="ExternalInput").ap()
    yout = nc.dram_tensor("yout", [NF * NT, D], F32, kind="ExternalOutput").ap()
    wscr = nc.dram_tensor("wscr", [NSLAB, 128, 4096], BF16, kind="Internal").ap()
    dbg_out = None
    if dbg:
        dbg_out = nc.dram_tensor("dbg", [128, 16, NT], F32, kind="ExternalOutput").ap()

    es = ExitStack()
    with es:
        P = Prog(nc, es)
        P.mkbanks()
        DR = Tile(None)

        cst = P.sb("cst", [128, 4, 128], F32)
        par = P.sb("par", [128, NPAR], F32)
        ident_b = P.sb("ident_b", [128, 128], BF16)
        identB4 = P.sb("identB4", [128, 4, 128], BF16)
        ones_b = P.sb("ones_b", [128, 128], BF16)
        ones_f = P.sb("ones_f", [128, 128], F32)
        epsc = P.sb("epsc", [128, 1], F32)
        lnc = P.sb("lnc", [128, 1], F32)
        expA = P.sb("expA", [128, 16], F32)
        wab = P.sb("wab", [128, KC, 8], BF16)
        NSLOT = 3
        slots = [P.sb("slot%d" % i, [128, 4096], BF16) for i in range(NSLOT)]
        KT = P.sb("KT", [128, 8, MEM], BF16)
        Vm = P.sb("Vm", [128, 2, D], BF16)
        hT = P.sb("hT", [128, KC, NT], F32)
        hnT = P.sb("hnT", [128, KC, NT], BF16)
        yT = P.sb("yT", [128, KC, NT], F32)
        xst = [P.sb("xst%d" % i, [128, D], F32) for i in range(2)]
        big = [P.sb("big%d" % i, [128, NT], BF16) for i in range(22)]
        zc = [P.sb("zc%d" % i, [128, 3 + NT], BF16) for i in range(4)]
        zhist = P.sb("zhist", [128, 12, 3], BF16)
        cbuf = [P.sb("cbuf%d" % i, [128, 30 + NT], BF16) for i in range(4)]
        ubuf = [P.sb("ubuf%d" % i, [128, 2 + NT], BF16) for i in range(4)]
        uhist = P.sb("uhist", [128, 44, 2], BF16)
        ocat = [P.sb("ocat%d" % i, [128, NT], BF16) for i in range(8)]
        sq = [P.sb("sq%d" % i, [128, NT], BF16) for i in range(2)]
        rstd = P.sb("rstd", [128, NT], F32)
        lnt = P.sb("lnt", [128, NT], F32)
        dg = [P.sb("dg%d" % i, [128, 128], BF16) for i in range(12)]
        rden = P.sb("rden", [128, NT], F32)
        msb = P.sb("msb", [128, NT], F32)
        var = rden
        sa = [P.sb("sa%d" % i, [128, NT], BF16) for i in range(2)]
        abtok = P.sb("abtok", [128, NB, 8], F32)
        ssq = P.sb("ssq", [128, NB, 8], F32)
        tk = {n: P.sb("tk_" + n, [128, NB, 4], F32) for n in
              ["a", "g", "lb", "lrq", "lrk", "gc", "gl", "vL", "vA", "bj", "b", "sw", "skd", "egl", "so", "t1", "t2"]}
        Sf = P.sb("Sf", [128, 4, 128], F32)
        Sb = P.sb("Sb", [128, 4, 128], BF16)
        NSET = 4
        BS = []
        for i in range(NSET):
            BS.append({n: P.sb("%s_%d" % (n, i), [128, 4, 128], dt) for n, dt in
                       [("kdec", BF16), ("bek", BF16), ("bv", BF16), ("M", BF16), ("M2", BF16), ("P", BF16), ("X", BF16),
                        ("attnT", BF16), ("usb", F32), ("wT", BF16)]})
        ELs = [P.sb("EL%d" % i, [128, 4, 128], BF16) for i in range(2)]
        EAs = [P.sb("EA%d" % i, [128, 4, 128], BF16) for i in range(2)]
        vnew = P.sb("vnew", [128, 4, 128], BF16)
        osb = P.sb("osb", [128, 4, 128], F32)
        scr4 = P.sb("scr4", [128, 4, 128], F32)
        o1s = scr4
        osq = scr4
        Stmp = P.sb("Stmp", [128, 4, 128], F32)
        ssqo = P.sb("ssqo", [128, 4], F32)
        rso = P.sb("rso", [128, 4], F32)
        onb = P.sb("onb", [128, 4, 128], BF16)
        ost = xst

        def pcol(name, i=0, n=1):
            o = PO[name] + i
            return par[:, o:o + n]

        ident_f = cst[:, 0, :]
        utri = cst[:, 1, :]
        negi = cst[:, 2, :]
        negs = cst[:, 3, :]

        P.dma("sp", lambda e: e.dma_start(out=cst[:], in_=consts.rearrange("p (a c) -> p a c", a=4)), [], [cst], "fresh")
        P.dma("sp", lambda e: e.dma_start(out=par[:], in_=params), [], [par], "fresh")
        P.dma("pool", lambda e: e.dma_start(out=wab[:], in_=w_in[:, 2048:2056].rearrange("(kc p) n -> p kc n", p=128)),
              [], [wab], "fresh")

        def conv_cols(W, s, nw, off, c0, n, r0=0, nk=KC):
            o = wscr[s, :, 0:nk * nw].rearrange("p (kc n) -> p kc n", n=nw)[:, :, off:off + n]
            i = W[r0:r0 + nk * 128, c0:c0 + n].rearrange("(kc p) n -> p kc n", p=128)
            early = s in (SL_IN + 1, SL_IN + 2, SL_KV, SL_KV + 1, SL_KV + 2, SL_KV + 3)
            conv_list.append((early, o, i))

        P.barrier_keys.add("wc1")
        P.barrier_keys.add("wc2")
        conv_list = []
        DR2 = Tile(None)
        for s in range(4):
            conv_cols(w_in, SL_IN + s, 512, 0, s * 512, 512)
        for s in range(2):
            for q in range(2):
                conv_cols(w_in, SL_IN + 4 + s, 512, q * 128, 2056 + (2 * s + q) * 128, 128)
                conv_cols(w_in, SL_IN + 4 + s, 512, 256 + q * 128, 2056 + 512 + (2 * s + q) * 128, 128)
        for s in range(4):
            conv_cols(w_kv, SL_KV + s, 512, 0, s * 512, 512)
        for s in range(2):
            conv_cols(w_out, SL_OUT + s, 512, 0, s * 512, 512)
            conv_cols(w_q, SL_Q + s, 512, 0, s * 512, 512)
            conv_cols(w_o, SL_O + s, 512, 0, s * 512, 512)
        for s in range(11):
            for q in range(2):
                conv_cols(w_up, SL_UP + s, 512, q * 128, (2 * s + q) * 128, 128)
                conv_cols(w_up, SL_UP + s, 512, 256 + q * 128, DFF + (2 * s + q) * 128, 128)
        for ng in range(4):
            for kh in range(2):
                conv_cols(w_dn, SL_DN + ng * 2 + kh, 256, 0, ng * 256, 256, r0=kh * 1408, nk=11)

        late_list = [c for c in conv_list if not c[0]]

        def issue_late(n):
            for _ in range(min(n, len(late_list))):
                (early, o, i) = late_list.pop(0)
                DR2.bufs[0].w = P.dma("pool", lambda e, o=o, i=i: e.dma_start(out=o, in_=i), [], [], "wc2")

        P.pool(lambda e: e.memset(ones_b[:], 1.0), [], [ones_b])
        P.pool(lambda e: e.memset(ones_f[:], 1.0), [], [ones_f])
        P.pool(lambda e: e.memset(epsc[:], EPS), [], [epsc])
        P.pool(lambda e: e.memset(lnc[:], -0.5 * float(np.log(128.0))), [], [lnc])
        P.pool(lambda e: e.memset(Sf[:], 0.0), [], [Sf])
        P.pool(lambda e: e.memset(Sb[:], 0.0), [], [Sb])
        P.pool(lambda e: e.memset(uhist[:], 0.0), [], [uhist])
        P.pool(lambda e: e.memset(zhist[:], 0.0), [], [zhist])
        for t in zc + cbuf + ubuf:
            P.pool(lambda e, t=t: e.memset(t[:], 0.0), [], [t])
        P.dve(lambda e: e.tensor_copy(out=ident_b[:], in_=ident_f), [cst], [ident_b])
        for h in range(4):
            P.dve(lambda e, h=h: e.tensor_copy(out=identB4[:, h, :], in_=ident_f), [cst], [identB4])
        P.act(lambda e: e.activation(out=expA[:], in_=pcol("alog", 0, 16), func=AF.Exp), [par], [expA])
        for t in (cst, par, ident_b, identB4, ones_b, ones_f, epsc, lnc, expA, wab):
            pass

        for (early, o, i) in [c for c in conv_list if c[0]]:
            DR.bufs[0].w = P.dma("pool", lambda e, o=o, i=i: e.dma_start(out=o, in_=i), [], [], "wc1")

        slot_ctr = [0]

        def load_slab(s):
            sl = slots[slot_ctr[0] % NSLOT]
            key = "slot%d" % (slot_ctr[0] % NSLOT)
            slot_ctr[0] += 1
            drt = DR if s in (SL_IN + 1, SL_IN + 2, SL_KV, SL_KV + 1, SL_KV + 2, SL_KV + 3) else DR2
            P.dma("sp", lambda e: e.dma_start(out=sl[:], in_=wscr[s]), [drt], [sl], key)
            return sl

        def slab_ap(sl, nw, nk=KC):
            return sl[:, 0:nk * nw].rearrange("p (kc n) -> p kc n", n=nw)

        ev_ctr = [0]

        def copy_ev(out_ap, in_ap, r, w, scale=None):
            ev_ctr[0] += 1
            if ev_ctr[0] % 2 == 0 and scale is None:
                P.dve(lambda e: e.tensor_copy(out=out_ap, in_=in_ap), r, w)
            else:
                if scale is None:
                    P.act(lambda e: e.activation(out=out_ap, in_=in_ap, func=AF.Copy), r, w)
                else:
                    P.act(lambda e: e.activation(out=out_ap, in_=in_ap, func=AF.Copy, scale=scale), r, w)

        def rsqrt_from(ps_ap, r, scale, out_t):
            P.act(lambda e: e.activation(out=lnt[:], in_=ps_ap, func=AF.Ln, bias=epsc[:], scale=scale), r + [epsc], [lnt])
            P.act(lambda e: e.activation(out=out_t[:], in_=lnt[:], func=AF.Exp, scale=-0.5), [lnt], [out_t])

        def prenorm(gi):
            st = P.bank()
            for c in range(KC):
                s = sq[c % 2]
                P.act(lambda e, c=c, s=s: e.activation(out=s[:], in_=hT[:, c, :], func=AF.Square), [hT], [s])
                P.pe(lambda e, c=c, s=s: e.matmul(st[:], lhsT=ones_b[:], rhs=s[:], start=(c == 0), stop=(c == KC - 1)),
                     [ones_b, s], [st])
            rsqrt_from(st[:], [st], 1.0 / D, rstd)
            P.free(st)
            for c in range(KC):
                P.dve(lambda e, c=c: e.scalar_tensor_tensor(out=hnT[:, c, :], in0=hT[:, c, :], scalar=pcol("ng", gi * 8 + c),
                                                            in1=rstd[:], op0=ALU.mult, op1=ALU.mult),
                      [hT, par, rstd], [hnT])

        def postnorm_residual(gi, st):
            rsqrt_from(st[:], [st], 1.0 / D, rstd)
            P.free(st)
            for c in range(KC):
                P.dve(lambda e, c=c: e.scalar_tensor_tensor(out=yT[:, c, :], in0=yT[:, c, :], scalar=pcol("ng", gi * 8 + c),
                                                            in1=rstd[:], op0=ALU.mult, op1=ALU.mult),
                      [yT, par, rstd], [yT])
                (P.pool if c % 2 == 0 else P.dve)(lambda e, c=c: e.tensor_tensor(out=hT[:, c, :], in0=hT[:, c, :], in1=yT[:, c, :], op=ALU.add),
                                                        [hT, yT], [hT])

        def proj_to_yT(slab_ids, rhs_list, nk_per=KC):
            st = P.bank()
            m = 0
            for s in slab_ids:
                sl = load_slab(s)
                sa_ = slab_ap(sl, 512)
                for q in range(4):
                    pb = P.bank()
                    for k in range(KC):
                        P.pe(lambda e, k=k, q=q, pb=pb, sa_=sa_: e.matmul(pb[:], lhsT=sa_[:, k, q * 128:(q + 1) * 128],
                                                                          rhs=rhs_list[k][0], start=(k == 0), stop=(k == KC - 1)),
                             [sl, rhs_list[k][1]], [pb])
                    s2 = sq[m % 2]
                    KV = int(os.environ.get("KVAR", "0"))
                    P.dve(lambda e, pb=pb, m=m: e.tensor_copy(out=yT[:, m, :], in_=pb[:]), [pb], [yT])
                    P.free(pb)
                    P.act(lambda e, m=m, s2=s2: e.activation(out=s2[:], in_=yT[:, m, :], func=AF.Square), [yT], [s2])
                    if KV not in (1, 2):
                        P.pe(lambda e, m=m, s2=s2: e.matmul(st[:], lhsT=ones_b[:], rhs=s2[:], start=(m == 0), stop=(m == 7)),
                             [ones_b, s2], [st])
                    m += 1
            return st

        dg_ctr = [0]

        def conv_mm(pb, src, src_t, ntap, wname, wbase):
            for k in range(ntap):
                d = dg[dg_ctr[0] % len(dg)]
                dg_ctr[0] += 1
                P.dve(lambda e, d=d, k=k: e.tensor_scalar(out=d[:], in0=ident_b[:], scalar1=pcol(wname, wbase + k), scalar2=None,
                                                           op0=ALU.mult),
                       [ident_b, par], [d])
                P.pe(lambda e, d=d, k=k: e.matmul(pb[:], lhsT=d[:], rhs=src[:, k:k + NT], start=(k == 0), stop=(k == ntap - 1)),
                     [d, src_t], [pb])

        if STOP > 0:
            for blk in range(2):
                xs = xst[blk % 2]
                P.dma("sp", lambda e, blk=blk, xs=xs: e.dma_start(out=xs[:], in_=memin[blk * 128:(blk + 1) * 128, :]), [], [xs],
                      "xst%d" % (blk % 2))
                for half in range(2):
                    pb = P.bank()
                    for q in range(4):
                        c = half * 4 + q
                        P.pe(lambda e, c=c, q=q, xs=xs, pb=pb: e.transpose(out=pb[:, q * 128:(q + 1) * 128], in_=xs[:, c * 128:(c + 1) * 128],
                                                                           identity=ident_f), [xs, cst], [pb])
                    P.dve(lambda e, half=half, blk=blk, pb=pb: e.tensor_copy(
                        out=yT[:, half * 4:half * 4 + 4, blk * 128:(blk + 1) * 128],
                        in_=pb[:].rearrange("p (q c) -> p q c", q=4)), [pb], [yT])
                    P.free(pb)
            st = P.bank()
            for c in range(KC):
                s = sq[c % 2]
                P.act(lambda e, c=c, s=s: e.activation(out=s[:, 0:MEM], in_=yT[:, c, 0:MEM], func=AF.Square), [yT], [s])
                P.pe(lambda e, c=c, s=s: e.matmul(st[:, 0:MEM], lhsT=ones_b[:], rhs=s[:, 0:MEM], start=(c == 0), stop=(c == KC - 1)),
                     [ones_b, s], [st])
            P.act(lambda e: e.activation(out=lnt[:, 0:MEM], in_=st[:, 0:MEM], func=AF.Ln, bias=epsc[:], scale=1.0 / D), [st, epsc], [lnt])
            P.act(lambda e: e.activation(out=rstd[:, 0:MEM], in_=lnt[:, 0:MEM], func=AF.Exp, scale=-0.5), [lnt], [rstd])
            P.free(st)
            for c in range(KC):
                P.dve(lambda e, c=c: e.scalar_tensor_tensor(out=hnT[:, c, 0:MEM], in0=yT[:, c, 0:MEM], scalar=pcol("mng", c),
                                                            in1=rstd[:, 0:MEM], op0=ALU.mult, op1=ALU.mult),
                      [yT, par, rstd], [hnT])
            for s in range(2):
                sl = load_slab(SL_KV + s)
                sa_ = slab_ap(sl, 512)
                for q in range(4):
                    pb = P.bank()
                    for k in range(KC):
                        P.pe(lambda e, k=k, q=q, pb=pb, sa_=sa_: e.matmul(pb[:, 0:MEM], lhsT=sa_[:, k, q * 128:(q + 1) * 128], rhs=hnT[:, k, 0:MEM],
                                                                          start=(k == 0), stop=(k == KC - 1)), [sl, hnT], [pb])
                    P.dve(lambda e, pb=pb, s=s, q=q: e.tensor_copy(out=KT[:, s * 4 + q, :], in_=pb[:, 0:MEM]), [pb], [KT])
                    P.free(pb)
            for s in range(2):
                sl = load_slab(SL_KV + 2 + s)
                sa_ = slab_ap(sl, 512)
                for mc in range(2):
                    pb = P.bank()
                    for k in range(KC):
                        P.pe(lambda e, k=k, mc=mc, pb=pb, sa_=sa_: e.matmul(pb[:], lhsT=hnT[:, k, mc * 128:(mc + 1) * 128], rhs=sa_[:, k, :],
                                                                            start=(k == 0), stop=(k == KC - 1)), [sl, hnT], [pb])
                    P.dve(lambda e, pb=pb, s=s, mc=mc: e.tensor_copy(out=Vm[:, mc, s * 512:(s + 1) * 512], in_=pb[:]), [pb], [Vm])
                    P.free(pb)

        def bc4(t, blk):
            return t[:, blk, :].unsqueeze(2).to_broadcast([128, 4, 128])

        def v3(b):
            return b[:].rearrange("p (h c) -> p h c", h=4)

        def vb3(b):
            return b[:].bitcast(BF16)[:, 0:512].rearrange("p (h c) -> p h c", h=4)

        def gdn_scalars(full):
            T = tk
            P.dve(lambda e: e.tensor_tensor(out=T["a"][:], in0=abtok[:, :, 0:4], in1=pcol("dtb", 0, 16).rearrange("p (b h) -> p b h", h=4),
                                            op=ALU.add), [abtok, par], [T["a"]])
            P.act(lambda e: e.activation(out=T["t1"][:], in_=T["a"][:], func=AF.Exp), [T["a"]], [T["t1"]])
            P.act(lambda e: e.activation(out=T["t2"][:], in_=T["t1"][:], func=AF.Ln, bias=1.0), [T["t1"]], [T["t2"]])
            P.dve(lambda e: e.scalar_tensor_tensor(out=T["g"][:], in0=T["t2"][:], scalar=-1.0, in1=expA[:].rearrange("p (b h) -> p b h", h=4),
                                                   op0=ALU.mult, op1=ALU.mult), [T["t2"], expA], [T["g"]])
            P.act(lambda e: e.activation(out=T["t1"][:], in_=abtok[:, :, 4:8], func=AF.Exp, scale=-1.0), [abtok], [T["t1"]])
            P.act(lambda e: e.activation(out=T["t2"][:], in_=T["t1"][:], func=AF.Ln, bias=1.0), [T["t1"]], [T["t2"]])
            P.dve(lambda e: e.tensor_scalar(out=T["lb"][:], in0=T["t2"][:], scalar1=-1.0, scalar2=None, op0=ALU.mult), [T["t2"]], [T["lb"]])
            P.act(lambda e: e.activation(out=T["t1"][:], in_=ssq[:, :, 4:8], func=AF.Ln, bias=epsc[:]), [ssq, epsc], [T["t1"]])
            P.dve(lambda e: e.tensor_scalar(out=T["lrk"][:], in0=T["t1"][:], scalar1=-0.5, scalar2=None, op0=ALU.mult), [T["t1"]], [T["lrk"]])
            if full:
                P.act(lambda e: e.activation(out=T["t1"][:], in_=ssq[:, :, 0:4], func=AF.Ln, bias=epsc[:]), [ssq, epsc], [T["t1"]])
                P.dve(lambda e: e.tensor_scalar(out=T["lrq"][:], in0=T["t1"][:], scalar1=-0.5, scalar2=None, op0=ALU.mult),
                      [T["t1"]], [T["lrq"]])
            pb = P.bank()
            g2 = T["g"][:].rearrange("p b h -> p (b h)")
            P.pe(lambda e: e.matmul(pb[:, 0:16], lhsT=utri, rhs=g2, start=True, stop=True), [cst, T["g"]], [pb])
            P.pe(lambda e: e.matmul(pb[:, 16:32], lhsT=ones_f[:], rhs=g2, start=True, stop=True), [ones_f, T["g"]], [pb])
            P.dve(lambda e: e.tensor_copy(out=T["gc"][:].rearrange("p b h -> p (b h)"), in_=pb[:, 0:16]), [pb], [T["gc"]])
            P.dve(lambda e: e.tensor_copy(out=T["gl"][:].rearrange("p b h -> p (b h)"), in_=pb[:, 16:32]), [pb], [T["gl"]])
            P.free(pb)
            P.dve(lambda e: e.tensor_tensor(out=T["bj"][:], in0=T["lrk"][:], in1=T["gc"][:], op=ALU.subtract), [T["lrk"], T["gc"]], [T["bj"]])
            P.dve(lambda e: e.tensor_tensor(out=T["t1"][:], in0=T["gc"][:], in1=T["lb"][:], op=ALU.add), [T["gc"], T["lb"]], [T["t1"]])
            P.dve(lambda e: e.tensor_tensor(out=T["vL"][:], in0=T["t1"][:], in1=T["lrk"][:], op=ALU.add), [T["t1"], T["lrk"]], [T["vL"]])
            P.act(lambda e: e.activation(out=T["b"][:], in_=T["lb"][:], func=AF.Exp), [T["lb"]], [T["b"]])
            P.act(lambda e: e.activation(out=T["sw"][:], in_=T["vL"][:], func=AF.Exp), [T["vL"]], [T["sw"]])
            P.dve(lambda e: e.tensor_tensor(out=T["t2"][:], in0=T["gl"][:], in1=T["bj"][:], op=ALU.add), [T["gl"], T["bj"]], [T["t2"]])
            P.act(lambda e: e.activation(out=T["skd"][:], in_=T["t2"][:], func=AF.Exp), [T["t2"]], [T["skd"]])
            P.act(lambda e: e.activation(out=T["egl"][:], in_=T["gl"][:], func=AF.Exp), [T["gl"]], [T["egl"]])
            if full:
                P.dve(lambda e: e.scalar_tensor_tensor(out=T["vA"][:], in0=T["gc"][:], scalar=lnc[:], in1=T["lrq"][:],
                                                       op0=ALU.add, op1=ALU.add), [T["gc"], lnc, T["lrq"]], [T["vA"]])
                P.act(lambda e: e.activation(out=T["so"][:], in_=T["vA"][:], func=AF.Exp), [T["vA"]], [T["so"]])

        def gdn_pre(blk, full, qT, kT_, vT_):
            T = tk
            B = BS[blk % NSET]
            kdec, bek, bv, Mx, Px, Xx, attnT, usb, wT = (B[n] for n in ("kdec", "bek", "bv", "M", "P", "X", "attnT", "usb", "wT"))
            EL = ELs[blk % 2]
            EA = EAs[blk % 2]
            cs = slice(blk * 128, (blk + 1) * 128)
            pk = P.bank()
            for h in range(4):
                P.pe(lambda e, h=h: e.transpose(out=pk[:].bitcast(BF16)[:, h * 128:(h + 1) * 128], in_=kT_[h][:, cs], identity=ident_b[:]),
                     [kT_[h], ident_b], [pk])
            P.dve(lambda e: e.tensor_tensor(out=kdec[:], in0=vb3(pk), in1=bc4(T["skd"], blk), op=ALU.mult), [pk, T["skd"]], [kdec])
            P.dve(lambda e: e.tensor_tensor(out=bek[:], in0=vb3(pk), in1=bc4(T["sw"], blk), op=ALU.mult), [pk, T["sw"]], [bek])
            P.free(pk)
            yield
            pv = P.bank()
            for h in range(4):
                P.pe(lambda e, h=h: e.transpose(out=pv[:].bitcast(BF16)[:, h * 128:(h + 1) * 128], in_=vT_[h][:, cs], identity=ident_b[:]),
                     [vT_[h], ident_b], [pv])
            P.dve(lambda e: e.tensor_tensor(out=bv[:], in0=vb3(pv), in1=bc4(T["b"], blk), op=ALU.mult), [pv, T["b"]], [bv])
            P.free(pv)
            yield

            def emat(vec, neg, out_t):
                pe_ = P.bank()
                for h in range(4):
                    o = pe_[:, h * 128:(h + 1) * 128]
                    P.pe(lambda e, h=h, o=o: e.matmul(o, lhsT=vec[:, blk, h:h + 1].to_broadcast([128, 128]), rhs=ident_f, start=True, stop=False),
                         [vec, cst], [pe_])
                    P.pe(lambda e, h=h, o=o: e.matmul(o, lhsT=ident_f, rhs=T["bj"][:, blk, h:h + 1].to_broadcast([128, 128]), start=False, stop=False),
                         [T["bj"], cst], [pe_])
                    P.pe(lambda e, h=h, o=o: e.matmul(o, lhsT=ident_f, rhs=neg, start=False, stop=True), [cst], [pe_])
                P.act(lambda e: e.activation(out=out_t[:].rearrange("p h c -> p (h c)"), in_=pe_[:], func=AF.Exp), [pe_], [out_t])
                P.free(pe_)

            emat(T["vL"], negs, EL)
            pl = P.bank()
            for h in range(4):
                P.pe(lambda e, h=h: e.matmul(pl[:, h * 128:(h + 1) * 128], lhsT=kT_[h][:, cs], rhs=kT_[h][:, cs], start=True, stop=True),
                     [kT_[h]], [pl])
            P.dve(lambda e: e.tensor_tensor(out=Mx[:], in0=v3(pl), in1=EL[:], op=ALU.mult), [pl, EL], [Mx])
            P.free(pl)
            yield
            if full:
                emat(T["vA"], negi, EA)
                pa = P.bank()
                for h in range(4):
                    P.pe(lambda e, h=h: e.matmul(pa[:, h * 128:(h + 1) * 128], lhsT=kT_[h][:, cs], rhs=qT[h][:, cs], start=True, stop=True),
                         [kT_[h], qT[h]], [pa])
                P.dve(lambda e: e.tensor_tensor(out=attnT[:], in0=v3(pa), in1=EA[:], op=ALU.mult), [pa, EA], [attnT])
                P.free(pa)
                yield
            pt = P.bank()
            for h in range(4):
                P.pe(lambda e, h=h: e.transpose(out=pt[:].bitcast(BF16)[:, h * 128:(h + 1) * 128], in_=Mx[:, h, :], identity=ident_b[:]),
                     [Mx, ident_b], [pt])
            P.act(lambda e: e.activation(out=Px[:], in_=vb3(pt), func=AF.Copy), [pt], [Px])
            P.free(pt)
            P.dve(lambda e: e.scalar_tensor_tensor(out=Xx[:], in0=Mx[:], scalar=-1.0, in1=identB4[:], op0=ALU.mult, op1=ALU.add),
                  [Mx, identB4], [Xx])
            yield
            Mc, Mo = Mx, B["M2"]
            for k in range(6):
                if k < 5:
                    pm = P.bank()
                    for h in range(4):
                        P.pe(lambda e, h=h, pm=pm, Mc=Mc: e.matmul(pm[:, h * 128:(h + 1) * 128], lhsT=Px[:, h, :], rhs=Mc[:, h, :], start=True, stop=True),
                             [Mc, Px], [pm])
                    yield
                    P.dve(lambda e, pm=pm, Mo=Mo: e.tensor_copy(out=Mo[:].rearrange("p h c -> p (h c)"), in_=pm[:]), [pm], [Mo])
                    P.free(pm)
                pp = P.bank()
                for h in range(4):
                    P.pe(lambda e, h=h, pp=pp, Mc=Mc: e.matmul(pp[:, h * 128:(h + 1) * 128], lhsT=Mc[:, h, :], rhs=Px[:, h, :], start=True, stop=True),
                         [Mc, Px], [pp])
                yield
                P.act(lambda e, pp=pp: e.activation(out=Px[:].rearrange("p h c -> p (h c)"), in_=pp[:], func=AF.Copy), [pp], [Px])
                P.free(pp)
                px = P.bank()
                for h in range(4):
                    P.pe(lambda e, h=h, px=px: e.matmul(px[:, h * 128:(h + 1) * 128], lhsT=Px[:, h, :], rhs=Xx[:, h, :], start=True, stop=True),
                         [Px, Xx], [px])
                yield
                P.dve(lambda e, px=px: e.tensor_tensor(out=Xx[:], in0=v3(px), in1=Xx[:], op=ALU.add), [px, Xx], [Xx])
                P.free(px)
                Mc, Mo = Mo, Mc
            AT = Xx
            pu = P.bank()
            for h in range(4):
                P.pe(lambda e, h=h: e.matmul(pu[:, h * 128:(h + 1) * 128], lhsT=AT[:, h, :], rhs=bv[:, h, :], start=True, stop=True), [AT, bv], [pu])
            P.act(lambda e: e.activation(out=usb[:].rearrange("p h c -> p (h c)"), in_=pu[:], func=AF.Copy), [pu], [usb])
            P.free(pu)
            yield
            pw = P.bank()
            for h in range(4):
                P.pe(lambda e, h=h: e.matmul(pw[:, h * 128:(h + 1) * 128], lhsT=bek[:, h, :], rhs=AT[:, h, :], start=True, stop=True), [AT, bek], [pw])
            P.dve(lambda e: e.tensor_copy(out=wT[:].rearrange("p h c -> p (h c)"), in_=pw[:]), [pw], [wT])
            P.free(pw)
            yield

        def gdn_scan_all(full, qT, sg):
            T = tk
            po = {}

            def main1(blk):
                B = BS[blk % NSET]
                pws = P.bank()
                for h in range(4):
                    P.pe(lambda e, h=h: e.matmul(pws[:, h * 128:(h + 1) * 128], lhsT=B["wT"][:, h, :], rhs=Sb[:, h, :], start=True, stop=True),
                         [B["wT"], Sb], [pws])
                P.dve(lambda e: e.tensor_tensor(out=vnew[:], in0=B["usb"][:], in1=v3(pws), op=ALU.subtract), [B["usb"], pws], [vnew])
                P.free(pws)

            def main2(blk):
                B = BS[blk % NSET]
                cs = slice(blk * 128, (blk + 1) * 128)
                pds = P.bank()
                for h in range(4):
                    P.pe(lambda e, h=h: e.matmul(pds[:, h * 128:(h + 1) * 128], lhsT=B["kdec"][:, h, :], rhs=vnew[:, h, :], start=True, stop=True),
                         [B["kdec"], vnew], [pds])
                if full:
                    po1 = P.bank()
                    po2 = P.bank()
                    for h in range(4):
                        P.pe(lambda e, h=h: e.matmul(po1[:, h * 128:(h + 1) * 128], lhsT=qT[h][:, cs], rhs=Sb[:, h, :], start=True, stop=True),
                             [qT[h], Sb], [po1])
                    for h in range(4):
                        P.pe(lambda e, h=h: e.matmul(po2[:, h * 128:(h + 1) * 128], lhsT=B["attnT"][:, h, :], rhs=vnew[:, h, :], start=True, stop=True),
                             [B["attnT"], vnew], [po2])
                    po[blk] = (po1, po2)
                for h in range(4):
                    P.dve(lambda e, h=h: e.scalar_tensor_tensor(out=Sf[:, h, :], in0=Sf[:, h, :], scalar=T["egl"][:, blk, h:h + 1],
                                                                in1=pds[:, h * 128:(h + 1) * 128], op0=ALU.mult, op1=ALU.add),
                          [Sf, T["egl"], pds], [Sf])
                P.free(pds)
                P.act(lambda e: e.activation(out=Sb[:], in_=Sf[:], func=AF.Copy), [Sf], [Sb])

            def tail1(blk):
                po1, po2 = po[blk]
                P.dve(lambda e: e.tensor_tensor(out=o1s[:], in0=v3(po1), in1=bc4(T["so"], blk), op=ALU.mult), [po1, T["so"]], [o1s])
                P.dve(lambda e: e.tensor_tensor(out=osb[:], in0=v3(po2), in1=o1s[:], op=ALU.add), [po2, o1s], [osb])
                P.free(po1, po2)
                P.pool(lambda e: e.tensor_tensor(out=osq[:], in0=osb[:], in1=osb[:], op=ALU.mult), [osb], [osq])
                P.dve(lambda e: e.tensor_reduce(out=ssqo[:], in_=osq[:], axis=AX.X, op=ALU.add), [osq], [ssqo])
                P.act(lambda e: e.activation(out=rso[:], in_=ssqo[:], func=AF.Ln, bias=epsc[:], scale=1.0 / 128), [ssqo, epsc], [rso])
                P.act(lambda e: e.activation(out=rso[:], in_=rso[:], func=AF.Exp, scale=-0.5), [rso], [rso])
                P.dve(lambda e: e.tensor_tensor(out=onb[:], in0=osb[:], in1=rso[:].unsqueeze(2).to_broadcast([128, 4, 128]), op=ALU.mult),
                      [osb, rso], [onb])

            def tail2(blk):
                cs = slice(blk * 128, (blk + 1) * 128)
                pot = P.bank()
                for h in range(4):
                    P.pe(lambda e, h=h: e.transpose(out=pot[:].bitcast(BF16)[:, h * 128:(h + 1) * 128], in_=onb[:, h, :], identity=ident_b[:]),
                         [onb, ident_b], [pot])
                for h in range(4):
                    P.dve(lambda e, h=h: e.scalar_tensor_tensor(out=ocat[h][:, cs], in0=pot[:].bitcast(BF16)[:, h * 128:(h + 1) * 128],
                                                                scalar=pcol("gng"), in1=sg[h][:, cs], op0=ALU.mult, op1=ALU.mult),
                          [pot, par, sg[h]], [ocat[h]])
                P.free(pot)

            for blk in range(NB):
                main1(blk)
                yield
                if full and blk >= 1:
                    tail1(blk - 1)
                    yield
                main2(blk)
                yield
                if full and blk >= 1:
                    tail2(blk - 1)
                    yield
            if full:
                tail1(NB - 1)
                yield
                tail2(NB - 1)
                yield

        def interleave(gens):
            gens = list(gens)
            while gens:
                for g in list(gens):
                    try:
                        next(g)
                    except StopIteration:
                        gens.remove(g)

        for t_ in [hT, hnT, yT, rstd, lnt, rden] + sq + big + ocat + ubuf + sa:
            t_.split()

        def prenorm_h(gi, c0, c1, hb):
            st = P.bank()
            for c in range(KC):
                s_ = sq[c % 2]
                P.act(lambda e, c=c, s_=s_: e.activation(out=s_[:, c0:c1], in_=hT[:, c, c0:c1], func=AF.Square), [hT.hv(hb)], [s_.hv(hb)])
                P.pe(lambda e, c=c, s_=s_: e.matmul(st[:, c0:c1], lhsT=ones_b[:], rhs=s_[:, c0:c1], start=(c == 0), stop=(c == KC - 1)),
                     [ones_b, s_.hv(hb)], [st])
            P.act(lambda e: e.activation(out=lnt[:, c0:c1], in_=st[:, c0:c1], func=AF.Ln, bias=epsc[:], scale=1.0 / D), [st, epsc], [lnt.hv(hb)])
            P.act(lambda e: e.activation(out=rstd[:, c0:c1], in_=lnt[:, c0:c1], func=AF.Exp, scale=-0.5), [lnt.hv(hb)], [rstd.hv(hb)])
            P.free(st)
            for c in range(KC):
                P.dve(lambda e, c=c: e.scalar_tensor_tensor(out=hnT[:, c, c0:c1], in0=hT[:, c, c0:c1], scalar=pcol("ng", gi * 8 + c),
                                                            in1=rstd[:, c0:c1], op0=ALU.mult, op1=ALU.mult),
                      [hT.hv(hb), par, rstd.hv(hb)], [hnT.hv(hb)])

        def postnorm_h(gi, st, c0, c1, hb):
            P.act(lambda e: e.activation(out=lnt[:, c0:c1], in_=st[:, c0:c1], func=AF.Ln, bias=epsc[:], scale=1.0 / D), [st, epsc], [lnt.hv(hb)])
            P.act(lambda e: e.activation(out=rstd[:, c0:c1], in_=lnt[:, c0:c1], func=AF.Exp, scale=-0.5), [lnt.hv(hb)], [rstd.hv(hb)])
            P.free(st)
            for c in range(KC):
                P.dve(lambda e, c=c: e.scalar_tensor_tensor(out=yT[:, c, c0:c1], in0=yT[:, c, c0:c1], scalar=pcol("ng", gi * 8 + c),
                                                            in1=rstd[:, c0:c1], op0=ALU.mult, op1=ALU.mult),
                      [yT.hv(hb), par, rstd.hv(hb)], [yT.hv(hb)])
                (P.pool if c % 2 == 0 else P.dve)(lambda e, c=c: e.tensor_tensor(out=hT[:, c, c0:c1], in0=hT[:, c, c0:c1], in1=yT[:, c, c0:c1], op=ALU.add),
                                                        [hT.hv(hb), yT.hv(hb)], [hT.hv(hb)])

        def proj_h(slab_ids, rhs_tiles, c0, c1, hb):
            st = P.bank()
            m = 0
            pend = None

            def stat_mm(m_, s2_):
                P.pe(lambda e: e.matmul(st[:, c0:c1], lhsT=ones_b[:], rhs=s2_[:, c0:c1], start=(m_ == 0), stop=(m_ == 7)),
                     [ones_b, s2_.hv(hb)], [st])

            for s_ in slab_ids:
                sl = load_slab(s_)
                sa_ = slab_ap(sl, 512)
                for q in range(4):
                    pb = P.bank()
                    for k in range(KC):
                        P.pe(lambda e, k=k, q=q, pb=pb, sa_=sa_: e.matmul(pb[:, c0:c1], lhsT=sa_[:, k, q * 128:(q + 1) * 128],
                                                                          rhs=rhs_tiles[k][:, c0:c1], start=(k == 0), stop=(k == KC - 1)),
                             [sl, rhs_tiles[k].hv(hb)], [pb])
                    if pend is not None:
                        stat_mm(*pend)
                    s2 = sq[m % 2]
                    P.dve(lambda e, pb=pb, m=m: e.tensor_copy(out=yT[:, m, c0:c1], in_=pb[:, c0:c1]), [pb], [yT.hv(hb)])
                    P.free(pb)
                    P.act(lambda e, m=m, s2=s2: e.activation(out=s2[:, c0:c1], in_=yT[:, m, c0:c1], func=AF.Square), [yT.hv(hb)], [s2.hv(hb)])
                    pend = (m, s2)
                    m += 1
            stat_mm(*pend)
            return st

        def xattn_h(c0, c1, hb):
            qx = big[0:8]
            ox = big[8:16]
            m = 0
            for s_ in range(2):
                sl = load_slab(SL_Q + s_)
                sa_ = slab_ap(sl, 512)
                for q in range(4):
                    pb = P.bank()
                    for k in range(KC):
                        P.pe(lambda e, k=k, q=q, pb=pb, sa_=sa_: e.matmul(pb[:, c0:c1], lhsT=sa_[:, k, q * 128:(q + 1) * 128], rhs=hnT[:, k, c0:c1],
                                                                          start=(k == 0), stop=(k == KC - 1)), [sl, hnT.hv(hb)], [pb])
                    P.act(lambda e, pb=pb, m=m: e.activation(out=qx[m][:, c0:c1], in_=pb[:, c0:c1], func=AF.Copy, scale=1.0 / 16.0), [pb], [qx[m].hv(hb)])
                    P.free(pb)
                    m += 1
            def xs_scores(h):
                pts = [ubuf[(2 * h) % 4], ubuf[(2 * h + 1) % 4]]
                for mc in range(2):
                    pb = P.bank()
                    for dc in range(2):
                        P.pe(lambda e, dc=dc, mc=mc, pb=pb: e.matmul(pb[:, c0:c1], lhsT=KT[:, 2 * h + dc, mc * 128:(mc + 1) * 128],
                                                                     rhs=qx[2 * h + dc][:, c0:c1], start=(dc == 0), stop=(dc == 1)),
                             [KT, qx[2 * h + dc].hv(hb)], [pb])
                    P.act(lambda e, pb=pb, mc=mc: e.activation(out=pts[mc][:, c0:c1], in_=pb[:, c0:c1], func=AF.Exp), [pb], [pts[mc].hv(hb)])
                    P.free(pb)

            def xs_pv(h):
                pts = [ubuf[(2 * h) % 4], ubuf[(2 * h + 1) % 4]]
                pd = P.bank()
                for mc in range(2):
                    P.pe(lambda e, mc=mc: e.matmul(pd[:, c0:c1], lhsT=ones_b[:], rhs=pts[mc][:, c0:c1], start=(mc == 0), stop=(mc == 1)),
                         [ones_b, pts[mc].hv(hb)], [pd])
                pbs_ = []
                for dvc in range(2):
                    pb = P.bank()
                    for mc in range(2):
                        P.pe(lambda e, mc=mc, dvc=dvc, pb=pb: e.matmul(pb[:, c0:c1], lhsT=Vm[:, mc, (2 * h + dvc) * 128:(2 * h + dvc + 1) * 128],
                                                                       rhs=pts[mc][:, c0:c1], start=(mc == 0), stop=(mc == 1)),
                             [Vm, pts[mc].hv(hb)], [pb])
                    pbs_.append(pb)
                P.act(lambda e: e.activation(out=rden[:, c0:c1], in_=pd[:, c0:c1], func=AF.Ln), [pd], [rden.hv(hb)])
                P.act(lambda e: e.activation(out=rden[:, c0:c1], in_=rden[:, c0:c1], func=AF.Exp, scale=-1.0), [rden.hv(hb)], [rden.hv(hb)])
                P.free(pd)
                for dvc in range(2):
                    pb = pbs_[dvc]
                    P.dve(lambda e, pb=pb, dvc=dvc: e.tensor_tensor(out=ox[2 * h + dvc][:, c0:c1], in0=pb[:, c0:c1], in1=rden[:, c0:c1], op=ALU.mult),
                          [pb, rden.hv(hb)], [ox[2 * h + dvc].hv(hb)])
                    P.free(pb)

            for h in range(5):
                if h < 4:
                    xs_scores(h)
                if h >= 1:
                    xs_pv(h - 1)
            return proj_h([SL_O, SL_O + 1], ox, c0, c1, hb)

        def ffn_h(c0, c1, hb, only_up):
            curu = {}
            n = c1 - c0

            def ffn_up_mm(i):
                s_, q = i // 2, i % 2
                if q == 0:
                    curu["sl"] = load_slab(SL_UP + s_)
                sl = curu["sl"]
                sa_ = slab_ap(sl, 512)
                pa_ = P.bank()
                pb_ = P.bank()
                for k in range(KC):
                    P.pe(lambda e, k=k: e.matmul(pa_[:, c0:c1], lhsT=sa_[:, k, q * 128:(q + 1) * 128], rhs=hnT[:, k, c0:c1],
                                                 start=(k == 0), stop=(k == KC - 1)), [sl, hnT.hv(hb)], [pa_])
                for k in range(KC):
                    P.pe(lambda e, k=k: e.matmul(pb_[:, c0:c1], lhsT=sa_[:, k, 256 + q * 128:256 + (q + 1) * 128], rhs=hnT[:, k, c0:c1],
                                                 start=(k == 0), stop=(k == KC - 1)), [sl, hnT.hv(hb)], [pb_])
                return pa_, pb_

            def ffn_up_ev(i, pa_, pb_):
                ua, ub = ubuf[2 * (i % 2)], ubuf[2 * (i % 2) + 1]
                P.pool(lambda e: e.tensor_copy(out=ua[:, c0:c0 + 2], in_=uhist[:, i, :]), [uhist], [ua.hv(hb)])
                P.pool(lambda e: e.tensor_copy(out=ub[:, c0:c0 + 2], in_=uhist[:, 22 + i, :]), [uhist], [ub.hv(hb)])
                P.act(lambda e: e.activation(out=ua[:, 2 + c0:2 + c1], in_=pa_[:, c0:c1], func=AF.Copy), [pa_], [ua.hv(hb)])
                P.dve(lambda e: e.tensor_copy(out=ub[:, 2 + c0:2 + c1], in_=pb_[:, c0:c1]), [pb_], [ub.hv(hb)])
                P.free(pa_, pb_)
                P.pool(lambda e: e.tensor_copy(out=uhist[:, i, :], in_=ua[:, c1:c1 + 2]), [ua.hv(hb)], [uhist])
                P.pool(lambda e: e.tensor_copy(out=uhist[:, 22 + i, :], in_=ub[:, c1:c1 + 2]), [ub.hv(hb)], [uhist])

            def ffn_conv(i):
                ua, ub = ubuf[2 * (i % 2)], ubuf[2 * (i % 2) + 1]
                ds = []
                for j in range(6):
                    d = dg[dg_ctr[0] % len(dg)]
                    dg_ctr[0] += 1
                    wb = (i * 3 + j) if j < 3 else ((22 + i) * 3 + j - 3)
                    P.dve(lambda e, d=d, wb=wb: e.tensor_scalar(out=d[:], in0=ident_b[:], scalar1=pcol("fcw", wb), scalar2=None, op0=ALU.mult),
                          [ident_b, par], [d])
                    ds.append(d)
                pca = P.bank()
                pcb = P.bank()
                for k in range(3):
                    P.pe(lambda e, k=k: e.matmul(pca[:, c0:c1], lhsT=ds[k][:], rhs=ua[:, c0 + k:c1 + k], start=(k == 0), stop=(k == 2)),
                         [ds[k], ua.hv(hb)], [pca])
                for k in range(3):
                    P.pe(lambda e, k=k: e.matmul(pcb[:, c0:c1], lhsT=ds[3 + k][:], rhs=ub[:, c0 + k:c1 + k], start=(k == 0), stop=(k == 2)),
                         [ds[3 + k], ub.hv(hb)], [pcb])
                return pca, pcb

            def ffn_act_ev(i, pca, pcb):
                sat = sa[i % 2]
                P.act(lambda e: e.activation(out=sat[:, c0:c1], in_=pca[:, c0:c1], func=AF.Silu, bias=pcol("fcb", i)), [pca, par], [sat.hv(hb)])
                P.dve(lambda e: e.scalar_tensor_tensor(out=big[i][:, c0:c1], in0=pcb[:, c0:c1], scalar=pcol("fcb", 22 + i), in1=sat[:, c0:c1],
                                                       op0=ALU.add, op1=ALU.mult), [pcb, par, sat.hv(hb)], [big[i].hv(hb)])
                P.free(pca, pcb)

            if only_up:
                for i in range(22):
                    ffn_up_ev(i, *ffn_up_mm(i))
                return None
            for i in range(23):
                if i < 22:
                    bk = ffn_up_mm(i)
                if i >= 1:
                    pc = ffn_conv(i - 1)
                if i < 22:
                    ffn_up_ev(i, *bk)
                if i >= 1:
                    ffn_act_ev(i - 1, *pc)
            st = P.bank()
            pendd = []
            for ng in range(4):
                pbs = [P.bank(), P.bank()]
                for kh in range(2):
                    sl = load_slab(SL_DN + ng * 2 + kh)
                    sa_ = slab_ap(sl, 256, 11)
                    for m2 in range(2):
                        for k in range(11):
                            kk = kh * 11 + k
                            P.pe(lambda e, k=k, kk=kk, m2=m2, pbs=pbs, sa_=sa_: e.matmul(pbs[m2][:, c0:c1], lhsT=sa_[:, k, m2 * 128:(m2 + 1) * 128],
                                                                                        rhs=big[kk][:, c0:c1], start=(kk == 0), stop=(kk == 21)),
                                 [sl, big[kk].hv(hb)], [pbs[m2]])
                for (m_, s2_) in pendd:
                    P.pe(lambda e, m_=m_, s2_=s2_: e.matmul(st[:, c0:c1], lhsT=ones_b[:], rhs=s2_[:, c0:c1], start=(m_ == 0), stop=(m_ == 7)),
                         [ones_b, s2_.hv(hb)], [st])
                pendd = []
                for m2 in range(2):
                    m = ng * 2 + m2
                    s2 = sq[m % 2]
                    P.dve(lambda e, m=m, pb=pbs[m2]: e.tensor_copy(out=yT[:, m, c0:c1], in_=pb[:, c0:c1]), [pbs[m2]], [yT.hv(hb)])
                    P.act(lambda e, s2=s2, m=m: e.activation(out=s2[:, c0:c1], in_=yT[:, m, c0:c1], func=AF.Square), [yT.hv(hb)], [s2.hv(hb)])
                    pendd.append((m, s2))
                P.free(*pbs)
            for (m_, s2_) in pendd:
                P.pe(lambda e, m_=m_, s2_=s2_: e.matmul(st[:, c0:c1], lhsT=ones_b[:], rhs=s2_[:, c0:c1], start=(m_ == 0), stop=(m_ == 7)),
                     [ones_b, s2_.hv(hb)], [st])
            return st

        def store_h(c0, c1, hb, out_idx):
            for blk in range(c0 // 128, c1 // 128):
                os_ = ost[blk % 2]
                for half in range(2):
                    pb = P.bank()
                    for q in range(4):
                        c = half * 4 + q
                        P.pe(lambda e, c=c, q=q, blk=blk, pb=pb: e.transpose(out=pb[:, q * 128:(q + 1) * 128], in_=hT[:, c, blk * 128:(blk + 1) * 128],
                                                                             identity=ident_f), [hT.hv(hb), cst], [pb])
                    copy_ev(os_[:, half * 512:(half + 1) * 512], pb[:], [pb], [os_])
                    P.free(pb)
                r0 = out_idx * NT + blk * 128
                P.dma("sp", lambda e, os_=os_, r0=r0: e.dma_start(out=yout[r0:r0 + 128, :], in_=os_[:]), [os_], [], "xst%d" % (blk % 2))

        def back_schedule(halves, out_idx):
            halo = out_idx is None
            if len(halves) == 1:
                (c0, c1, hb) = halves[0]
                st = proj_h([SL_OUT, SL_OUT + 1], ocat, c0, c1, hb)
                postnorm_h(1, st, c0, c1, hb)
                prenorm_h(2, c0, c1, hb)
                st = xattn_h(c0, c1, hb)
                postnorm_h(3, st, c0, c1, hb)
                prenorm_h(4, c0, c1, hb)
                st = ffn_h(c0, c1, hb, halo)
                if not halo:
                    postnorm_h(5, st, c0, c1, hb)
                    store_h(c0, c1, hb, out_idx)
            else:
                A, B = halves
                stA = proj_h([SL_OUT, SL_OUT + 1], ocat, *A)
                postnorm_h(1, stA, *A)
                stB = proj_h([SL_OUT, SL_OUT + 1], ocat, *B)
                prenorm_h(2, *A)
                postnorm_h(1, stB, *B)
                ckpt(7)
                stA = xattn_h(*A)
                prenorm_h(2, *B)
                postnorm_h(3, stA, *A)
                stB = xattn_h(*B)
                prenorm_h(4, *A)
                postnorm_h(3, stB, *B)
                ckpt(8)
                stA = ffn_h(A[0], A[1], A[2], False)
                prenorm_h(4, *B)
                postnorm_h(5, stA, *A)
                stB = ffn_h(B[0], B[1], B[2], False)
                store_h(A[0], A[1], A[2], out_idx)
                postnorm_h(5, stB, *B)
                ckpt(9)
                store_h(B[0], B[1], B[2], out_idx)
            if halo:
                P.pool(lambda e: e.tensor_scalar(out=uhist[:], in0=uhist[:], scalar1=pcol("flag"), scalar2=None, op0=ALU.mult), [uhist, par], [uhist])

        def tile(ti, full, out_idx):
            tok0 = ti * NT
            for blk in range(NB):
                xs = xst[blk % 2]
                P.dma("sp", lambda e, blk=blk, xs=xs: e.dma_start(out=xs[:], in_=xin[tok0 + blk * 128: tok0 + (blk + 1) * 128, :]),
                      [], [xs], "xst%d" % (blk % 2))
                for half in range(2):
                    pb = P.bank()
                    for q in range(4):
                        c = half * 4 + q
                        P.pe(lambda e, c=c, q=q, xs=xs, pb=pb: e.transpose(out=pb[:, q * 128:(q + 1) * 128], in_=xs[:, c * 128:(c + 1) * 128],
                                                                           identity=ident_f), [xs, cst], [pb])
                    copy_ev(hT[:, half * 4:half * 4 + 4, blk * 128:(blk + 1) * 128], pb[:].rearrange("p (q c) -> p q c", q=4), [pb], [hT])
                    P.free(pb)
            ckpt(1)
            prenorm(0)
            ckpt(2)
            qs, ks, vs, sgs = big[0:4], big[4:8], big[8:12], big[12:16]
            chunks = []
            for s_, dst in ([(0, qs)] if full else []) + [(1, ks), (2, vs)]:
                for q in range(4):
                    chunks.append((s_, q, dst))
            cur = {}

            def qkv_proj_mm(s_, q):
                if q == 0:
                    cur["sl"] = load_slab(SL_IN + s_)
                sl = cur["sl"]
                sa_ = slab_ap(sl, 512)
                pb = P.bank()
                for k in range(KC):
                    P.pe(lambda e, k=k: e.matmul(pb[:], lhsT=sa_[:, k, q * 128:(q + 1) * 128], rhs=hnT[:, k, :],
                                                 start=(k == 0), stop=(k == KC - 1)), [sl, hnT], [pb])
                return pb

            def qkv_proj_ev(s_, q, pb):
                ch = s_ * 4 + q
                z = zc[ch % 4]
                P.pool(lambda e: e.tensor_copy(out=z[:, 0:3], in_=zhist[:, ch, :]), [zhist], [z])
                copy_ev(z[:, 3:3 + NT], pb[:], [pb], [z])
                P.free(pb)
                P.pool(lambda e: e.tensor_copy(out=zhist[:, ch, :], in_=z[:, NT:NT + 3]), [z], [zhist])

            def qkv_conv_mm(s_, q, dst):
                ch = s_ * 4 + q
                z = zc[ch % 4]
                pc = P.bank()
                conv_mm(pc, z, z, 4, "gcw", ch * 4)
                return pc

            def qkv_conv_ev(q, dst, pc):
                P.act(lambda e: e.activation(out=dst[q][:], in_=pc[:], func=AF.Silu), [pc], [dst[q]])
                P.free(pc)

            for i in range(len(chunks) + 1):
                if i < len(chunks):
                    pbq = qkv_proj_mm(chunks[i][0], chunks[i][1])
                if i >= 1:
                    pcq = qkv_conv_mm(*chunks[i - 1])
                if i < len(chunks):
                    qkv_proj_ev(chunks[i][0], chunks[i][1], pbq)
                if i >= 1:
                    qkv_conv_ev(chunks[i - 1][1], chunks[i - 1][2], pcq)
            pab = P.bank()
            for blk in range(NB):
                for k in range(KC):
                    P.pe(lambda e, k=k, blk=blk: e.matmul(pab[:, blk * 8:(blk + 1) * 8], lhsT=hnT[:, k, blk * 128:(blk + 1) * 128], rhs=wab[:, k, :],
                                                          start=(k == 0), stop=(k == KC - 1)), [hnT, wab], [pab])
            P.dve(lambda e: e.tensor_copy(out=abtok[:].rearrange("p b c -> p (b c)"), in_=pab[:, 0:NB * 8]), [pab], [abtok])
            P.free(pab)
            pss = P.bank()
            lst = ([(qs[h], h) for h in range(4)] if full else []) + [(ks[h], 4 + h) for h in range(4)]
            for i, (src, col) in enumerate(lst):
                s2 = sq[i % 2]
                P.pool(lambda e, src=src, s2=s2: e.tensor_tensor(out=s2[:], in0=src[:], in1=src[:], op=ALU.mult), [src], [s2])
                for blk in range(NB):
                    P.pe(lambda e, blk=blk, s2=s2, col=col: e.matmul(pss[:, blk * 8 + col:blk * 8 + col + 1], lhsT=s2[:, blk * 128:(blk + 1) * 128],
                                                                      rhs=ones_b[:, 0:1], start=True, stop=True), [s2, ones_b], [pss])
            if full:
                P.dve(lambda e: e.tensor_copy(out=ssq[:].rearrange("p b c -> p (b c)"), in_=pss[:, 0:NB * 8]), [pss], [ssq])
            else:
                P.dve(lambda e: e.tensor_copy(out=ssq[:, :, 4:8], in_=pss[:, 0:NB * 8].rearrange("p (b c) -> p b c", c=8)[:, :, 4:8]), [pss], [ssq])
            P.free(pss)
            ckpt(3)
            gdn_scalars(full)
            ckpt(4)

            def filler():
                sl = load_slab(SL_IN + 3)
                sa_ = slab_ap(sl, 512)
                for q in range(4):
                    pb = P.bank()
                    for k in range(KC):
                        P.pe(lambda e, k=k, q=q, pb=pb, sa_=sa_: e.matmul(pb[:], lhsT=sa_[:, k, q * 128:(q + 1) * 128], rhs=hnT[:, k, :],
                                                                          start=(k == 0), stop=(k == KC - 1)), [sl, hnT], [pb])
                        if k % 4 == 3:
                            yield
                    P.act(lambda e, pb=pb, q=q: e.activation(out=sgs[q][:], in_=pb[:], func=AF.Silu), [pb], [sgs[q]])
                    P.free(pb)
                for s_ in range(2):
                    sl = load_slab(SL_IN + 4 + s_)
                    sa_ = slab_ap(sl, 512)
                    for q in range(2):
                        ch = 2 * s_ + q
                        pa_ = P.bank()
                        pb_ = P.bank()
                        for k in range(KC):
                            P.pe(lambda e, k=k, q=q, pa_=pa_, sa_=sa_: e.matmul(pa_[:], lhsT=sa_[:, k, q * 128:(q + 1) * 128], rhs=hnT[:, k, :],
                                                                                start=(k == 0), stop=(k == KC - 1)), [sl, hnT], [pa_])
                            if k % 4 == 3:
                                yield
                        for k in range(KC):
                            P.pe(lambda e, k=k, q=q, pb_=pb_, sa_=sa_: e.matmul(pb_[:], lhsT=sa_[:, k, 256 + q * 128:256 + (q + 1) * 128], rhs=hnT[:, k, :],
                                                                                start=(k == 0), stop=(k == KC - 1)), [sl, hnT], [pb_])
                            if k % 4 == 3:
                                yield
                        cb = cbuf[ch]
                        P.pool(lambda e, cb=cb: e.tensor_copy(out=cb[:, 0:30], in_=cb[:, NT:NT + 30]), [cb], [cb])
                        P.act(lambda e, pb_=pb_: e.activation(out=lnt[:], in_=pb_[:], func=AF.Sigmoid), [pb_], [lnt])
                        P.dve(lambda e, pa_=pa_, cb=cb: e.tensor_tensor(out=cb[:, 30:30 + NT], in0=pa_[:], in1=lnt[:], op=ALU.mult), [pa_, lnt], [cb])
                        P.free(pa_, pb_)
                pmean = P.bank()
                pe2 = P.bank()
                for ch in range(4):
                    pc = P.bank()
                    for k in range(31):
                        d = dg[dg_ctr[0] % len(dg)]
                        dg_ctr[0] += 1
                        P.dve(lambda e, d=d, k=k, ch=ch: e.tensor_scalar(out=d[:], in0=ident_b[:], scalar1=pcol("cdw", ch * 31 + k), scalar2=None,
                                                                         op0=ALU.mult), [ident_b, par], [d])
                        P.pe(lambda e, d=d, k=k, ch=ch, pc=pc: e.matmul(pc[:], lhsT=d[:], rhs=cbuf[ch][:, k:k + NT], start=(k == 0), stop=(k == 30)),
                             [d, cbuf[ch]], [pc])
                        if k % 4 == 3:
                            yield
                    P.act(lambda e, pc=pc, ch=ch: e.activation(out=yT[:, ch, :], in_=pc[:], func=AF.Identity, bias=pcol("cdb", ch)), [pc, par], [yT])
                    s2 = sq[ch % 2]
                    P.act(lambda e, pc=pc, ch=ch, s2=s2: e.activation(out=s2[:], in_=pc[:], func=AF.Square, bias=pcol("cdb", ch)), [pc, par], [s2])
                    P.free(pc)
                    P.pe(lambda e, ch=ch: e.matmul(pmean[:], lhsT=ones_f[:], rhs=yT[:, ch, :], start=(ch == 0), stop=(ch == 3)), [ones_f, yT], [pmean])
                    P.pe(lambda e, ch=ch, s2=s2: e.matmul(pe2[:], lhsT=ones_b[:], rhs=s2[:], start=(ch == 0), stop=(ch == 3)), [ones_b, s2], [pe2])
                    yield
                P.act(lambda e: e.activation(out=msb[:], in_=pmean[:], func=AF.Copy, scale=1.0 / 512), [pmean], [msb])
                P.pool(lambda e: e.tensor_tensor(out=var[:], in0=msb[:], in1=msb[:], op=ALU.mult), [msb], [var])
                P.dve(lambda e: e.scalar_tensor_tensor(out=var[:], in0=pe2[:], scalar=1.0 / 512, in1=var[:], op0=ALU.mult, op1=ALU.subtract),
                      [pe2, var], [var])
                P.free(pmean, pe2)
                P.act(lambda e: e.activation(out=lnt[:], in_=var[:], func=AF.Ln, bias=epsc[:]), [var, epsc], [lnt])
                P.act(lambda e: e.activation(out=rstd[:], in_=lnt[:], func=AF.Exp, scale=-0.5), [lnt], [rstd])
                yield
                for ch in range(4):
                    P.pool(lambda e, ch=ch: e.tensor_tensor(out=yT[:, ch, :], in0=yT[:, ch, :], in1=msb[:], op=ALU.subtract), [yT, msb], [yT])
                    P.dve(lambda e, ch=ch: e.tensor_tensor(out=yT[:, ch, :], in0=yT[:, ch, :], in1=rstd[:], op=ALU.mult), [yT, rstd], [yT])
                    P.act(lambda e, ch=ch: e.activation(out=ocat[4 + ch][:], in_=yT[:, ch, :], func=AF.Silu, bias=pcol("clb", ch), scale=pcol("clg", ch)),
                          [yT, par], [ocat[4 + ch]])
                    yield

            fil = filler() if full else None

            def run_with_filler(gens, nfill, every):
                nonlocal fil
                gens = list(gens)
                rnd = 0
                while gens:
                    rnd += 1
                    if rnd % every == 0:
                        for _ in range(nfill):
                            if fil is not None:
                                try:
                                    next(fil)
                                except StopIteration:
                                    fil = None
                    for g in list(gens):
                        try:
                            next(g)
                        except StopIteration:
                            gens.remove(g)

            run_with_filler([gdn_pre(blk, full, qs, ks, vs) for blk in range(NB)], 1, 2)
            ckpt(5)
            run_with_filler([gdn_scan_all(full, qs, sgs)], 3, 1)
            if fil is not None:
                for _ in fil:
                    pass
            if not full:
                return
            ckpt(6)
            if out_idx is None:
                halves = [(NT - 128, NT, None)]
            else:
                halves = [(0, NT, None)]
            back_schedule(halves, out_idx)

        try:
            ti = 0
            per = (len(late_list) + max(NS, 1) - 1) // max(NS, 1)
            for _ in range(NS):
                tile(ti, False, None)
                issue_late(per)
                ti += 1
            issue_late(len(late_list))
            tile(ti, True, None)
            ti += 1
            for i in range(NF):
                tile(ti, True, i)
                ti += 1
            P.final = ["xst0", "xst1"]
        except StopBuild:
            P.final = [k for k in P.dma_cnt.keys()]
        P.emit()
    return nc


def make_consts():
    c = np.zeros((128, 4, 128), np.float32)
    c[:, 0, :] = np.eye(128, dtype=np.float32)
    j = np.arange(128)[:, None]
    cc = np.arange(128)[None, :]
    c[:, 1, :] = (j <= cc).astype(np.float32)
    c[:, 2, :] = np.where(cc >= j, 0.0, NEGV)
    c[:, 3, :] = np.where(cc > j, 0.0, NEGV)
    return c.reshape(128, 512)


def make_params(inp, flag):
    p = np.zeros((128, NPAR), np.float32)

    def put(name, arr):
        p[:, PO[name]:PO[name] + arr.shape[1]] = arr

    def chunked(v):
        return np.ascontiguousarray(v.reshape(-1, 128).T)

    ng = inp["norm_g"][0]
    put("ng", np.concatenate([chunked(ng[i]) for i in range(6)], axis=1))
    gcw = inp["gdn_conv_w"][0]
    put("gcw", np.ascontiguousarray(gcw.reshape(4, 12, 128).transpose(2, 1, 0)).reshape(128, 48))
    cdw = inp["cfm_dw_w"][0]
    put("cdw", np.ascontiguousarray(cdw.reshape(31, 4, 128).transpose(2, 1, 0)).reshape(128, 124))
    put("cdb", chunked(inp["cfm_dw_b"][0]))
    put("clg", chunked(inp["cfm_ln_g"][0]))
    put("clb", chunked(inp["cfm_ln_b"][0]))
    fcw = inp["ffn_conv_w"][0]
    put("fcw", np.ascontiguousarray(fcw.reshape(3, 44, 128).transpose(2, 1, 0)).reshape(128, 132))
    put("fcb", chunked(inp["ffn_conv_b"][0]))
    put("gng", inp["gdn_norm_g"][0].reshape(128, 1))
    put("alog", np.tile(inp["gdn_a_log"][0][None, :], (128, 4)))
    put("dtb", np.tile(inp["gdn_dt_bias"][0][None, :], (128, 4)))
    put("mng", chunked(inp["mem_norm_g"][0]))
    p[:, PO["flag"]] = flag
    return p


_NC_CACHE = {}


def run(inputs, NF, NS, n_batch, dbg=False):
    inp = {k: np.asarray(v, dtype=np.float32) for k, v in inputs.items()}
    key = (NF, NS, dbg)
    if key not in _NC_CACHE:
        _NC_CACHE[key] = build(NF, NS, dbg)
    nc = _NC_CACHE[key]
    half = NF * NT
    nprev = (NS + 1) * NT
    consts = make_consts()
    in_maps = []
    for b in range(n_batch):
        for j in range(2):
            xall = np.zeros((nprev + half, D), np.float32)
            if j == 1:
                xall[:nprev] = inp["x"][b, 0:half]
            xall[nprev:] = inp["x"][b, j * half:(j + 1) * half]
            in_maps.append({
                "xin": xall, "memin": np.ascontiguousarray(inp["mem"][b]),
                "params": make_params(inp, float(j)), "consts": consts,
                "w_in": np.ascontiguousarray(inp["w_in"][0]), "w_out": np.ascontiguousarray(inp["w_out"][0]),
                "w_q": np.ascontiguousarray(inp["xa_w_q"][0]), "w_kv": np.ascontiguousarray(inp["xa_w_kv"][0]),
                "w_o": np.ascontiguousarray(inp["xa_w_o"][0]), "w_up": np.ascontiguousarray(inp["ffn_w_up"][0]),
                "w_dn": np.ascontiguousarray(inp["ffn_w_down"][0]),
            })
    res = run_bass_kernel_spmd(nc, in_maps, core_ids=list(range(len(in_maps))))
    out = np.zeros((n_batch, 2 * half, D), np.float32)
    for b in range(n_batch):
        for j in range(2):
            out[b, j * half:(j + 1) * half] = res.results[2 * b + j]["yout"]
    return out, res


def kernel(**inputs):
    out, _ = run(inputs, NF=8, NS=7, n_batch=4)
    return out
```

```python
import numpy as np
from contextlib import ExitStack
import concourse.bass as bass
import concourse.mybir as mybir
from concourse.bass_utils import run_bass_kernel_spmd

F32 = mybir.dt.float32
BF16 = mybir.dt.bfloat16
AF = mybir.ActivationFunctionType
ALU = mybir.AluOpType
AX = mybir.AxisListType

D = 1024
NT = 512
NB = NT // 128
H = 4
DFF = 2816
MEM = 256
EPS = 1e-6
NEGV = -30000.0
KC = 8

PO = {}
_o = 0
for _n, _w in [("ng", 48), ("gcw", 48), ("cdw", 124), ("cdb", 4), ("clg", 4), ("clb", 4),
               ("fcw", 132), ("fcb", 44), ("gng", 1), ("alog", 16), ("dtb", 16), ("mng", 8),
               ("flag", 1)]:
    PO[_n] = _o
    _o += _w
NPAR = _o

SL_IN, SL_OUT, SL_Q, SL_KV, SL_O, SL_UP, SL_DN = 0, 6, 8, 10, 14, 16, 27
NSLAB = 35


import os
STOP = float(os.environ.get("KSTOP", "1000"))


class StopBuild(Exception):
    pass


def ckpt(n):
    if n >= STOP:
        raise StopBuild()


class Buf:
    __slots__ = ("w", "rs", "const")

    def __init__(self):
        self.w = None
        self.rs = []
        self.const = False


class Tile:
    def __init__(self, t, bufs=None):
        self.t = t
        self.bufs = bufs if bufs is not None else [Buf()]
        self.h = None

    def split(self):
        self.bufs = [Buf(), Buf()]
        self.h = [Tile(self.t, [self.bufs[0]]), Tile(self.t, [self.bufs[1]])]
        return self

    def split_chunks(self, n):
        self.bufs = [Buf() for _ in range(n)]
        self.c = [Tile(self.t, [b]) for b in self.bufs]
        return self

    def cv(self, k):
        return self.c[k]

    def __getitem__(self, idx):
        return self.t[idx]


class Ins:
    __slots__ = ("eng", "fn", "idx", "dma_key", "dma_val", "deps", "needs_inc", "ordinal", "waits")

    def __init__(self, eng, fn, idx, dma_key):
        self.eng = eng
        self.fn = fn
        self.idx = idx
        self.dma_key = dma_key
        self.dma_val = None
        self.deps = []
        self.needs_inc = False
        self.ordinal = 0
        self.waits = []


class Prog:
    ENGS = ["pe", "act", "dve", "pool", "sp"]

    def __init__(self, nc, es):
        self.nc = nc
        self.es = es
        self.ins = {e: [] for e in self.ENGS}
        self.dma_cnt = {}
        self.dma_sem = {}
        self.barrier_keys = set()
        self.sem = {}
        self.nfresh = 0
        self.free_banks = []
        self.final = []

    def sb(self, name, shape, dt):
        return Tile(self.es.enter_context(self.nc.sbuf_tensor(name, list(shape), dt)))

    def mkbanks(self):
        for i in range(8):
            t = Tile(self.es.enter_context(self.nc.psum_tensor("bank%d" % i, [128, 512], F32)))
            self.free_banks.append(t)

    def bank(self):
        assert self.free_banks, "out of PSUM banks"
        return self.free_banks.pop(0)

    def free(self, *bs):
        for b in bs:
            self.free_banks.append(b)

    def _add(self, eng, fn, r, w, dma_key=None):
        ins = Ins(eng, fn, len(self.ins[eng]), dma_key)
        deps = {}
        rb = [b for t in r for b in t.bufs]
        wb = [b for t in w for b in t.bufs]
        for b in rb:
            if b.w is not None:
                deps[id(b.w)] = (b.w, True)
        for b in wb:
            if b.w is not None and id(b.w) not in deps:
                deps[id(b.w)] = (b.w, False)
            for x in b.rs:
                if id(x) not in deps:
                    deps[id(x)] = (x, False)
        for b in rb:
            if not b.const:
                b.rs.append(ins)
        for b in wb:
            b.w = ins
            b.rs = []
        ins.deps = [v for k, v in deps.items() if v[0] is not ins]
        self.ins[eng].append(ins)
        if dma_key is not None:
            if dma_key == "fresh":
                dma_key = "fresh%d" % self.nfresh
                self.nfresh += 1
                ins.dma_key = dma_key
            self.dma_cnt[dma_key] = self.dma_cnt.get(dma_key, 0) + 16
            ins.dma_val = self.dma_cnt[dma_key]
        return ins

    def pe(self, fn, r, w):
        return self._add("pe", fn, r, w)

    def act(self, fn, r, w):
        return self._add("act", fn, r, w)

    def dve(self, fn, r, w):
        return self._add("dve", fn, r, w)

    def pool(self, fn, r, w):
        return self._add("pool", fn, r, w)

    def dma(self, q, fn, r, w, key):
        return self._add(q, fn, r, w, dma_key=key)

    def resolve(self):
        for e in self.ENGS:
            for ins in self.ins[e]:
                for (y, raw) in ins.deps:
                    if y.dma_key is not None:
                        continue
                    if y.eng == e:
                        if e in ("act", "dve", "pool") and raw:
                            y.needs_inc = True
                        continue
                    y.needs_inc = True
        for e in self.ENGS:
            c = 0
            for ins in self.ins[e]:
                if ins.dma_key is None and ins.needs_inc:
                    c += 1
                    ins.ordinal = c
        for k in list(self.dma_cnt.keys()):
            self.dma_sem[k] = self.es.enter_context(self.nc.semaphore("d_" + k))
        for e in self.ENGS:
            self.sem[e] = self.es.enter_context(self.nc.semaphore("e_" + e))
        for e in self.ENGS:
            waited = {}
            for ins in self.ins[e]:
                ws = {}
                for (y, raw) in ins.deps:
                    if y.dma_key is not None:
                        k = ("d", y.dma_key)
                        v = self.dma_cnt[y.dma_key] if y.dma_key in self.barrier_keys else y.dma_val
                    else:
                        if y.eng == e and not (e in ("act", "dve", "pool") and raw):
                            continue
                        k = ("e", y.eng)
                        v = y.ordinal
                    if waited.get(k, 0) >= v:
                        continue
                    if ws.get(k, 0) < v:
                        ws[k] = v
                for k, v in ws.items():
                    waited[k] = v
                    sem = self.dma_sem[k[1]] if k[0] == "d" else self.sem[k[1]]
                    ins.waits.append((sem, v))

    def emit(self):
        self.resolve()
        nc = self.nc
        prog = self

        def run(e, eng):
            for ins in prog.ins[e]:
                for (sem, v) in ins.waits:
                    eng.wait_ge(sem, v)
                bi = ins.fn(eng)
                if ins.dma_key is not None:
                    bi.then_inc(prog.dma_sem[ins.dma_key], 16)
                elif ins.needs_inc:
                    bi.then_inc(prog.sem[e], 1)
            if e == "sp":
                for k in prog.final:
                    eng.wait_ge(prog.dma_sem[k], prog.dma_cnt[k])

        with nc.Block() as block:
            @block.tensor
            def _(eng):
                run("pe", eng)

            @block.scalar
            def _(eng):
                run("act", eng)

            @block.vector
            def _(eng):
                run("dve", eng)

            @block.gpsimd
            def _(eng):
                run("pool", eng)

            @block.sync
            def _(eng):
                run("sp", eng)


def build(NF, NS, dbg=False):
    NTOK = (NS + 1 + NF) * NT
    nc = bass.Bass("TRN2", target_bir_lowering=False)
    xin = nc.dram_tensor("xin", [NTOK, D], F32, kind="ExternalInput").ap()
    memin = nc.dram_tensor("memin", [MEM, D], F32, kind="ExternalInput").ap()
    params = nc.dram_tensor("params", [128, NPAR], F32, kind="ExternalInput").ap()
    consts = nc.dram_tensor("consts", [128, 4 * 128], F32, kind="ExternalInput").ap()
    w_in = nc.dram_tensor("w_in", [D, 3080], F32, kind="ExternalInput").ap()
    w_out = nc.dram_tensor("w_out", [D, D], F32, kind="ExternalInput").ap()
    w_q = nc.dram_tensor("w_q", [D, D], F32, kind="ExternalInput").ap()
    w_kv = nc.dram_tensor("w_kv", [D, 2 * D], F32, kind="ExternalInput").ap()
    w_o = nc.dram_tensor("w_o", [D, D], F32, kind="ExternalInput").ap()
    w_up = nc.dram_tensor("w_up", [D, 2 * DFF], F32, kind="ExternalInput").ap()
    w_dn = nc.dram_tensor("w_dn", [DFF, D], F32, kind="ExternalInput").ap()
    yout = nc.dram_tensor("yout", [NF * NT, D], F32, kind="ExternalOutput").ap()
    wscr = nc.dram_tensor("wscr", [NSLAB, 128, 4096], BF16, kind="Internal").ap()
    dbg_out = None
    if dbg:
        dbg_out = nc.dram_tensor("dbg", [128, 16, NT], F32, kind="ExternalOutput").ap()

    es = ExitStack()
    with es:
        P = Prog(nc, es)
        P.mkbanks()
        DR = Tile(None)

        cst = P.sb("cst", [128, 4, 128], F32)
        par = P.sb("par", [128, NPAR], F32)
        ident_b = P.sb("ident_b", [128, 128], BF16)
        identB4 = P.sb("identB4", [128, 4, 128], BF16)
        ones_b = P.sb("ones_b", [128, 128], BF16)
        ones_f = P.sb("ones_f", [128, 128], F32)
        epsc = P.sb("epsc", [128, 1], F32)
        lnc = P.sb("lnc", [128, 1], F32)
        expA = P.sb("expA", [128, 16], F32)
        wab = P.sb("wab", [128, KC, 8], BF16)
        NSLOT = 3
        slots = [P.sb("slot%d" % i, [128, 4096], BF16) for i in range(NSLOT)]
        KT = P.sb("KT", [128, 8, MEM], BF16)
        Vm = P.sb("Vm", [128, 2, D], BF16)
        hT = P.sb("hT", [128, KC, NT], F32)
        hnT = P.sb("hnT", [128, KC, NT], BF16)
        yT = P.sb("yT", [128, KC, NT], F32)
        for t_ in [hT, hnT, yT]:
            t_.split_chunks(KC)
        xst = [P.sb("xst%d" % i, [128, D], F32) for i in range(2)]
        big = [P.sb("big%d" % i, [128, NT], BF16) for i in range(22)]
        zc = [P.sb("zc%d" % i, [128, 3 + NT], BF16) for i in range(4)]
        zhist = P.sb("zhist", [128, 12, 3], BF16)
        cbuf = [P.sb("cbuf%d" % i, [128, 30 + NT], BF16) for i in range(4)]
        ubuf = [P.sb("ubuf%d" % i, [128, 2 + NT], BF16) for i in range(4)]
        uhist = P.sb("uhist", [128, 44, 2], BF16)
        ocat = [P.sb("ocat%d" % i, [128, NT], BF16) for i in range(8)]
        sq = [P.sb("sq%d" % i, [128, NT], BF16) for i in range(2)]
        rstd = P.sb("rstd", [128, NT], F32)
        lnt = P.sb("lnt", [128, NT], F32)
        dg = [P.sb("dg%d" % i, [128, 128], BF16) for i in range(12)]
        rden = P.sb("rden", [128, NT], F32)
        msb = P.sb("msb", [128, NT], F32)
        var = rden
        sa = [P.sb("sa%d" % i, [128, NT], BF16) for i in range(2)]
        abtok = P.sb("abtok", [128, NB, 8], F32)
        ssq = P.sb("ssq", [128, NB, 8], F32)
        tk = {n: P.sb("tk_" + n, [128, NB, 4], F32) for n in
              ["a", "g", "lb", "lrq", "lrk", "gc", "gl", "vL", "vA", "bj", "b", "sw", "skd", "egl", "so", "t1", "t2"]}
        Sf = P.sb("Sf", [128, 4, 128], F32)
        Sb = P.sb("Sb", [128, 4, 128], BF16)
        NSET = 4
        BS = []
        for i in range(NSET):
            BS.append({n: P.sb("%s_%d" % (n, i), [128, 4, 128], dt) for n, dt in
                       [("kdec", BF16), ("bek", BF16), ("bv", BF16), ("M", BF16), ("M2", BF16), ("P", BF16), ("X", BF16),
                        ("attnT", BF16), ("usb", F32), ("wT", BF16)]})
        ELs = [P.sb("EL%d" % i, [128, 4, 128], BF16) for i in range(2)]
        EAs = [P.sb("EA%d" % i, [128, 4, 128], BF16) for i in range(2)]
        vnew = P.sb("vnew", [128, 4, 128], BF16)
        osb = P.sb("osb", [128, 4, 128], F32)
        scr4 = P.sb("scr4", [128, 4, 128], F32)
        o1s = scr4
        osq = scr4
        Stmp = P.sb("Stmp", [128, 4, 128], F32)
        ssqo = P.sb("ssqo", [128, 4], F32)
        rso = P.sb("rso", [128, 4], F32)
        onb = P.sb("onb", [128, 4, 128], BF16)
        ost = xst

        def pcol(name, i=0, n=1):
            o = PO[name] + i
            return par[:, o:o + n]

        ident_f = cst[:, 0, :]
        utri = cst[:, 1, :]
        negi = cst[:, 2, :]
        negs = cst[:, 3, :]

        P.dma("sp", lambda e: e.dma_start(out=cst[:], in_=consts.rearrange("p (a c) -> p a c", a=4)), [], [cst], "fresh")
        P.dma("sp", lambda e: e.dma_start(out=par[:], in_=params), [], [par], "fresh")
        P.dma("pool", lambda e: e.dma_start(out=wab[:], in_=w_in[:, 2048:2056].rearrange("(kc p) n -> p kc n", p=128)),
              [], [wab], "fresh")

        def conv_cols(W, s, nw, off, c0, n, r0=0, nk=KC):
            o = wscr[s, :, 0:nk * nw].rearrange("p (kc n) -> p kc n", n=nw)[:, :, off:off + n]
            i = W[r0:r0 + nk * 128, c0:c0 + n].rearrange("(kc p) n -> p kc n", p=128)
            early = s in (SL_IN + 1, SL_IN + 2, SL_KV, SL_KV + 1, SL_KV + 2, SL_KV + 3)
            conv_list.append((early, o, i))

        P.barrier_keys.add("wc1")
        P.barrier_keys.add("wc2")
        conv_list = []
        DR2 = Tile(None)
        for s in range(4):
            conv_cols(w_in, SL_IN + s, 512, 0, s * 512, 512)
        for s in range(2):
            for q in range(2):
                conv_cols(w_in, SL_IN + 4 + s, 512, q * 128, 2056 + (2 * s + q) * 128, 128)
                conv_cols(w_in, SL_IN + 4 + s, 512, 256 + q * 128, 2056 + 512 + (2 * s + q) * 128, 128)
        for s in range(4):
            conv_cols(w_kv, SL_KV + s, 512, 0, s * 512, 512)
        for s in range(2):
            conv_cols(w_out, SL_OUT + s, 512, 0, s * 512, 512)
            conv_cols(w_q, SL_Q + s, 512, 0, s * 512, 512)
            conv_cols(w_o, SL_O + s, 512, 0, s * 512, 512)
        for s in range(11):
            for q in range(2):
                conv_cols(w_up, SL_UP + s, 512, q * 128, (2 * s + q) * 128, 128)
                conv_cols(w_up, SL_UP + s, 512, 256 + q * 128, DFF + (2 * s + q) * 128, 128)
        for ng in range(4):
            for kh in range(2):
                conv_cols(w_dn, SL_DN + ng * 2 + kh, 256, 0, ng * 256, 256, r0=kh * 1408, nk=11)

        late_list = [c for c in conv_list if not c[0]]

        def issue_late(n):
            for _ in range(min(n, len(late_list))):
                (early, o, i) = late_list.pop(0)
                DR2.bufs[0].w = P.dma("pool", lambda e, o=o, i=i: e.dma_start(out=o, in_=i), [], [], "wc2")

        P.pool(lambda e: e.memset(ones_b[:], 1.0), [], [ones_b])
        P.pool(lambda e: e.memset(ones_f[:], 1.0), [], [ones_f])
        P.pool(lambda e: e.memset(epsc[:], EPS), [], [epsc])
        P.pool(lambda e: e.memset(lnc[:], -0.5 * float(np.log(128.0))), [], [lnc])
        P.pool(lambda e: e.memset(Sf[:], 0.0), [], [Sf])
        P.pool(lambda e: e.memset(Sb[:], 0.0), [], [Sb])
        P.pool(lambda e: e.memset(uhist[:], 0.0), [], [uhist])
        P.pool(lambda e: e.memset(zhist[:], 0.0), [], [zhist])
        for t in zc + cbuf + ubuf:
            P.pool(lambda e, t=t: e.memset(t[:], 0.0), [], [t])
        P.dve(lambda e: e.tensor_copy(out=ident_b[:], in_=ident_f), [cst], [ident_b])
        for h in range(4):
            P.dve(lambda e, h=h: e.tensor_copy(out=identB4[:, h, :], in_=ident_f), [cst], [identB4])
        P.act(lambda e: e.activation(out=expA[:], in_=pcol("alog", 0, 16), func=AF.Exp), [par], [expA])
        for t in (cst, par, ident_b, identB4, ones_b, ones_f, epsc, lnc, expA, wab):
            pass

        for (early, o, i) in [c for c in conv_list if c[0]]:
            DR.bufs[0].w = P.dma("pool", lambda e, o=o, i=i: e.dma_start(out=o, in_=i), [], [], "wc1")

        slot_ctr = [0]

        def load_slab(s):
            sl = slots[slot_ctr[0] % NSLOT]
            key = "slot%d" % (slot_ctr[0] % NSLOT)
            slot_ctr[0] += 1
            drt = DR if s in (SL_IN + 1, SL_IN + 2, SL_KV, SL_KV + 1, SL_KV + 2, SL_KV + 3) else DR2
            P.dma("sp", lambda e: e.dma_start(out=sl[:], in_=wscr[s]), [drt], [sl], key)
            return sl

        def slab_ap(sl, nw, nk=KC):
            return sl[:, 0:nk * nw].rearrange("p (kc n) -> p kc n", n=nw)

        ev_ctr = [0]

        def copy_ev(out_ap, in_ap, r, w, scale=None):
            ev_ctr[0] += 1
            if ev_ctr[0] % 2 == 0 and scale is None:
                P.dve(lambda e: e.tensor_copy(out=out_ap, in_=in_ap), r, w)
            else:
                if scale is None:
                    P.act(lambda e: e.activation(out=out_ap, in_=in_ap, func=AF.Copy), r, w)
                else:
                    P.act(lambda e: e.activation(out=out_ap, in_=in_ap, func=AF.Copy, scale=scale), r, w)

        def rsqrt_from(ps_ap, r, scale, out_t):
            P.act(lambda e: e.activation(out=lnt[:], in_=ps_ap, func=AF.Ln, bias=epsc[:], scale=scale), r + [epsc], [lnt])
            P.act(lambda e: e.activation(out=out_t[:], in_=lnt[:], func=AF.Exp, scale=-0.5), [lnt], [out_t])

        def prenorm(gi):
            st = P.bank()
            for c in range(KC):
                s = sq[c % 2]
                P.act(lambda e, c=c, s=s: e.activation(out=s[:], in_=hT[:, c, :], func=AF.Square), [hT], [s])
                P.pe(lambda e, c=c, s=s: e.matmul(st[:], lhsT=ones_b[:], rhs=s[:], start=(c == 0), stop=(c == KC - 1)),
                     [ones_b, s], [st])
            rsqrt_from(st[:], [st], 1.0 / D, rstd)
            P.free(st)
            for c in range(KC):
                P.dve(lambda e, c=c: e.scalar_tensor_tensor(out=hnT[:, c, :], in0=hT[:, c, :], scalar=pcol("ng", gi * 8 + c),
                                                            in1=rstd[:], op0=ALU.mult, op1=ALU.mult),
                      [hT, par, rstd], [hnT])

        def postnorm_residual(gi, st):
            rsqrt_from(st[:], [st], 1.0 / D, rstd)
            P.free(st)
            for c in range(KC):
                P.dve(lambda e, c=c: e.scalar_tensor_tensor(out=yT[:, c, :], in0=yT[:, c, :], scalar=pcol("ng", gi * 8 + c),
                                                            in1=rstd[:], op0=ALU.mult, op1=ALU.mult),
                      [yT, par, rstd], [yT])
                (P.pool if c % 2 == 0 else P.dve)(lambda e, c=c: e.tensor_tensor(out=hT[:, c, :], in0=hT[:, c, :], in1=yT[:, c, :], op=ALU.add),
                                                        [hT, yT], [hT])

        def proj_to_yT(slab_ids, rhs_list, nk_per=KC):
            st = P.bank()
            m = 0
            for s in slab_ids:
                sl = load_slab(s)
                sa_ = slab_ap(sl, 512)
                for q in range(4):
                    pb = P.bank()
                    for k in range(KC):
                        P.pe(lambda e, k=k, q=q, pb=pb, sa_=sa_: e.matmul(pb[:], lhsT=sa_[:, k, q * 128:(q + 1) * 128],
                                                                          rhs=rhs_list[k][0], start=(k == 0), stop=(k == KC - 1)),
                             [sl, rhs_list[k][1]], [pb])
                    s2 = sq[m % 2]
                    KV = int(os.environ.get("KVAR", "0"))
                    P.dve(lambda e, pb=pb, m=m: e.tensor_copy(out=yT[:, m, :], in_=pb[:]), [pb], [yT])
                    P.free(pb)
                    P.act(lambda e, m=m, s2=s2: e.activation(out=s2[:], in_=yT[:, m, :], func=AF.Square), [yT], [s2])
                    if KV not in (1, 2):
                        P.pe(lambda e, m=m, s2=s2: e.matmul(st[:], lhsT=ones_b[:], rhs=s2[:], start=(m == 0), stop=(m == 7)),
                             [ones_b, s2], [st])
                    m += 1
            return st

        dg_ctr = [0]

        def conv_mm(pb, src, src_t, ntap, wname, wbase):
            for k in range(ntap):
                d = dg[dg_ctr[0] % len(dg)]
                dg_ctr[0] += 1
                P.dve(lambda e, d=d, k=k: e.tensor_scalar(out=d[:], in0=ident_b[:], scalar1=pcol(wname, wbase + k), scalar2=None,
                                                           op0=ALU.mult),
                       [ident_b, par], [d])
                P.pe(lambda e, d=d, k=k: e.matmul(pb[:], lhsT=d[:], rhs=src[:, k:k + NT], start=(k == 0), stop=(k == ntap - 1)),
                     [d, src_t], [pb])

        if STOP > 0:
            for blk in range(2):
                xs = xst[blk % 2]
                P.dma("sp", lambda e, blk=blk, xs=xs: e.dma_start(out=xs[:], in_=memin[blk * 128:(blk + 1) * 128, :]), [], [xs],
                      "xst%d" % (blk % 2))
                for half in range(2):
                    pb = P.bank()
                    for q in range(4):
                        c = half * 4 + q
                        P.pe(lambda e, c=c, q=q, xs=xs, pb=pb: e.transpose(out=pb[:, q * 128:(q + 1) * 128], in_=xs[:, c * 128:(c + 1) * 128],
                                                                           identity=ident_f), [xs, cst], [pb])
                    P.dve(lambda e, half=half, blk=blk, pb=pb: e.tensor_copy(
                        out=yT[:, half * 4:half * 4 + 4, blk * 128:(blk + 1) * 128],
                        in_=pb[:].rearrange("p (q c) -> p q c", q=4)), [pb], [yT])
                    P.free(pb)
            st = P.bank()
            for c in range(KC):
                s = sq[c % 2]
                P.act(lambda e, c=c, s=s: e.activation(out=s[:, 0:MEM], in_=yT[:, c, 0:MEM], func=AF.Square), [yT], [s])
                P.pe(lambda e, c=c, s=s: e.matmul(st[:, 0:MEM], lhsT=ones_b[:], rhs=s[:, 0:MEM], start=(c == 0), stop=(c == KC - 1)),
                     [ones_b, s], [st])
            P.act(lambda e: e.activation(out=lnt[:, 0:MEM], in_=st[:, 0:MEM], func=AF.Ln, bias=epsc[:], scale=1.0 / D), [st, epsc], [lnt])
            P.act(lambda e: e.activation(out=rstd[:, 0:MEM], in_=lnt[:, 0:MEM], func=AF.Exp, scale=-0.5), [lnt], [rstd])
            P.free(st)
            for c in range(KC):
                P.dve(lambda e, c=c: e.scalar_tensor_tensor(out=hnT[:, c, 0:MEM], in0=yT[:, c, 0:MEM], scalar=pcol("mng", c),
                                                            in1=rstd[:, 0:MEM], op0=ALU.mult, op1=ALU.mult),
                      [yT, par, rstd], [hnT])
            for s in range(2):
                sl = load_slab(SL_KV + s)
                sa_ = slab_ap(sl, 512)
                for q in range(4):
                    pb = P.bank()
                    for k in range(KC):
                        P.pe(lambda e, k=k, q=q, pb=pb, sa_=sa_: e.matmul(pb[:, 0:MEM], lhsT=sa_[:, k, q * 128:(q + 1) * 128], rhs=hnT[:, k, 0:MEM],
                                                                          start=(k == 0), stop=(k == KC - 1)), [sl, hnT.cv(k)], [pb])
                    P.dve(lambda e, pb=pb, s=s, q=q: e.tensor_copy(out=KT[:, s * 4 + q, :], in_=pb[:, 0:MEM]), [pb], [KT])
                    P.free(pb)
            for s in range(2):
                sl = load_slab(SL_KV + 2 + s)
                sa_ = slab_ap(sl, 512)
                for mc in range(2):
                    pb = P.bank()
                    for k in range(KC):
                        P.pe(lambda e, k=k, mc=mc, pb=pb, sa_=sa_: e.matmul(pb[:], lhsT=hnT[:, k, mc * 128:(mc + 1) * 128], rhs=sa_[:, k, :],
                                                                            start=(k == 0), stop=(k == KC - 1)), [sl, hnT.cv(k)], [pb])
                    P.dve(lambda e, pb=pb, s=s, mc=mc: e.tensor_copy(out=Vm[:, mc, s * 512:(s + 1) * 512], in_=pb[:]), [pb], [Vm])
                    P.free(pb)

        def bc4(t, blk):
            return t[:, blk, :].unsqueeze(2).to_broadcast([128, 4, 128])

        def v3(b):
            return b[:].rearrange("p (h c) -> p h c", h=4)

        def vb3(b):
            return b[:].bitcast(BF16)[:, 0:512].rearrange("p (h c) -> p h c", h=4)

        def gdn_scalars(full):
            T = tk
            P.dve(lambda e: e.tensor_tensor(out=T["a"][:], in0=abtok[:, :, 0:4], in1=pcol("dtb", 0, 16).rearrange("p (b h) -> p b h", h=4),
                                            op=ALU.add), [abtok, par], [T["a"]])
            P.act(lambda e: e.activation(out=T["t1"][:], in_=T["a"][:], func=AF.Exp), [T["a"]], [T["t1"]])
            P.act(lambda e: e.activation(out=T["t2"][:], in_=T["t1"][:], func=AF.Ln, bias=1.0), [T["t1"]], [T["t2"]])
            P.dve(lambda e: e.scalar_tensor_tensor(out=T["g"][:], in0=T["t2"][:], scalar=-1.0, in1=expA[:].rearrange("p (b h) -> p b h", h=4),
                                                   op0=ALU.mult, op1=ALU.mult), [T["t2"], expA], [T["g"]])
            P.act(lambda e: e.activation(out=T["t1"][:], in_=abtok[:, :, 4:8], func=AF.Exp, scale=-1.0), [abtok], [T["t1"]])
            P.act(lambda e: e.activation(out=T["t2"][:], in_=T["t1"][:], func=AF.Ln, bias=1.0), [T["t1"]], [T["t2"]])
            P.dve(lambda e: e.tensor_scalar(out=T["lb"][:], in0=T["t2"][:], scalar1=-1.0, scalar2=None, op0=ALU.mult), [T["t2"]], [T["lb"]])
            P.act(lambda e: e.activation(out=T["t1"][:], in_=ssq[:, :, 4:8], func=AF.Ln, bias=epsc[:]), [ssq, epsc], [T["t1"]])
            P.dve(lambda e: e.tensor_scalar(out=T["lrk"][:], in0=T["t1"][:], scalar1=-0.5, scalar2=None, op0=ALU.mult), [T["t1"]], [T["lrk"]])
            if full:
                P.act(lambda e: e.activation(out=T["t1"][:], in_=ssq[:, :, 0:4], func=AF.Ln, bias=epsc[:]), [ssq, epsc], [T["t1"]])
                P.dve(lambda e: e.tensor_scalar(out=T["lrq"][:], in0=T["t1"][:], scalar1=-0.5, scalar2=None, op0=ALU.mult),
                      [T["t1"]], [T["lrq"]])
            pb = P.bank()
            g2 = T["g"][:].rearrange("p b h -> p (b h)")
            P.pe(lambda e: e.matmul(pb[:, 0:16], lhsT=utri, rhs=g2, start=True, stop=True), [cst, T["g"]], [pb])
            P.pe(lambda e: e.matmul(pb[:, 16:32], lhsT=ones_f[:], rhs=g2, start=True, stop=True), [ones_f, T["g"]], [pb])
            P.dve(lambda e: e.tensor_copy(out=T["gc"][:].rearrange("p b h -> p (b h)"), in_=pb[:, 0:16]), [pb], [T["gc"]])
            P.dve(lambda e: e.tensor_copy(out=T["gl"][:].rearrange("p b h -> p (b h)"), in_=pb[:, 16:32]), [pb], [T["gl"]])
            P.free(pb)
            P.dve(lambda e: e.tensor_tensor(out=T["bj"][:], in0=T["lrk"][:], in1=T["gc"][:], op=ALU.subtract), [T["lrk"], T["gc"]], [T["bj"]])
            P.dve(lambda e: e.tensor_tensor(out=T["t1"][:], in0=T["gc"][:], in1=T["lb"][:], op=ALU.add), [T["gc"], T["lb"]], [T["t1"]])
            P.dve(lambda e: e.tensor_tensor(out=T["vL"][:], in0=T["t1"][:], in1=T["lrk"][:], op=ALU.add), [T["t1"], T["lrk"]], [T["vL"]])
            P.act(lambda e: e.activation(out=T["b"][:], in_=T["lb"][:], func=AF.Exp), [T["lb"]], [T["b"]])
            P.act(lambda e: e.activation(out=T["sw"][:], in_=T["vL"][:], func=AF.Exp), [T["vL"]], [T["sw"]])
            P.dve(lambda e: e.tensor_tensor(out=T["t2"][:], in0=T["gl"][:], in1=T["bj"][:], op=ALU.add), [T["gl"], T["bj"]], [T["t2"]])
            P.act(lambda e: e.activation(out=T["skd"][:], in_=T["t2"][:], func=AF.Exp), [T["t2"]], [T["skd"]])
            P.act(lambda e: e.activation(out=T["egl"][:], in_=T["gl"][:], func=AF.Exp), [T["gl"]], [T["egl"]])
            if full:
                P.dve(lambda e: e.scalar_tensor_tensor(out=T["vA"][:], in0=T["gc"][:], scalar=lnc[:], in1=T["lrq"][:],
                                                       op0=ALU.add, op1=ALU.add), [T["gc"], lnc, T["lrq"]], [T["vA"]])
                P.act(lambda e: e.activation(out=T["so"][:], in_=T["vA"][:], func=AF.Exp), [T["vA"]], [T["so"]])

        def gdn_pre(blk, full, qT, kT_, vT_):
            T = tk
            B = BS[blk % NSET]
            kdec, bek, bv, Mx, Px, Xx, attnT, usb, wT = (B[n] for n in ("kdec", "bek", "bv", "M", "P", "X", "attnT", "usb", "wT"))
            EL = ELs[blk % 2]
            EA = EAs[blk % 2]
            cs = slice(blk * 128, (blk + 1) * 128)
            pk = P.bank()
            for h in range(4):
                P.pe(lambda e, h=h: e.transpose(out=pk[:].bitcast(BF16)[:, h * 128:(h + 1) * 128], in_=kT_[h][:, cs], identity=ident_b[:]),
                     [kT_[h], ident_b], [pk])
            P.dve(lambda e: e.tensor_tensor(out=kdec[:], in0=vb3(pk), in1=bc4(T["skd"], blk), op=ALU.mult), [pk, T["skd"]], [kdec])
            P.dve(lambda e: e.tensor_tensor(out=bek[:], in0=vb3(pk), in1=bc4(T["sw"], blk), op=ALU.mult), [pk, T["sw"]], [bek])
            P.free(pk)
            yield
            pv = P.bank()
            for h in range(4):
                P.pe(lambda e, h=h: e.transpose(out=pv[:].bitcast(BF16)[:, h * 128:(h + 1) * 128], in_=vT_[h][:, cs], identity=ident_b[:]),
                     [vT_[h], ident_b], [pv])
            P.dve(lambda e: e.tensor_tensor(out=bv[:], in0=vb3(pv), in1=bc4(T["b"], blk), op=ALU.mult), [pv, T["b"]], [bv])
            P.free(pv)
            yield

            def emat(vec, neg, out_t):
                pe_ = P.bank()
                for h in range(4):
                    o = pe_[:, h * 128:(h + 1) * 128]
                    P.pe(lambda e, h=h, o=o: e.matmul(o, lhsT=vec[:, blk, h:h + 1].to_broadcast([128, 128]), rhs=ident_f, start=True, stop=False),
                         [vec, cst], [pe_])
                    P.pe(lambda e, h=h, o=o: e.matmul(o, lhsT=ident_f, rhs=T["bj"][:, blk, h:h + 1].to_broadcast([128, 128]), start=False, stop=False),
                         [T["bj"], cst], [pe_])
                    P.pe(lambda e, h=h, o=o: e.matmul(o, lhsT=ident_f, rhs=neg, start=False, stop=True), [cst], [pe_])
                P.act(lambda e: e.activation(out=out_t[:].rearrange("p h c -> p (h c)"), in_=pe_[:], func=AF.Exp), [pe_], [out_t])
                P.free(pe_)

            emat(T["vL"], negs, EL)
            pl = P.bank()
            for h in range(4):
                P.pe(lambda e, h=h: e.matmul(pl[:, h * 128:(h + 1) * 128], lhsT=kT_[h][:, cs], rhs=kT_[h][:, cs], start=True, stop=True),
                     [kT_[h]], [pl])
            P.dve(lambda e: e.tensor_tensor(out=Mx[:], in0=v3(pl), in1=EL[:], op=ALU.mult), [pl, EL], [Mx])
            P.free(pl)
            yield
            if full:
                emat(T["vA"], negi, EA)
                pa = P.bank()
                for h in range(4):
                    P.pe(lambda e, h=h: e.matmul(pa[:, h * 128:(h + 1) * 128], lhsT=kT_[h][:, cs], rhs=qT[h][:, cs], start=True, stop=True),
                         [kT_[h], qT[h]], [pa])
                P.dve(lambda e: e.tensor_tensor(out=attnT[:], in0=v3(pa), in1=EA[:], op=ALU.mult), [pa, EA], [attnT])
                P.free(pa)
                yield
            pt = P.bank()
            for h in range(4):
                P.pe(lambda e, h=h: e.transpose(out=pt[:].bitcast(BF16)[:, h * 128:(h + 1) * 128], in_=Mx[:, h, :], identity=ident_b[:]),
                     [Mx, ident_b], [pt])
            P.act(lambda e: e.activation(out=Px[:], in_=vb3(pt), func=AF.Copy), [pt], [Px])
            P.free(pt)
            P.dve(lambda e: e.scalar_tensor_tensor(out=Xx[:], in0=Mx[:], scalar=-1.0, in1=identB4[:], op0=ALU.mult, op1=ALU.add),
                  [Mx, identB4], [Xx])
            yield
            Mc, Mo = Mx, B["M2"]
            for k in range(6):
                if k < 5:
                    pm = P.bank()
                    for h in range(4):
                        P.pe(lambda e, h=h, pm=pm, Mc=Mc: e.matmul(pm[:, h * 128:(h + 1) * 128], lhsT=Px[:, h, :], rhs=Mc[:, h, :], start=True, stop=True),
                             [Mc, Px], [pm])
                    yield
                    P.dve(lambda e, pm=pm, Mo=Mo: e.tensor_copy(out=Mo[:].rearrange("p h c -> p (h c)"), in_=pm[:]), [pm], [Mo])
                    P.free(pm)
                pp = P.bank()
                for h in range(4):
                    P.pe(lambda e, h=h, pp=pp, Mc=Mc: e.matmul(pp[:, h * 128:(h + 1) * 128], lhsT=Mc[:, h, :], rhs=Px[:, h, :], start=True, stop=True),
                         [Mc, Px], [pp])
                yield
                P.act(lambda e, pp=pp: e.activation(out=Px[:].rearrange("p h c -> p (h c)"), in_=pp[:], func=AF.Copy), [pp], [Px])
                P.free(pp)
                px = P.bank()
                for h in range(4):
                    P.pe(lambda e, h=h, px=px: e.matmul(px[:, h * 128:(h + 1) * 128], lhsT=Px[:, h, :], rhs=Xx[:, h, :], start=True, stop=True),
                         [Px, Xx], [px])
                yield
                P.dve(lambda e, px=px: e.tensor_tensor(out=Xx[:], in0=v3(px), in1=Xx[:], op=ALU.add), [px, Xx], [Xx])
                P.free(px)
                Mc, Mo = Mo, Mc
            AT = Xx
            pu = P.bank()
            for h in range(4):
                P.pe(lambda e, h=h: e.matmul(pu[:, h * 128:(h + 1) * 128], lhsT=AT[:, h, :], rhs=bv[:, h, :], start=True, stop=True), [AT, bv], [pu])
            P.act(lambda e: e.activation(out=usb[:].rearrange("p h c -> p (h c)"), in_=pu[:], func=AF.Copy), [pu], [usb])
            P.free(pu)
            yield
            pw = P.bank()
            for h in range(4):
                P.pe(lambda e, h=h: e.matmul(pw[:, h * 128:(h + 1) * 128], lhsT=bek[:, h, :], rhs=AT[:, h, :], start=True, stop=True), [AT, bek], [pw])
            P.dve(lambda e: e.tensor_copy(out=wT[:].rearrange("p h c -> p (h c)"), in_=pw[:]), [pw], [wT])
            P.free(pw)
            yield

        def gdn_scan_all(full, qT, sg):
            T = tk
            po = {}

            def main1(blk):
                B = BS[blk % NSET]
                pws = P.bank()
                for h in range(4):
                    P.pe(lambda e, h=h: e.matmul(pws[:, h * 128:(h + 1) * 128], lhsT=B["wT"][:, h, :], rhs=Sb[:, h, :], start=True, stop=True),
                         [B["wT"], Sb], [pws])
                P.dve(lambda e: e.tensor_tensor(out=vnew[:], in0=B["usb"][:], in1=v3(pws), op=ALU.subtract), [B["usb"], pws], [vnew])
                P.free(pws)

            def main2(blk):
                B = BS[blk % NSET]
                cs = slice(blk * 128, (blk + 1) * 128)
                pds = P.bank()
                for h in range(4):
                    P.pe(lambda e, h=h: e.matmul(pds[:, h * 128:(h + 1) * 128], lhsT=B["kdec"][:, h, :], rhs=vnew[:, h, :], start=True, stop=True),
                         [B["kdec"], vnew], [pds])
                if full:
                    po1 = P.bank()
                    po2 = P.bank()
                    for h in range(4):
                        P.pe(lambda e, h=h: e.matmul(po1[:, h * 128:(h + 1) * 128], lhsT=qT[h][:, cs], rhs=Sb[:, h, :], start=True, stop=True),
                             [qT[h], Sb], [po1])
                    for h in range(4):
                        P.pe(lambda e, h=h: e.matmul(po2[:, h * 128:(h + 1) * 128], lhsT=B["attnT"][:, h, :], rhs=vnew[:, h, :], start=True, stop=True),
                             [B["attnT"], vnew], [po2])
                    po[blk] = (po1, po2)
                for h in range(4):
                    P.dve(lambda e, h=h: e.scalar_tensor_tensor(out=Sf[:, h, :], in0=Sf[:, h, :], scalar=T["egl"][:, blk, h:h + 1],
                                                                in1=pds[:, h * 128:(h + 1) * 128], op0=ALU.mult, op1=ALU.add),
                          [Sf, T["egl"], pds], [Sf])
                P.free(pds)
                P.act(lambda e: e.activation(out=Sb[:], in_=Sf[:], func=AF.Copy), [Sf], [Sb])

            def tail1(blk):
                po1, po2 = po[blk]
                P.dve(lambda e: e.tensor_tensor(out=o1s[:], in0=v3(po1), in1=bc4(T["so"], blk), op=ALU.mult), [po1, T["so"]], [o1s])
                P.dve(lambda e: e.tensor_tensor(out=osb[:], in0=v3(po2), in1=o1s[:], op=ALU.add), [po2, o1s], [osb])
                P.free(po1, po2)
                P.pool(lambda e: e.tensor_tensor(out=osq[:], in0=osb[:], in1=osb[:], op=ALU.mult), [osb], [osq])
                P.dve(lambda e: e.tensor_reduce(out=ssqo[:], in_=osq[:], axis=AX.X, op=ALU.add), [osq], [ssqo])
                P.act(lambda e: e.activation(out=rso[:], in_=ssqo[:], func=AF.Ln, bias=epsc[:], scale=1.0 / 128), [ssqo, epsc], [rso])
                P.act(lambda e: e.activation(out=rso[:], in_=rso[:], func=AF.Exp, scale=-0.5), [rso], [rso])
                P.dve(lambda e: e.tensor_tensor(out=onb[:], in0=osb[:], in1=rso[:].unsqueeze(2).to_broadcast([128, 4, 128]), op=ALU.mult),
                      [osb, rso], [onb])

            def tail2(blk):
                cs = slice(blk * 128, (blk + 1) * 128)
                pot = P.bank()
                for h in range(4):
                    P.pe(lambda e, h=h: e.transpose(out=pot[:].bitcast(BF16)[:, h * 128:(h + 1) * 128], in_=onb[:, h, :], identity=ident_b[:]),
                         [onb, ident_b], [pot])
                for h in range(4):
                    P.dve(lambda e, h=h: e.scalar_tensor_tensor(out=ocat[h][:, cs], in0=pot[:].bitcast(BF16)[:, h * 128:(h + 1) * 128],
                                                                scalar=pcol("gng"), in1=sg[h][:, cs], op0=ALU.mult, op1=ALU.mult),
                          [pot, par, sg[h]], [ocat[h]])
                P.free(pot)

            for blk in range(NB):
                main1(blk)
                yield
                if full and blk >= 1:
                    tail1(blk - 1)
                    yield
                main2(blk)
                yield
                if full and blk >= 1:
                    tail2(blk - 1)
                    yield
            if full:
                tail1(NB - 1)
                yield
                tail2(NB - 1)
                yield

        def interleave(gens):
            gens = list(gens)
            while gens:
                for g in list(gens):
                    try:
                        next(g)
                    except StopIteration:
                        gens.remove(g)


        def prenorm_h(gi, c0, c1, hb):
            st = P.bank()
            for c in range(KC):
                s_ = sq[c % 2]
                P.act(lambda e, c=c, s_=s_: e.activation(out=s_[:, c0:c1], in_=hT[:, c, c0:c1], func=AF.Square), [hT.cv(c)], [s_])
                P.pe(lambda e, c=c, s_=s_: e.matmul(st[:, c0:c1], lhsT=ones_b[:], rhs=s_[:, c0:c1], start=(c == 0), stop=(c == KC - 1)),
                     [ones_b, s_], [st])
            P.act(lambda e: e.activation(out=lnt[:, c0:c1], in_=st[:, c0:c1], func=AF.Ln, bias=epsc[:], scale=1.0 / D), [st, epsc], [lnt])
            P.act(lambda e: e.activation(out=rstd[:, c0:c1], in_=lnt[:, c0:c1], func=AF.Exp, scale=-0.5), [lnt], [rstd])
            P.free(st)
            for c in range(KC):
                P.dve(lambda e, c=c: e.scalar_tensor_tensor(out=hnT[:, c, c0:c1], in0=hT[:, c, c0:c1], scalar=pcol("ng", gi * 8 + c),
                                                            in1=rstd[:, c0:c1], op0=ALU.mult, op1=ALU.mult),
                      [hT.cv(c), par, rstd], [hnT.cv(c)])

        def postnorm_h(gi, st, c0, c1, hb):
            P.act(lambda e: e.activation(out=lnt[:, c0:c1], in_=st[:, c0:c1], func=AF.Ln, bias=epsc[:], scale=1.0 / D), [st, epsc], [lnt])
            P.act(lambda e: e.activation(out=rstd[:, c0:c1], in_=lnt[:, c0:c1], func=AF.Exp, scale=-0.5), [lnt], [rstd])
            P.free(st)
            for c in range(KC):
                P.dve(lambda e, c=c: e.scalar_tensor_tensor(out=yT[:, c, c0:c1], in0=yT[:, c, c0:c1], scalar=pcol("ng", gi * 8 + c),
                                                            in1=rstd[:, c0:c1], op0=ALU.mult, op1=ALU.mult),
                      [yT.cv(c), par, rstd], [yT.cv(c)])
                (P.pool if c % 2 == 0 else P.dve)(lambda e, c=c: e.tensor_tensor(out=hT[:, c, c0:c1], in0=hT[:, c, c0:c1], in1=yT[:, c, c0:c1], op=ALU.add),
                                                        [hT.cv(c), yT.cv(c)], [hT.cv(c)])

        def proj_h(slab_ids, rhs_tiles, c0, c1, hb):
            st = P.bank()
            m = 0
            pend = None

            def stat_mm(m_, s2_):
                P.pe(lambda e: e.matmul(st[:, c0:c1], lhsT=ones_b[:], rhs=s2_[:, c0:c1], start=(m_ == 0), stop=(m_ == 7)),
                     [ones_b, s2_], [st])

            for s_ in slab_ids:
                sl = load_slab(s_)
                sa_ = slab_ap(sl, 512)
                for q in range(4):
                    pb = P.bank()
                    for k in range(KC):
                        P.pe(lambda e, k=k, q=q, pb=pb, sa_=sa_: e.matmul(pb[:, c0:c1], lhsT=sa_[:, k, q * 128:(q + 1) * 128],
                                                                          rhs=rhs_tiles[k][:, c0:c1], start=(k == 0), stop=(k == KC - 1)),
                             [sl, rhs_tiles[k]], [pb])
                    if pend is not None:
                        stat_mm(*pend)
                    s2 = sq[m % 2]
                    P.dve(lambda e, pb=pb, m=m: e.tensor_copy(out=yT[:, m, c0:c1], in_=pb[:, c0:c1]), [pb], [yT.cv(m)])
                    P.free(pb)
                    P.act(lambda e, m=m, s2=s2: e.activation(out=s2[:, c0:c1], in_=yT[:, m, c0:c1], func=AF.Square), [yT.cv(m)], [s2])
                    pend = (m, s2)
                    m += 1
            stat_mm(*pend)
            return st

        def xattn_h(c0, c1, hb):
            qx = big[0:8]
            ox = big[8:16]
            m = 0
            for s_ in range(2):
                sl = load_slab(SL_Q + s_)
                sa_ = slab_ap(sl, 512)
                for q in range(4):
                    pb = P.bank()
                    for k in range(KC):
                        P.pe(lambda e, k=k, q=q, pb=pb, sa_=sa_: e.matmul(pb[:, c0:c1], lhsT=sa_[:, k, q * 128:(q + 1) * 128], rhs=hnT[:, k, c0:c1],
                                                                          start=(k == 0), stop=(k == KC - 1)), [sl, hnT.cv(k)], [pb])
                    P.act(lambda e, pb=pb, m=m: e.activation(out=qx[m][:, c0:c1], in_=pb[:, c0:c1], func=AF.Copy, scale=1.0 / 16.0), [pb], [qx[m]])
                    P.free(pb)
                    m += 1
            def xs_scores(h):
                pts = [ubuf[(2 * h) % 4], ubuf[(2 * h + 1) % 4]]
                for mc in range(2):
                    pb = P.bank()
                    for dc in range(2):
                        P.pe(lambda e, dc=dc, mc=mc, pb=pb: e.matmul(pb[:, c0:c1], lhsT=KT[:, 2 * h + dc, mc * 128:(mc + 1) * 128],
                                                                     rhs=qx[2 * h + dc][:, c0:c1], start=(dc == 0), stop=(dc == 1)),
                             [KT, qx[2 * h + dc]], [pb])
                    P.act(lambda e, pb=pb, mc=mc: e.activation(out=pts[mc][:, c0:c1], in_=pb[:, c0:c1], func=AF.Exp), [pb], [pts[mc]])
                    P.free(pb)

            def xs_pv(h):
                pts = [ubuf[(2 * h) % 4], ubuf[(2 * h + 1) % 4]]
                pd = P.bank()
                for mc in range(2):
                    P.pe(lambda e, mc=mc: e.matmul(pd[:, c0:c1], lhsT=ones_b[:], rhs=pts[mc][:, c0:c1], start=(mc == 0), stop=(mc == 1)),
                         [ones_b, pts[mc]], [pd])
                pbs_ = []
                for dvc in range(2):
                    pb = P.bank()
                    for mc in range(2):
                        P.pe(lambda e, mc=mc, dvc=dvc, pb=pb: e.matmul(pb[:, c0:c1], lhsT=Vm[:, mc, (2 * h + dvc) * 128:(2 * h + dvc + 1) * 128],
                                                                       rhs=pts[mc][:, c0:c1], start=(mc == 0), stop=(mc == 1)),
                             [Vm, pts[mc]], [pb])
                    pbs_.append(pb)
                P.act(lambda e: e.activation(out=rden[:, c0:c1], in_=pd[:, c0:c1], func=AF.Ln), [pd], [rden])
                P.act(lambda e: e.activation(out=rden[:, c0:c1], in_=rden[:, c0:c1], func=AF.Exp, scale=-1.0), [rden], [rden])
                P.free(pd)
                for dvc in range(2):
                    pb = pbs_[dvc]
                    P.dve(lambda e, pb=pb, dvc=dvc: e.tensor_tensor(out=ox[2 * h + dvc][:, c0:c1], in0=pb[:, c0:c1], in1=rden[:, c0:c1], op=ALU.mult),
                          [pb, rden], [ox[2 * h + dvc]])
                    P.free(pb)

            for h in range(5):
                if h < 4:
                    xs_scores(h)
                if h >= 1:
                    xs_pv(h - 1)
            return proj_h([SL_O, SL_O + 1], ox, c0, c1, hb)

        def ffn_h(c0, c1, hb, only_up):
            curu = {}
            n = c1 - c0

            def ffn_up_mm(i):
                s_, q = i // 2, i % 2
                if q == 0:
                    curu["sl"] = load_slab(SL_UP + s_)
                sl = curu["sl"]
                sa_ = slab_ap(sl, 512)
                pa_ = P.bank()
                pb_ = P.bank()
                for k in range(KC):
                    P.pe(lambda e, k=k: e.matmul(pa_[:, c0:c1], lhsT=sa_[:, k, q * 128:(q + 1) * 128], rhs=hnT[:, k, c0:c1],
                                                 start=(k == 0), stop=(k == KC - 1)), [sl, hnT.cv(k)], [pa_])
                for k in range(KC):
                    P.pe(lambda e, k=k: e.matmul(pb_[:, c0:c1], lhsT=sa_[:, k, 256 + q * 128:256 + (q + 1) * 128], rhs=hnT[:, k, c0:c1],
                                                 start=(k == 0), stop=(k == KC - 1)), [sl, hnT.cv(k)], [pb_])
                return pa_, pb_

            def ffn_up_ev(i, pa_, pb_):
                ua, ub = ubuf[2 * (i % 2)], ubuf[2 * (i % 2) + 1]
                P.pool(lambda e: e.tensor_copy(out=ua[:, c0:c0 + 2], in_=uhist[:, i, :]), [uhist], [ua])
                P.pool(lambda e: e.tensor_copy(out=ub[:, c0:c0 + 2], in_=uhist[:, 22 + i, :]), [uhist], [ub])
                P.act(lambda e: e.activation(out=ua[:, 2 + c0:2 + c1], in_=pa_[:, c0:c1], func=AF.Copy), [pa_], [ua])
                P.dve(lambda e: e.tensor_copy(out=ub[:, 2 + c0:2 + c1], in_=pb_[:, c0:c1]), [pb_], [ub])
                P.free(pa_, pb_)
                P.pool(lambda e: e.tensor_copy(out=uhist[:, i, :], in_=ua[:, c1:c1 + 2]), [ua], [uhist])
                P.pool(lambda e: e.tensor_copy(out=uhist[:, 22 + i, :], in_=ub[:, c1:c1 + 2]), [ub], [uhist])

            def ffn_conv(i):
                ua, ub = ubuf[2 * (i % 2)], ubuf[2 * (i % 2) + 1]
                ds = []
                for j in range(6):
                    d = dg[dg_ctr[0] % len(dg)]
                    dg_ctr[0] += 1
                    wb = (i * 3 + j) if j < 3 else ((22 + i) * 3 + j - 3)
                    P.dve(lambda e, d=d, wb=wb: e.tensor_scalar(out=d[:], in0=ident_b[:], scalar1=pcol("fcw", wb), scalar2=None, op0=ALU.mult),
                          [ident_b, par], [d])
                    ds.append(d)
                pca = P.bank()
                pcb = P.bank()
                for k in range(3):
                    P.pe(lambda e, k=k: e.matmul(pca[:, c0:c1], lhsT=ds[k][:], rhs=ua[:, c0 + k:c1 + k], start=(k == 0), stop=(k == 2)),
                         [ds[k], ua], [pca])
                for k in range(3):
                    P.pe(lambda e, k=k: e.matmul(pcb[:, c0:c1], lhsT=ds[3 + k][:], rhs=ub[:, c0 + k:c1 + k], start=(k == 0), stop=(k == 2)),
                         [ds[3 + k], ub], [pcb])
                return pca, pcb

            def ffn_act_ev(i, pca, pcb):
                sat = sa[i % 2]
                P.act(lambda e: e.activation(out=sat[:, c0:c1], in_=pca[:, c0:c1], func=AF.Silu, bias=pcol("fcb", i)), [pca, par], [sat])
                P.dve(lambda e: e.scalar_tensor_tensor(out=big[i][:, c0:c1], in0=pcb[:, c0:c1], scalar=pcol("fcb", 22 + i), in1=sat[:, c0:c1],
                                                       op0=ALU.add, op1=ALU.mult), [pcb, par, sat], [big[i]])
                P.free(pca, pcb)

            if only_up:
                for i in range(22):
                    ffn_up_ev(i, *ffn_up_mm(i))
                return None
            for i in range(23):
                if i < 22:
                    bk = ffn_up_mm(i)
                if i >= 1:
                    pc = ffn_conv(i - 1)
                if i < 22:
                    ffn_up_ev(i, *bk)
                if i >= 1:
                    ffn_act_ev(i - 1, *pc)
            st = P.bank()
            pendd = []
            for ng in range(4):
                pbs = [P.bank(), P.bank()]
                for kh in range(2):
                    sl = load_slab(SL_DN + ng * 2 + kh)
                    sa_ = slab_ap(sl, 256, 11)
                    for m2 in range(2):
                        for k in range(11):
                            kk = kh * 11 + k
                            P.pe(lambda e, k=k, kk=kk, m2=m2, pbs=pbs, sa_=sa_: e.matmul(pbs[m2][:, c0:c1], lhsT=sa_[:, k, m2 * 128:(m2 + 1) * 128],
                                                                                        rhs=big[kk][:, c0:c1], start=(kk == 0), stop=(kk == 21)),
                                 [sl, big[kk]], [pbs[m2]])
                for (m_, s2_) in pendd:
                    P.pe(lambda e, m_=m_, s2_=s2_: e.matmul(st[:, c0:c1], lhsT=ones_b[:], rhs=s2_[:, c0:c1], start=(m_ == 0), stop=(m_ == 7)),
                         [ones_b, s2_], [st])
                pendd = []
                for m2 in range(2):
                    m = ng * 2 + m2
                    s2 = sq[m % 2]
                    P.dve(lambda e, m=m, pb=pbs[m2]: e.tensor_copy(out=yT[:, m, c0:c1], in_=pb[:, c0:c1]), [pbs[m2]], [yT.cv(m)])
                    P.act(lambda e, s2=s2, m=m: e.activation(out=s2[:, c0:c1], in_=yT[:, m, c0:c1], func=AF.Square), [yT.cv(m)], [s2])
                    pendd.append((m, s2))
                P.free(*pbs)
            for (m_, s2_) in pendd:
                P.pe(lambda e, m_=m_, s2_=s2_: e.matmul(st[:, c0:c1], lhsT=ones_b[:], rhs=s2_[:, c0:c1], start=(m_ == 0), stop=(m_ == 7)),
                     [ones_b, s2_], [st])
            return st

        def store_h(c0, c1, hb, out_idx):
            for blk in range(c0 // 128, c1 // 128):
                os_ = ost[blk % 2]
                for half in range(2):
                    pb = P.bank()
                    for q in range(4):
                        c = half * 4 + q
                        P.pe(lambda e, c=c, q=q, blk=blk, pb=pb: e.transpose(out=pb[:, q * 128:(q + 1) * 128], in_=hT[:, c, blk * 128:(blk + 1) * 128],
                                                                             identity=ident_f), [hT.cv(c), cst], [pb])
                    copy_ev(os_[:, half * 512:(half + 1) * 512], pb[:], [pb], [os_])
                    P.free(pb)
                r0 = out_idx * NT + blk * 128
                P.dma("sp", lambda e, os_=os_, r0=r0: e.dma_start(out=yout[r0:r0 + 128, :], in_=os_[:]), [os_], [], "xst%d" % (blk % 2))

        def back_schedule(halves, out_idx):
            halo = out_idx is None
            if len(halves) == 1:
                (c0, c1, hb) = halves[0]
                st = proj_h([SL_OUT, SL_OUT + 1], ocat, c0, c1, hb)
                postnorm_h(1, st, c0, c1, hb)
                prenorm_h(2, c0, c1, hb)
                st = xattn_h(c0, c1, hb)
                postnorm_h(3, st, c0, c1, hb)
                prenorm_h(4, c0, c1, hb)
                st = ffn_h(c0, c1, hb, halo)
                if not halo:
                    postnorm_h(5, st, c0, c1, hb)
                    store_h(c0, c1, hb, out_idx)
            else:
                A, B = halves
                stA = proj_h([SL_OUT, SL_OUT + 1], ocat, *A)
                postnorm_h(1, stA, *A)
                stB = proj_h([SL_OUT, SL_OUT + 1], ocat, *B)
                prenorm_h(2, *A)
                postnorm_h(1, stB, *B)
                ckpt(7)
                stA = xattn_h(*A)
                prenorm_h(2, *B)
                postnorm_h(3, stA, *A)
                stB = xattn_h(*B)
                prenorm_h(4, *A)
                postnorm_h(3, stB, *B)
                ckpt(8)
                stA = ffn_h(A[0], A[1], A[2], False)
                prenorm_h(4, *B)
                postnorm_h(5, stA, *A)
                stB = ffn_h(B[0], B[1], B[2], False)
                store_h(A[0], A[1], A[2], out_idx)
                postnorm_h(5, stB, *B)
                ckpt(9)
                store_h(B[0], B[1], B[2], out_idx)
            if halo:
                P.pool(lambda e: e.tensor_scalar(out=uhist[:], in0=uhist[:], scalar1=pcol("flag"), scalar2=None, op0=ALU.mult), [uhist, par], [uhist])

        def tile(ti, full, out_idx):
            tok0 = ti * NT
            for blk in range(NB):
                xs = xst[blk % 2]
                P.dma("sp", lambda e, blk=blk, xs=xs: e.dma_start(out=xs[:], in_=xin[tok0 + blk * 128: tok0 + (blk + 1) * 128, :]),
                      [], [xs], "xst%d" % (blk % 2))
                for half in range(2):
                    pb = P.bank()
                    for q in range(4):
                        c = half * 4 + q
                        P.pe(lambda e, c=c, q=q, xs=xs, pb=pb: e.transpose(out=pb[:, q * 128:(q + 1) * 128], in_=xs[:, c * 128:(c + 1) * 128],
                                                                           identity=ident_f), [xs, cst], [pb])
                    copy_ev(hT[:, half * 4:half * 4 + 4, blk * 128:(blk + 1) * 128], pb[:].rearrange("p (q c) -> p q c", q=4), [pb], [hT])
                    P.free(pb)
            ckpt(1)
            prenorm_h(0, 0, NT, None)
            ckpt(2)
            qs, ks, vs, sgs = big[0:4], big[4:8], big[8:12], big[12:16]
            chunks = []
            for s_, dst in ([(0, qs)] if full else []) + [(1, ks), (2, vs)]:
                for q in range(4):
                    chunks.append((s_, q, dst))
            cur = {}

            def qkv_proj_mm(s_, q):
                if q == 0:
                    cur["sl"] = load_slab(SL_IN + s_)
                sl = cur["sl"]
                sa_ = slab_ap(sl, 512)
                pb = P.bank()
                for k in range(KC):
                    P.pe(lambda e, k=k: e.matmul(pb[:], lhsT=sa_[:, k, q * 128:(q + 1) * 128], rhs=hnT[:, k, :],
                                                 start=(k == 0), stop=(k == KC - 1)), [sl, hnT.cv(k)], [pb])
                return pb

            def qkv_proj_ev(s_, q, pb):
                ch = s_ * 4 + q
                z = zc[ch % 4]
                P.pool(lambda e: e.tensor_copy(out=z[:, 0:3], in_=zhist[:, ch, :]), [zhist], [z])
                copy_ev(z[:, 3:3 + NT], pb[:], [pb], [z])
                P.free(pb)
                P.pool(lambda e: e.tensor_copy(out=zhist[:, ch, :], in_=z[:, NT:NT + 3]), [z], [zhist])

            def qkv_conv_mm(s_, q, dst):
                ch = s_ * 4 + q
                z = zc[ch % 4]
                pc = P.bank()
                conv_mm(pc, z, z, 4, "gcw", ch * 4)
                return pc

            def qkv_conv_ev(q, dst, pc):
                P.act(lambda e: e.activation(out=dst[q][:], in_=pc[:], func=AF.Silu), [pc], [dst[q]])
                P.free(pc)

            for i in range(len(chunks) + 1):
                if i < len(chunks):
                    pbq = qkv_proj_mm(chunks[i][0], chunks[i][1])
                if i >= 1:
                    pcq = qkv_conv_mm(*chunks[i - 1])
                if i < len(chunks):
                    qkv_proj_ev(chunks[i][0], chunks[i][1], pbq)
                if i >= 1:
                    qkv_conv_ev(chunks[i - 1][1], chunks[i - 1][2], pcq)
            pab = P.bank()
            for blk in range(NB):
                for k in range(KC):
                    P.pe(lambda e, k=k, blk=blk: e.matmul(pab[:, blk * 8:(blk + 1) * 8], lhsT=hnT[:, k, blk * 128:(blk + 1) * 128], rhs=wab[:, k, :],
                                                          start=(k == 0), stop=(k == KC - 1)), [hnT, wab], [pab])
            P.dve(lambda e: e.tensor_copy(out=abtok[:].rearrange("p b c -> p (b c)"), in_=pab[:, 0:NB * 8]), [pab], [abtok])
            P.free(pab)
            pss = P.bank()
            lst = ([(qs[h], h) for h in range(4)] if full else []) + [(ks[h], 4 + h) for h in range(4)]
            for i, (src, col) in enumerate(lst):
                s2 = sq[i % 2]
                P.pool(lambda e, src=src, s2=s2: e.tensor_tensor(out=s2[:], in0=src[:], in1=src[:], op=ALU.mult), [src], [s2])
                for blk in range(NB):
                    P.pe(lambda e, blk=blk, s2=s2, col=col: e.matmul(pss[:, blk * 8 + col:blk * 8 + col + 1], lhsT=s2[:, blk * 128:(blk + 1) * 128],
                                                                      rhs=ones_b[:, 0:1], start=True, stop=True), [s2, ones_b], [pss])
            if full:
                P.dve(lambda e: e.tensor_copy(out=ssq[:].rearrange("p b c -> p (b c)"), in_=pss[:, 0:NB * 8]), [pss], [ssq])
            else:
                P.dve(lambda e: e.tensor_copy(out=ssq[:, :, 4:8], in_=pss[:, 0:NB * 8].rearrange("p (b c) -> p b c", c=8)[:, :, 4:8]), [pss], [ssq])
            P.free(pss)
            ckpt(3)
            gdn_scalars(full)
            ckpt(4)

            def filler():
                sl = load_slab(SL_IN + 3)
                sa_ = slab_ap(sl, 512)
                for q in range(4):
                    pb = P.bank()
                    for k in range(KC):
                        P.pe(lambda e, k=k, q=q, pb=pb, sa_=sa_: e.matmul(pb[:], lhsT=sa_[:, k, q * 128:(q + 1) * 128], rhs=hnT[:, k, :],
                                                                          start=(k == 0), stop=(k == KC - 1)), [sl, hnT.cv(k)], [pb])
                        if k % 4 == 3:
                            yield
                    P.act(lambda e, pb=pb, q=q: e.activation(out=sgs[q][:], in_=pb[:], func=AF.Silu), [pb], [sgs[q]])
                    P.free(pb)
                for s_ in range(2):
                    sl = load_slab(SL_IN + 4 + s_)
                    sa_ = slab_ap(sl, 512)
                    for q in range(2):
                        ch = 2 * s_ + q
                        pa_ = P.bank()
                        pb_ = P.bank()
                        for k in range(KC):
                            P.pe(lambda e, k=k, q=q, pa_=pa_, sa_=sa_: e.matmul(pa_[:], lhsT=sa_[:, k, q * 128:(q + 1) * 128], rhs=hnT[:, k, :],
                                                                                start=(k == 0), stop=(k == KC - 1)), [sl, hnT.cv(k)], [pa_])
                            if k % 4 == 3:
                                yield
                        for k in range(KC):
                            P.pe(lambda e, k=k, q=q, pb_=pb_, sa_=sa_: e.matmul(pb_[:], lhsT=sa_[:, k, 256 + q * 128:256 + (q + 1) * 128], rhs=hnT[:, k, :],
                                                                                start=(k == 0), stop=(k == KC - 1)), [sl, hnT.cv(k)], [pb_])
                            if k % 4 == 3:
                                yield
                        cb = cbuf[ch]
                        P.pool(lambda e, cb=cb: e.tensor_copy(out=cb[:, 0:30], in_=cb[:, NT:NT + 30]), [cb], [cb])
                        P.act(lambda e, pb_=pb_: e.activation(out=lnt[:], in_=pb_[:], func=AF.Sigmoid), [pb_], [lnt])
                        P.dve(lambda e, pa_=pa_, cb=cb: e.tensor_tensor(out=cb[:, 30:30 + NT], in0=pa_[:], in1=lnt[:], op=ALU.mult), [pa_, lnt], [cb])
                        P.free(pa_, pb_)
                pmean = P.bank()
                pe2 = P.bank()
                for ch in range(4):
                    pc = P.bank()
                    for k in range(31):
                        d = dg[dg_ctr[0] % len(dg)]
                        dg_ctr[0] += 1
                        P.dve(lambda e, d=d, k=k, ch=ch: e.tensor_scalar(out=d[:], in0=ident_b[:], scalar1=pcol("cdw", ch * 31 + k), scalar2=None,
                                                                         op0=ALU.mult), [ident_b, par], [d])
                        P.pe(lambda e, d=d, k=k, ch=ch, pc=pc: e.matmul(pc[:], lhsT=d[:], rhs=cbuf[ch][:, k:k + NT], start=(k == 0), stop=(k == 30)),
                             [d, cbuf[ch]], [pc])
                        if k % 4 == 3:
                            yield
                    P.act(lambda e, pc=pc, ch=ch: e.activation(out=yT[:, ch, :], in_=pc[:], func=AF.Identity, bias=pcol("cdb", ch)), [pc, par], [yT])
                    s2 = sq[ch % 2]
                    P.act(lambda e, pc=pc, ch=ch, s2=s2: e.activation(out=s2[:], in_=pc[:], func=AF.Square, bias=pcol("cdb", ch)), [pc, par], [s2])
                    P.free(pc)
                    P.pe(lambda e, ch=ch: e.matmul(pmean[:], lhsT=ones_f[:], rhs=yT[:, ch, :], start=(ch == 0), stop=(ch == 3)), [ones_f, yT], [pmean])
                    P.pe(lambda e, ch=ch, s2=s2: e.matmul(pe2[:], lhsT=ones_b[:], rhs=s2[:], start=(ch == 0), stop=(ch == 3)), [ones_b, s2], [pe2])
                    yield
                P.act(lambda e: e.activation(out=msb[:], in_=pmean[:], func=AF.Copy, scale=1.0 / 512), [pmean], [msb])
                P.pool(lambda e: e.tensor_tensor(out=var[:], in0=msb[:], in1=msb[:], op=ALU.mult), [msb], [var])
                P.dve(lambda e: e.scalar_tensor_tensor(out=var[:], in0=pe2[:], scalar=1.0 / 512, in1=var[:], op0=ALU.mult, op1=ALU.subtract),
                      [pe2, var], [var])
                P.free(pmean, pe2)
                P.act(lambda e: e.activation(out=lnt[:], in_=var[:], func=AF.Ln, bias=epsc[:]), [var, epsc], [lnt])
                P.act(lambda e: e.activation(out=rstd[:], in_=lnt[:], func=AF.Exp, scale=-0.5), [lnt], [rstd])
                yield
                for ch in range(4):
                    P.pool(lambda e, ch=ch: e.tensor_tensor(out=yT[:, ch, :], in0=yT[:, ch, :], in1=msb[:], op=ALU.subtract), [yT, msb], [yT])
                    P.dve(lambda e, ch=ch: e.tensor_tensor(out=yT[:, ch, :], in0=yT[:, ch, :], in1=rstd[:], op=ALU.mult), [yT, rstd], [yT])
                    P.act(lambda e, ch=ch: e.activation(out=ocat[4 + ch][:], in_=yT[:, ch, :], func=AF.Silu, bias=pcol("clb", ch), scale=pcol("clg", ch)),
                          [yT, par], [ocat[4 + ch]])
                    yield

            fil = filler() if full else None

            def run_with_filler(gens, nfill, every):
                nonlocal fil
                gens = list(gens)
                rnd = 0
                while gens:
                    rnd += 1
                    if rnd % every == 0:
                        for _ in range(nfill):
                            if fil is not None:
                                try:
                                    next(fil)
                                except StopIteration:
                                    fil = None
                    for g in list(gens):
                        try:
                            next(g)
                        except StopIteration:
                            gens.remove(g)

            run_with_filler([gdn_pre(blk, full, qs, ks, vs) for blk in range(NB)], 1, 2)
            ckpt(5)
            run_with_filler([gdn_scan_all(full, qs, sgs)], 3, 1)
            if fil is not None:
                for _ in fil:
                    pass
            if not full:
                return
            ckpt(6)
            if out_idx is None:
                halves = [(NT - 128, NT, None)]
            else:
                halves = [(0, NT, None)]
            back_schedule(halves, out_idx)

        try:
            ti = 0
            per = (len(late_list) + max(NS, 1) - 1) // max(NS, 1)
            for _ in range(NS):
                tile(ti, False, None)
                issue_late(per)
                ti += 1
            issue_late(len(late_list))
            tile(ti, True, None)
            ti += 1
            for i in range(NF):
                tile(ti, True, i)
                ti += 1
            P.final = ["xst0", "xst1"]
        except StopBuild:
            P.final = [k for k in P.dma_cnt.keys()]
        P.emit()
    return nc


def make_consts():
    c = np.zeros((128, 4, 128), np.float32)
    c[:, 0, :] = np.eye(128, dtype=np.float32)
    j = np.arange(128)[:, None]
    cc = np.arange(128)[None, :]
    c[:, 1, :] = (j <= cc).astype(np.float32)
    c[:, 2, :] = np.where(cc >= j, 0.0, NEGV)
    c[:, 3, :] = np.where(cc > j, 0.0, NEGV)
    return c.reshape(128, 512)


def make_params(inp, flag):
    p = np.zeros((128, NPAR), np.float32)

    def put(name, arr):
        p[:, PO[name]:PO[name] + arr.shape[1]] = arr

    def chunked(v):
        return np.ascontiguousarray(v.reshape(-1, 128).T)

    ng = inp["norm_g"][0]
    put("ng", np.concatenate([chunked(ng[i]) for i in range(6)], axis=1))
    gcw = inp["gdn_conv_w"][0]
    put("gcw", np.ascontiguousarray(gcw.reshape(4, 12, 128).transpose(2, 1, 0)).reshape(128, 48))
    cdw = inp["cfm_dw_w"][0]
    put("cdw", np.ascontiguousarray(cdw.reshape(31, 4, 128).transpose(2, 1, 0)).reshape(128, 124))
    put("cdb", chunked(inp["cfm_dw_b"][0]))
    put("clg", chunked(inp["cfm_ln_g"][0]))
    put("clb", chunked(inp["cfm_ln_b"][0]))
    fcw = inp["ffn_conv_w"][0]
    put("fcw", np.ascontiguousarray(fcw.reshape(3, 44, 128).transpose(2, 1, 0)).reshape(128, 132))
    put("fcb", chunked(inp["ffn_conv_b"][0]))
    put("gng", inp["gdn_norm_g"][0].reshape(128, 1))
    put("alog", np.tile(inp["gdn_a_log"][0][None, :], (128, 4)))
    put("dtb", np.tile(inp["gdn_dt_bias"][0][None, :], (128, 4)))
    put("mng", chunked(inp["mem_norm_g"][0]))
    p[:, PO["flag"]] = flag
    return p


_NC_CACHE = {}


def run(inputs, NF, NS, n_batch, dbg=False):
    inp = {k: np.asarray(v, dtype=np.float32) for k, v in inputs.items()}
    key = (NF, NS, dbg)
    if key not in _NC_CACHE:
        _NC_CACHE[key] = build(NF, NS, dbg)
    nc = _NC_CACHE[key]
    half = NF * NT
    nprev = (NS + 1) * NT
    consts = make_consts()
    in_maps = []
    for b in range(n_batch):
        for j in range(2):
            xall = np.zeros((nprev + half, D), np.float32)
            if j == 1:
                xall[:nprev] = inp["x"][b, 0:half]
            xall[nprev:] = inp["x"][b, j * half:(j + 1) * half]
            in_maps.append({
                "xin": xall, "memin": np.ascontiguousarray(inp["mem"][b]),
                "params": make_params(inp, float(j)), "consts": consts,
                "w_in": np.ascontiguousarray(inp["w_in"][0]), "w_out": np.ascontiguousarray(inp["w_out"][0]),
                "w_q": np.ascontiguousarray(inp["xa_w_q"][0]), "w_kv": np.ascontiguousarray(inp["xa_w_kv"][0]),
                "w_o": np.ascontiguousarray(inp["xa_w_o"][0]), "w_up": np.ascontiguousarray(inp["ffn_w_up"][0]),
                "w_dn": np.ascontiguousarray(inp["ffn_w_down"][0]),
            })
    res = run_bass_kernel_spmd(nc, in_maps, core_ids=list(range(len(in_maps))))
    out = np.zeros((n_batch, 2 * half, D), np.float32)
    for b in range(n_batch):
        for j in range(2):
            out[b, j * half:(j + 1) * half] = res.results[2 * b + j]["yout"]
    return out, res


def kernel(**inputs):
    out, _ = run(inputs, NF=8, NS=7, n_batch=4)
    return out
```

```python
import numpy as np
from contextlib import ExitStack
import concourse.bass as bass
import concourse.mybir as mybir
from concourse.bass_utils import run_bass_kernel_spmd

F32 = mybir.dt.float32
BF16 = mybir.dt.bfloat16
AF = mybir.ActivationFunctionType
ALU = mybir.AluOpType
AX = mybir.AxisListType

D = 1024
NT = 512
NB = NT // 128
H = 4
DFF = 2816
MEM = 256
EPS = 1e-6
NEGV = -30000.0
KC = 8

PO = {}
_o = 0
for _n, _w in [("ng", 48), ("gcw", 48), ("cdw", 124), ("cdb", 4), ("clg", 4), ("clb", 4),
               ("fcw", 132), ("fcb", 44), ("gng", 1), ("alog", 16), ("dtb", 16), ("mng", 8),
               ("flag", 1)]:
    PO[_n] = _o
    _o += _w
NPAR = _o

SL_IN, SL_OUT, SL_Q, SL_KV, SL_O, SL_UP, SL_DN = 0, 6, 8, 10, 14, 16, 27
NSLAB = 35


import os
STOP = float(os.environ.get("KSTOP", "1000"))


class StopBuild(Exception):
    pass


def ckpt(n):
    if n >= STOP:
        raise StopBuild()


class Buf:
    __slots__ = ("w", "rs", "const")

    def __init__(self):
        self.w = None
        self.rs = []
        self.const = False


class Tile:
    def __init__(self, t, bufs=None):
        self.t = t
        self.bufs = bufs if bufs is not None else [Buf()]
        self.h = None

    def split(self):
        self.bufs = [Buf(), Buf()]
        self.h = [Tile(self.t, [self.bufs[0]]), Tile(self.t, [self.bufs[1]])]
        return self

    def split_chunks(self, n):
        self.bufs = [Buf() for _ in range(n)]
        self.c = [Tile(self.t, [b]) for b in self.bufs]
        return self

    def cv(self, k):
        return self.c[k]

    def __getitem__(self, idx):
        return self.t[idx]


class Ins:
    __slots__ = ("eng", "fn", "idx", "dma_key", "dma_val", "deps", "needs_inc", "ordinal", "waits")

    def __init__(self, eng, fn, idx, dma_key):
        self.eng = eng
        self.fn = fn
        self.idx = idx
        self.dma_key = dma_key
        self.dma_val = None
        self.deps = []
        self.needs_inc = False
        self.ordinal = 0
        self.waits = []


class Prog:
    ENGS = ["pe", "act", "dve", "pool", "sp"]

    def __init__(self, nc, es):
        self.nc = nc
        self.es = es
        self.ins = {e: [] for e in self.ENGS}
        self.dma_cnt = {}
        self.dma_sem = {}
        self.barrier_keys = set()
        self.sem = {}
        self.nfresh = 0
        self.free_banks = []
        self.final = []

    def sb(self, name, shape, dt):
        return Tile(self.es.enter_context(self.nc.sbuf_tensor(name, list(shape), dt)))

    def mkbanks(self):
        for i in range(8):
            t = Tile(self.es.enter_context(self.nc.psum_tensor("bank%d" % i, [128, 512], F32)))
            self.free_banks.append(t)

    def bank(self):
        assert self.free_banks, "out of PSUM banks"
        return self.free_banks.pop(0)

    def free(self, *bs):
        for b in bs:
            self.free_banks.append(b)

    def _add(self, eng, fn, r, w, dma_key=None):
        ins = Ins(eng, fn, len(self.ins[eng]), dma_key)
        deps = {}
        rb = [b for t in r for b in t.bufs]
        wb = [b for t in w for b in t.bufs]
        for b in rb:
            if b.w is not None:
                deps[id(b.w)] = (b.w, True)
        for b in wb:
            if b.w is not None and id(b.w) not in deps:
                deps[id(b.w)] = (b.w, False)
            for x in b.rs:
                if id(x) not in deps:
                    deps[id(x)] = (x, False)
        for b in rb:
            if not b.const:
                b.rs.append(ins)
        for b in wb:
            b.w = ins
            b.rs = []
        ins.deps = [v for k, v in deps.items() if v[0] is not ins]
        self.ins[eng].append(ins)
        if dma_key is not None:
            if dma_key == "fresh":
                dma_key = "fresh%d" % self.nfresh
                self.nfresh += 1
                ins.dma_key = dma_key
            self.dma_cnt[dma_key] = self.dma_cnt.get(dma_key, 0) + 16
            ins.dma_val = self.dma_cnt[dma_key]
        return ins

    def pe(self, fn, r, w):
        return self._add("pe", fn, r, w)

    def act(self, fn, r, w):
        return self._add("act", fn, r, w)

    def dve(self, fn, r, w):
        return self._add("dve", fn, r, w)

    def pool(self, fn, r, w):
        return self._add("pool", fn, r, w)

    def dma(self, q, fn, r, w, key):
        return self._add(q, fn, r, w, dma_key=key)

    def resolve(self):
        for e in self.ENGS:
            for ins in self.ins[e]:
                for (y, raw) in ins.deps:
                    if y.dma_key is not None:
                        continue
                    if y.eng == e:
                        if e in ("act", "dve", "pool") and raw:
                            y.needs_inc = True
                        continue
                    y.needs_inc = True
        for e in self.ENGS:
            c = 0
            for ins in self.ins[e]:
                if ins.dma_key is None and ins.needs_inc:
                    c += 1
                    ins.ordinal = c
        for k in list(self.dma_cnt.keys()):
            self.dma_sem[k] = self.es.enter_context(self.nc.semaphore("d_" + k))
        for e in self.ENGS:
            self.sem[e] = self.es.enter_context(self.nc.semaphore("e_" + e))
        for e in self.ENGS:
            waited = {}
            for ins in self.ins[e]:
                ws = {}
                for (y, raw) in ins.deps:
                    if y.dma_key is not None:
                        k = ("d", y.dma_key)
                        v = self.dma_cnt[y.dma_key] if y.dma_key in self.barrier_keys else y.dma_val
                    else:
                        if y.eng == e and not (e in ("act", "dve", "pool") and raw):
                            continue
                        k = ("e", y.eng)
                        v = y.ordinal
                    if waited.get(k, 0) >= v:
                        continue
                    if ws.get(k, 0) < v:
                        ws[k] = v
                for k, v in ws.items():
                    waited[k] = v
                    sem = self.dma_sem[k[1]] if k[0] == "d" else self.sem[k[1]]
                    ins.waits.append((sem, v))

    def emit(self):
        self.resolve()
        nc = self.nc
        prog = self

        def run(e, eng):
            for ins in prog.ins[e]:
                for (sem, v) in ins.waits:
                    eng.wait_ge(sem, v)
                bi = ins.fn(eng)
                if ins.dma_key is not None:
                    bi.then_inc(prog.dma_sem[ins.dma_key], 16)
                elif ins.needs_inc:
                    bi.then_inc(prog.sem[e], 1)
            if e == "sp":
                for k in prog.final:
                    eng.wait_ge(prog.dma_sem[k], prog.dma_cnt[k])

        with nc.Block() as block:
            @block.tensor
            def _(eng):
                run("pe", eng)

            @block.scalar
            def _(eng):
                run("act", eng)

            @block.vector
            def _(eng):
                run("dve", eng)

            @block.gpsimd
            def _(eng):
                run("pool", eng)

            @block.sync
            def _(eng):
                run("sp", eng)


def build(NF, NS, dbg=False):
    NTOK = (NS + 1 + NF) * NT
    nc = bass.Bass("TRN2", target_bir_lowering=False)
    xin = nc.dram_tensor("xin", [NTOK, D], F32, kind="ExternalInput").ap()
    memin = nc.dram_tensor("memin", [MEM, D], F32, kind="ExternalInput").ap()
    params = nc.dram_tensor("params", [128, NPAR], F32, kind="ExternalInput").ap()
    consts = nc.dram_tensor("consts", [128, 4 * 128], F32, kind="ExternalInput").ap()
    w_in = nc.dram_tensor("w_in", [D, 3080], F32, kind="ExternalInput").ap()
    w_out = nc.dram_tensor("w_out", [D, D], F32, kind="ExternalInput").ap()
    w_q = nc.dram_tensor("w_q", [D, D], F32, kind="ExternalInput").ap()
    w_kv = nc.dram_tensor("w_kv", [D, 2 * D], F32, kind="ExternalInput").ap()
    w_o = nc.dram_tensor("w_o", [D, D], F32, kind="ExternalInput").ap()
    w_up = nc.dram_tensor("w_up", [D, 2 * DFF], F32, kind="ExternalInput").ap()
    w_dn = nc.dram_tensor("w_dn", [DFF, D], F32, kind="ExternalInput").ap()
    yout = nc.dram_tensor("yout", [NF * NT, D], F32, kind="ExternalOutput").ap()
    wscr = nc.dram_tensor("wscr", [NSLAB, 128, 4096], BF16, kind="Internal").ap()
    dbg_out = None
    if dbg:
        dbg_out = nc.dram_tensor("dbg", [128, 16, NT], F32, kind="ExternalOutput").ap()

    es = ExitStack()
    with es:
        P = Prog(nc, es)
        P.mkbanks()
        DR = Tile(None)

        cst = P.sb("cst", [128, 4, 128], F32)
        par = P.sb("par", [128, NPAR], F32)
        ident_b = P.sb("ident_b", [128, 128], BF16)
        identB4 = P.sb("identB4", [128, 4, 128], BF16)
        ones_b = P.sb("ones_b", [128, 128], BF16)
        ones_f = P.sb("ones_f", [128, 128], F32)
        epsc = P.sb("epsc", [128, 1], F32)
        lnc = P.sb("lnc", [128, 1], F32)
        expA = P.sb("expA", [128, 16], F32)
        wab = P.sb("wab", [128, KC, 8], BF16)
        NSLOT = 3
        slots = [P.sb("slot%d" % i, [128, 4096], BF16) for i in range(NSLOT)]
        KT = P.sb("KT", [128, 8, MEM], BF16)
        Vm = P.sb("Vm", [128, 2, D], BF16)
        hT = P.sb("hT", [128, KC, NT], F32)
        hnT = P.sb("hnT", [128, KC, NT], BF16)
        yT = P.sb("yT", [128, KC, NT], F32)
        for t_ in [hT, hnT, yT]:
            t_.split_chunks(KC)
        xst = [P.sb("xst%d" % i, [128, D], F32) for i in range(2)]
        big = [P.sb("big%d" % i, [128, NT], BF16) for i in range(22)]
        zc = [P.sb("zc%d" % i, [128, 3 + NT], BF16) for i in range(4)]
        zhist = P.sb("zhist", [128, 12, 3], BF16)
        cbuf = [P.sb("cbuf%d" % i, [128, 30 + NT], BF16) for i in range(4)]
        ubuf = [P.sb("ubuf%d" % i, [128, 2 + NT], BF16) for i in range(4)]
        uhist = P.sb("uhist", [128, 44, 2], BF16)
        ocat = [P.sb("ocat%d" % i, [128, NT], BF16) for i in range(8)]
        sq = [P.sb("sq%d" % i, [128, NT], BF16) for i in range(2)]
        rstd = P.sb("rstd", [128, NT], F32)
        lnt = P.sb("lnt", [128, NT], F32)
        dg = [P.sb("dg%d" % i, [128, 128], BF16) for i in range(12)]
        rden = P.sb("rden", [128, NT], F32)
        msb = P.sb("msb", [128, NT], F32)
        var = rden
        sa = [P.sb("sa%d" % i, [128, NT], BF16) for i in range(2)]
        abtok = P.sb("abtok", [128, NB, 8], F32)
        ssq = P.sb("ssq", [128, NB, 8], F32)
        tk = {n: P.sb("tk_" + n, [128, NB, 4], F32) for n in
              ["a", "g", "lb", "lrq", "lrk", "gc", "gl", "vL", "vA", "bj", "b", "sw", "skd", "egl", "so", "t1", "t2"]}
        Sf = P.sb("Sf", [128, 4, 128], F32)
        Sb = P.sb("Sb", [128, 4, 128], BF16)
        NSET = 4
        BS = []
        for i in range(NSET):
            BS.append({n: P.sb("%s_%d" % (n, i), [128, 4, 128], dt) for n, dt in
                       [("kdec", BF16), ("bek", BF16), ("bv", BF16), ("M", BF16), ("M2", BF16), ("P", BF16), ("X", BF16),
                        ("attnT", BF16), ("usb", F32), ("wT", BF16)]})
        ELs = [P.sb("EL%d" % i, [128, 4, 128], BF16) for i in range(2)]
        EAs = [P.sb("EA%d" % i, [128, 4, 128], BF16) for i in range(2)]
        vnew = P.sb("vnew", [128, 4, 128], BF16)
        osb = P.sb("osb", [128, 4, 128], F32)
        scr4 = P.sb("scr4", [128, 4, 128], F32)
        o1s = scr4
        osq = scr4
        Stmp = P.sb("Stmp", [128, 4, 128], F32)
        ssqo = P.sb("ssqo", [128, 4], F32)
        rso = P.sb("rso", [128, 4], F32)
        onb = P.sb("onb", [128, 4, 128], BF16)
        otmp = P.sb("otmp", [128, 4, 128], BF16)
        ost = xst

        def pcol(name, i=0, n=1):
            o = PO[name] + i
            return par[:, o:o + n]

        ident_f = cst[:, 0, :]
        utri = cst[:, 1, :]
        negi = cst[:, 2, :]
        negs = cst[:, 3, :]

        P.dma("sp", lambda e: e.dma_start(out=cst[:], in_=consts.rearrange("p (a c) -> p a c", a=4)), [], [cst], "fresh")
        P.dma("sp", lambda e: e.dma_start(out=par[:], in_=params), [], [par], "fresh")
        P.dma("pool", lambda e: e.dma_start(out=wab[:], in_=w_in[:, 2048:2056].rearrange("(kc p) n -> p kc n", p=128)),
              [], [wab], "fresh")

        def conv_cols(W, s, nw, off, c0, n, r0=0, nk=KC):
            o = wscr[s, :, 0:nk * nw].rearrange("p (kc n) -> p kc n", n=nw)[:, :, off:off + n]
            i = W[r0:r0 + nk * 128, c0:c0 + n].rearrange("(kc p) n -> p kc n", p=128)
            early = s in (SL_IN + 1, SL_IN + 2, SL_KV, SL_KV + 1, SL_KV + 2, SL_KV + 3)
            conv_list.append((early, o, i))

        P.barrier_keys.add("wc1")
        P.barrier_keys.add("wc2")
        conv_list = []
        DR2 = Tile(None)
        for s in range(4):
            conv_cols(w_in, SL_IN + s, 512, 0, s * 512, 512)
        for s in range(2):
            for q in range(2):
                conv_cols(w_in, SL_IN + 4 + s, 512, q * 128, 2056 + (2 * s + q) * 128, 128)
                conv_cols(w_in, SL_IN + 4 + s, 512, 256 + q * 128, 2056 + 512 + (2 * s + q) * 128, 128)
        for s in range(4):
            conv_cols(w_kv, SL_KV + s, 512, 0, s * 512, 512)
        for s in range(2):
            conv_cols(w_out, SL_OUT + s, 512, 0, s * 512, 512)
            conv_cols(w_q, SL_Q + s, 512, 0, s * 512, 512)
            conv_cols(w_o, SL_O + s, 512, 0, s * 512, 512)
        for s in range(11):
            for q in range(2):
                conv_cols(w_up, SL_UP + s, 512, q * 128, (2 * s + q) * 128, 128)
                conv_cols(w_up, SL_UP + s, 512, 256 + q * 128, DFF + (2 * s + q) * 128, 128)
        for ng in range(4):
            for kh in range(2):
                conv_cols(w_dn, SL_DN + ng * 2 + kh, 256, 0, ng * 256, 256, r0=kh * 1408, nk=11)

        late_list = [c for c in conv_list if not c[0]]

        def issue_late(n):
            for _ in range(min(n, len(late_list))):
                (early, o, i) = late_list.pop(0)
                DR2.bufs[0].w = P.dma("pool", lambda e, o=o, i=i: e.dma_start(out=o, in_=i), [], [], "wc2")

        P.pool(lambda e: e.memset(ones_b[:], 1.0), [], [ones_b])
        P.pool(lambda e: e.memset(ones_f[:], 1.0), [], [ones_f])
        P.pool(lambda e: e.memset(epsc[:], EPS), [], [epsc])
        P.pool(lambda e: e.memset(lnc[:], -0.5 * float(np.log(128.0))), [], [lnc])
        P.pool(lambda e: e.memset(Sf[:], 0.0), [], [Sf])
        P.pool(lambda e: e.memset(Sb[:], 0.0), [], [Sb])
        P.pool(lambda e: e.memset(uhist[:], 0.0), [], [uhist])
        P.pool(lambda e: e.memset(zhist[:], 0.0), [], [zhist])
        for t in zc + cbuf + ubuf:
            P.pool(lambda e, t=t: e.memset(t[:], 0.0), [], [t])
        P.dve(lambda e: e.tensor_copy(out=ident_b[:], in_=ident_f), [cst], [ident_b])
        for h in range(4):
            P.dve(lambda e, h=h: e.tensor_copy(out=identB4[:, h, :], in_=ident_f), [cst], [identB4])
        P.act(lambda e: e.activation(out=expA[:], in_=pcol("alog", 0, 16), func=AF.Exp), [par], [expA])
        for t in (cst, par, ident_b, identB4, ones_b, ones_f, epsc, lnc, expA, wab):
            pass

        for (early, o, i) in [c for c in conv_list if c[0]]:
            DR.bufs[0].w = P.dma("pool", lambda e, o=o, i=i: e.dma_start(out=o, in_=i), [], [], "wc1")

        slot_ctr = [0]

        def load_slab(s):
            sl = slots[slot_ctr[0] % NSLOT]
            key = "slot%d" % (slot_ctr[0] % NSLOT)
            slot_ctr[0] += 1
            drt = DR if s in (SL_IN + 1, SL_IN + 2, SL_KV, SL_KV + 1, SL_KV + 2, SL_KV + 3) else DR2
            P.dma("sp", lambda e: e.dma_start(out=sl[:], in_=wscr[s]), [drt], [sl], key)
            return sl

        def slab_ap(sl, nw, nk=KC):
            return sl[:, 0:nk * nw].rearrange("p (kc n) -> p kc n", n=nw)

        ev_ctr = [0]

        def copy_ev(out_ap, in_ap, r, w, scale=None):
            ev_ctr[0] += 1
            if ev_ctr[0] % 2 == 0 and scale is None:
                P.dve(lambda e: e.tensor_copy(out=out_ap, in_=in_ap), r, w)
            else:
                if scale is None:
                    P.act(lambda e: e.activation(out=out_ap, in_=in_ap, func=AF.Copy), r, w)
                else:
                    P.act(lambda e: e.activation(out=out_ap, in_=in_ap, func=AF.Copy, scale=scale), r, w)

        def rsqrt_from(ps_ap, r, scale, out_t):
            P.act(lambda e: e.activation(out=lnt[:], in_=ps_ap, func=AF.Ln, bias=epsc[:], scale=scale), r + [epsc], [lnt])
            P.act(lambda e: e.activation(out=out_t[:], in_=lnt[:], func=AF.Exp, scale=-0.5), [lnt], [out_t])

        def prenorm(gi):
            st = P.bank()
            for c in range(KC):
                s = sq[c % 2]
                P.act(lambda e, c=c, s=s: e.activation(out=s[:], in_=hT[:, c, :], func=AF.Square), [hT], [s])
                P.pe(lambda e, c=c, s=s: e.matmul(st[:], lhsT=ones_b[:], rhs=s[:], start=(c == 0), stop=(c == KC - 1)),
                     [ones_b, s], [st])
            rsqrt_from(st[:], [st], 1.0 / D, rstd)
            P.free(st)
            for c in range(KC):
                P.dve(lambda e, c=c: e.scalar_tensor_tensor(out=hnT[:, c, :], in0=hT[:, c, :], scalar=pcol("ng", gi * 8 + c),
                                                            in1=rstd[:], op0=ALU.mult, op1=ALU.mult),
                      [hT, par, rstd], [hnT])

        def postnorm_residual(gi, st):
            rsqrt_from(st[:], [st], 1.0 / D, rstd)
            P.free(st)
            for c in range(KC):
                P.dve(lambda e, c=c: e.scalar_tensor_tensor(out=yT[:, c, :], in0=yT[:, c, :], scalar=pcol("ng", gi * 8 + c),
                                                            in1=rstd[:], op0=ALU.mult, op1=ALU.mult),
                      [yT, par, rstd], [yT])
                (P.pool if c % 2 == 0 else P.dve)(lambda e, c=c: e.tensor_tensor(out=hT[:, c, :], in0=hT[:, c, :], in1=yT[:, c, :], op=ALU.add),
                                                        [hT, yT], [hT])

        def proj_to_yT(slab_ids, rhs_list, nk_per=KC):
            st = P.bank()
            m = 0
            for s in slab_ids:
                sl = load_slab(s)
                sa_ = slab_ap(sl, 512)
                for q in range(4):
                    pb = P.bank()
                    for k in range(KC):
                        P.pe(lambda e, k=k, q=q, pb=pb, sa_=sa_: e.matmul(pb[:], lhsT=sa_[:, k, q * 128:(q + 1) * 128],
                                                                          rhs=rhs_list[k][0], start=(k == 0), stop=(k == KC - 1)),
                             [sl, rhs_list[k][1]], [pb])
                    s2 = sq[m % 2]
                    KV = int(os.environ.get("KVAR", "0"))
                    P.dve(lambda e, pb=pb, m=m: e.tensor_copy(out=yT[:, m, :], in_=pb[:]), [pb], [yT])
                    P.free(pb)
                    P.act(lambda e, m=m, s2=s2: e.activation(out=s2[:], in_=yT[:, m, :], func=AF.Square), [yT], [s2])
                    if KV not in (1, 2):
                        P.pe(lambda e, m=m, s2=s2: e.matmul(st[:], lhsT=ones_b[:], rhs=s2[:], start=(m == 0), stop=(m == 7)),
                             [ones_b, s2], [st])
                    m += 1
            return st

        dg_ctr = [0]

        def conv_mm(pb, src, src_t, ntap, wname, wbase):
            for k in range(ntap):
                d = dg[dg_ctr[0] % len(dg)]
                dg_ctr[0] += 1
                P.dve(lambda e, d=d, k=k: e.tensor_scalar(out=d[:], in0=ident_b[:], scalar1=pcol(wname, wbase + k), scalar2=None,
                                                           op0=ALU.mult),
                       [ident_b, par], [d])
                P.pe(lambda e, d=d, k=k: e.matmul(pb[:], lhsT=d[:], rhs=src[:, k:k + NT], start=(k == 0), stop=(k == ntap - 1)),
                     [d, src_t], [pb])

        if STOP > 0:
            for blk in range(2):
                xs = xst[blk % 2]
                P.dma("sp", lambda e, blk=blk, xs=xs: e.dma_start(out=xs[:], in_=memin[blk * 128:(blk + 1) * 128, :]), [], [xs],
                      "xst%d" % (blk % 2))
                for half in range(2):
                    pb = P.bank()
                    for q in range(4):
                        c = half * 4 + q
                        P.pe(lambda e, c=c, q=q, xs=xs, pb=pb: e.transpose(out=pb[:, q * 128:(q + 1) * 128], in_=xs[:, c * 128:(c + 1) * 128],
                                                                           identity=ident_f), [xs, cst], [pb])
                    P.dve(lambda e, half=half, blk=blk, pb=pb: e.tensor_copy(
                        out=yT[:, half * 4:half * 4 + 4, blk * 128:(blk + 1) * 128],
                        in_=pb[:].rearrange("p (q c) -> p q c", q=4)), [pb], [yT])
                    P.free(pb)
            st = P.bank()
            for c in range(KC):
                s = sq[c % 2]
                P.act(lambda e, c=c, s=s: e.activation(out=s[:, 0:MEM], in_=yT[:, c, 0:MEM], func=AF.Square), [yT], [s])
                P.pe(lambda e, c=c, s=s: e.matmul(st[:, 0:MEM], lhsT=ones_b[:], rhs=s[:, 0:MEM], start=(c == 0), stop=(c == KC - 1)),
                     [ones_b, s], [st])
            P.act(lambda e: e.activation(out=lnt[:, 0:MEM], in_=st[:, 0:MEM], func=AF.Ln, bias=epsc[:], scale=1.0 / D), [st, epsc], [lnt])
            P.act(lambda e: e.activation(out=rstd[:, 0:MEM], in_=lnt[:, 0:MEM], func=AF.Exp, scale=-0.5), [lnt], [rstd])
            P.free(st)
            for c in range(KC):
                P.dve(lambda e, c=c: e.scalar_tensor_tensor(out=hnT[:, c, 0:MEM], in0=yT[:, c, 0:MEM], scalar=pcol("mng", c),
                                                            in1=rstd[:, 0:MEM], op0=ALU.mult, op1=ALU.mult),
                      [yT, par, rstd], [hnT])
            for s in range(2):
                sl = load_slab(SL_KV + s)
                sa_ = slab_ap(sl, 512)
                for q in range(4):
                    pb = P.bank()
                    for k in range(KC):
                        P.pe(lambda e, k=k, q=q, pb=pb, sa_=sa_: e.matmul(pb[:, 0:MEM], lhsT=sa_[:, k, q * 128:(q + 1) * 128], rhs=hnT[:, k, 0:MEM],
                                                                          start=(k == 0), stop=(k == KC - 1)), [sl, hnT.cv(k)], [pb])
                    P.dve(lambda e, pb=pb, s=s, q=q: e.tensor_copy(out=KT[:, s * 4 + q, :], in_=pb[:, 0:MEM]), [pb], [KT])
                    P.free(pb)
            for s in range(2):
                sl = load_slab(SL_KV + 2 + s)
                sa_ = slab_ap(sl, 512)
                for mc in range(2):
                    pb = P.bank()
                    for k in range(KC):
                        P.pe(lambda e, k=k, mc=mc, pb=pb, sa_=sa_: e.matmul(pb[:], lhsT=hnT[:, k, mc * 128:(mc + 1) * 128], rhs=sa_[:, k, :],
                                                                            start=(k == 0), stop=(k == KC - 1)), [sl, hnT.cv(k)], [pb])
                    P.dve(lambda e, pb=pb, s=s, mc=mc: e.tensor_copy(out=Vm[:, mc, s * 512:(s + 1) * 512], in_=pb[:]), [pb], [Vm])
                    P.free(pb)

        def bc4(t, blk):
            return t[:, blk, :].unsqueeze(2).to_broadcast([128, 4, 128])

        def v3(b):
            return b[:].rearrange("p (h c) -> p h c", h=4)

        def vb3(b):
            return b[:].bitcast(BF16)[:, 0:512].rearrange("p (h c) -> p h c", h=4)

        def gdn_scalars(full):
            T = tk
            P.dve(lambda e: e.tensor_tensor(out=T["a"][:], in0=abtok[:, :, 0:4], in1=pcol("dtb", 0, 16).rearrange("p (b h) -> p b h", h=4),
                                            op=ALU.add), [abtok, par], [T["a"]])
            P.act(lambda e: e.activation(out=T["t1"][:], in_=T["a"][:], func=AF.Exp), [T["a"]], [T["t1"]])
            P.act(lambda e: e.activation(out=T["t2"][:], in_=T["t1"][:], func=AF.Ln, bias=1.0), [T["t1"]], [T["t2"]])
            P.dve(lambda e: e.scalar_tensor_tensor(out=T["g"][:], in0=T["t2"][:], scalar=-1.0, in1=expA[:].rearrange("p (b h) -> p b h", h=4),
                                                   op0=ALU.mult, op1=ALU.mult), [T["t2"], expA], [T["g"]])
            P.act(lambda e: e.activation(out=T["t1"][:], in_=abtok[:, :, 4:8], func=AF.Exp, scale=-1.0), [abtok], [T["t1"]])
            P.act(lambda e: e.activation(out=T["t2"][:], in_=T["t1"][:], func=AF.Ln, bias=1.0), [T["t1"]], [T["t2"]])
            P.dve(lambda e: e.tensor_scalar(out=T["lb"][:], in0=T["t2"][:], scalar1=-1.0, scalar2=None, op0=ALU.mult), [T["t2"]], [T["lb"]])
            P.act(lambda e: e.activation(out=T["t1"][:], in_=ssq[:, :, 4:8], func=AF.Ln, bias=epsc[:]), [ssq, epsc], [T["t1"]])
            P.dve(lambda e: e.tensor_scalar(out=T["lrk"][:], in0=T["t1"][:], scalar1=-0.5, scalar2=None, op0=ALU.mult), [T["t1"]], [T["lrk"]])
            if full:
                P.act(lambda e: e.activation(out=T["t1"][:], in_=ssq[:, :, 0:4], func=AF.Ln, bias=epsc[:]), [ssq, epsc], [T["t1"]])
                P.dve(lambda e: e.tensor_scalar(out=T["lrq"][:], in0=T["t1"][:], scalar1=-0.5, scalar2=None, op0=ALU.mult),
                      [T["t1"]], [T["lrq"]])
            pb = P.bank()
            g2 = T["g"][:].rearrange("p b h -> p (b h)")
            P.pe(lambda e: e.matmul(pb[:, 0:16], lhsT=utri, rhs=g2, start=True, stop=True), [cst, T["g"]], [pb])
            P.pe(lambda e: e.matmul(pb[:, 16:32], lhsT=ones_f[:], rhs=g2, start=True, stop=True), [ones_f, T["g"]], [pb])
            P.dve(lambda e: e.tensor_copy(out=T["gc"][:].rearrange("p b h -> p (b h)"), in_=pb[:, 0:16]), [pb], [T["gc"]])
            P.dve(lambda e: e.tensor_copy(out=T["gl"][:].rearrange("p b h -> p (b h)"), in_=pb[:, 16:32]), [pb], [T["gl"]])
            P.free(pb)
            P.dve(lambda e: e.tensor_tensor(out=T["bj"][:], in0=T["lrk"][:], in1=T["gc"][:], op=ALU.subtract), [T["lrk"], T["gc"]], [T["bj"]])
            P.dve(lambda e: e.tensor_tensor(out=T["t1"][:], in0=T["gc"][:], in1=T["lb"][:], op=ALU.add), [T["gc"], T["lb"]], [T["t1"]])
            P.dve(lambda e: e.tensor_tensor(out=T["vL"][:], in0=T["t1"][:], in1=T["lrk"][:], op=ALU.add), [T["t1"], T["lrk"]], [T["vL"]])
            P.act(lambda e: e.activation(out=T["b"][:], in_=T["lb"][:], func=AF.Exp), [T["lb"]], [T["b"]])
            P.act(lambda e: e.activation(out=T["sw"][:], in_=T["vL"][:], func=AF.Exp), [T["vL"]], [T["sw"]])
            P.dve(lambda e: e.tensor_tensor(out=T["t2"][:], in0=T["gl"][:], in1=T["bj"][:], op=ALU.add), [T["gl"], T["bj"]], [T["t2"]])
            P.act(lambda e: e.activation(out=T["skd"][:], in_=T["t2"][:], func=AF.Exp), [T["t2"]], [T["skd"]])
            P.act(lambda e: e.activation(out=T["egl"][:], in_=T["gl"][:], func=AF.Exp), [T["gl"]], [T["egl"]])
            if full:
                P.dve(lambda e: e.scalar_tensor_tensor(out=T["vA"][:], in0=T["gc"][:], scalar=lnc[:], in1=T["lrq"][:],
                                                       op0=ALU.add, op1=ALU.add), [T["gc"], lnc, T["lrq"]], [T["vA"]])
                P.act(lambda e: e.activation(out=T["so"][:], in_=T["vA"][:], func=AF.Exp), [T["vA"]], [T["so"]])

        def gdn_pre(blk, full, qT, kT_, vT_):
            T = tk
            B = BS[blk % NSET]
            kdec, bek, bv, Mx, Px, Xx, attnT, usb, wT = (B[n] for n in ("kdec", "bek", "bv", "M", "P", "X", "attnT", "usb", "wT"))
            EL = ELs[blk % 2]
            EA = EAs[blk % 2]
            cs = slice(blk * 128, (blk + 1) * 128)
            pk = P.bank()
            for h in range(4):
                P.pe(lambda e, h=h: e.transpose(out=pk[:].bitcast(BF16)[:, h * 128:(h + 1) * 128], in_=kT_[h][:, cs], identity=ident_b[:]),
                     [kT_[h], ident_b], [pk])
            P.dve(lambda e: e.tensor_tensor(out=kdec[:], in0=vb3(pk), in1=bc4(T["skd"], blk), op=ALU.mult), [pk, T["skd"]], [kdec])
            P.dve(lambda e: e.tensor_tensor(out=bek[:], in0=vb3(pk), in1=bc4(T["sw"], blk), op=ALU.mult), [pk, T["sw"]], [bek])
            P.free(pk)
            yield
            pv = P.bank()
            for h in range(4):
                P.pe(lambda e, h=h: e.transpose(out=pv[:].bitcast(BF16)[:, h * 128:(h + 1) * 128], in_=vT_[h][:, cs], identity=ident_b[:]),
                     [vT_[h], ident_b], [pv])
            P.dve(lambda e: e.tensor_tensor(out=bv[:], in0=vb3(pv), in1=bc4(T["b"], blk), op=ALU.mult), [pv, T["b"]], [bv])
            P.free(pv)
            yield

            def emat(vec, neg, out_t):
                pe_ = P.bank()
                for h in range(4):
                    o = pe_[:, h * 128:(h + 1) * 128]
                    P.pe(lambda e, h=h, o=o: e.matmul(o, lhsT=vec[:, blk, h:h + 1].to_broadcast([128, 128]), rhs=ident_f, start=True, stop=False),
                         [vec, cst], [pe_])
                    P.pe(lambda e, h=h, o=o: e.matmul(o, lhsT=ident_f, rhs=T["bj"][:, blk, h:h + 1].to_broadcast([128, 128]), start=False, stop=False),
                         [T["bj"], cst], [pe_])
                    P.pe(lambda e, h=h, o=o: e.matmul(o, lhsT=ident_f, rhs=neg, start=False, stop=True), [cst], [pe_])
                P.act(lambda e: e.activation(out=out_t[:].rearrange("p h c -> p (h c)"), in_=pe_[:], func=AF.Exp), [pe_], [out_t])
                P.free(pe_)

            emat(T["vL"], negs, EL)
            pl = P.bank()
            for h in range(4):
                P.pe(lambda e, h=h: e.matmul(pl[:, h * 128:(h + 1) * 128], lhsT=kT_[h][:, cs], rhs=kT_[h][:, cs], start=True, stop=True),
                     [kT_[h]], [pl])
            P.dve(lambda e: e.tensor_tensor(out=Mx[:], in0=v3(pl), in1=EL[:], op=ALU.mult), [pl, EL], [Mx])
            P.free(pl)
            yield
            if full:
                emat(T["vA"], negi, EA)
                pa = P.bank()
                for h in range(4):
                    P.pe(lambda e, h=h: e.matmul(pa[:, h * 128:(h + 1) * 128], lhsT=kT_[h][:, cs], rhs=qT[h][:, cs], start=True, stop=True),
                         [kT_[h], qT[h]], [pa])
                P.dve(lambda e: e.tensor_tensor(out=attnT[:], in0=v3(pa), in1=EA[:], op=ALU.mult), [pa, EA], [attnT])
                P.free(pa)
                yield
            pt = P.bank()
            for h in range(4):
                P.pe(lambda e, h=h: e.transpose(out=pt[:].bitcast(BF16)[:, h * 128:(h + 1) * 128], in_=Mx[:, h, :], identity=ident_b[:]),
                     [Mx, ident_b], [pt])
            P.act(lambda e: e.activation(out=Px[:], in_=vb3(pt), func=AF.Copy), [pt], [Px])
            P.free(pt)
            P.dve(lambda e: e.scalar_tensor_tensor(out=Xx[:], in0=Mx[:], scalar=-1.0, in1=identB4[:], op0=ALU.mult, op1=ALU.add),
                  [Mx, identB4], [Xx])
            yield
            Mc, Mo = Mx, B["M2"]
            for k in range(6):
                if k < 5:
                    pm = P.bank()
                    for h in range(4):
                        P.pe(lambda e, h=h, pm=pm, Mc=Mc: e.matmul(pm[:, h * 128:(h + 1) * 128], lhsT=Px[:, h, :], rhs=Mc[:, h, :], start=True, stop=True),
                             [Mc, Px], [pm])
                    yield
                    P.dve(lambda e, pm=pm, Mo=Mo: e.tensor_copy(out=Mo[:].rearrange("p h c -> p (h c)"), in_=pm[:]), [pm], [Mo])
                    P.free(pm)
                pp = P.bank()
                for h in range(4):
                    P.pe(lambda e, h=h, pp=pp, Mc=Mc: e.matmul(pp[:, h * 128:(h + 1) * 128], lhsT=Mc[:, h, :], rhs=Px[:, h, :], start=True, stop=True),
                         [Mc, Px], [pp])
                yield
                P.act(lambda e, pp=pp: e.activation(out=Px[:].rearrange("p h c -> p (h c)"), in_=pp[:], func=AF.Copy), [pp], [Px])
                P.free(pp)
                px = P.bank()
                for h in range(4):
                    P.pe(lambda e, h=h, px=px: e.matmul(px[:, h * 128:(h + 1) * 128], lhsT=Px[:, h, :], rhs=Xx[:, h, :], start=True, stop=True),
                         [Px, Xx], [px])
                yield
                P.dve(lambda e, px=px: e.tensor_tensor(out=Xx[:], in0=v3(px), in1=Xx[:], op=ALU.add), [px, Xx], [Xx])
                P.free(px)
                Mc, Mo = Mo, Mc
            AT = Xx
            pu = P.bank()
            for h in range(4):
                P.pe(lambda e, h=h: e.matmul(pu[:, h * 128:(h + 1) * 128], lhsT=AT[:, h, :], rhs=bv[:, h, :], start=True, stop=True), [AT, bv], [pu])
            P.act(lambda e: e.activation(out=usb[:].rearrange("p h c -> p (h c)"), in_=pu[:], func=AF.Copy), [pu], [usb])
            P.free(pu)
            yield
            pw = P.bank()
            for h in range(4):
                P.pe(lambda e, h=h: e.matmul(pw[:, h * 128:(h + 1) * 128], lhsT=bek[:, h, :], rhs=AT[:, h, :], start=True, stop=True), [AT, bek], [pw])
            P.dve(lambda e: e.tensor_copy(out=wT[:].rearrange("p h c -> p (h c)"), in_=pw[:]), [pw], [wT])
            P.free(pw)
            yield

        def gdn_scan_all(full, qT, sg):
            T = tk
            po = {}

            def main1(blk):
                B = BS[blk % NSET]
                pws = P.bank()
                for h in range(4):
                    P.pe(lambda e, h=h: e.matmul(pws[:, h * 128:(h + 1) * 128], lhsT=B["wT"][:, h, :], rhs=Sb[:, h, :], start=True, stop=True),
                         [B["wT"], Sb], [pws])
                P.dve(lambda e: e.tensor_tensor(out=vnew[:], in0=B["usb"][:], in1=v3(pws), op=ALU.subtract), [B["usb"], pws], [vnew])
                P.free(pws)

            def main2(blk):
                B = BS[blk % NSET]
                cs = slice(blk * 128, (blk + 1) * 128)
                pds = P.bank()
                for h in range(4):
                    P.pe(lambda e, h=h: e.matmul(pds[:, h * 128:(h + 1) * 128], lhsT=B["kdec"][:, h, :], rhs=vnew[:, h, :], start=True, stop=True),
                         [B["kdec"], vnew], [pds])
                if full:
                    po1 = P.bank()
                    po2 = P.bank()
                    for h in range(4):
                        P.pe(lambda e, h=h: e.matmul(po1[:, h * 128:(h + 1) * 128], lhsT=qT[h][:, cs], rhs=Sb[:, h, :], start=True, stop=True),
                             [qT[h], Sb], [po1])
                    for h in range(4):
                        P.pe(lambda e, h=h: e.matmul(po2[:, h * 128:(h + 1) * 128], lhsT=B["attnT"][:, h, :], rhs=vnew[:, h, :], start=True, stop=True),
                             [B["attnT"], vnew], [po2])
                    po[blk] = (po1, po2)
                for h in range(4):
                    P.dve(lambda e, h=h: e.scalar_tensor_tensor(out=Sb[:, h, :], in0=Sf[:, h, :], scalar=T["egl"][:, blk, h:h + 1],
                                                                in1=pds[:, h * 128:(h + 1) * 128], op0=ALU.mult, op1=ALU.add),
                          [Sf, T["egl"], pds], [Sb])
                for h in range(4):
                    P.dve(lambda e, h=h: e.scalar_tensor_tensor(out=Sf[:, h, :], in0=Sf[:, h, :], scalar=T["egl"][:, blk, h:h + 1],
                                                                in1=pds[:, h * 128:(h + 1) * 128], op0=ALU.mult, op1=ALU.add),
                          [Sf, T["egl"], pds], [Sf])
                P.free(pds)

            def tail1(blk):
                po1, po2 = po[blk]
                P.dve(lambda e: e.tensor_tensor(out=o1s[:], in0=v3(po1), in1=bc4(T["so"], blk), op=ALU.mult), [po1, T["so"]], [o1s])
                P.dve(lambda e: e.tensor_tensor(out=osb[:], in0=v3(po2), in1=o1s[:], op=ALU.add), [po2, o1s], [osb])
                P.free(po1, po2)
                P.pool(lambda e: e.memset(ssqo[:], 0.0), [], [ssqo])
                for h in range(4):
                    P.act(lambda e, h=h: e.activation(out=osq[:, h, :], in_=osb[:, h, :], func=AF.Square, accum_out=ssqo[:, h:h + 1]),
                          [osb, ssqo], [osq, ssqo])
                P.act(lambda e: e.activation(out=rso[:], in_=ssqo[:], func=AF.Ln, bias=epsc[:], scale=1.0 / 128), [ssqo, epsc], [rso])
                P.act(lambda e: e.activation(out=rso[:], in_=rso[:], func=AF.Exp, scale=-0.5), [rso], [rso])
                P.pool(lambda e: e.tensor_tensor(out=onb[:], in0=osb[:], in1=rso[:].unsqueeze(2).to_broadcast([128, 4, 128]), op=ALU.mult),
                       [osb, rso], [onb])

            def tail2(blk):
                cs = slice(blk * 128, (blk + 1) * 128)
                pot = P.bank()
                for h in range(4):
                    P.pe(lambda e, h=h: e.transpose(out=pot[:].bitcast(BF16)[:, h * 128:(h + 1) * 128], in_=onb[:, h, :], identity=ident_b[:]),
                         [onb, ident_b], [pot])
                P.act(lambda e: e.activation(out=otmp[:], in_=vb3(pot), func=AF.Copy, scale=pcol("gng")), [pot, par], [otmp])
                P.free(pot)
                for h in range(4):
                    P.pool(lambda e, h=h: e.tensor_tensor(out=ocat[h][:, cs], in0=otmp[:, h, :], in1=sg[h][:, cs], op=ALU.mult),
                           [otmp, sg[h]], [ocat[h]])

            for blk in range(NB):
                main1(blk)
                yield
                if full and blk >= 1:
                    tail1(blk - 1)
                    yield
                main2(blk)
                yield
                if full and blk >= 1:
                    tail2(blk - 1)
                    yield
            if full:
                tail1(NB - 1)
                yield
                tail2(NB - 1)
                yield

        def interleave(gens):
            gens = list(gens)
            while gens:
                for g in list(gens):
                    try:
                        next(g)
                    except StopIteration:
                        gens.remove(g)


        def prenorm_h(gi, c0, c1, hb):
            st = P.bank()
            for c in range(KC):
                s_ = sq[c % 2]
                P.act(lambda e, c=c, s_=s_: e.activation(out=s_[:, c0:c1], in_=hT[:, c, c0:c1], func=AF.Square), [hT.cv(c)], [s_])
                P.pe(lambda e, c=c, s_=s_: e.matmul(st[:, c0:c1], lhsT=ones_b[:], rhs=s_[:, c0:c1], start=(c == 0), stop=(c == KC - 1)),
                     [ones_b, s_], [st])
            P.act(lambda e: e.activation(out=lnt[:, c0:c1], in_=st[:, c0:c1], func=AF.Ln, bias=epsc[:], scale=1.0 / D), [st, epsc], [lnt])
            P.act(lambda e: e.activation(out=rstd[:, c0:c1], in_=lnt[:, c0:c1], func=AF.Exp, scale=-0.5), [lnt], [rstd])
            P.free(st)
            for c in range(KC):
                P.dve(lambda e, c=c: e.scalar_tensor_tensor(out=hnT[:, c, c0:c1], in0=hT[:, c, c0:c1], scalar=pcol("ng", gi * 8 + c),
                                                            in1=rstd[:, c0:c1], op0=ALU.mult, op1=ALU.mult),
                      [hT.cv(c), par, rstd], [hnT.cv(c)])

        def postnorm_h(gi, st, c0, c1, hb):
            P.act(lambda e: e.activation(out=lnt[:, c0:c1], in_=st[:, c0:c1], func=AF.Ln, bias=epsc[:], scale=1.0 / D), [st, epsc], [lnt])
            P.act(lambda e: e.activation(out=rstd[:, c0:c1], in_=lnt[:, c0:c1], func=AF.Exp, scale=-0.5), [lnt], [rstd])
            P.free(st)
            for c in range(KC):
                P.dve(lambda e, c=c: e.scalar_tensor_tensor(out=yT[:, c, c0:c1], in0=yT[:, c, c0:c1], scalar=pcol("ng", gi * 8 + c),
                                                            in1=rstd[:, c0:c1], op0=ALU.mult, op1=ALU.mult),
                      [yT.cv(c), par, rstd], [yT.cv(c)])
                (P.pool if c % 2 == 0 else P.dve)(lambda e, c=c: e.tensor_tensor(out=hT[:, c, c0:c1], in0=hT[:, c, c0:c1], in1=yT[:, c, c0:c1], op=ALU.add),
                                                        [hT.cv(c), yT.cv(c)], [hT.cv(c)])

        def proj_h(slab_ids, rhs_tiles, c0, c1, hb):
            st = P.bank()
            m = 0
            pend = None

            def stat_mm(m_, s2_):
                P.pe(lambda e: e.matmul(st[:, c0:c1], lhsT=ones_b[:], rhs=s2_[:, c0:c1], start=(m_ == 0), stop=(m_ == 7)),
                     [ones_b, s2_], [st])

            for s_ in slab_ids:
                sl = load_slab(s_)
                sa_ = slab_ap(sl, 512)
                for q in range(4):
                    pb = P.bank()
                    for k in range(KC):
                        P.pe(lambda e, k=k, q=q, pb=pb, sa_=sa_: e.matmul(pb[:, c0:c1], lhsT=sa_[:, k, q * 128:(q + 1) * 128],
                                                                          rhs=rhs_tiles[k][:, c0:c1], start=(k == 0), stop=(k == KC - 1)),
                             [sl, rhs_tiles[k]], [pb])
                    if pend is not None:
                        stat_mm(*pend)
                    s2 = sq[m % 2]
                    P.dve(lambda e, pb=pb, m=m: e.tensor_copy(out=yT[:, m, c0:c1], in_=pb[:, c0:c1]), [pb], [yT.cv(m)])
                    P.free(pb)
                    P.act(lambda e, m=m, s2=s2: e.activation(out=s2[:, c0:c1], in_=yT[:, m, c0:c1], func=AF.Square), [yT.cv(m)], [s2])
                    pend = (m, s2)
                    m += 1
            stat_mm(*pend)
            return st

        def xattn_h(c0, c1, hb):
            qx = big[0:8]
            ox = big[8:16]
            m = 0
            for s_ in range(2):
                sl = load_slab(SL_Q + s_)
                sa_ = slab_ap(sl, 512)
                for q in range(4):
                    pb = P.bank()
                    for k in range(KC):
                        P.pe(lambda e, k=k, q=q, pb=pb, sa_=sa_: e.matmul(pb[:, c0:c1], lhsT=sa_[:, k, q * 128:(q + 1) * 128], rhs=hnT[:, k, c0:c1],
                                                                          start=(k == 0), stop=(k == KC - 1)), [sl, hnT.cv(k)], [pb])
                    P.act(lambda e, pb=pb, m=m: e.activation(out=qx[m][:, c0:c1], in_=pb[:, c0:c1], func=AF.Copy, scale=1.0 / 16.0), [pb], [qx[m]])
                    P.free(pb)
                    m += 1
            def xs_scores(h):
                pts = [ubuf[(2 * h) % 4], ubuf[(2 * h + 1) % 4]]
                for mc in range(2):
                    pb = P.bank()
                    for dc in range(2):
                        P.pe(lambda e, dc=dc, mc=mc, pb=pb: e.matmul(pb[:, c0:c1], lhsT=KT[:, 2 * h + dc, mc * 128:(mc + 1) * 128],
                                                                     rhs=qx[2 * h + dc][:, c0:c1], start=(dc == 0), stop=(dc == 1)),
                             [KT, qx[2 * h + dc]], [pb])
                    P.act(lambda e, pb=pb, mc=mc: e.activation(out=pts[mc][:, c0:c1], in_=pb[:, c0:c1], func=AF.Exp), [pb], [pts[mc]])
                    P.free(pb)

            def xs_pv(h):
                pts = [ubuf[(2 * h) % 4], ubuf[(2 * h + 1) % 4]]
                pd = P.bank()
                for mc in range(2):
                    P.pe(lambda e, mc=mc: e.matmul(pd[:, c0:c1], lhsT=ones_b[:], rhs=pts[mc][:, c0:c1], start=(mc == 0), stop=(mc == 1)),
                         [ones_b, pts[mc]], [pd])
                pbs_ = []
                for dvc in range(2):
                    pb = P.bank()
                    for mc in range(2):
                        P.pe(lambda e, mc=mc, dvc=dvc, pb=pb: e.matmul(pb[:, c0:c1], lhsT=Vm[:, mc, (2 * h + dvc) * 128:(2 * h + dvc + 1) * 128],
                                                                       rhs=pts[mc][:, c0:c1], start=(mc == 0), stop=(mc == 1)),
                             [Vm, pts[mc]], [pb])
                    pbs_.append(pb)
                P.act(lambda e: e.activation(out=rden[:, c0:c1], in_=pd[:, c0:c1], func=AF.Ln), [pd], [rden])
                P.act(lambda e: e.activation(out=rden[:, c0:c1], in_=rden[:, c0:c1], func=AF.Exp, scale=-1.0), [rden], [rden])
                P.free(pd)
                for dvc in range(2):
                    pb = pbs_[dvc]
                    P.dve(lambda e, pb=pb, dvc=dvc: e.tensor_tensor(out=ox[2 * h + dvc][:, c0:c1], in0=pb[:, c0:c1], in1=rden[:, c0:c1], op=ALU.mult),
                          [pb, rden], [ox[2 * h + dvc]])
                    P.free(pb)

            for h in range(5):
                if h < 4:
                    xs_scores(h)
                if h >= 1:
                    xs_pv(h - 1)
            return proj_h([SL_O, SL_O + 1], ox, c0, c1, hb)

        def ffn_h(c0, c1, hb, only_up):
            curu = {}
            n = c1 - c0

            def ffn_up_mm(i):
                s_, q = i // 2, i % 2
                if q == 0:
                    curu["sl"] = load_slab(SL_UP + s_)
                sl = curu["sl"]
                sa_ = slab_ap(sl, 512)
                pa_ = P.bank()
                pb_ = P.bank()
                for k in range(KC):
                    P.pe(lambda e, k=k: e.matmul(pa_[:, c0:c1], lhsT=sa_[:, k, q * 128:(q + 1) * 128], rhs=hnT[:, k, c0:c1],
                                                 start=(k == 0), stop=(k == KC - 1)), [sl, hnT.cv(k)], [pa_])
                for k in range(KC):
                    P.pe(lambda e, k=k: e.matmul(pb_[:, c0:c1], lhsT=sa_[:, k, 256 + q * 128:256 + (q + 1) * 128], rhs=hnT[:, k, c0:c1],
                                                 start=(k == 0), stop=(k == KC - 1)), [sl, hnT.cv(k)], [pb_])
                return pa_, pb_

            def ffn_up_ev(i, pa_, pb_):
                ua, ub = ubuf[2 * (i % 2)], ubuf[2 * (i % 2) + 1]
                P.pool(lambda e: e.tensor_copy(out=ua[:, c0:c0 + 2], in_=uhist[:, i, :]), [uhist], [ua])
                P.pool(lambda e: e.tensor_copy(out=ub[:, c0:c0 + 2], in_=uhist[:, 22 + i, :]), [uhist], [ub])
                P.act(lambda e: e.activation(out=ua[:, 2 + c0:2 + c1], in_=pa_[:, c0:c1], func=AF.Copy), [pa_], [ua])
                P.dve(lambda e: e.tensor_copy(out=ub[:, 2 + c0:2 + c1], in_=pb_[:, c0:c1]), [pb_], [ub])
                P.free(pa_, pb_)
                P.pool(lambda e: e.tensor_copy(out=uhist[:, i, :], in_=ua[:, c1:c1 + 2]), [ua], [uhist])
                P.pool(lambda e: e.tensor_copy(out=uhist[:, 22 + i, :], in_=ub[:, c1:c1 + 2]), [ub], [uhist])

            def ffn_conv(i):
                ua, ub = ubuf[2 * (i % 2)], ubuf[2 * (i % 2) + 1]
                ds = []
                for j in range(6):
                    d = dg[dg_ctr[0] % len(dg)]
                    dg_ctr[0] += 1
                    wb = (i * 3 + j) if j < 3 else ((22 + i) * 3 + j - 3)
                    P.dve(lambda e, d=d, wb=wb: e.tensor_scalar(out=d[:], in0=ident_b[:], scalar1=pcol("fcw", wb), scalar2=None, op0=ALU.mult),
                          [ident_b, par], [d])
                    ds.append(d)
                pca = P.bank()
                pcb = P.bank()
                for k in range(3):
                    P.pe(lambda e, k=k: e.matmul(pca[:, c0:c1], lhsT=ds[k][:], rhs=ua[:, c0 + k:c1 + k], start=(k == 0), stop=(k == 2)),
                         [ds[k], ua], [pca])
                for k in range(3):
                    P.pe(lambda e, k=k: e.matmul(pcb[:, c0:c1], lhsT=ds[3 + k][:], rhs=ub[:, c0 + k:c1 + k], start=(k == 0), stop=(k == 2)),
                         [ds[3 + k], ub], [pcb])
                return pca, pcb

            def ffn_act_ev(i, pca, pcb):
                sat = sa[i % 2]
                P.act(lambda e: e.activation(out=sat[:, c0:c1], in_=pca[:, c0:c1], func=AF.Silu, bias=pcol("fcb", i)), [pca, par], [sat])
                P.dve(lambda e: e.scalar_tensor_tensor(out=big[i][:, c0:c1], in0=pcb[:, c0:c1], scalar=pcol("fcb", 22 + i), in1=sat[:, c0:c1],
                                                       op0=ALU.add, op1=ALU.mult), [pcb, par, sat], [big[i]])
                P.free(pca, pcb)

            if only_up:
                for i in range(22):
                    ffn_up_ev(i, *ffn_up_mm(i))
                return None
            for i in range(23):
                if i < 22:
                    bk = ffn_up_mm(i)
                if i >= 1:
                    pc = ffn_conv(i - 1)
                if i < 22:
                    ffn_up_ev(i, *bk)
                if i >= 1:
                    ffn_act_ev(i - 1, *pc)
            st = P.bank()
            pendd = []
            for ng in range(4):
                pbs = [P.bank(), P.bank()]
                for kh in range(2):
                    sl = load_slab(SL_DN + ng * 2 + kh)
                    sa_ = slab_ap(sl, 256, 11)
                    for m2 in range(2):
                        for k in range(11):
                            kk = kh * 11 + k
                            P.pe(lambda e, k=k, kk=kk, m2=m2, pbs=pbs, sa_=sa_: e.matmul(pbs[m2][:, c0:c1], lhsT=sa_[:, k, m2 * 128:(m2 + 1) * 128],
                                                                                        rhs=big[kk][:, c0:c1], start=(kk == 0), stop=(kk == 21)),
                                 [sl, big[kk]], [pbs[m2]])
                for (m_, s2_) in pendd:
                    P.pe(lambda e, m_=m_, s2_=s2_: e.matmul(st[:, c0:c1], lhsT=ones_b[:], rhs=s2_[:, c0:c1], start=(m_ == 0), stop=(m_ == 7)),
                         [ones_b, s2_], [st])
                pendd = []
                for m2 in range(2):
                    m = ng * 2 + m2
                    s2 = sq[m % 2]
                    P.dve(lambda e, m=m, pb=pbs[m2]: e.tensor_copy(out=yT[:, m, c0:c1], in_=pb[:, c0:c1]), [pbs[m2]], [yT.cv(m)])
                    P.act(lambda e, s2=s2, m=m: e.activation(out=s2[:, c0:c1], in_=yT[:, m, c0:c1], func=AF.Square), [yT.cv(m)], [s2])
                    pendd.append((m, s2))
                P.free(*pbs)
            for (m_, s2_) in pendd:
                P.pe(lambda e, m_=m_, s2_=s2_: e.matmul(st[:, c0:c1], lhsT=ones_b[:], rhs=s2_[:, c0:c1], start=(m_ == 0), stop=(m_ == 7)),
                     [ones_b, s2_], [st])
            return st

        def store_h(c0, c1, hb, out_idx):
            for blk in range(c0 // 128, c1 // 128):
                for half in range(2):
                    pb = P.bank()
                    for q in range(4):
                        c = half * 4 + q
                        P.pe(lambda e, c=c, q=q, blk=blk, pb=pb: e.transpose(out=pb[:, q * 128:(q + 1) * 128], in_=hT[:, c, blk * 128:(blk + 1) * 128],
                                                                             identity=ident_f), [hT.cv(c), cst], [pb])
                    copy_ev(yT[:, 2 * blk + half, :], pb[:], [pb], [yT.cv(2 * blk + half)])
                    P.free(pb)
                r0 = out_idx * NT + blk * 128
                P.dma("sp", lambda e, blk=blk, r0=r0: e.dma_start(out=yout[r0:r0 + 128, :].rearrange("p (a c) -> p a c", a=2),
                                                                   in_=yT[:, 2 * blk:2 * blk + 2, :]),
                      [yT.cv(2 * blk), yT.cv(2 * blk + 1)], [], "ost%d" % blk)

        def back_schedule(halves, out_idx):
            halo = out_idx is None
            if len(halves) == 1:
                (c0, c1, hb) = halves[0]
                st = proj_h([SL_OUT, SL_OUT + 1], ocat, c0, c1, hb)
                postnorm_h(1, st, c0, c1, hb)
                prenorm_h(2, c0, c1, hb)
                st = xattn_h(c0, c1, hb)
                postnorm_h(3, st, c0, c1, hb)
                prenorm_h(4, c0, c1, hb)
                st = ffn_h(c0, c1, hb, halo)
                if not halo:
                    postnorm_h(5, st, c0, c1, hb)
                    store_h(c0, c1, hb, out_idx)
            else:
                A, B = halves
                stA = proj_h([SL_OUT, SL_OUT + 1], ocat, *A)
                postnorm_h(1, stA, *A)
                stB = proj_h([SL_OUT, SL_OUT + 1], ocat, *B)
                prenorm_h(2, *A)
                postnorm_h(1, stB, *B)
                ckpt(7)
                stA = xattn_h(*A)
                prenorm_h(2, *B)
                postnorm_h(3, stA, *A)
                stB = xattn_h(*B)
                prenorm_h(4, *A)
                postnorm_h(3, stB, *B)
                ckpt(8)
                stA = ffn_h(A[0], A[1], A[2], False)
                prenorm_h(4, *B)
                postnorm_h(5, stA, *A)
                stB = ffn_h(B[0], B[1], B[2], False)
                store_h(A[0], A[1], A[2], out_idx)
                postnorm_h(5, stB, *B)
                ckpt(9)
                store_h(B[0], B[1], B[2], out_idx)
            if halo:
                P.pool(lambda e: e.tensor_scalar(out=uhist[:], in0=uhist[:], scalar1=pcol("flag"), scalar2=None, op0=ALU.mult), [uhist, par], [uhist])

        prefetched = set()

        def tile(ti, full, out_idx):
            tok0 = ti * NT
            for blk in range(NB):
                xs = xst[blk % 2]
                if not (blk < 2 and ti in prefetched):
                    P.dma("sp", lambda e, blk=blk, xs=xs: e.dma_start(out=xs[:], in_=xin[tok0 + blk * 128: tok0 + (blk + 1) * 128, :]),
                          [], [xs], "xst%d" % (blk % 2))
                for half in range(2):
                    pb = P.bank()
                    for q in range(4):
                        c = half * 4 + q
                        P.pe(lambda e, c=c, q=q, xs=xs, pb=pb: e.transpose(out=pb[:, q * 128:(q + 1) * 128], in_=xs[:, c * 128:(c + 1) * 128],
                                                                           identity=ident_f), [xs, cst], [pb])
                    copy_ev(hT[:, half * 4:half * 4 + 4, blk * 128:(blk + 1) * 128], pb[:].rearrange("p (q c) -> p q c", q=4), [pb], [hT])
                    P.free(pb)
            if ti + 1 < NS + 1 + NF:
                for blk in range(2):
                    xs = xst[blk]
                    t1 = (ti + 1) * NT
                    P.dma("sp", lambda e, blk=blk, xs=xs, t1=t1: e.dma_start(out=xs[:], in_=xin[t1 + blk * 128: t1 + (blk + 1) * 128, :]),
                          [], [xs], "xst%d" % blk)
                prefetched.add(ti + 1)
            ckpt(1)
            prenorm_h(0, 0, NT, None)
            ckpt(2)
            qs, ks, vs, sgs = big[0:4], big[4:8], big[8:12], big[12:16]
            chunks = []
            for s_, dst in ([(0, qs)] if full else []) + [(1, ks), (2, vs)]:
                for q in range(4):
                    chunks.append((s_, q, dst))
            cur = {}

            def qkv_proj_mm(s_, q):
                if q == 0:
                    cur["sl"] = load_slab(SL_IN + s_)
                sl = cur["sl"]
                sa_ = slab_ap(sl, 512)
                pb = P.bank()
                for k in range(KC):
                    P.pe(lambda e, k=k: e.matmul(pb[:], lhsT=sa_[:, k, q * 128:(q + 1) * 128], rhs=hnT[:, k, :],
                                                 start=(k == 0), stop=(k == KC - 1)), [sl, hnT.cv(k)], [pb])
                return pb

            def qkv_proj_ev(s_, q, pb):
                ch = s_ * 4 + q
                z = zc[ch % 4]
                P.pool(lambda e: e.tensor_copy(out=z[:, 0:3], in_=zhist[:, ch, :]), [zhist], [z])
                copy_ev(z[:, 3:3 + NT], pb[:], [pb], [z])
                P.free(pb)
                P.pool(lambda e: e.tensor_copy(out=zhist[:, ch, :], in_=z[:, NT:NT + 3]), [z], [zhist])

            def qkv_conv_mm(s_, q, dst):
                ch = s_ * 4 + q
                z = zc[ch % 4]
                pc = P.bank()
                conv_mm(pc, z, z, 4, "gcw", ch * 4)
                return pc

            def qkv_conv_ev(q, dst, pc):
                P.act(lambda e: e.activation(out=dst[q][:], in_=pc[:], func=AF.Silu), [pc], [dst[q]])
                P.free(pc)

            for i in range(len(chunks) + 1):
                if i < len(chunks):
                    pbq = qkv_proj_mm(chunks[i][0], chunks[i][1])
                if i >= 1:
                    pcq = qkv_conv_mm(*chunks[i - 1])
                if i < len(chunks):
                    qkv_proj_ev(chunks[i][0], chunks[i][1], pbq)
                if i >= 1:
                    qkv_conv_ev(chunks[i - 1][1], chunks[i - 1][2], pcq)
            pab = P.bank()
            for blk in range(NB):
                for k in range(KC):
                    P.pe(lambda e, k=k, blk=blk: e.matmul(pab[:, blk * 8:(blk + 1) * 8], lhsT=hnT[:, k, blk * 128:(blk + 1) * 128], rhs=wab[:, k, :],
                                                          start=(k == 0), stop=(k == KC - 1)), [hnT, wab], [pab])
            P.dve(lambda e: e.tensor_copy(out=abtok[:].rearrange("p b c -> p (b c)"), in_=pab[:, 0:NB * 8]), [pab], [abtok])
            P.free(pab)
            pss = P.bank()
            lst = ([(qs[h], h) for h in range(4)] if full else []) + [(ks[h], 4 + h) for h in range(4)]
            for i, (src, col) in enumerate(lst):
                s2 = sq[i % 2]
                P.pool(lambda e, src=src, s2=s2: e.tensor_tensor(out=s2[:], in0=src[:], in1=src[:], op=ALU.mult), [src], [s2])
                for blk in range(NB):
                    P.pe(lambda e, blk=blk, s2=s2, col=col: e.matmul(pss[:, blk * 8 + col:blk * 8 + col + 1], lhsT=s2[:, blk * 128:(blk + 1) * 128],
                                                                      rhs=ones_b[:, 0:1], start=True, stop=True), [s2, ones_b], [pss])
            if full:
                P.dve(lambda e: e.tensor_copy(out=ssq[:].rearrange("p b c -> p (b c)"), in_=pss[:, 0:NB * 8]), [pss], [ssq])
            else:
                P.dve(lambda e: e.tensor_copy(out=ssq[:, :, 4:8], in_=pss[:, 0:NB * 8].rearrange("p (b c) -> p b c", c=8)[:, :, 4:8]), [pss], [ssq])
            P.free(pss)
            ckpt(3)
            gdn_scalars(full)
            ckpt(4)

            def filler():
                sl = load_slab(SL_IN + 3)
                sa_ = slab_ap(sl, 512)
                for q in range(4):
                    pb = P.bank()
                    for k in range(KC):
                        P.pe(lambda e, k=k, q=q, pb=pb, sa_=sa_: e.matmul(pb[:], lhsT=sa_[:, k, q * 128:(q + 1) * 128], rhs=hnT[:, k, :],
                                                                          start=(k == 0), stop=(k == KC - 1)), [sl, hnT.cv(k)], [pb])
                        if k % 4 == 3:
                            yield
                    P.act(lambda e, pb=pb, q=q: e.activation(out=sgs[q][:], in_=pb[:], func=AF.Silu), [pb], [sgs[q]])
                    P.free(pb)
                for s_ in range(2):
                    sl = load_slab(SL_IN + 4 + s_)
                    sa_ = slab_ap(sl, 512)
                    for q in range(2):
                        ch = 2 * s_ + q
                        pa_ = P.bank()
                        pb_ = P.bank()
                        for k in range(KC):
                            P.pe(lambda e, k=k, q=q, pa_=pa_, sa_=sa_: e.matmul(pa_[:], lhsT=sa_[:, k, q * 128:(q + 1) * 128], rhs=hnT[:, k, :],
                                                                                start=(k == 0), stop=(k == KC - 1)), [sl, hnT.cv(k)], [pa_])
                            if k % 4 == 3:
                                yield
                        for k in range(KC):
                            P.pe(lambda e, k=k, q=q, pb_=pb_, sa_=sa_: e.matmul(pb_[:], lhsT=sa_[:, k, 256 + q * 128:256 + (q + 1) * 128], rhs=hnT[:, k, :],
                                                                                start=(k == 0), stop=(k == KC - 1)), [sl, hnT.cv(k)], [pb_])
                            if k % 4 == 3:
                                yield
                        cb = cbuf[ch]
                        P.pool(lambda e, cb=cb: e.tensor_copy(out=cb[:, 0:30], in_=cb[:, NT:NT + 30]), [cb], [cb])
                        P.act(lambda e, pb_=pb_: e.activation(out=lnt[:], in_=pb_[:], func=AF.Sigmoid), [pb_], [lnt])
                        P.dve(lambda e, pa_=pa_, cb=cb: e.tensor_tensor(out=cb[:, 30:30 + NT], in0=pa_[:], in1=lnt[:], op=ALU.mult), [pa_, lnt], [cb])
                        P.free(pa_, pb_)
                pmean = P.bank()
                pe2 = P.bank()
                for ch in range(4):
                    pc = P.bank()
                    for k in range(31):
                        d = dg[dg_ctr[0] % len(dg)]
                        dg_ctr[0] += 1
                        P.dve(lambda e, d=d, k=k, ch=ch: e.tensor_scalar(out=d[:], in0=ident_b[:], scalar1=pcol("cdw", ch * 31 + k), scalar2=None,
                                                                         op0=ALU.mult), [ident_b, par], [d])
                        P.pe(lambda e, d=d, k=k, ch=ch, pc=pc: e.matmul(pc[:], lhsT=d[:], rhs=cbuf[ch][:, k:k + NT], start=(k == 0), stop=(k == 30)),
                             [d, cbuf[ch]], [pc])
                        if k % 4 == 3:
                            yield
                    P.act(lambda e, pc=pc, ch=ch: e.activation(out=yT[:, ch, :], in_=pc[:], func=AF.Identity, bias=pcol("cdb", ch)), [pc, par], [yT])
                    s2 = sq[ch % 2]
                    P.act(lambda e, pc=pc, ch=ch, s2=s2: e.activation(out=s2[:], in_=pc[:], func=AF.Square, bias=pcol("cdb", ch)), [pc, par], [s2])
                    P.free(pc)
                    P.pe(lambda e, ch=ch: e.matmul(pmean[:], lhsT=ones_f[:], rhs=yT[:, ch, :], start=(ch == 0), stop=(ch == 3)), [ones_f, yT], [pmean])
                    P.pe(lambda e, ch=ch, s2=s2: e.matmul(pe2[:], lhsT=ones_b[:], rhs=s2[:], start=(ch == 0), stop=(ch == 3)), [ones_b, s2], [pe2])
                    yield
                P.act(lambda e: e.activation(out=msb[:], in_=pmean[:], func=AF.Copy, scale=1.0 / 512), [pmean], [msb])
                P.pool(lambda e: e.tensor_tensor(out=var[:], in0=msb[:], in1=msb[:], op=ALU.mult), [msb], [var])
                P.dve(lambda e: e.scalar_tensor_tensor(out=var[:], in0=pe2[:], scalar=1.0 / 512, in1=var[:], op0=ALU.mult, op1=ALU.subtract),
                      [pe2, var], [var])
                P.free(pmean, pe2)
                P.act(lambda e: e.activation(out=lnt[:], in_=var[:], func=AF.Ln, bias=epsc[:]), [var, epsc], [lnt])
                P.act(lambda e: e.activation(out=rstd[:], in_=lnt[:], func=AF.Exp, scale=-0.5), [lnt], [rstd])
                yield
                for ch in range(4):
                    P.pool(lambda e, ch=ch: e.tensor_tensor(out=yT[:, ch, :], in0=yT[:, ch, :], in1=msb[:], op=ALU.subtract), [yT, msb], [yT])
                    P.dve(lambda e, ch=ch: e.tensor_tensor(out=yT[:, ch, :], in0=yT[:, ch, :], in1=rstd[:], op=ALU.mult), [yT, rstd], [yT])
                    P.act(lambda e, ch=ch: e.activation(out=ocat[4 + ch][:], in_=yT[:, ch, :], func=AF.Silu, bias=pcol("clb", ch), scale=pcol("clg", ch)),
                          [yT, par], [ocat[4 + ch]])
                    yield

            fil = filler() if full else None

            def run_with_filler(gens, nfill, every):
                nonlocal fil
                gens = list(gens)
                rnd = 0
                while gens:
                    rnd += 1
                    if rnd % every == 0:
                        for _ in range(nfill):
                            if fil is not None:
                                try:
                                    next(fil)
                                except StopIteration:
                                    fil = None
                    for g in list(gens):
                        try:
                            next(g)
                        except StopIteration:
                            gens.remove(g)

            run_with_filler([gdn_pre(blk, full, qs, ks, vs) for blk in range(NB)], 1, 2)
            ckpt(5)
            run_with_filler([gdn_scan_all(full, qs, sgs)], 3, 1)
            if fil is not None:
                for _ in fil:
                    pass
            if not full:
                return
            ckpt(6)
            if out_idx is None:
                halves = [(NT - 128, NT, None)]
            else:
                halves = [(0, NT, None)]
            back_schedule(halves, out_idx)

        try:
            ti = 0
            per = (len(late_list) + max(NS, 1) - 1) // max(NS, 1)
            for _ in range(NS):
                tile(ti, False, None)
                issue_late(per)
                ti += 1
            issue_late(len(late_list))
            tile(ti, True, None)
            ti += 1
            for i in range(NF):
                tile(ti, True, i)
                ti += 1
            P.final = ["ost0", "ost1", "ost2", "ost3"]
        except StopBuild:
            P.final = [k for k in P.dma_cnt.keys()]
        P.emit()
    return nc


def make_consts():
    c = np.zeros((128, 4, 128), np.float32)
    c[:, 0, :] = np.eye(128, dtype=np.float32)
    j = np.arange(128)[:, None]
    cc = np.arange(128)[None, :]
    c[:, 1, :] = (j <= cc).astype(np.float32)
    c[:, 2, :] = np.where(cc >= j, 0.0, NEGV)
    c[:, 3, :] = np.where(cc > j, 0.0, NEGV)
    return c.reshape(128, 512)


def make_params(inp, flag):
    p = np.zeros((128, NPAR), np.float32)

    def put(name, arr):
        p[:, PO[name]:PO[name] + arr.shape[1]] = arr

    def chunked(v):
        return np.ascontiguousarray(v.reshape(-1, 128).T)

    ng = inp["norm_g"][0]
    put("ng", np.concatenate([chunked(ng[i]) for i in range(6)], axis=1))
    gcw = inp["gdn_conv_w"][0]
    put("gcw", np.ascontiguousarray(gcw.reshape(4, 12, 128).transpose(2, 1, 0)).reshape(128, 48))
    cdw = inp["cfm_dw_w"][0]
    put("cdw", np.ascontiguousarray(cdw.reshape(31, 4, 128).transpose(2, 1, 0)).reshape(128, 124))
    put("cdb", chunked(inp["cfm_dw_b"][0]))
    put("clg", chunked(inp["cfm_ln_g"][0]))
    put("clb", chunked(inp["cfm_ln_b"][0]))
    fcw = inp["ffn_conv_w"][0]
    put("fcw", np.ascontiguousarray(fcw.reshape(3, 44, 128).transpose(2, 1, 0)).reshape(128, 132))
    put("fcb", chunked(inp["ffn_conv_b"][0]))
    put("gng", inp["gdn_norm_g"][0].reshape(128, 1))
    put("alog", np.tile(inp["gdn_a_log"][0][None, :], (128, 4)))
    put("dtb", np.tile(inp["gdn_dt_bias"][0][None, :], (128, 4)))
    put("mng", chunked(inp["mem_norm_g"][0]))
    p[:, PO["flag"]] = flag
    return p


_NC_CACHE = {}


def run(inputs, NF, NS, n_batch, dbg=False):
    inp = {k: np.asarray(v, dtype=np.float32) for k, v in inputs.items()}
    key = (NF, NS, dbg)
    if key not in _NC_CACHE:
        _NC_CACHE[key] = build(NF, NS, dbg)
    nc = _NC_CACHE[key]
    half = NF * NT
    nprev = (NS + 1) * NT
    consts = make_consts()
    in_maps = []
    for b in range(n_batch):
        for j in range(2):
            xall = np.zeros((nprev + half, D), np.float32)
            if j == 1:
                xall[:nprev] = inp["x"][b, 0:half]
            xall[nprev:] = inp["x"][b, j * half:(j + 1) * half]
            in_maps.append({
                "xin": xall, "memin": np.ascontiguousarray(inp["mem"][b]),
                "params": make_params(inp, float(j)), "consts": consts,
                "w_in": np.ascontiguousarray(inp["w_in"][0]), "w_out": np.ascontiguousarray(inp["w_out"][0]),
                "w_q": np.ascontiguousarray(inp["xa_w_q"][0]), "w_kv": np.ascontiguousarray(inp["xa_w_kv"][0]),
                "w_o": np.ascontiguousarray(inp["xa_w_o"][0]), "w_up": np.ascontiguousarray(inp["ffn_w_up"][0]),
                "w_dn": np.ascontiguousarray(inp["ffn_w_down"][0]),
            })
    res = run_bass_kernel_spmd(nc, in_maps, core_ids=list(range(len(in_maps))))
    out = np.zeros((n_batch, 2 * half, D), np.float32)
    for b in range(n_batch):
        for j in range(2):
            out[b, j * half:(j + 1) * half] = res.results[2 * b + j]["yout"]
    return out, res


def kernel(**inputs):
    out, _ = run(inputs, NF=8, NS=7, n_batch=4)
    return out
```

```python
import numpy as np
from contextlib import ExitStack
import concourse.bass as bass
import concourse.mybir as mybir
from concourse.bass_utils import run_bass_kernel_spmd

F32 = mybir.dt.float32
BF16 = mybir.dt.bfloat16
AF = mybir.ActivationFunctionType
ALU = mybir.AluOpType
AX = mybir.AxisListType

D = 1024
NT = 512
NB = NT // 128
H = 4
DFF = 2816
MEM = 256
EPS = 1e-6
NEGV = -30000.0
KC = 8

PO = {}
_o = 0
for _n, _w in [("ng", 48), ("gcw", 48), ("cdw", 124), ("cdb", 4), ("clg", 4), ("clb", 4),
               ("fcw", 132), ("fcb", 44), ("gng", 1), ("alog", 16), ("dtb", 16), ("mng", 8),
               ("flag", 1)]:
    PO[_n] = _o
    _o += _w
NPAR = _o

SL_IN, SL_OUT, SL_Q, SL_KV, SL_O, SL_UP, SL_DN = 0, 6, 8, 10, 14, 16, 27
NSLAB = 35


import os
STOP = float(os.environ.get("KSTOP", "1000"))


class StopBuild(Exception):
    pass


def ckpt(n):
    if n >= STOP:
        raise StopBuild()


class Buf:
    __slots__ = ("w", "rs", "const")

    def __init__(self):
        self.w = None
        self.rs = []
        self.const = False


class Tile:
    def __init__(self, t, bufs=None):
        self.t = t
        self.bufs = bufs if bufs is not None else [Buf()]
        self.h = None

    def split(self):
        self.bufs = [Buf(), Buf()]
        self.h = [Tile(self.t, [self.bufs[0]]), Tile(self.t, [self.bufs[1]])]
        return self

    def split_chunks(self, n):
        self.bufs = [Buf() for _ in range(n)]
        self.c = [Tile(self.t, [b]) for b in self.bufs]
        return self

    def cv(self, k):
        return self.c[k]

    def __getitem__(self, idx):
        return self.t[idx]


class Ins:
    __slots__ = ("eng", "fn", "idx", "dma_key", "dma_val", "deps", "needs_inc", "ordinal", "waits")

    def __init__(self, eng, fn, idx, dma_key):
        self.eng = eng
        self.fn = fn
        self.idx = idx
        self.dma_key = dma_key
        self.dma_val = None
        self.deps = []
        self.needs_inc = False
        self.ordinal = 0
        self.waits = []


class Prog:
    ENGS = ["pe", "act", "dve", "pool", "sp"]

    def __init__(self, nc, es):
        self.nc = nc
        self.es = es
        self.ins = {e: [] for e in self.ENGS}
        self.dma_cnt = {}
        self.dma_sem = {}
        self.barrier_keys = set()
        self.sem = {}
        self.nfresh = 0
        self.free_banks = []
        self.final = []

    def sb(self, name, shape, dt):
        return Tile(self.es.enter_context(self.nc.sbuf_tensor(name, list(shape), dt)))

    def mkbanks(self):
        for i in range(8):
            t = Tile(self.es.enter_context(self.nc.psum_tensor("bank%d" % i, [128, 512], F32)))
            self.free_banks.append(t)

    def bank(self):
        assert self.free_banks, "out of PSUM banks"
        return self.free_banks.pop(0)

    def free(self, *bs):
        for b in bs:
            self.free_banks.append(b)

    def _add(self, eng, fn, r, w, dma_key=None):
        ins = Ins(eng, fn, len(self.ins[eng]), dma_key)
        deps = {}
        rb = [b for t in r for b in t.bufs]
        wb = [b for t in w for b in t.bufs]
        for b in rb:
            if b.w is not None:
                deps[id(b.w)] = (b.w, True)
        for b in wb:
            if b.w is not None and id(b.w) not in deps:
                deps[id(b.w)] = (b.w, False)
            for x in b.rs:
                if id(x) not in deps:
                    deps[id(x)] = (x, False)
        for b in rb:
            if not b.const:
                b.rs.append(ins)
        for b in wb:
            b.w = ins
            b.rs = []
        ins.deps = [v for k, v in deps.items() if v[0] is not ins]
        self.ins[eng].append(ins)
        if dma_key is not None:
            if dma_key == "fresh":
                dma_key = "fresh%d" % self.nfresh
                self.nfresh += 1
                ins.dma_key = dma_key
            self.dma_cnt[dma_key] = self.dma_cnt.get(dma_key, 0) + 16
            ins.dma_val = self.dma_cnt[dma_key]
        return ins

    def pe(self, fn, r, w):
        return self._add("pe", fn, r, w)

    def act(self, fn, r, w):
        return self._add("act", fn, r, w)

    def dve(self, fn, r, w):
        return self._add("dve", fn, r, w)

    def pool(self, fn, r, w):
        return self._add("pool", fn, r, w)

    def dma(self, q, fn, r, w, key):
        return self._add(q, fn, r, w, dma_key=key)

    def resolve(self):
        for e in self.ENGS:
            for ins in self.ins[e]:
                for (y, raw) in ins.deps:
                    if y.dma_key is not None:
                        continue
                    if y.eng == e:
                        if e in ("act", "dve", "pool") and raw:
                            y.needs_inc = True
                        continue
                    y.needs_inc = True
        for e in self.ENGS:
            c = 0
            for ins in self.ins[e]:
                if ins.dma_key is None and ins.needs_inc:
                    c += 1
                    ins.ordinal = c
        for k in list(self.dma_cnt.keys()):
            self.dma_sem[k] = self.es.enter_context(self.nc.semaphore("d_" + k))
        for e in self.ENGS:
            self.sem[e] = self.es.enter_context(self.nc.semaphore("e_" + e))
        for e in self.ENGS:
            waited = {}
            for ins in self.ins[e]:
                ws = {}
                for (y, raw) in ins.deps:
                    if y.dma_key is not None:
                        k = ("d", y.dma_key)
                        v = self.dma_cnt[y.dma_key] if y.dma_key in self.barrier_keys else y.dma_val
                    else:
                        if y.eng == e and not (e in ("act", "dve", "pool") and raw):
                            continue
                        k = ("e", y.eng)
                        v = y.ordinal
                    if waited.get(k, 0) >= v:
                        continue
                    if ws.get(k, 0) < v:
                        ws[k] = v
                for k, v in ws.items():
                    waited[k] = v
                    sem = self.dma_sem[k[1]] if k[0] == "d" else self.sem[k[1]]
                    ins.waits.append((sem, v))

    def emit(self):
        self.resolve()
        nc = self.nc
        prog = self

        def run(e, eng):
            for ins in prog.ins[e]:
                for (sem, v) in ins.waits:
                    eng.wait_ge(sem, v)
                bi = ins.fn(eng)
                if ins.dma_key is not None:
                    bi.then_inc(prog.dma_sem[ins.dma_key], 16)
                elif ins.needs_inc:
                    bi.then_inc(prog.sem[e], 1)
            if e == "sp":
                for k in prog.final:
                    eng.wait_ge(prog.dma_sem[k], prog.dma_cnt[k])

        with nc.Block() as block:
            @block.tensor
            def _(eng):
                run("pe", eng)

            @block.scalar
            def _(eng):
                run("act", eng)

            @block.vector
            def _(eng):
                run("dve", eng)

            @block.gpsimd
            def _(eng):
                run("pool", eng)

            @block.sync
            def _(eng):
                run("sp", eng)


def build(NF, NS, dbg=False):
    NTOK = (NS + 1 + NF) * NT
    nc = bass.Bass("TRN2", target_bir_lowering=False)
    xin = nc.dram_tensor("xin", [NTOK, D], F32, kind="ExternalInput").ap()
    memin = nc.dram_tensor("memin", [MEM, D], F32, kind="ExternalInput").ap()
    params = nc.dram_tensor("params", [128, NPAR], F32, kind="ExternalInput").ap()
    consts = nc.dram_tensor("consts", [128, 4 * 128], F32, kind="ExternalInput").ap()
    w_in = nc.dram_tensor("w_in", [D, 3080], F32, kind="ExternalInput").ap()
    w_out = nc.dram_tensor("w_out", [D, D], F32, kind="ExternalInput").ap()
    w_q = nc.dram_tensor("w_q", [D, D], F32, kind="ExternalInput").ap()
    w_kv = nc.dram_tensor("w_kv", [D, 2 * D], F32, kind="ExternalInput").ap()
    w_o = nc.dram_tensor("w_o", [D, D], F32, kind="ExternalInput").ap()
    w_up = nc.dram_tensor("w_up", [D, 2 * DFF], F32, kind="ExternalInput").ap()
    w_dn = nc.dram_tensor("w_dn", [DFF, D], F32, kind="ExternalInput").ap()
    yout = nc.dram_tensor("yout", [NF * NT, D], F32, kind="ExternalOutput").ap()
    wscr = nc.dram_tensor("wscr", [NSLAB, 128, 4096], BF16, kind="Internal").ap()
    dbg_out = None
    if dbg:
        dbg_out = nc.dram_tensor("dbg", [128, 16, NT], F32, kind="ExternalOutput").ap()

    es = ExitStack()
    with es:
        P = Prog(nc, es)
        P.mkbanks()
        DR = Tile(None)

        cst = P.sb("cst", [128, 4, 128], F32)
        par = P.sb("par", [128, NPAR], F32)
        ident_b = P.sb("ident_b", [128, 128], BF16)
        identB4 = P.sb("identB4", [128, 4, 128], BF16)
        ones_b = P.sb("ones_b", [128, 128], BF16)
        ones_f = P.sb("ones_f", [128, 128], F32)
        epsc = P.sb("epsc", [128, 1], F32)
        lnc = P.sb("lnc", [128, 1], F32)
        expA = P.sb("expA", [128, 16], F32)
        wab = P.sb("wab", [128, KC, 8], BF16)
        NSLOT = 3
        slots = [P.sb("slot%d" % i, [128, 4096], BF16) for i in range(NSLOT)]
        KT = P.sb("KT", [128, 8, MEM], BF16)
        Vm = P.sb("Vm", [128, 2, D], BF16)
        hT = P.sb("hT", [128, KC, NT], F32)
        hnT = P.sb("hnT", [128, KC, NT], BF16)
        yT = P.sb("yT", [128, KC, NT], F32)
        for t_ in [hT, hnT, yT]:
            t_.split_chunks(KC)
        xst = [P.sb("xst%d" % i, [128, D], F32) for i in range(2)]
        big = [P.sb("big%d" % i, [128, NT], BF16) for i in range(22)]
        zc = [P.sb("zc%d" % i, [128, 3 + NT], BF16) for i in range(4)]
        zhist = P.sb("zhist", [128, 12, 3], BF16)
        cbuf = [P.sb("cbuf%d" % i, [128, 30 + NT], BF16) for i in range(4)]
        ubuf = [P.sb("ubuf%d" % i, [128, 2 + NT], BF16) for i in range(4)]
        uhist = P.sb("uhist", [128, 44, 2], BF16)
        ocat = [P.sb("ocat%d" % i, [128, NT], BF16) for i in range(8)]
        sq = [P.sb("sq%d" % i, [128, NT], BF16) for i in range(2)]
        rstd = P.sb("rstd", [128, NT], F32)
        lnt = P.sb("lnt", [128, NT], F32)
        dg = [P.sb("dg%d" % i, [128, 128], BF16) for i in range(12)]
        rden = P.sb("rden", [128, NT], F32)
        msb = P.sb("msb", [128, NT], F32)
        var = rden
        sa = [P.sb("sa%d" % i, [128, NT], BF16) for i in range(2)]
        abtok = P.sb("abtok", [128, NB, 8], F32)
        ssq = P.sb("ssq", [128, NB, 8], F32)
        tk = {n: P.sb("tk_" + n, [128, NB, 4], F32) for n in
              ["a", "g", "lb", "lrq", "lrk", "gc", "gl", "vL", "vA", "bj", "b", "sw", "skd", "egl", "so", "t1", "t2"]}
        Sf = P.sb("Sf", [128, 4, 128], F32)
        Sb = P.sb("Sb", [128, 4, 128], BF16)
        NSET = 4
        BS = []
        for i in range(NSET):
            BS.append({n: P.sb("%s_%d" % (n, i), [128, 4, 128], dt) for n, dt in
                       [("kdec", BF16), ("bek", BF16), ("bv", BF16), ("M", BF16), ("M2", BF16), ("P", BF16), ("X", BF16),
                        ("attnT", BF16), ("usb", F32), ("wT", BF16)]})
        ELs = [P.sb("EL%d" % i, [128, 4, 128], BF16) for i in range(2)]
        EAs = [P.sb("EA%d" % i, [128, 4, 128], BF16) for i in range(2)]
        vnew = P.sb("vnew", [128, 4, 128], BF16)
        osb = P.sb("osb", [128, 4, 128], F32)
        scr4 = P.sb("scr4", [128, 4, 128], F32)
        o1s = scr4
        osq = scr4
        Stmp = P.sb("Stmp", [128, 4, 128], F32)
        ssqo = P.sb("ssqo", [128, 4], F32)
        rso = P.sb("rso", [128, 4], F32)
        onb = P.sb("onb", [128, 4, 128], BF16)
        otmp = P.sb("otmp", [128, 4, 128], BF16)
        ost = xst

        def pcol(name, i=0, n=1):
            o = PO[name] + i
            return par[:, o:o + n]

        ident_f = cst[:, 0, :]
        utri = cst[:, 1, :]
        negi = cst[:, 2, :]
        negs = cst[:, 3, :]

        P.dma("sp", lambda e: e.dma_start(out=cst[:], in_=consts.rearrange("p (a c) -> p a c", a=4)), [], [cst], "fresh")
        P.dma("sp", lambda e: e.dma_start(out=par[:], in_=params), [], [par], "fresh")
        P.dma("pool", lambda e: e.dma_start(out=wab[:], in_=w_in[:, 2048:2056].rearrange("(kc p) n -> p kc n", p=128)),
              [], [wab], "fresh")

        def conv_cols(W, s, nw, off, c0, n, r0=0, nk=KC):
            o = wscr[s, :, 0:nk * nw].rearrange("p (kc n) -> p kc n", n=nw)[:, :, off:off + n]
            i = W[r0:r0 + nk * 128, c0:c0 + n].rearrange("(kc p) n -> p kc n", p=128)
            early = s in (SL_IN + 1, SL_IN + 2, SL_KV, SL_KV + 1, SL_KV + 2, SL_KV + 3)
            conv_list.append((early, o, i))

        P.barrier_keys.add("wc1")
        P.barrier_keys.add("wc2")
        conv_list = []
        DR2 = Tile(None)
        for s in range(4):
            conv_cols(w_in, SL_IN + s, 512, 0, s * 512, 512)
        for s in range(2):
            for q in range(2):
                conv_cols(w_in, SL_IN + 4 + s, 512, q * 128, 2056 + (2 * s + q) * 128, 128)
                conv_cols(w_in, SL_IN + 4 + s, 512, 256 + q * 128, 2056 + 512 + (2 * s + q) * 128, 128)
        for s in range(4):
            conv_cols(w_kv, SL_KV + s, 512, 0, s * 512, 512)
        for s in range(2):
            conv_cols(w_out, SL_OUT + s, 512, 0, s * 512, 512)
            conv_cols(w_q, SL_Q + s, 512, 0, s * 512, 512)
            conv_cols(w_o, SL_O + s, 512, 0, s * 512, 512)
        for s in range(11):
            for q in range(2):
                conv_cols(w_up, SL_UP + s, 512, q * 128, (2 * s + q) * 128, 128)
                conv_cols(w_up, SL_UP + s, 512, 256 + q * 128, DFF + (2 * s + q) * 128, 128)
        for ng in range(4):
            for kh in range(2):
                conv_cols(w_dn, SL_DN + ng * 2 + kh, 256, 0, ng * 256, 256, r0=kh * 1408, nk=11)

        late_list = [c for c in conv_list if not c[0]]

        def issue_late(n):
            for _ in range(min(n, len(late_list))):
                (early, o, i) = late_list.pop(0)
                DR2.bufs[0].w = P.dma("pool", lambda e, o=o, i=i: e.dma_start(out=o, in_=i), [], [], "wc2")

        P.pool(lambda e: e.memset(ones_b[:], 1.0), [], [ones_b])
        P.pool(lambda e: e.memset(ones_f[:], 1.0), [], [ones_f])
        P.pool(lambda e: e.memset(epsc[:], EPS), [], [epsc])
        P.pool(lambda e: e.memset(lnc[:], -0.5 * float(np.log(128.0))), [], [lnc])
        P.pool(lambda e: e.memset(Sf[:], 0.0), [], [Sf])
        P.pool(lambda e: e.memset(Sb[:], 0.0), [], [Sb])
        P.pool(lambda e: e.memset(uhist[:], 0.0), [], [uhist])
        P.pool(lambda e: e.memset(zhist[:], 0.0), [], [zhist])
        for t in zc + cbuf + ubuf:
            P.pool(lambda e, t=t: e.memset(t[:], 0.0), [], [t])
        P.dve(lambda e: e.tensor_copy(out=ident_b[:], in_=ident_f), [cst], [ident_b])
        for h in range(4):
            P.dve(lambda e, h=h: e.tensor_copy(out=identB4[:, h, :], in_=ident_f), [cst], [identB4])
        P.act(lambda e: e.activation(out=expA[:], in_=pcol("alog", 0, 16), func=AF.Exp), [par], [expA])
        for t in (cst, par, ident_b, identB4, ones_b, ones_f, epsc, lnc, expA, wab):
            pass

        for (early, o, i) in [c for c in conv_list if c[0]]:
            DR.bufs[0].w = P.dma("pool", lambda e, o=o, i=i: e.dma_start(out=o, in_=i), [], [], "wc1")

        slot_ctr = [0]

        def load_slab(s):
            sl = slots[slot_ctr[0] % NSLOT]
            key = "slot%d" % (slot_ctr[0] % NSLOT)
            slot_ctr[0] += 1
            drt = DR if s in (SL_IN + 1, SL_IN + 2, SL_KV, SL_KV + 1, SL_KV + 2, SL_KV + 3) else DR2
            P.dma("sp", lambda e: e.dma_start(out=sl[:], in_=wscr[s]), [drt], [sl], key)
            return sl

        def slab_ap(sl, nw, nk=KC):
            return sl[:, 0:nk * nw].rearrange("p (kc n) -> p kc n", n=nw)

        ev_ctr = [0]

        def copy_ev(out_ap, in_ap, r, w, scale=None):
            ev_ctr[0] += 1
            if ev_ctr[0] % 2 == 0 and scale is None:
                P.dve(lambda e: e.tensor_copy(out=out_ap, in_=in_ap), r, w)
            else:
                if scale is None:
                    P.act(lambda e: e.activation(out=out_ap, in_=in_ap, func=AF.Copy), r, w)
                else:
                    P.act(lambda e: e.activation(out=out_ap, in_=in_ap, func=AF.Copy, scale=scale), r, w)

        def rsqrt_from(ps_ap, r, scale, out_t):
            P.act(lambda e: e.activation(out=lnt[:], in_=ps_ap, func=AF.Ln, bias=epsc[:], scale=scale), r + [epsc], [lnt])
            P.act(lambda e: e.activation(out=out_t[:], in_=lnt[:], func=AF.Exp, scale=-0.5), [lnt], [out_t])

        def prenorm(gi):
            st = P.bank()
            for c in range(KC):
                s = sq[c % 2]
                P.act(lambda e, c=c, s=s: e.activation(out=s[:], in_=hT[:, c, :], func=AF.Square), [hT], [s])
                P.pe(lambda e, c=c, s=s: e.matmul(st[:], lhsT=ones_b[:], rhs=s[:], start=(c == 0), stop=(c == KC - 1)),
                     [ones_b, s], [st])
            rsqrt_from(st[:], [st], 1.0 / D, rstd)
            P.free(st)
            for c in range(KC):
                P.dve(lambda e, c=c: e.scalar_tensor_tensor(out=hnT[:, c, :], in0=hT[:, c, :], scalar=pcol("ng", gi * 8 + c),
                                                            in1=rstd[:], op0=ALU.mult, op1=ALU.mult),
                      [hT, par, rstd], [hnT])

        def postnorm_residual(gi, st):
            rsqrt_from(st[:], [st], 1.0 / D, rstd)
            P.free(st)
            for c in range(KC):
                P.dve(lambda e, c=c: e.scalar_tensor_tensor(out=yT[:, c, :], in0=yT[:, c, :], scalar=pcol("ng", gi * 8 + c),
                                                            in1=rstd[:], op0=ALU.mult, op1=ALU.mult),
                      [yT, par, rstd], [yT])
                (P.pool if c % 2 == 0 else P.dve)(lambda e, c=c: e.tensor_tensor(out=hT[:, c, :], in0=hT[:, c, :], in1=yT[:, c, :], op=ALU.add),
                                                        [hT, yT], [hT])

        def proj_to_yT(slab_ids, rhs_list, nk_per=KC):
            st = P.bank()
            m = 0
            for s in slab_ids:
                sl = load_slab(s)
                sa_ = slab_ap(sl, 512)
                for q in range(4):
                    pb = P.bank()
                    for k in range(KC):
                        P.pe(lambda e, k=k, q=q, pb=pb, sa_=sa_: e.matmul(pb[:], lhsT=sa_[:, k, q * 128:(q + 1) * 128],
                                                                          rhs=rhs_list[k][0], start=(k == 0), stop=(k == KC - 1)),
                             [sl, rhs_list[k][1]], [pb])
                    s2 = sq[m % 2]
                    KV = int(os.environ.get("KVAR", "0"))
                    P.dve(lambda e, pb=pb, m=m: e.tensor_copy(out=yT[:, m, :], in_=pb[:]), [pb], [yT])
                    P.free(pb)
                    P.act(lambda e, m=m, s2=s2: e.activation(out=s2[:], in_=yT[:, m, :], func=AF.Square), [yT], [s2])
                    if KV not in (1, 2):
                        P.pe(lambda e, m=m, s2=s2: e.matmul(st[:], lhsT=ones_b[:], rhs=s2[:], start=(m == 0), stop=(m == 7)),
                             [ones_b, s2], [st])
                    m += 1
            return st

        dg_ctr = [0]

        def conv_mm(pb, src, src_t, ntap, wname, wbase):
            for k in range(ntap):
                d = dg[dg_ctr[0] % len(dg)]
                dg_ctr[0] += 1
                P.dve(lambda e, d=d, k=k: e.tensor_scalar(out=d[:], in0=ident_b[:], scalar1=pcol(wname, wbase + k), scalar2=None,
                                                           op0=ALU.mult),
                       [ident_b, par], [d])
                P.pe(lambda e, d=d, k=k: e.matmul(pb[:], lhsT=d[:], rhs=src[:, k:k + NT], start=(k == 0), stop=(k == ntap - 1)),
                     [d, src_t], [pb])

        if STOP > 0:
            for blk in range(2):
                xs = xst[blk % 2]
                P.dma("sp", lambda e, blk=blk, xs=xs: e.dma_start(out=xs[:], in_=memin[blk * 128:(blk + 1) * 128, :]), [], [xs],
                      "xst%d" % (blk % 2))
                for half in range(2):
                    pb = P.bank()
                    for q in range(4):
                        c = half * 4 + q
                        P.pe(lambda e, c=c, q=q, xs=xs, pb=pb: e.transpose(out=pb[:, q * 128:(q + 1) * 128], in_=xs[:, c * 128:(c + 1) * 128],
                                                                           identity=ident_f), [xs, cst], [pb])
                    P.dve(lambda e, half=half, blk=blk, pb=pb: e.tensor_copy(
                        out=yT[:, half * 4:half * 4 + 4, blk * 128:(blk + 1) * 128],
                        in_=pb[:].rearrange("p (q c) -> p q c", q=4)), [pb], [yT])
                    P.free(pb)
            st = P.bank()
            for c in range(KC):
                s = sq[c % 2]
                P.act(lambda e, c=c, s=s: e.activation(out=s[:, 0:MEM], in_=yT[:, c, 0:MEM], func=AF.Square), [yT], [s])
                P.pe(lambda e, c=c, s=s: e.matmul(st[:, 0:MEM], lhsT=ones_b[:], rhs=s[:, 0:MEM], start=(c == 0), stop=(c == KC - 1)),
                     [ones_b, s], [st])
            P.act(lambda e: e.activation(out=lnt[:, 0:MEM], in_=st[:, 0:MEM], func=AF.Ln, bias=epsc[:], scale=1.0 / D), [st, epsc], [lnt])
            P.act(lambda e: e.activation(out=rstd[:, 0:MEM], in_=lnt[:, 0:MEM], func=AF.Exp, scale=-0.5), [lnt], [rstd])
            P.free(st)
            for c in range(KC):
                P.dve(lambda e, c=c: e.scalar_tensor_tensor(out=hnT[:, c, 0:MEM], in0=yT[:, c, 0:MEM], scalar=pcol("mng", c),
                                                            in1=rstd[:, 0:MEM], op0=ALU.mult, op1=ALU.mult),
                      [yT, par, rstd], [hnT])
            for s in range(2):
                sl = load_slab(SL_KV + s)
                sa_ = slab_ap(sl, 512)
                for q in range(4):
                    pb = P.bank()
                    for k in range(KC):
                        P.pe(lambda e, k=k, q=q, pb=pb, sa_=sa_: e.matmul(pb[:, 0:MEM], lhsT=sa_[:, k, q * 128:(q + 1) * 128], rhs=hnT[:, k, 0:MEM],
                                                                          start=(k == 0), stop=(k == KC - 1)), [sl, hnT.cv(k)], [pb])
                    P.dve(lambda e, pb=pb, s=s, q=q: e.tensor_copy(out=KT[:, s * 4 + q, :], in_=pb[:, 0:MEM]), [pb], [KT])
                    P.free(pb)
            for s in range(2):
                sl = load_slab(SL_KV + 2 + s)
                sa_ = slab_ap(sl, 512)
                for mc in range(2):
                    pb = P.bank()
                    for k in range(KC):
                        P.pe(lambda e, k=k, mc=mc, pb=pb, sa_=sa_: e.matmul(pb[:], lhsT=hnT[:, k, mc * 128:(mc + 1) * 128], rhs=sa_[:, k, :],
                                                                            start=(k == 0), stop=(k == KC - 1)), [sl, hnT.cv(k)], [pb])
                    P.dve(lambda e, pb=pb, s=s, mc=mc: e.tensor_copy(out=Vm[:, mc, s * 512:(s + 1) * 512], in_=pb[:]), [pb], [Vm])
                    P.free(pb)

        def bc4(t, blk):
            return t[:, blk, :].unsqueeze(2).to_broadcast([128, 4, 128])

        def v3(b):
            return b[:].rearrange("p (h c) -> p h c", h=4)

        def vb3(b):
            return b[:].bitcast(BF16)[:, 0:512].rearrange("p (h c) -> p h c", h=4)

        def gdn_scalars(full):
            T = tk
            P.dve(lambda e: e.tensor_tensor(out=T["a"][:], in0=abtok[:, :, 0:4], in1=pcol("dtb", 0, 16).rearrange("p (b h) -> p b h", h=4),
                                            op=ALU.add), [abtok, par], [T["a"]])
            P.act(lambda e: e.activation(out=T["t1"][:], in_=T["a"][:], func=AF.Exp), [T["a"]], [T["t1"]])
            P.act(lambda e: e.activation(out=T["t2"][:], in_=T["t1"][:], func=AF.Ln, bias=1.0), [T["t1"]], [T["t2"]])
            P.dve(lambda e: e.scalar_tensor_tensor(out=T["g"][:], in0=T["t2"][:], scalar=-1.0, in1=expA[:].rearrange("p (b h) -> p b h", h=4),
                                                   op0=ALU.mult, op1=ALU.mult), [T["t2"], expA], [T["g"]])
            P.act(lambda e: e.activation(out=T["t1"][:], in_=abtok[:, :, 4:8], func=AF.Exp, scale=-1.0), [abtok], [T["t1"]])
            P.act(lambda e: e.activation(out=T["t2"][:], in_=T["t1"][:], func=AF.Ln, bias=1.0), [T["t1"]], [T["t2"]])
            P.dve(lambda e: e.tensor_scalar(out=T["lb"][:], in0=T["t2"][:], scalar1=-1.0, scalar2=None, op0=ALU.mult), [T["t2"]], [T["lb"]])
            P.act(lambda e: e.activation(out=T["t1"][:], in_=ssq[:, :, 4:8], func=AF.Ln, bias=epsc[:]), [ssq, epsc], [T["t1"]])
            P.dve(lambda e: e.tensor_scalar(out=T["lrk"][:], in0=T["t1"][:], scalar1=-0.5, scalar2=None, op0=ALU.mult), [T["t1"]], [T["lrk"]])
            if full:
                P.act(lambda e: e.activation(out=T["t1"][:], in_=ssq[:, :, 0:4], func=AF.Ln, bias=epsc[:]), [ssq, epsc], [T["t1"]])
                P.dve(lambda e: e.tensor_scalar(out=T["lrq"][:], in0=T["t1"][:], scalar1=-0.5, scalar2=None, op0=ALU.mult),
                      [T["t1"]], [T["lrq"]])
            pb = P.bank()
            g2 = T["g"][:].rearrange("p b h -> p (b h)")
            P.pe(lambda e: e.matmul(pb[:, 0:16], lhsT=utri, rhs=g2, start=True, stop=True), [cst, T["g"]], [pb])
            P.pe(lambda e: e.matmul(pb[:, 16:32], lhsT=ones_f[:], rhs=g2, start=True, stop=True), [ones_f, T["g"]], [pb])
            P.dve(lambda e: e.tensor_copy(out=T["gc"][:].rearrange("p b h -> p (b h)"), in_=pb[:, 0:16]), [pb], [T["gc"]])
            P.dve(lambda e: e.tensor_copy(out=T["gl"][:].rearrange("p b h -> p (b h)"), in_=pb[:, 16:32]), [pb], [T["gl"]])
            P.free(pb)
            P.dve(lambda e: e.tensor_tensor(out=T["bj"][:], in0=T["lrk"][:], in1=T["gc"][:], op=ALU.subtract), [T["lrk"], T["gc"]], [T["bj"]])
            P.dve(lambda e: e.tensor_tensor(out=T["t1"][:], in0=T["gc"][:], in1=T["lb"][:], op=ALU.add), [T["gc"], T["lb"]], [T["t1"]])
            P.dve(lambda e: e.tensor_tensor(out=T["vL"][:], in0=T["t1"][:], in1=T["lrk"][:], op=ALU.add), [T["t1"], T["lrk"]], [T["vL"]])
            P.act(lambda e: e.activation(out=T["b"][:], in_=T["lb"][:], func=AF.Exp), [T["lb"]], [T["b"]])
            P.act(lambda e: e.activation(out=T["sw"][:], in_=T["vL"][:], func=AF.Exp), [T["vL"]], [T["sw"]])
            P.dve(lambda e: e.tensor_tensor(out=T["t2"][:], in0=T["gl"][:], in1=T["bj"][:], op=ALU.add), [T["gl"], T["bj"]], [T["t2"]])
            P.act(lambda e: e.activation(out=T["skd"][:], in_=T["t2"][:], func=AF.Exp), [T["t2"]], [T["skd"]])
            P.act(lambda e: e.activation(out=T["egl"][:], in_=T["gl"][:], func=AF.Exp), [T["gl"]], [T["egl"]])
            if full:
                P.dve(lambda e: e.scalar_tensor_tensor(out=T["vA"][:], in0=T["gc"][:], scalar=lnc[:], in1=T["lrq"][:],
                                                       op0=ALU.add, op1=ALU.add), [T["gc"], lnc, T["lrq"]], [T["vA"]])
                P.act(lambda e: e.activation(out=T["so"][:], in_=T["vA"][:], func=AF.Exp), [T["vA"]], [T["so"]])

        def gdn_pre(blk, full, qT, kT_, vT_):
            T = tk
            B = BS[blk % NSET]
            kdec, bek, bv, Mx, Px, Xx, attnT, usb, wT = (B[n] for n in ("kdec", "bek", "bv", "M", "P", "X", "attnT", "usb", "wT"))
            EL = ELs[blk % 2]
            EA = EAs[blk % 2]
            cs = slice(blk * 128, (blk + 1) * 128)
            pk = P.bank()
            for h in range(4):
                P.pe(lambda e, h=h: e.transpose(out=pk[:].bitcast(BF16)[:, h * 128:(h + 1) * 128], in_=kT_[h][:, cs], identity=ident_b[:]),
                     [kT_[h], ident_b], [pk])
            P.dve(lambda e: e.tensor_tensor(out=kdec[:], in0=vb3(pk), in1=bc4(T["skd"], blk), op=ALU.mult), [pk, T["skd"]], [kdec])
            P.dve(lambda e: e.tensor_tensor(out=bek[:], in0=vb3(pk), in1=bc4(T["sw"], blk), op=ALU.mult), [pk, T["sw"]], [bek])
            P.free(pk)
            yield
            pv = P.bank()
            for h in range(4):
                P.pe(lambda e, h=h: e.transpose(out=pv[:].bitcast(BF16)[:, h * 128:(h + 1) * 128], in_=vT_[h][:, cs], identity=ident_b[:]),
                     [vT_[h], ident_b], [pv])
            P.dve(lambda e: e.tensor_tensor(out=bv[:], in0=vb3(pv), in1=bc4(T["b"], blk), op=ALU.mult), [pv, T["b"]], [bv])
            P.free(pv)
            yield

            def emat(vec, neg, out_t):
                pe_ = P.bank()
                for h in range(4):
                    o = pe_[:, h * 128:(h + 1) * 128]
                    P.pe(lambda e, h=h, o=o: e.matmul(o, lhsT=vec[:, blk, h:h + 1].to_broadcast([128, 128]), rhs=ident_f, start=True, stop=False),
                         [vec, cst], [pe_])
                    P.pe(lambda e, h=h, o=o: e.matmul(o, lhsT=ident_f, rhs=T["bj"][:, blk, h:h + 1].to_broadcast([128, 128]), start=False, stop=False),
                         [T["bj"], cst], [pe_])
                    P.pe(lambda e, h=h, o=o: e.matmul(o, lhsT=ident_f, rhs=neg, start=False, stop=True), [cst], [pe_])
                P.act(lambda e: e.activation(out=out_t[:].rearrange("p h c -> p (h c)"), in_=pe_[:], func=AF.Exp), [pe_], [out_t])
                P.free(pe_)

            emat(T["vL"], negs, EL)
            pl = P.bank()
            for h in range(4):
                P.pe(lambda e, h=h: e.matmul(pl[:, h * 128:(h + 1) * 128], lhsT=kT_[h][:, cs], rhs=kT_[h][:, cs], start=True, stop=True),
                     [kT_[h]], [pl])
            P.dve(lambda e: e.tensor_tensor(out=Mx[:], in0=v3(pl), in1=EL[:], op=ALU.mult), [pl, EL], [Mx])
            P.free(pl)
            yield
            if full:
                emat(T["vA"], negi, EA)
                pa = P.bank()
                for h in range(4):
                    P.pe(lambda e, h=h: e.matmul(pa[:, h * 128:(h + 1) * 128], lhsT=kT_[h][:, cs], rhs=qT[h][:, cs], start=True, stop=True),
                         [kT_[h], qT[h]], [pa])
                P.dve(lambda e: e.tensor_tensor(out=attnT[:], in0=v3(pa), in1=EA[:], op=ALU.mult), [pa, EA], [attnT])
                P.free(pa)
                yield
            pt = P.bank()
            for h in range(4):
                P.pe(lambda e, h=h: e.transpose(out=pt[:].bitcast(BF16)[:, h * 128:(h + 1) * 128], in_=Mx[:, h, :], identity=ident_b[:]),
                     [Mx, ident_b], [pt])
            P.act(lambda e: e.activation(out=Px[:], in_=vb3(pt), func=AF.Copy), [pt], [Px])
            P.free(pt)
            P.dve(lambda e: e.scalar_tensor_tensor(out=Xx[:], in0=Mx[:], scalar=-1.0, in1=identB4[:], op0=ALU.mult, op1=ALU.add),
                  [Mx, identB4], [Xx])
            yield
            Mc, Mo = Mx, B["M2"]
            for k in range(6):
                if k < 5:
                    pm = P.bank()
                    for h in range(4):
                        P.pe(lambda e, h=h, pm=pm, Mc=Mc: e.matmul(pm[:, h * 128:(h + 1) * 128], lhsT=Px[:, h, :], rhs=Mc[:, h, :], start=True, stop=True),
                             [Mc, Px], [pm])
                    yield
                    P.dve(lambda e, pm=pm, Mo=Mo: e.tensor_copy(out=Mo[:].rearrange("p h c -> p (h c)"), in_=pm[:]), [pm], [Mo])
                    P.free(pm)
                pp = P.bank()
                for h in range(4):
                    P.pe(lambda e, h=h, pp=pp, Mc=Mc: e.matmul(pp[:, h * 128:(h + 1) * 128], lhsT=Mc[:, h, :], rhs=Px[:, h, :], start=True, stop=True),
                         [Mc, Px], [pp])
                yield
                P.act(lambda e, pp=pp: e.activation(out=Px[:].rearrange("p h c -> p (h c)"), in_=pp[:], func=AF.Copy), [pp], [Px])
                P.free(pp)
                px = P.bank()
                for h in range(4):
                    P.pe(lambda e, h=h, px=px: e.matmul(px[:, h * 128:(h + 1) * 128], lhsT=Px[:, h, :], rhs=Xx[:, h, :], start=True, stop=True),
                         [Px, Xx], [px])
                yield
                P.dve(lambda e, px=px: e.tensor_tensor(out=Xx[:], in0=v3(px), in1=Xx[:], op=ALU.add), [px, Xx], [Xx])
                P.free(px)
                Mc, Mo = Mo, Mc
            AT = Xx
            pu = P.bank()
            for h in range(4):
                P.pe(lambda e, h=h: e.matmul(pu[:, h * 128:(h + 1) * 128], lhsT=AT[:, h, :], rhs=bv[:, h, :], start=True, stop=True), [AT, bv], [pu])
            P.act(lambda e: e.activation(out=usb[:].rearrange("p h c -> p (h c)"), in_=pu[:], func=AF.Copy), [pu], [usb])
            P.free(pu)
            yield
            pw = P.bank()
            for h in range(4):
                P.pe(lambda e, h=h: e.matmul(pw[:, h * 128:(h + 1) * 128], lhsT=bek[:, h, :], rhs=AT[:, h, :], start=True, stop=True), [AT, bek], [pw])
            P.dve(lambda e: e.tensor_copy(out=wT[:].rearrange("p h c -> p (h c)"), in_=pw[:]), [pw], [wT])
            P.free(pw)
            yield

        def gdn_scan_all(full, qT, sg):
            T = tk
            po = {}

            def main1(blk):
                B = BS[blk % NSET]
                pws = P.bank()
                for h in range(4):
                    P.pe(lambda e, h=h: e.matmul(pws[:, h * 128:(h + 1) * 128], lhsT=B["wT"][:, h, :], rhs=Sb[:, h, :], start=True, stop=True),
                         [B["wT"], Sb], [pws])
                P.dve(lambda e: e.tensor_tensor(out=vnew[:], in0=B["usb"][:], in1=v3(pws), op=ALU.subtract), [B["usb"], pws], [vnew])
                P.free(pws)

            def main2(blk):
                B = BS[blk % NSET]
                cs = slice(blk * 128, (blk + 1) * 128)
                pds = P.bank()
                for h in range(4):
                    P.pe(lambda e, h=h: e.matmul(pds[:, h * 128:(h + 1) * 128], lhsT=B["kdec"][:, h, :], rhs=vnew[:, h, :], start=True, stop=True),
                         [B["kdec"], vnew], [pds])
                if full:
                    po1 = P.bank()
                    po2 = P.bank()
                    for h in range(4):
                        P.pe(lambda e, h=h: e.matmul(po1[:, h * 128:(h + 1) * 128], lhsT=qT[h][:, cs], rhs=Sb[:, h, :], start=True, stop=True),
                             [qT[h], Sb], [po1])
                    for h in range(4):
                        P.pe(lambda e, h=h: e.matmul(po2[:, h * 128:(h + 1) * 128], lhsT=B["attnT"][:, h, :], rhs=vnew[:, h, :], start=True, stop=True),
                             [B["attnT"], vnew], [po2])
                    po[blk] = (po1, po2)
                for h in range(4):
                    P.dve(lambda e, h=h: e.scalar_tensor_tensor(out=Sb[:, h, :], in0=Sf[:, h, :], scalar=T["egl"][:, blk, h:h + 1],
                                                                in1=pds[:, h * 128:(h + 1) * 128], op0=ALU.mult, op1=ALU.add),
                          [Sf, T["egl"], pds], [Sb])
                for h in range(4):
                    P.dve(lambda e, h=h: e.scalar_tensor_tensor(out=Sf[:, h, :], in0=Sf[:, h, :], scalar=T["egl"][:, blk, h:h + 1],
                                                                in1=pds[:, h * 128:(h + 1) * 128], op0=ALU.mult, op1=ALU.add),
                          [Sf, T["egl"], pds], [Sf])
                P.free(pds)

            def tail1(blk):
                po1, po2 = po[blk]
                P.dve(lambda e: e.tensor_tensor(out=o1s[:], in0=v3(po1), in1=bc4(T["so"], blk), op=ALU.mult), [po1, T["so"]], [o1s])
                P.dve(lambda e: e.tensor_tensor(out=osb[:], in0=v3(po2), in1=o1s[:], op=ALU.add), [po2, o1s], [osb])
                P.free(po1, po2)
                P.pool(lambda e: e.memset(ssqo[:], 0.0), [], [ssqo])
                for h in range(4):
                    P.act(lambda e, h=h: e.activation(out=osq[:, h, :], in_=osb[:, h, :], func=AF.Square, accum_out=ssqo[:, h:h + 1]),
                          [osb, ssqo], [osq, ssqo])
                P.act(lambda e: e.activation(out=rso[:], in_=ssqo[:], func=AF.Ln, bias=epsc[:], scale=1.0 / 128), [ssqo, epsc], [rso])
                P.act(lambda e: e.activation(out=rso[:], in_=rso[:], func=AF.Exp, scale=-0.5), [rso], [rso])
                (P.dve if blk == NB - 1 else P.pool)(
                    lambda e: e.tensor_tensor(out=onb[:], in0=osb[:], in1=rso[:].unsqueeze(2).to_broadcast([128, 4, 128]), op=ALU.mult),
                    [osb, rso], [onb])

            def tail2(blk):
                cs = slice(blk * 128, (blk + 1) * 128)
                pot = P.bank()
                for h in range(4):
                    P.pe(lambda e, h=h: e.transpose(out=pot[:].bitcast(BF16)[:, h * 128:(h + 1) * 128], in_=onb[:, h, :], identity=ident_b[:]),
                         [onb, ident_b], [pot])
                if blk == NB - 1:
                    for h in range(4):
                        P.dve(lambda e, h=h: e.scalar_tensor_tensor(out=ocat[h][:, cs], in0=pot[:].bitcast(BF16)[:, h * 128:(h + 1) * 128],
                                                                    scalar=pcol("gng"), in1=sg[h][:, cs], op0=ALU.mult, op1=ALU.mult),
                              [pot, par, sg[h]], [ocat[h]])
                    P.free(pot)
                    return
                P.act(lambda e: e.activation(out=otmp[:], in_=vb3(pot), func=AF.Copy, scale=pcol("gng")), [pot, par], [otmp])
                P.free(pot)
                for h in range(4):
                    P.pool(lambda e, h=h: e.tensor_tensor(out=ocat[h][:, cs], in0=otmp[:, h, :], in1=sg[h][:, cs], op=ALU.mult),
                           [otmp, sg[h]], [ocat[h]])

            for blk in range(NB):
                main1(blk)
                yield
                if full and blk >= 1:
                    tail1(blk - 1)
                    yield
                main2(blk)
                yield
                if full and blk >= 1:
                    tail2(blk - 1)
                    yield
            if full:
                tail1(NB - 1)
                yield
                tail2(NB - 1)
                yield

        def interleave(gens):
            gens = list(gens)
            while gens:
                for g in list(gens):
                    try:
                        next(g)
                    except StopIteration:
                        gens.remove(g)


        def prenorm_h(gi, c0, c1, hb):
            st = P.bank()
            for c in range(KC):
                s_ = sq[c % 2]
                P.act(lambda e, c=c, s_=s_: e.activation(out=s_[:, c0:c1], in_=hT[:, c, c0:c1], func=AF.Square), [hT.cv(c)], [s_])
                P.pe(lambda e, c=c, s_=s_: e.matmul(st[:, c0:c1], lhsT=ones_b[:], rhs=s_[:, c0:c1], start=(c == 0), stop=(c == KC - 1)),
                     [ones_b, s_], [st])
            P.act(lambda e: e.activation(out=lnt[:, c0:c1], in_=st[:, c0:c1], func=AF.Ln, bias=epsc[:], scale=1.0 / D), [st, epsc], [lnt])
            P.act(lambda e: e.activation(out=rstd[:, c0:c1], in_=lnt[:, c0:c1], func=AF.Exp, scale=-0.5), [lnt], [rstd])
            P.free(st)
            for c in range(KC):
                P.dve(lambda e, c=c: e.scalar_tensor_tensor(out=hnT[:, c, c0:c1], in0=hT[:, c, c0:c1], scalar=pcol("ng", gi * 8 + c),
                                                            in1=rstd[:, c0:c1], op0=ALU.mult, op1=ALU.mult),
                      [hT.cv(c), par, rstd], [hnT.cv(c)])

        def postnorm_h(gi, st, c0, c1, hb):
            P.act(lambda e: e.activation(out=lnt[:, c0:c1], in_=st[:, c0:c1], func=AF.Ln, bias=epsc[:], scale=1.0 / D), [st, epsc], [lnt])
            P.act(lambda e: e.activation(out=rstd[:, c0:c1], in_=lnt[:, c0:c1], func=AF.Exp, scale=-0.5), [lnt], [rstd])
            P.free(st)
            for c in range(KC):
                P.dve(lambda e, c=c: e.scalar_tensor_tensor(out=yT[:, c, c0:c1], in0=yT[:, c, c0:c1], scalar=pcol("ng", gi * 8 + c),
                                                            in1=rstd[:, c0:c1], op0=ALU.mult, op1=ALU.mult),
                      [yT.cv(c), par, rstd], [yT.cv(c)])
                (P.pool if c % 2 == 0 else P.dve)(lambda e, c=c: e.tensor_tensor(out=hT[:, c, c0:c1], in0=hT[:, c, c0:c1], in1=yT[:, c, c0:c1], op=ALU.add),
                                                        [hT.cv(c), yT.cv(c)], [hT.cv(c)])

        def proj_h(slab_ids, rhs_tiles, c0, c1, hb):
            st = P.bank()
            m = 0
            pend = None

            def stat_mm(m_, s2_):
                P.pe(lambda e: e.matmul(st[:, c0:c1], lhsT=ones_b[:], rhs=s2_[:, c0:c1], start=(m_ == 0), stop=(m_ == 7)),
                     [ones_b, s2_], [st])

            for s_ in slab_ids:
                sl = load_slab(s_)
                sa_ = slab_ap(sl, 512)
                for q in range(4):
                    pb = P.bank()
                    for k in range(KC):
                        P.pe(lambda e, k=k, q=q, pb=pb, sa_=sa_: e.matmul(pb[:, c0:c1], lhsT=sa_[:, k, q * 128:(q + 1) * 128],
                                                                          rhs=rhs_tiles[k][:, c0:c1], start=(k == 0), stop=(k == KC - 1)),
                             [sl, rhs_tiles[k]], [pb])
                    if pend is not None:
                        stat_mm(*pend)
                    s2 = sq[m % 2]
                    P.dve(lambda e, pb=pb, m=m: e.tensor_copy(out=yT[:, m, c0:c1], in_=pb[:, c0:c1]), [pb], [yT.cv(m)])
                    P.free(pb)
                    P.act(lambda e, m=m, s2=s2: e.activation(out=s2[:, c0:c1], in_=yT[:, m, c0:c1], func=AF.Square), [yT.cv(m)], [s2])
                    pend = (m, s2)
                    m += 1
            stat_mm(*pend)
            return st

        def xattn_h(c0, c1, hb):
            qx = big[0:8]
            ox = big[8:16]
            m = 0
            for s_ in range(2):
                sl = load_slab(SL_Q + s_)
                sa_ = slab_ap(sl, 512)
                for q in range(4):
                    pb = P.bank()
                    for k in range(KC):
                        P.pe(lambda e, k=k, q=q, pb=pb, sa_=sa_: e.matmul(pb[:, c0:c1], lhsT=sa_[:, k, q * 128:(q + 1) * 128], rhs=hnT[:, k, c0:c1],
                                                                          start=(k == 0), stop=(k == KC - 1)), [sl, hnT.cv(k)], [pb])
                    P.act(lambda e, pb=pb, m=m: e.activation(out=qx[m][:, c0:c1], in_=pb[:, c0:c1], func=AF.Copy, scale=1.0 / 16.0), [pb], [qx[m]])
                    P.free(pb)
                    m += 1
            def xs_scores(h):
                pts = [ubuf[(2 * h) % 4], ubuf[(2 * h + 1) % 4]]
                for mc in range(2):
                    pb = P.bank()
                    for dc in range(2):
                        P.pe(lambda e, dc=dc, mc=mc, pb=pb: e.matmul(pb[:, c0:c1], lhsT=KT[:, 2 * h + dc, mc * 128:(mc + 1) * 128],
                                                                     rhs=qx[2 * h + dc][:, c0:c1], start=(dc == 0), stop=(dc == 1)),
                             [KT, qx[2 * h + dc]], [pb])
                    P.act(lambda e, pb=pb, mc=mc: e.activation(out=pts[mc][:, c0:c1], in_=pb[:, c0:c1], func=AF.Exp), [pb], [pts[mc]])
                    P.free(pb)

            def xs_pv(h):
                pts = [ubuf[(2 * h) % 4], ubuf[(2 * h + 1) % 4]]
                pd = P.bank()
                for mc in range(2):
                    P.pe(lambda e, mc=mc: e.matmul(pd[:, c0:c1], lhsT=ones_b[:], rhs=pts[mc][:, c0:c1], start=(mc == 0), stop=(mc == 1)),
                         [ones_b, pts[mc]], [pd])
                pbs_ = []
                for dvc in range(2):
                    pb = P.bank()
                    for mc in range(2):
                        P.pe(lambda e, mc=mc, dvc=dvc, pb=pb: e.matmul(pb[:, c0:c1], lhsT=Vm[:, mc, (2 * h + dvc) * 128:(2 * h + dvc + 1) * 128],
                                                                       rhs=pts[mc][:, c0:c1], start=(mc == 0), stop=(mc == 1)),
                             [Vm, pts[mc]], [pb])
                    pbs_.append(pb)
                P.act(lambda e: e.activation(out=rden[:, c0:c1], in_=pd[:, c0:c1], func=AF.Ln), [pd], [rden])
                P.act(lambda e: e.activation(out=rden[:, c0:c1], in_=rden[:, c0:c1], func=AF.Exp, scale=-1.0), [rden], [rden])
                P.free(pd)
                for dvc in range(2):
                    pb = pbs_[dvc]
                    P.dve(lambda e, pb=pb, dvc=dvc: e.tensor_tensor(out=ox[2 * h + dvc][:, c0:c1], in0=pb[:, c0:c1], in1=rden[:, c0:c1], op=ALU.mult),
                          [pb, rden], [ox[2 * h + dvc]])
                    P.free(pb)

            for h in range(5):
                if h < 4:
                    xs_scores(h)
                if h >= 1:
                    xs_pv(h - 1)
            return proj_h([SL_O, SL_O + 1], ox, c0, c1, hb)

        def ffn_h(c0, c1, hb, only_up):
            curu = {}
            n = c1 - c0

            def ffn_up_mm(i):
                s_, q = i // 2, i % 2
                if q == 0:
                    curu["sl"] = load_slab(SL_UP + s_)
                sl = curu["sl"]
                sa_ = slab_ap(sl, 512)
                pa_ = P.bank()
                pb_ = P.bank()
                for k in range(KC):
                    P.pe(lambda e, k=k: e.matmul(pa_[:, c0:c1], lhsT=sa_[:, k, q * 128:(q + 1) * 128], rhs=hnT[:, k, c0:c1],
                                                 start=(k == 0), stop=(k == KC - 1)), [sl, hnT.cv(k)], [pa_])
                for k in range(KC):
                    P.pe(lambda e, k=k: e.matmul(pb_[:, c0:c1], lhsT=sa_[:, k, 256 + q * 128:256 + (q + 1) * 128], rhs=hnT[:, k, c0:c1],
                                                 start=(k == 0), stop=(k == KC - 1)), [sl, hnT.cv(k)], [pb_])
                return pa_, pb_

            def ffn_up_ev(i, pa_, pb_):
                ua, ub = ubuf[2 * (i % 2)], ubuf[2 * (i % 2) + 1]
                P.pool(lambda e: e.tensor_copy(out=ua[:, c0:c0 + 2], in_=uhist[:, i, :]), [uhist], [ua])
                P.pool(lambda e: e.tensor_copy(out=ub[:, c0:c0 + 2], in_=uhist[:, 22 + i, :]), [uhist], [ub])
                P.act(lambda e: e.activation(out=ua[:, 2 + c0:2 + c1], in_=pa_[:, c0:c1], func=AF.Copy), [pa_], [ua])
                P.dve(lambda e: e.tensor_copy(out=ub[:, 2 + c0:2 + c1], in_=pb_[:, c0:c1]), [pb_], [ub])
                P.free(pa_, pb_)
                P.pool(lambda e: e.tensor_copy(out=uhist[:, i, :], in_=ua[:, c1:c1 + 2]), [ua], [uhist])
                P.pool(lambda e: e.tensor_copy(out=uhist[:, 22 + i, :], in_=ub[:, c1:c1 + 2]), [ub], [uhist])

            def ffn_conv(i):
                ua, ub = ubuf[2 * (i % 2)], ubuf[2 * (i % 2) + 1]
                ds = []
                for j in range(6):
                    d = dg[dg_ctr[0] % len(dg)]
                    dg_ctr[0] += 1
                    wb = (i * 3 + j) if j < 3 else ((22 + i) * 3 + j - 3)
                    P.dve(lambda e, d=d, wb=wb: e.tensor_scalar(out=d[:], in0=ident_b[:], scalar1=pcol("fcw", wb), scalar2=None, op0=ALU.mult),
                          [ident_b, par], [d])
                    ds.append(d)
                pca = P.bank()
                pcb = P.bank()
                for k in range(3):
                    P.pe(lambda e, k=k: e.matmul(pca[:, c0:c1], lhsT=ds[k][:], rhs=ua[:, c0 + k:c1 + k], start=(k == 0), stop=(k == 2)),
                         [ds[k], ua], [pca])
                for k in range(3):
                    P.pe(lambda e, k=k: e.matmul(pcb[:, c0:c1], lhsT=ds[3 + k][:], rhs=ub[:, c0 + k:c1 + k], start=(k == 0), stop=(k == 2)),
                         [ds[3 + k], ub], [pcb])
                return pca, pcb

            def ffn_act_ev(i, pca, pcb):
                sat = sa[i % 2]
                P.act(lambda e: e.activation(out=sat[:, c0:c1], in_=pca[:, c0:c1], func=AF.Silu, bias=pcol("fcb", i)), [pca, par], [sat])
                P.dve(lambda e: e.scalar_tensor_tensor(out=big[i][:, c0:c1], in0=pcb[:, c0:c1], scalar=pcol("fcb", 22 + i), in1=sat[:, c0:c1],
                                                       op0=ALU.add, op1=ALU.mult), [pcb, par, sat], [big[i]])
                P.free(pca, pcb)

            if only_up:
                for i in range(22):
                    ffn_up_ev(i, *ffn_up_mm(i))
                return None
            for i in range(23):
                if i < 22:
                    bk = ffn_up_mm(i)
                if i >= 1:
                    pc = ffn_conv(i - 1)
                if i < 22:
                    ffn_up_ev(i, *bk)
                if i >= 1:
                    ffn_act_ev(i - 1, *pc)
            st = P.bank()
            pendd = []
            for ng in range(4):
                pbs = [P.bank(), P.bank()]
                for kh in range(2):
                    sl = load_slab(SL_DN + ng * 2 + kh)
                    sa_ = slab_ap(sl, 256, 11)
                    for m2 in range(2):
                        for k in range(11):
                            kk = kh * 11 + k
                            P.pe(lambda e, k=k, kk=kk, m2=m2, pbs=pbs, sa_=sa_: e.matmul(pbs[m2][:, c0:c1], lhsT=sa_[:, k, m2 * 128:(m2 + 1) * 128],
                                                                                        rhs=big[kk][:, c0:c1], start=(kk == 0), stop=(kk == 21)),
                                 [sl, big[kk]], [pbs[m2]])
                for (m_, s2_) in pendd:
                    P.pe(lambda e, m_=m_, s2_=s2_: e.matmul(st[:, c0:c1], lhsT=ones_b[:], rhs=s2_[:, c0:c1], start=(m_ == 0), stop=(m_ == 7)),
                         [ones_b, s2_], [st])
                pendd = []
                for m2 in range(2):
                    m = ng * 2 + m2
                    s2 = sq[m % 2]
                    P.dve(lambda e, m=m, pb=pbs[m2]: e.tensor_copy(out=yT[:, m, c0:c1], in_=pb[:, c0:c1]), [pbs[m2]], [yT.cv(m)])
                    P.act(lambda e, s2=s2, m=m: e.activation(out=s2[:, c0:c1], in_=yT[:, m, c0:c1], func=AF.Square), [yT.cv(m)], [s2])
                    pendd.append((m, s2))
                P.free(*pbs)
            for (m_, s2_) in pendd:
                P.pe(lambda e, m_=m_, s2_=s2_: e.matmul(st[:, c0:c1], lhsT=ones_b[:], rhs=s2_[:, c0:c1], start=(m_ == 0), stop=(m_ == 7)),
                     [ones_b, s2_], [st])
            return st

        def store_h(c0, c1, hb, out_idx):
            for blk in range(c0 // 128, c1 // 128):
                for half in range(2):
                    pb = P.bank()
                    for q in range(4):
                        c = half * 4 + q
                        P.pe(lambda e, c=c, q=q, blk=blk, pb=pb: e.transpose(out=pb[:, q * 128:(q + 1) * 128], in_=hT[:, c, blk * 128:(blk + 1) * 128],
                                                                             identity=ident_f), [hT.cv(c), cst], [pb])
                    copy_ev(yT[:, 2 * blk + half, :], pb[:], [pb], [yT.cv(2 * blk + half)])
                    P.free(pb)
                r0 = out_idx * NT + blk * 128
                P.dma("sp", lambda e, blk=blk, r0=r0: e.dma_start(out=yout[r0:r0 + 128, :].rearrange("p (a c) -> p a c", a=2),
                                                                   in_=yT[:, 2 * blk:2 * blk + 2, :]),
                      [yT.cv(2 * blk), yT.cv(2 * blk + 1)], [], "ost%d" % blk)

        def back_schedule(halves, out_idx):
            halo = out_idx is None
            if len(halves) == 1:
                (c0, c1, hb) = halves[0]
                st = proj_h([SL_OUT, SL_OUT + 1], ocat, c0, c1, hb)
                postnorm_h(1, st, c0, c1, hb)
                prenorm_h(2, c0, c1, hb)
                st = xattn_h(c0, c1, hb)
                postnorm_h(3, st, c0, c1, hb)
                prenorm_h(4, c0, c1, hb)
                st = ffn_h(c0, c1, hb, halo)
                if not halo:
                    postnorm_h(5, st, c0, c1, hb)
                    store_h(c0, c1, hb, out_idx)
            else:
                A, B = halves
                stA = proj_h([SL_OUT, SL_OUT + 1], ocat, *A)
                postnorm_h(1, stA, *A)
                stB = proj_h([SL_OUT, SL_OUT + 1], ocat, *B)
                prenorm_h(2, *A)
                postnorm_h(1, stB, *B)
                ckpt(7)
                stA = xattn_h(*A)
                prenorm_h(2, *B)
                postnorm_h(3, stA, *A)
                stB = xattn_h(*B)
                prenorm_h(4, *A)
                postnorm_h(3, stB, *B)
                ckpt(8)
                stA = ffn_h(A[0], A[1], A[2], False)
                prenorm_h(4, *B)
                postnorm_h(5, stA, *A)
                stB = ffn_h(B[0], B[1], B[2], False)
                store_h(A[0], A[1], A[2], out_idx)
                postnorm_h(5, stB, *B)
                ckpt(9)
                store_h(B[0], B[1], B[2], out_idx)
            if halo:
                P.pool(lambda e: e.tensor_scalar(out=uhist[:], in0=uhist[:], scalar1=pcol("flag"), scalar2=None, op0=ALU.mult), [uhist, par], [uhist])

        prefetched = set()

        def tile(ti, full, out_idx):
            tok0 = ti * NT
            for blk in range(NB):
                xs = xst[blk % 2]
                if not (blk < 2 and ti in prefetched):
                    P.dma("sp", lambda e, blk=blk, xs=xs: e.dma_start(out=xs[:], in_=xin[tok0 + blk * 128: tok0 + (blk + 1) * 128, :]),
                          [], [xs], "xst%d" % (blk % 2))
                for half in range(2):
                    pb = P.bank()
                    for q in range(4):
                        c = half * 4 + q
                        P.pe(lambda e, c=c, q=q, xs=xs, pb=pb: e.transpose(out=pb[:, q * 128:(q + 1) * 128], in_=xs[:, c * 128:(c + 1) * 128],
                                                                           identity=ident_f), [xs, cst], [pb])
                    copy_ev(hT[:, half * 4:half * 4 + 4, blk * 128:(blk + 1) * 128], pb[:].rearrange("p (q c) -> p q c", q=4), [pb], [hT])
                    P.free(pb)
            if ti + 1 < NS + 1 + NF:
                for blk in range(2):
                    xs = xst[blk]
                    t1 = (ti + 1) * NT
                    P.dma("sp", lambda e, blk=blk, xs=xs, t1=t1: e.dma_start(out=xs[:], in_=xin[t1 + blk * 128: t1 + (blk + 1) * 128, :]),
                          [], [xs], "xst%d" % blk)
                prefetched.add(ti + 1)
            ckpt(1)
            prenorm_h(0, 0, NT, None)
            ckpt(2)
            qs, ks, vs, sgs = big[0:4], big[4:8], big[8:12], big[12:16]
            chunks = []
            for s_, dst in ([(0, qs)] if full else []) + [(1, ks), (2, vs)]:
                for q in range(4):
                    chunks.append((s_, q, dst))
            cur = {}

            def qkv_proj_mm(s_, q):
                if q == 0:
                    cur["sl"] = load_slab(SL_IN + s_)
                sl = cur["sl"]
                sa_ = slab_ap(sl, 512)
                pb = P.bank()
                for k in range(KC):
                    P.pe(lambda e, k=k: e.matmul(pb[:], lhsT=sa_[:, k, q * 128:(q + 1) * 128], rhs=hnT[:, k, :],
                                                 start=(k == 0), stop=(k == KC - 1)), [sl, hnT.cv(k)], [pb])
                return pb

            def qkv_proj_ev(s_, q, pb):
                ch = s_ * 4 + q
                z = zc[ch % 4]
                P.pool(lambda e: e.tensor_copy(out=z[:, 0:3], in_=zhist[:, ch, :]), [zhist], [z])
                copy_ev(z[:, 3:3 + NT], pb[:], [pb], [z])
                P.free(pb)
                P.pool(lambda e: e.tensor_copy(out=zhist[:, ch, :], in_=z[:, NT:NT + 3]), [z], [zhist])

            def qkv_conv_mm(s_, q, dst):
                ch = s_ * 4 + q
                z = zc[ch % 4]
                pc = P.bank()
                conv_mm(pc, z, z, 4, "gcw", ch * 4)
                return pc

            def qkv_conv_ev(q, dst, pc):
                P.act(lambda e: e.activation(out=dst[q][:], in_=pc[:], func=AF.Silu), [pc], [dst[q]])
                P.free(pc)

            for i in range(len(chunks) + 1):
                if i < len(chunks):
                    pbq = qkv_proj_mm(chunks[i][0], chunks[i][1])
                if i >= 1:
                    pcq = qkv_conv_mm(*chunks[i - 1])
                if i < len(chunks):
                    qkv_proj_ev(chunks[i][0], chunks[i][1], pbq)
                if i >= 1:
                    qkv_conv_ev(chunks[i - 1][1], chunks[i - 1][2], pcq)
            pab = P.bank()
            for blk in range(NB):
                for k in range(KC):
                    P.pe(lambda e, k=k, blk=blk: e.matmul(pab[:, blk * 8:(blk + 1) * 8], lhsT=hnT[:, k, blk * 128:(blk + 1) * 128], rhs=wab[:, k, :],
                                                          start=(k == 0), stop=(k == KC - 1)), [hnT, wab], [pab])
            P.dve(lambda e: e.tensor_copy(out=abtok[:].rearrange("p b c -> p (b c)"), in_=pab[:, 0:NB * 8]), [pab], [abtok])
            P.free(pab)
            pss = P.bank()
            lst = ([(qs[h], h) for h in range(4)] if full else []) + [(ks[h], 4 + h) for h in range(4)]
            for i, (src, col) in enumerate(lst):
                s2 = sq[i % 2]
                P.pool(lambda e, src=src, s2=s2: e.tensor_tensor(out=s2[:], in0=src[:], in1=src[:], op=ALU.mult), [src], [s2])
                for blk in range(NB):
                    P.pe(lambda e, blk=blk, s2=s2, col=col: e.matmul(pss[:, blk * 8 + col:blk * 8 + col + 1], lhsT=s2[:, blk * 128:(blk + 1) * 128],
                                                                      rhs=ones_b[:, 0:1], start=True, stop=True), [s2, ones_b], [pss])
            if full:
                P.dve(lambda e: e.tensor_copy(out=ssq[:].rearrange("p b c -> p (b c)"), in_=pss[:, 0:NB * 8]), [pss], [ssq])
            else:
                P.dve(lambda e: e.tensor_copy(out=ssq[:, :, 4:8], in_=pss[:, 0:NB * 8].rearrange("p (b c) -> p b c", c=8)[:, :, 4:8]), [pss], [ssq])
            P.free(pss)
            ckpt(3)
            gdn_scalars(full)
            ckpt(4)

            def filler():
                sl = load_slab(SL_IN + 3)
                sa_ = slab_ap(sl, 512)
                for q in range(4):
                    pb = P.bank()
                    for k in range(KC):
                        P.pe(lambda e, k=k, q=q, pb=pb, sa_=sa_: e.matmul(pb[:], lhsT=sa_[:, k, q * 128:(q + 1) * 128], rhs=hnT[:, k, :],
                                                                          start=(k == 0), stop=(k == KC - 1)), [sl, hnT.cv(k)], [pb])
                        if k % 4 == 3:
                            yield
                    P.act(lambda e, pb=pb, q=q: e.activation(out=sgs[q][:], in_=pb[:], func=AF.Silu), [pb], [sgs[q]])
                    P.free(pb)
                for s_ in range(2):
                    sl = load_slab(SL_IN + 4 + s_)
                    sa_ = slab_ap(sl, 512)
                    for q in range(2):
                        ch = 2 * s_ + q
                        pa_ = P.bank()
                        pb_ = P.bank()
                        for k in range(KC):
                            P.pe(lambda e, k=k, q=q, pa_=pa_, sa_=sa_: e.matmul(pa_[:], lhsT=sa_[:, k, q * 128:(q + 1) * 128], rhs=hnT[:, k, :],
                                                                                start=(k == 0), stop=(k == KC - 1)), [sl, hnT.cv(k)], [pa_])
                            if k % 4 == 3:
                                yield
                        for k in range(KC):
                            P.pe(lambda e, k=k, q=q, pb_=pb_, sa_=sa_: e.matmul(pb_[:], lhsT=sa_[:, k, 256 + q * 128:256 + (q + 1) * 128], rhs=hnT[:, k, :],
                                                                                start=(k == 0), stop=(k == KC - 1)), [sl, hnT.cv(k)], [pb_])
                            if k % 4 == 3:
                                yield
                        cb = cbuf[ch]
                        P.pool(lambda e, cb=cb: e.tensor_copy(out=cb[:, 0:30], in_=cb[:, NT:NT + 30]), [cb], [cb])
                        P.act(lambda e, pb_=pb_: e.activation(out=lnt[:], in_=pb_[:], func=AF.Sigmoid), [pb_], [lnt])
                        P.dve(lambda e, pa_=pa_, cb=cb: e.tensor_tensor(out=cb[:, 30:30 + NT], in0=pa_[:], in1=lnt[:], op=ALU.mult), [pa_, lnt], [cb])
                        P.free(pa_, pb_)
                pmean = P.bank()
                pe2 = P.bank()
                for ch in range(4):
                    pc = P.bank()
                    for k in range(31):
                        d = dg[dg_ctr[0] % len(dg)]
                        dg_ctr[0] += 1
                        P.dve(lambda e, d=d, k=k, ch=ch: e.tensor_scalar(out=d[:], in0=ident_b[:], scalar1=pcol("cdw", ch * 31 + k), scalar2=None,
                                                                         op0=ALU.mult), [ident_b, par], [d])
                        P.pe(lambda e, d=d, k=k, ch=ch, pc=pc: e.matmul(pc[:], lhsT=d[:], rhs=cbuf[ch][:, k:k + NT], start=(k == 0), stop=(k == 30)),
                             [d, cbuf[ch]], [pc])
                        if k % 4 == 3:
                            yield
                    P.act(lambda e, pc=pc, ch=ch: e.activation(out=yT[:, ch, :], in_=pc[:], func=AF.Identity, bias=pcol("cdb", ch)), [pc, par], [yT])
                    s2 = sq[ch % 2]
                    P.act(lambda e, pc=pc, ch=ch, s2=s2: e.activation(out=s2[:], in_=pc[:], func=AF.Square, bias=pcol("cdb", ch)), [pc, par], [s2])
                    P.free(pc)
                    P.pe(lambda e, ch=ch: e.matmul(pmean[:], lhsT=ones_f[:], rhs=yT[:, ch, :], start=(ch == 0), stop=(ch == 3)), [ones_f, yT], [pmean])
                    P.pe(lambda e, ch=ch, s2=s2: e.matmul(pe2[:], lhsT=ones_b[:], rhs=s2[:], start=(ch == 0), stop=(ch == 3)), [ones_b, s2], [pe2])
                    yield
                P.act(lambda e: e.activation(out=msb[:], in_=pmean[:], func=AF.Copy, scale=1.0 / 512), [pmean], [msb])
                P.pool(lambda e: e.tensor_tensor(out=var[:], in0=msb[:], in1=msb[:], op=ALU.mult), [msb], [var])
                P.dve(lambda e: e.scalar_tensor_tensor(out=var[:], in0=pe2[:], scalar=1.0 / 512, in1=var[:], op0=ALU.mult, op1=ALU.subtract),
                      [pe2, var], [var])
                P.free(pmean, pe2)
                P.act(lambda e: e.activation(out=lnt[:], in_=var[:], func=AF.Ln, bias=epsc[:]), [var, epsc], [lnt])
                P.act(lambda e: e.activation(out=rstd[:], in_=lnt[:], func=AF.Exp, scale=-0.5), [lnt], [rstd])
                yield
                for ch in range(4):
                    P.pool(lambda e, ch=ch: e.tensor_tensor(out=yT[:, ch, :], in0=yT[:, ch, :], in1=msb[:], op=ALU.subtract), [yT, msb], [yT])
                    P.dve(lambda e, ch=ch: e.tensor_tensor(out=yT[:, ch, :], in0=yT[:, ch, :], in1=rstd[:], op=ALU.mult), [yT, rstd], [yT])
                    P.act(lambda e, ch=ch: e.activation(out=ocat[4 + ch][:], in_=yT[:, ch, :], func=AF.Silu, bias=pcol("clb", ch), scale=pcol("clg", ch)),
                          [yT, par], [ocat[4 + ch]])
                    yield

            fil = filler() if full else None

            def run_with_filler(gens, nfill, every):
                nonlocal fil
                gens = list(gens)
                rnd = 0
                while gens:
                    rnd += 1
                    if rnd % every == 0:
                        for _ in range(nfill):
                            if fil is not None:
                                try:
                                    next(fil)
                                except StopIteration:
                                    fil = None
                    for g in list(gens):
                        try:
                            next(g)
                        except StopIteration:
                            gens.remove(g)

            run_with_filler([gdn_pre(blk, full, qs, ks, vs) for blk in range(NB)], 1, 2)
            ckpt(5)
            run_with_filler([gdn_scan_all(full, qs, sgs)], 3, 1)
            if fil is not None:
                for _ in fil:
                    pass
            if not full:
                return
            ckpt(6)
            if out_idx is None:
                halves = [(NT - 128, NT, None)]
            else:
                halves = [(0, NT, None)]
            back_schedule(halves, out_idx)

        try:
            ti = 0
            per = (len(late_list) + max(NS, 1) - 1) // max(NS, 1)
            for _ in range(NS):
                tile(ti, False, None)
                issue_late(per)
                ti += 1
            issue_late(len(late_list))
            tile(ti, True, None)
            ti += 1
            for i in range(NF):
                tile(ti, True, i)
                ti += 1
            P.final = ["ost0", "ost1", "ost2", "ost3"]
        except StopBuild:
            P.final = [k for k in P.dma_cnt.keys()]
        P.emit()
    return nc


def make_consts():
    c = np.zeros((128, 4, 128), np.float32)
    c[:, 0, :] = np.eye(128, dtype=np.float32)
    j = np.arange(128)[:, None]
    cc = np.arange(128)[None, :]
    c[:, 1, :] = (j <= cc).astype(np.float32)
    c[:, 2, :] = np.where(cc >= j, 0.0, NEGV)
    c[:, 3, :] = np.where(cc > j, 0.0, NEGV)
    return c.reshape(128, 512)


def make_params(inp, flag):
    p = np.zeros((128, NPAR), np.float32)

    def put(name, arr):
        p[:, PO[name]:PO[name] + arr.shape[1]] = arr

    def chunked(v):
        return np.ascontiguousarray(v.reshape(-1, 128).T)

    ng = inp["norm_g"][0]
    put("ng", np.concatenate([chunked(ng[i]) for i in range(6)], axis=1))
    gcw = inp["gdn_conv_w"][0]
    put("gcw", np.ascontiguousarray(gcw.reshape(4, 12, 128).transpose(2, 1, 0)).reshape(128, 48))
    cdw = inp["cfm_dw_w"][0]
    put("cdw", np.ascontiguousarray(cdw.reshape(31, 4, 128).transpose(2, 1, 0)).reshape(128, 124))
    put("cdb", chunked(inp["cfm_dw_b"][0]))
    put("clg", chunked(inp["cfm_ln_g"][0]))
    put("clb", chunked(inp["cfm_ln_b"][0]))
    fcw = inp["ffn_conv_w"][0]
    put("fcw", np.ascontiguousarray(fcw.reshape(3, 44, 128).transpose(2, 1, 0)).reshape(128, 132))
    put("fcb", chunked(inp["ffn_conv_b"][0]))
    put("gng", inp["gdn_norm_g"][0].reshape(128, 1))
    put("alog", np.tile(inp["gdn_a_log"][0][None, :], (128, 4)))
    put("dtb", np.tile(inp["gdn_dt_bias"][0][None, :], (128, 4)))
    put("mng", chunked(inp["mem_norm_g"][0]))
    p[:, PO["flag"]] = flag
    return p


_NC_CACHE = {}


def run(inputs, NF, NS, n_batch, dbg=False):
    inp = {k: np.asarray(v, dtype=np.float32) for k, v in inputs.items()}
    key = (NF, NS, dbg)
    if key not in _NC_CACHE:
        _NC_CACHE[key] = build(NF, NS, dbg)
    nc = _NC_CACHE[key]
    half = NF * NT
    nprev = (NS + 1) * NT
    consts = make_consts()
    in_maps = []
    for b in range(n_batch):
        for j in range(2):
            xall = np.zeros((nprev + half, D), np.float32)
            if j == 1:
                xall[:nprev] = inp["x"][b, 0:half]
            xall[nprev:] = inp["x"][b, j * half:(j + 1) * half]
            in_maps.append({
                "xin": xall, "memin": np.ascontiguousarray(inp["mem"][b]),
                "params": make_params(inp, float(j)), "consts": consts,
                "w_in": np.ascontiguousarray(inp["w_in"][0]), "w_out": np.ascontiguousarray(inp["w_out"][0]),
                "w_q": np.ascontiguousarray(inp["xa_w_q"][0]), "w_kv": np.ascontiguousarray(inp["xa_w_kv"][0]),
                "w_o": np.ascontiguousarray(inp["xa_w_o"][0]), "w_up": np.ascontiguousarray(inp["ffn_w_up"][0]),
                "w_dn": np.ascontiguousarray(inp["ffn_w_down"][0]),
            })
    res = run_bass_kernel_spmd(nc, in_maps, core_ids=list(range(len(in_maps))))
    out = np.zeros((n_batch, 2 * half, D), np.float32)
    for b in range(n_batch):
        for j in range(2):
            out[b, j * half:(j + 1) * half] = res.results[2 * b + j]["yout"]
    return out, res


def kernel(**inputs):
    out, _ = run(inputs, NF=8, NS=7, n_batch=4)
    return out
```
